# Optimizing a Trainium2 kernel written in Bass

```python
import math
import jax, jax.numpy as jnp
from jax import lax
import numpy as np

D_MODEL = 2048
BATCH = 16
SEQ = 2048
DEPTH = 1

CHUNK = 64
Q_BLOCK = 128

A_WIDTH = D_MODEL // 2
A_HEAD_DIM = 64
A_HEADS = A_WIDTH // A_HEAD_DIM
DECAY_LORA = max(32, int(round(1.8 * D_MODEL ** 0.5 / 32)) * 32)
AAA_LORA = max(32, int(round(1.8 * D_MODEL ** 0.5 / 32)) * 32)
GATE_LORA = max(32, int(round(0.6 * D_MODEL ** 0.8 / 32)) * 32)
GN_EPS = 64e-5

B_QK_DIM = 64
B_V_DIM = 2 * B_QK_DIM
B_HEADS = (D_MODEL // 2) // B_V_DIM
B_WIDTH = B_HEADS * B_V_DIM
SUBLN_EPS = 1e-5
N_BUCKETS = 32
MAX_DISTANCE = 128
NEG_INF = -1e30

D_FF = 4 * D_MODEL
RMS_EPS = 1e-6

A_COLS = 3 * A_WIDTH + DECAY_LORA + AAA_LORA + GATE_LORA
B_QK_COLS = 2 * B_HEADS * B_QK_DIM
B_COLS = 2 * B_QK_COLS + B_WIDTH
GATE_COLS = 2 * D_MODEL
IN_COLS = A_COLS + B_COLS + GATE_COLS

kernel_name = "hybrid_rwkv7_diffattn_gated_block"


def rmsnorm(x, g, eps=RMS_EPS):
    xf = x.astype(jnp.float32)
    y = xf * lax.rsqrt(jnp.mean(xf * xf, axis=-1, keepdims=True) + eps)
    return (y * g.astype(jnp.float32)).astype(x.dtype)


def token_shift(z):
    return jnp.pad(z, ((0, 0), (1, 0), (0, 0)))[:, :-1]


def t5_bucket(rel):
    nb = N_BUCKETS // 2
    max_exact = nb // 2
    ret = jnp.where(rel > 0, nb, 0)
    n = jnp.abs(rel)
    nf = jnp.maximum(n, 1).astype(jnp.float32)
    large = max_exact + (jnp.log(nf / max_exact) / math.log(MAX_DISTANCE / max_exact)
                         * (nb - max_exact)).astype(jnp.int32)
    large = jnp.minimum(large, nb - 1)
    return ret + jnp.where(n < max_exact, n, large)


def rwkv7_branch(zA, mu, w0, w_up, a0, a_up, g_up, k_k, k_a, r_k, lnx_g, lnx_b):
    f32 = jnp.float32
    Bz, S, _ = zA.shape
    zs = zA + (token_shift(zA) - zA) * mu
    cut = [A_WIDTH, 2 * A_WIDTH, 3 * A_WIDTH, 3 * A_WIDTH + DECAY_LORA, 3 * A_WIDTH + DECAY_LORA + AAA_LORA]
    r, k, v, wd, ad, gd = jnp.split(zs, cut, axis=-1)
    w = -jax.nn.softplus(-(w0 + jnp.tanh(wd) @ w_up).astype(f32)) - 0.5
    decay = jnp.exp(-jnp.exp(w))
    a = jax.nn.sigmoid((a0 + ad @ a_up).astype(f32))
    g = jax.nn.sigmoid(gd) @ g_up

    def heads(t):
        return t.astype(f32).reshape(Bz, S, A_HEADS, A_HEAD_DIM)

    r, k, v, decay, a = heads(r), heads(k), heads(v), heads(decay), heads(a)
    kk = k * k_k.astype(f32).reshape(A_HEADS, A_HEAD_DIM)
    kk = kk * lax.rsqrt(jnp.maximum(jnp.sum(kk * kk, axis=-1, keepdims=True), 1e-24))
    k = k * (1.0 + (a - 1.0) * k_a.astype(f32).reshape(A_HEADS, A_HEAD_DIM))

    xs = tuple(jnp.moveaxis(t, 1, 0) for t in (r, decay, k, v, -kk, kk * a))

    def step(state, inp):
        r_t, w_t, k_t, v_t, a_t, b_t = inp
        sa = jnp.einsum('bhvk,bhk->bhv', state, a_t)
        state = (state * w_t[:, :, None, :] + sa[..., None] * b_t[:, :, None, :]
                 + v_t[..., None] * k_t[:, :, None, :])
        return state, jnp.einsum('bhvk,bhk->bhv', state, r_t)

    s0 = jnp.zeros((Bz, A_HEADS, A_HEAD_DIM, A_HEAD_DIM), f32)
    _, y = lax.scan(step, s0, xs)
    y = jnp.moveaxis(y, 0, 1)
    mean = jnp.mean(y, axis=-1, keepdims=True)
    var = jnp.mean(jnp.square(y - mean), axis=-1, keepdims=True)
    y = ((y - mean) * lax.rsqrt(var + GN_EPS)).reshape(Bz, S, A_WIDTH)
    y = y * lnx_g.astype(f32) + lnx_b.astype(f32)
    bonus = jnp.sum(r * k * r_k.astype(f32), axis=-1, keepdims=True) * v
    y = y + bonus.reshape(Bz, S, A_WIDTH)
    return y.astype(zA.dtype) * g


def diff_attention_branch(zB, lq1, lk1, lq2, lk2, subln_g, rel_bias, lambda_init):
    f32 = jnp.float32
    Bz, S, _ = zB.shape
    q, k, v = jnp.split(zB, [B_QK_COLS, 2 * B_QK_COLS], axis=-1)
    q = q.reshape(Bz, S, 2 * B_HEADS, B_QK_DIM)
    k = k.reshape(Bz, S, 2 * B_HEADS, B_QK_DIM)
    v = v.reshape(Bz, S, B_HEADS, B_V_DIM)
    lam = (jnp.exp(jnp.sum(lq1.astype(f32) * lk1.astype(f32)))
           - jnp.exp(jnp.sum(lq2.astype(f32) * lk2.astype(f32))) + lambda_init)
    scale = B_QK_DIM ** -0.5
    table = rel_bias.astype(f32)
    outs = []
    for start in range(0, S, Q_BLOCK):
        end = start + Q_BLOCK
        qpos = jnp.arange(start, end)
        kpos = jnp.arange(end)
        logits = jnp.einsum('bqhd,bkhd->bhqk', q[:, start:end], k[:, :end]).astype(f32) * scale
        bias = table[t5_bucket(kpos[None, :] - qpos[:, None])]
        logits = logits + jnp.transpose(bias, (2, 0, 1))[None]
        allowed = (kpos[None, :] // CHUNK) <= (qpos[:, None] // CHUNK)
        logits = jnp.where(allowed, logits, NEG_INF)
        p = jax.nn.softmax(logits, axis=-1).reshape(Bz, B_HEADS, 2, Q_BLOCK, end)
        attn = p[:, :, 0] - lam * p[:, :, 1]
        outs.append(jnp.einsum('bhqk,bkhe->bqhe', attn.astype(v.dtype), v[:, :end]))
    o = jnp.concatenate(outs, axis=1).astype(f32)
    o = o * lax.rsqrt(jnp.mean(o * o, axis=-1, keepdims=True) + SUBLN_EPS) * subln_g.astype(f32)
    o = o * (1.0 - lambda_init)
    return o.reshape(Bz, S, B_WIDTH).astype(zB.dtype)


def setup_inputs(seed: int = 0) -> dict:
    key = jax.random.key(seed)
    ks = jax.random.split(key, 27)
    f32 = jnp.float32
    L = DEPTH

    def nrm(k, shape, scale):
        return jax.random.normal(k, shape, f32) * scale

    return {
        "x": nrm(ks[0], (BATCH, SEQ, D_MODEL), 1.0),
        "norm_mix_g": 1.0 + nrm(ks[1], (L, D_MODEL), 0.02),
        "w_in": nrm(ks[2], (L, D_MODEL, IN_COLS), D_MODEL ** -0.5),
        "mu_shift": jax.random.uniform(ks[3], (L, A_COLS), f32),
        "w0": jax.random.uniform(ks[4], (L, A_WIDTH), f32, -6.0, -1.0),
        "w_up": nrm(ks[5], (L, DECAY_LORA, A_WIDTH), 0.5 * DECAY_LORA ** -0.5),
        "a0": nrm(ks[6], (L, A_WIDTH), 0.5),
        "a_up": nrm(ks[7], (L, AAA_LORA, A_WIDTH), AAA_LORA ** -0.5),
        "g_up": nrm(ks[8], (L, GATE_LORA, A_WIDTH), GATE_LORA ** -0.5),
        "k_k": 0.85 + nrm(ks[9], (L, A_WIDTH), 0.05),
        "k_a": 1.0 + nrm(ks[10], (L, A_WIDTH), 0.05),
        "r_k": nrm(ks[11], (L, A_HEADS, A_HEAD_DIM), 0.1),
        "lnx_g": 1.0 + nrm(ks[12], (L, A_WIDTH), 0.02),
        "lnx_b": nrm(ks[13], (L, A_WIDTH), 0.02),
        "lambda_q1": nrm(ks[14], (L, B_QK_DIM), 0.1),
        "lambda_k1": nrm(ks[15], (L, B_QK_DIM), 0.1),
        "lambda_q2": nrm(ks[16], (L, B_QK_DIM), 0.1),
        "lambda_k2": nrm(ks[17], (L, B_QK_DIM), 0.1),
        "subln_g": 1.0 + nrm(ks[18], (L, B_V_DIM), 0.02),
        "rel_bias": nrm(ks[19], (N_BUCKETS, 2 * B_HEADS), 0.2),
        "p_a": nrm(ks[20], (L, A_WIDTH, D_MODEL), A_WIDTH ** -0.5),
        "p_b": nrm(ks[21], (L, B_WIDTH, D_MODEL), B_WIDTH ** -0.5),
        "w_out": nrm(ks[22], (L, D_MODEL, D_MODEL), D_MODEL ** -0.5),
        "norm_mlp_g": 1.0 + nrm(ks[23], (L, D_MODEL), 0.02),
        "w_ff1": nrm(ks[24], (L, D_MODEL, D_FF), D_MODEL ** -0.5),
        "w_ff2": nrm(ks[25], (L, D_FF, D_MODEL), D_FF ** -0.5),
        "norm_final_g": 1.0 + nrm(ks[26], (D_MODEL,), 0.02),
    }


def reference(x, norm_mix_g, w_in, mu_shift, w0, w_up, a0, a_up, g_up, k_k, k_a, r_k, lnx_g, lnx_b,
              lambda_q1, lambda_k1, lambda_q2, lambda_k2, subln_g, rel_bias, p_a, p_b, w_out,
              norm_mlp_g, w_ff1, w_ff2, norm_final_g):
    h = x
    for l in range(DEPTH):
        lambda_init = 0.8 - 0.6 * math.exp(-0.3 * l)
        u = rmsnorm(h, norm_mix_g[l])
        z = u @ w_in[l]
        zA, zB, zg = jnp.split(z, [A_COLS, A_COLS + B_COLS], axis=-1)
        oA = rwkv7_branch(zA, mu_shift[l], w0[l], w_up[l], a0[l], a_up[l], g_up[l],
                          k_k[l], k_a[l], r_k[l], lnx_g[l], lnx_b[l]) @ p_a[l]
        oB = diff_attention_branch(zB, lambda_q1[l], lambda_k1[l], lambda_q2[l], lambda_k2[l],
                                   subln_g[l], rel_bias, lambda_init) @ p_b[l]
        gA, gB = jnp.split(zg, 2, axis=-1)
        merged = jax.nn.sigmoid(gA) * oA + jax.nn.sigmoid(gB) * oB
        h = h + merged @ w_out[l]
        m = rmsnorm(h, norm_mlp_g[l])
        h = h + jnp.square(jax.nn.relu(m @ w_ff1[l])) @ w_ff2[l]
    return rmsnorm(h, norm_final_g)
```

```python
import numpy as np
import ml_dtypes
import concourse.bass as bass
import concourse.mybir as mybir
from concourse.bass_utils import run_bass_kernel_spmd

F32 = mybir.dt.float32
BF16 = mybir.dt.bfloat16
ALU = mybir.AluOpType
AF = mybir.ActivationFunctionType
AX = mybir.AxisListType


class Dep:
    __slots__ = ("name", "w", "rs")

    def __init__(self, name):
        self.name = name
        self.w = None
        self.rs = []


class Op:
    __slots__ = ("eng", "fn", "deps", "is_dma", "sem", "semval", "need_sig", "sigidx", "prev_same_sem")

    def __init__(self, eng, fn, is_dma):
        self.eng = eng
        self.fn = fn
        self.deps = []
        self.is_dma = is_dma
        self.sem = None
        self.semval = 0
        self.need_sig = False
        self.sigidx = 0
        self.prev_same_sem = None


class SB:
    def __init__(self, handle, dep):
        self.h = handle
        self.full = handle.ap()
        self.d = dep

    def __getitem__(self, k):
        return self.full[k]


ENGS = ("pe", "act", "dve", "pool", "sp")
N_DMA_SEMS = 40


class Prog:
    G = None

    @staticmethod
    def init_global(nc, st):
        g = {}
        g["sems"] = {e: st.enter_context(nc.semaphore("gs_" + e)) for e in ENGS}
        g["dsems"] = [st.enter_context(nc.semaphore("gd_%d" % i)) for i in range(N_DMA_SEMS)]
        g["cnt"] = {e: 0 for e in ENGS}
        g["n_dma"] = 0
        g["dma_last"] = [None] * N_DMA_SEMS
        g["dma_cnt"] = [0] * N_DMA_SEMS
        Prog.G = g

    def __init__(self, nc, st):
        self.nc = nc
        self.st = st
        self.ops = []
        self.deps = {}

    def dep(self, name):
        d = self.deps.get(name)
        if d is None:
            d = Dep(name)
            self.deps[name] = d
        return d

    _uid = [0]

    def sb(self, name, shape, dtype):
        Prog._uid[0] += 1
        h = self.st.enter_context(self.nc.sbuf_tensor("s%d_%s" % (Prog._uid[0], name), list(shape), dtype))
        return SB(h, Dep(name))

    def psum(self, name, dtype=F32):
        n = 512 if dtype == F32 else 1024
        Prog._uid[0] += 1
        h = self.st.enter_context(self.nc.psum_tensor("p%d_%s" % (Prog._uid[0], name), [128, n], dtype))
        return SB(h, Dep(name))

    def _track(self, op, reads, writes):
        ds = set()
        for b in reads:
            if b.w is not None:
                ds.add(b.w)
        for b in writes:
            if b.w is not None:
                ds.add(b.w)
            for r in b.rs:
                ds.add(r)
        ds.discard(op)
        for d in ds:
            if d.eng == "pe" and op.eng == "pe" and not d.is_dma and not op.is_dma:
                continue
            op.deps.append(d)
            d.need_sig = True
        for b in reads:
            b.rs.append(op)
        for b in writes:
            b.w = op
            b.rs = []

    def op(self, eng, fn, reads=(), writes=()):
        o = Op(eng, fn, False)
        self._track(o, reads, writes)
        self.ops.append(o)
        return o

    def dma(self, queue, out, in_, reads=(), writes=(), **kw):
        o = Op(queue, lambda e: e.dma_start(out=out, in_=in_, **kw), True)
        g = Prog.G
        s = g["n_dma"] % N_DMA_SEMS
        g["n_dma"] += 1
        o.sem = s
        g["dma_cnt"][s] += 16
        o.semval = g["dma_cnt"][s]
        o.prev_same_sem = g["dma_last"][s]
        g["dma_last"][s] = o
        self._track(o, reads, writes)
        self.ops.append(o)
        return o

    def emit(self):
        nc = self.nc
        g = Prog.G
        sems = g["sems"]
        dsems = g["dsems"]
        from contextlib import ExitStack
        with ExitStack() as st:
            cnt = g["cnt"]
            for o in self.ops:
                if not o.is_dma and o.need_sig:
                    cnt[o.eng] += 1
                    o.sigidx = cnt[o.eng]
            per_eng = {e: [] for e in ENGS}
            for o in self.ops:
                per_eng[o.eng].append(o)
            block = st.enter_context(nc.Block())

            def run(engname, eng):
                seen = {}

                def wait(sem, val, key):
                    if seen.get(key, 0) >= val:
                        return
                    seen[key] = val
                    eng.wait_ge(sem, val)

                for o in per_eng[engname]:
                    for d in o.deps:
                        if d.is_dma:
                            wait(dsems[d.sem], d.semval, ("d", d.sem))
                        else:
                            wait(sems[d.eng], d.sigidx, ("e", d.eng))
                    if o.is_dma:
                        p = o.prev_same_sem
                        if p is not None:
                            wait(dsems[p.sem], p.semval, ("d", p.sem))
                        o.fn(eng).then_inc(dsems[o.sem], 16)
                    else:
                        ins = o.fn(eng)
                        if o.need_sig:
                            ins.then_inc(sems[engname], 1)
                if engname == "sp":
                    for s in range(N_DMA_SEMS):
                        if g["dma_cnt"][s]:
                            wait(dsems[s], g["dma_cnt"][s], ("d", s))

            @block.tensor
            def _(eng):
                run("pe", eng)

            @block.scalar
            def _(eng):
                run("act", eng)

            @block.vector
            def _(eng):
                run("dve", eng)

            @block.gpsimd
            def _(eng):
                run("pool", eng)

            @block.sync
            def _(eng):
                run("sp", eng)
        self.ops = []


import math
from contextlib import ExitStack

D = 2048
AW = 1024
DL = 96
GL = 256
A_COLS = 3 * AW + 2 * DL + GL
QKC = 1024
IN_COLS = 10688
DFF = 8192
NCORES = 8
C0 = math.exp(-0.5)
LAMBDA_INIT = 0.8 - 0.6 * math.exp(-0.3 * 0)
GN_EPS = 64e-5
SUBLN_EPS = 1e-5
RMS_EPS = 1e-6

VC = {}
_o = 0
for _n, _w in [("g_mix", 16), ("g_mlp", 16), ("g_fin", 16), ("mu_r", 8), ("mu_k", 8), ("mu_v", 8),
               ("mu_wd", 1), ("mu_ad", 1), ("mu_gd", 2), ("w0", 8), ("a0", 8), ("k_k", 8), ("k_a", 8),
               ("r_k", 8), ("lnx_g", 8), ("lnx_b", 8)]:
    VC[_n] = _o
    _o += _w
NV = _o


class Cfg:
    def __init__(self, S=2048, NB=2, phases="ABCDE", dump=(), inject=()):
        self.S = S
        self.NB = NB
        self.T = S * NB
        self.phases = phases
        self.dump = set(dump)
        self.inject = set(inject)


def _pm(v, n):
    return np.ascontiguousarray(np.asarray(v, np.float32).reshape(n, 128).T)


class Res:
    pass


def load_consts(P, dr, names):
    r = Res()
    if "vecs" in names:
        r.vecs = P.sb("vecs", [128, NV], F32)
        P.dma("sp", r.vecs.full, dr["vecs"], writes=[r.vecs.d])
    if "cb" in names:
        r.cb = P.sb("cb", [128, 4 * 512 + 256], BF16)
        P.dma("sp", r.cb.full, dr["cb"], writes=[r.cb.d])
        r.ML = r.cb.full[:, 0:512]
        r.MU = r.cb.full[:, 512:1024]
        r.MUI = r.cb.full[:, 1024:1536]
        r.IB = r.cb.full[:, 1536:2048]
        r.ident = r.cb.full[:, 2048:2176]
        r.ones = r.cb.full[:, 2176:2304]
    if "cf" in names:
        r.cf = P.sb("cf", [128, 640], F32)
        P.dma("sp", r.cf.full, dr["cf"], writes=[r.cf.d])
        r.blockones = r.cf.full[:, 0:128]
        r.scanmask = r.cf.full[:, 128:640]
    return r


def vcol(r, name, i=0):
    c = VC[name] + i
    return r.vecs.full[:, c:c + 1]


class WStream:
    def __init__(self, P, nbuf=3):
        self.P = P
        self.st = [P.sb("wst%d" % i, [128, 16, 128], F32) for i in range(nbuf)]
        self.bf = [P.sb("wbf%d" % i, [128, 16, 128], BF16) for i in range(nbuf)]
        self.i = 0
        self.q = 0

    def load(self, w_dram, k0, col0, width, nk=16):
        P = self.P
        i = self.i % len(self.st)
        self.i += 1
        st, bf = self.st[i], self.bf[i]
        src = w_dram[k0 * 128:(k0 + nk) * 128, col0:col0 + width].rearrange("(kc p) n -> p kc n", p=128)
        P.dma("sp", st.full[:, 0:nk, 0:width], src, writes=[st.d])
        P.op("pool", lambda e, st=st, bf=bf, width=width, nk=nk: e.tensor_copy(bf.full[:, 0:nk, 0:width], st.full[:, 0:nk, 0:width]),
             reads=[st.d], writes=[bf.d])
        return bf


def linear_fm(P, ws, w_dram, KC, blocks, xT, xdeps, NTG, banks, evac, tgw=512):
    kgs = min(16, KC)
    nkg = KC // kgs
    bk = 0
    for bi, (col0, width, tag) in enumerate(blocks):
        pss = []
        for tg in range(NTG):
            pss.append(banks[bk % len(banks)])
            bk += 1
        for kg in range(nkg):
            wb = ws.load(w_dram, kg * kgs, col0, width, kgs)
            for tg in range(NTG):
                ps = pss[tg]
                for kc in range(kgs):
                    kk = kg * kgs + kc
                    P.op("pe", lambda e, ps=ps, wb=wb, kc=kc, kk=kk, tg=tg, width=width:
                         e.matmul(ps.full[0:width, 0:tgw], wb.full[:, kc, 0:width], xT(kk, tg),
                                  start=(kk == 0), stop=(kk == KC - 1)),
                         reads=[wb.d] + xdeps(kk, tg), writes=[ps.d])
        for tg in range(NTG):
            evac(tag, bi, tg, pss[tg], width)


def rmsnorm_to_bf16(P, r, src_dram, c0, ntok, uT, gname, banks_small, xs_bufs, sq, rs, rinv, eps=RMS_EPS, q="act"):
    G = 256
    for gi in range(ntok // G):
        xs = xs_bufs[gi % len(xs_bufs)]
        src = src_dram[:, c0 + gi * G:c0 + (gi + 1) * G].rearrange("(kc p) t -> p kc t", p=128)
        P.dma(q, xs.full, src, writes=[xs.d])
        P.op("act", lambda e, xs=xs: e.activation(sq.full, xs.full, AF.Square), reads=[xs.d], writes=[sq.d])
        ps = banks_small[gi % len(banks_small)]
        for kc in range(16):
            P.op("pe", lambda e, ps=ps, kc=kc: e.matmul(ps.full[:, 0:G], r.ones, sq.full[:, kc, :],
                                                        start=(kc == 0), stop=(kc == 15)),
                 reads=[sq.d, r.cb.d], writes=[ps.d])
        P.op("act", lambda e, ps=ps: e.activation(rs.full, ps.full[:, 0:G], AF.Sqrt, bias=eps, scale=1.0 / D),
             reads=[ps.d], writes=[rs.d])
        P.op("dve", lambda e: e.reciprocal(rinv.full, rs.full), reads=[rs.d], writes=[rinv.d])
        for kc in range(16):
            P.op("dve", lambda e, xs=xs, kc=kc, gi=gi: e.scalar_tensor_tensor(
                uT.full[:, kc, gi * G:(gi + 1) * G], xs.full[:, kc, :], vcol(r, gname, kc), rinv.full,
                ALU.mult, ALU.mult),
                reads=[xs.d, rinv.d, r.vecs.d], writes=[uT.d])


def phase_A(nc, cfg, dr, b):
    S = cfg.S
    c0 = b * S
    NTG = S // 512
    with ExitStack() as st:
        P = Prog(nc, st)
        r = load_consts(P, dr, ["vecs", "cb"])
        uT = P.sb("uT", [128, 16, S], BF16)
        xs_bufs = [P.sb("xs%d" % i, [128, 16, 256], F32) for i in range(2)]
        sq = P.sb("sq", [128, 16, 256], BF16)
        rs = P.sb("rs", [128, 256], F32)
        rinv = P.sb("rinv", [128, 256], F32)
        banks = [P.psum("pm%d" % i) for i in range(6)]
        bsm = [P.psum("psm%d" % i) for i in range(2)]
        ws = WStream(P)
        stf = [P.sb("stf%d" % i, [128, S], F32) for i in range(2)]
        stb = [P.sb("stb%d" % i, [128, S], BF16) for i in range(2)]
        rmsnorm_to_bf16(P, r, dr["xT"], c0, S, uT, "g_mix", bsm, xs_bufs, sq, rs, rinv)

        blocks = []
        for i in range(24):
            blocks.append((i * 128, 128, ("zA", i * 128)))
        blocks.append((3072, 96, ("zA", 3072)))
        blocks.append((3168, 96, ("zA", 3168)))
        blocks.append((3264, 128, ("zA", 3264)))
        blocks.append((3392, 128, ("zA", 3392)))
        for i in range(24):
            blocks.append((A_COLS + i * 128, 128, ("qkv", i * 128)))
        for i in range(32):
            blocks.append((A_COLS + 3072 + i * 128, 128, ("gate", i * 128)))
        cnt = {"f": 0, "b": 0}

        def evac(tag, bi, tg, ps, width):
            kind, row0 = tag
            if kind == "zA":
                sbuf = stf[cnt["f"] % 2]
                P.op("act", lambda e: e.activation(sbuf.full[0:width, tg * 512:(tg + 1) * 512], ps.full[0:width, :], AF.Copy),
                     reads=[ps.d], writes=[sbuf.d])
                if tg == NTG - 1:
                    P.dma("pool", dr["zA"][row0:row0 + width, c0:c0 + S], sbuf.full[0:width, :], reads=[sbuf.d])
                    cnt["f"] += 1
            else:
                sbuf = stb[cnt["b"] % 2]
                fn = AF.Copy if kind == "qkv" else AF.Sigmoid
                P.op("act", lambda e: e.activation(sbuf.full[0:width, tg * 512:(tg + 1) * 512], ps.full[0:width, :], fn),
                     reads=[ps.d], writes=[sbuf.d])
                if tg == NTG - 1:
                    dst = dr["qkv"] if kind == "qkv" else dr["gates"]
                    P.dma("pool", dst[row0:row0 + width, c0:c0 + S], sbuf.full[0:width, :], reads=[sbuf.d])
                    cnt["b"] += 1

        import os
        if os.environ.get("DBG_A") == "norm":
            blocks = []
        elif os.environ.get("DBG_A"):
            blocks = blocks[:int(os.environ["DBG_A"])]
        linear_fm(P, ws, dr["w_in"], 16, blocks, lambda kk, tg: uT.full[:, kk, tg * 512:(tg + 1) * 512],
                  lambda kk, tg: [uT.d], NTG, banks, evac)
        P.emit()
    nc.all_engine_barrier()


SCRATCH = {
    "zA": ([A_COLS, None], F32),
    "qkv": ([3072, None], BF16),
    "gates": ([4096, None], BF16),
    "yA": ([AW, None], BF16),
    "yB": ([AW, None], BF16),
    "h": ([D, None], F32),
}


def build_nc(cfg):
    nc = bass.Bass("TRN2", target_bir_lowering=False)
    T = cfg.T
    dr = {}
    gst = ExitStack()
    Prog.init_global(nc, gst)

    def inp(name, shape, dt=F32):
        dr[name] = nc.dram_tensor(name, list(shape), dt, kind="ExternalInput").ap()

    inp("xT", [D, T])
    if "A" in cfg.phases:
        inp("w_in", [D, IN_COLS])
    if "D" in cfg.phases:
        inp("p_a", [AW, D])
        inp("p_b", [AW, D])
        inp("w_out", [D, D])
    if "E" in cfg.phases:
        inp("w_ff1", [D, DFF])
        inp("w_ff2", [DFF, D])
    if "B" in cfg.phases:
        inp("w_up", [DL, AW])
        inp("a_up", [DL, AW])
        inp("g_up", [GL, AW])
    inp("vecs", [128, NV])
    inp("bc", [128, 16 + 256 + 128])
    inp("biasT", [128, 2 * 16 * 128])
    inp("cb", [128, 4 * 512 + 256], BF16)
    inp("cf", [128, 640])
    for name, (shape, dt) in SCRATCH.items():
        shp = [shape[0], T]
        if name in cfg.inject:
            kind = "ExternalInput"
        elif name in cfg.dump:
            kind = "ExternalOutput"
        else:
            kind = "Internal"
        dr[name] = nc.dram_tensor(name, shp, dt, kind=kind).ap()
    dr["outT"] = nc.dram_tensor("outT", [D, T], F32, kind="ExternalOutput").ap()
    for b in range(cfg.NB):
        if "A" in cfg.phases:
            phase_A(nc, cfg, dr, b)
        if "B" in cfg.phases:
            phase_B(nc, cfg, dr, b)
        if "C" in cfg.phases:
            phase_C(nc, cfg, dr, b)
        if "D" in cfg.phases:
            phase_D(nc, cfg, dr, b)
        if "E" in cfg.phases:
            phase_E(nc, cfg, dr, b)
    return nc


def t5_bucket_np(rel):
    nb = 16
    max_exact = 8
    ret = np.where(rel > 0, nb, 0)
    n = np.abs(rel)
    nf = np.maximum(n, 1).astype(np.float32)
    large = max_exact + (np.log(nf / max_exact) / math.log(128 / max_exact) * (nb - max_exact)).astype(np.int32)
    large = np.minimum(large, nb - 1)
    return ret + np.where(n < max_exact, n, large)


def host_consts(inp):
    f = np.float32
    vecs = np.zeros((128, NV), f)

    def put(name, v, n):
        vecs[:, VC[name]:VC[name] + n] = _pm(v, n)

    put("g_mix", inp["norm_mix_g"][0], 16)
    put("g_mlp", inp["norm_mlp_g"][0], 16)
    put("g_fin", inp["norm_final_g"], 16)
    mu = np.asarray(inp["mu_shift"][0], f)
    put("mu_r", mu[0:1024], 8)
    put("mu_k", mu[1024:2048], 8)
    put("mu_v", mu[2048:3072], 8)
    vecs[0:96, VC["mu_wd"]] = mu[3072:3168]
    vecs[0:96, VC["mu_ad"]] = mu[3168:3264]
    put("mu_gd", mu[3264:3520], 2)
    put("w0", inp["w0"][0], 8)
    put("a0", inp["a0"][0], 8)
    put("k_k", inp["k_k"][0], 8)
    put("k_a", inp["k_a"][0], 8)
    put("r_k", np.asarray(inp["r_k"][0]).reshape(-1), 8)
    put("lnx_g", inp["lnx_g"][0], 8)
    put("lnx_b", inp["lnx_b"][0], 8)

    rb = np.asarray(inp["rel_bias"], f)
    bc = np.zeros((128, 16 + 256 + 128), f)
    bc[:, 0:16] = rb[15][None, :]
    for i, nme in enumerate(["lambda_q1", "lambda_k1", "lambda_q2", "lambda_k2"]):
        bc[:, 16 + 64 * i:16 + 64 * (i + 1)] = np.asarray(inp[nme][0], f)[None, :]
    bc[:, 272:400] = np.asarray(inp["subln_g"][0], f)[None, :]
    kl = np.arange(128)[:, None]
    ql = np.arange(128)[None, :]
    biasT = np.zeros((128, 2, 16, 128), f)
    for ty, off in enumerate([0, -128]):
        bidx = t5_bucket_np(off + kl - ql)
        biasT[:, ty, :, :] = np.transpose(rb[bidx], (0, 2, 1))
    biasT = biasT.reshape(128, -1)
    row = (np.arange(128) % 64)[:, None]
    col = (np.arange(512) % 64)[None, :]
    ML = (col < row).astype(f)
    MU = (row < col).astype(f)
    MUI = (row <= col).astype(f)
    IB = (row == col).astype(f)
    ident = np.eye(128, dtype=f)
    ones = np.ones((128, 128), f)
    cb = np.concatenate([ML, MU, MUI, IB, ident, ones], axis=1).astype(ml_dtypes.bfloat16)
    blockones = np.zeros((128, 128), f)
    blockones[0:64, 0:64] = 1
    blockones[64:, 64:] = 1
    scanmask = np.broadcast_to((np.arange(512) % 64 != 0).astype(f)[None, :], (128, 512))
    cf = np.concatenate([blockones, scanmask], axis=1).astype(f)
    out = {"vecs": vecs, "bc": bc, "biasT": np.ascontiguousarray(biasT), "cb": np.ascontiguousarray(cb),
           "cf": np.ascontiguousarray(cf)}
    for nme in ["w_in", "p_a", "p_b", "w_out", "w_ff1", "w_ff2", "w_up", "a_up", "g_up"]:
        out[nme] = np.ascontiguousarray(np.asarray(inp[nme][0], f))
    return out


def kernel(**inputs):
    cfg = Cfg()
    x = np.asarray(inputs["x"], np.float32)
    B, S, _ = x.shape
    nc = build_nc(cfg)
    shared = host_consts(inputs)
    in_maps = []
    for c in range(NCORES):
        m = dict(shared)
        xs = x[c * cfg.NB:(c + 1) * cfg.NB].reshape(cfg.T, D)
        m["xT"] = np.ascontiguousarray(xs.T)
        in_maps.append(m)
    res = run_bass_kernel_spmd(nc, in_maps, core_ids=list(range(NCORES)))
    out = np.empty((B, S, D), np.float32)
    for c in range(NCORES):
        o = res.results[c]["outT"]
        out[c * cfg.NB:(c + 1) * cfg.NB] = o.T.reshape(cfg.NB, S, D)
    return out


def phase_D(nc, cfg, dr, b):
    S = cfg.S
    c0 = b * S
    NTG = S // 512
    with ExitStack() as st:
        P = Prog(nc, st)
        yAs = P.sb("yAs", [128, 8, S], BF16)
        yBs = P.sb("yBs", [128, 8, S], BF16)
        mT = P.sb("mT", [128, 16, S], BF16)
        P.dma("act", yAs.full, dr["yA"][:, c0:c0 + S].rearrange("(kc p) t -> p kc t", p=128), writes=[yAs.d])
        P.dma("act", yBs.full, dr["yB"][:, c0:c0 + S].rearrange("(kc p) t -> p kc t", p=128), writes=[yBs.d])
        ws = WStream(P, nbuf=2)
        banks = [P.psum("pm%d" % i) for i in range(8)]
        gts = [P.sb("gt%d" % i, [128, 2, S], BF16) for i in range(2)]
        t1 = P.sb("t1", [128, 512], F32)
        t2 = P.sb("t2", [128, 512], F32)
        xts = [P.sb("xt%d" % i, [128, S], F32) for i in range(2)]
        sth = [P.sb("sth%d" % i, [128, S], F32) for i in range(2)]
        for cb in range(16):
            gt = gts[cb % 2]
            P.dma("act", gt.full[:, 0, :], dr["gates"][cb * 128:(cb + 1) * 128, c0:c0 + S], writes=[gt.d])
            P.dma("act", gt.full[:, 1, :], dr["gates"][2048 + cb * 128:2048 + (cb + 1) * 128, c0:c0 + S], writes=[gt.d])
            for tg in range(NTG):
                psA = banks[(2 * (cb * NTG + tg)) % 8]
                psB = banks[(2 * (cb * NTG + tg) + 1) % 8]
                if tg == 0:
                    wa = ws.load(dr["p_a"], 0, cb * 128, 128, 8)
                    wb = ws.load(dr["p_b"], 0, cb * 128, 128, 8)
                for (ps, w, ys) in ((psA, wa, yAs), (psB, wb, yBs)):
                    for kc in range(8):
                        P.op("pe", lambda e, ps=ps, w=w, ys=ys, kc=kc, tg=tg: e.matmul(
                            ps.full[:, :], w.full[:, kc, :], ys.full[:, kc, tg * 512:(tg + 1) * 512],
                            start=(kc == 0), stop=(kc == 7)), reads=[w.d, ys.d], writes=[ps.d])
                P.op("dve", lambda e, psA=psA, gt=gt, tg=tg: e.tensor_tensor(
                    t1.full, psA.full, gt.full[:, 0, tg * 512:(tg + 1) * 512], ALU.mult), reads=[psA.d, gt.d], writes=[t1.d])
                P.op("dve", lambda e, psB=psB, gt=gt, tg=tg: e.tensor_tensor(
                    t2.full, psB.full, gt.full[:, 1, tg * 512:(tg + 1) * 512], ALU.mult), reads=[psB.d, gt.d], writes=[t2.d])
                P.op("dve", lambda e, cb=cb, tg=tg: e.tensor_tensor(
                    mT.full[:, cb, tg * 512:(tg + 1) * 512], t1.full, t2.full, ALU.add), reads=[t1.d, t2.d], writes=[mT.d])
        cnt = [0]

        def evac(tag, bi, tg, ps, width):
            xt = xts[bi % 2]
            sb_ = sth[bi % 2]
            if tg == 0:
                P.dma("act", xt.full, dr["xT"][bi * 128:(bi + 1) * 128, c0:c0 + S], writes=[xt.d])
            P.op("dve", lambda e: e.tensor_tensor(sb_.full[:, tg * 512:(tg + 1) * 512], ps.full,
                                                  xt.full[:, tg * 512:(tg + 1) * 512], ALU.add),
                 reads=[ps.d, xt.d], writes=[sb_.d])
            if tg == NTG - 1:
                P.dma("pool", dr["h"][bi * 128:(bi + 1) * 128, c0:c0 + S], sb_.full, reads=[sb_.d])

        linear_fm(P, ws, dr["w_out"], 16, [(i * 128, 128, None) for i in range(16)],
                  lambda kk, tg: mT.full[:, kk, tg * 512:(tg + 1) * 512], lambda kk, tg: [mT.d], NTG, banks[0:6], evac)
        P.emit()
    nc.all_engine_barrier()


def phase_E(nc, cfg, dr, b):
    S = cfg.S
    NTG = S // 512
    with ExitStack() as st:
        P = Prog(nc, st)
        r = load_consts(P, dr, ["vecs", "cb"])
        mT = P.sb("mT", [128, 16, 512], BF16)
        hid = P.sb("hid", [128, 64, 512], BF16)
        h2 = P.sb("h2", [128, 16, 512], F32)
        xs_bufs = [P.sb("xs0", [128, 16, 256], F32)]
        sq = P.sb("sq", [128, 16, 256], BF16)
        rs = P.sb("rs", [128, 256], F32)
        rinv = P.sb("rinv", [128, 256], F32)
        rl = P.sb("rl", [128, 512], BF16)
        sq2 = P.sb("sq2", [128, 512], BF16)
        rs2 = P.sb("rs2", [128, 512], F32)
        rinv2 = P.sb("rinv2", [128, 512], F32)
        hts = [P.sb("ht%d" % i, [128, 512], F32) for i in range(2)]
        ost = [P.sb("ost%d" % i, [128, 512], F32) for i in range(2)]
        banks = [P.psum("pm%d" % i) for i in range(6)]
        bsm = [P.psum("psm%d" % i) for i in range(2)]
        ws = WStream(P, nbuf=2)
        for tg in range(NTG):
            c0 = b * S + tg * 512
            rmsnorm_to_bf16(P, r, dr["h"], c0, 512, mT, "g_mlp", bsm, xs_bufs, sq, rs, rinv)

            def ev1(tag, bi, tg_, ps, width):
                P.op("act", lambda e: e.activation(rl.full, ps.full, AF.Relu), reads=[ps.d], writes=[rl.d])
                P.op("dve", lambda e: e.tensor_tensor(hid.full[:, bi, :], rl.full, rl.full, ALU.mult),
                     reads=[rl.d], writes=[hid.d])

            linear_fm(P, ws, dr["w_ff1"], 16, [(i * 128, 128, None) for i in range(64)],
                      lambda kk, t: mT.full[:, kk, :], lambda kk, t: [mT.d], 1, banks, ev1)

            def ev2(tag, bi, tg_, ps, width):
                ht = hts[bi % 2]
                P.dma("act", ht.full, dr["h"][bi * 128:(bi + 1) * 128, c0:c0 + 512], writes=[ht.d])
                P.op("dve", lambda e: e.tensor_tensor(h2.full[:, bi, :], ps.full, ht.full, ALU.add),
                     reads=[ps.d, ht.d], writes=[h2.d])

            linear_fm(P, ws, dr["w_ff2"], 64, [(i * 128, 128, None) for i in range(16)],
                      lambda kk, t: hid.full[:, kk, :], lambda kk, t: [hid.d], 1, banks, ev2)
            ps = bsm[0]
            for kc in range(16):
                P.op("act", lambda e, kc=kc: e.activation(sq2.full, h2.full[:, kc, :], AF.Square), reads=[h2.d], writes=[sq2.d])
                P.op("pe", lambda e, kc=kc: e.matmul(ps.full, r.ones, sq2.full, start=(kc == 0), stop=(kc == 15)),
                     reads=[sq2.d, r.cb.d], writes=[ps.d])
            P.op("act", lambda e: e.activation(rs2.full, ps.full, AF.Sqrt, bias=RMS_EPS, scale=1.0 / D), reads=[ps.d], writes=[rs2.d])
            P.op("dve", lambda e: e.reciprocal(rinv2.full, rs2.full), reads=[rs2.d], writes=[rinv2.d])
            for kc in range(16):
                o = ost[kc % 2]
                P.op("dve", lambda e, kc=kc, o=o: e.scalar_tensor_tensor(o.full, h2.full[:, kc, :], vcol(r, "g_fin", kc),
                                                                         rinv2.full, ALU.mult, ALU.mult),
                     reads=[h2.d, rinv2.d, r.vecs.d], writes=[o.d])
                P.dma("pool", dr["outT"][kc * 128:(kc + 1) * 128, c0:c0 + 512], o.full, reads=[o.d])
        P.emit()
    nc.all_engine_barrier()


def phase_C(nc, cfg, dr, b):
    S = cfg.S
    c0 = b * S
    NQB = S // 128
    with ExitStack() as st:
        P = Prog(nc, st)
        r = load_consts(P, dr, ["cb"])
        bc = P.sb("bc", [128, 400], F32)
        P.dma("sp", bc.full, dr["bc"], writes=[bc.d])
        biasT = P.sb("biasT", [128, 2 * 16 * 128], F32)
        P.dma("sp", biasT.full, dr["biasT"], writes=[biasT.d])
        sm = P.sb("sm", [128, 16], F32)
        lt = P.sb("lt", [128, 128], F32)
        sgb = P.sb("sgb", [128, 128], F32)
        P.op("dve", lambda e: e.tensor_tensor(lt.full[:, 0:64], bc.full[:, 16:80], bc.full[:, 80:144], ALU.mult), reads=[bc.d], writes=[lt.d])
        P.op("dve", lambda e: e.tensor_tensor(lt.full[:, 64:128], bc.full[:, 144:208], bc.full[:, 208:272], ALU.mult), reads=[bc.d], writes=[lt.d])
        P.op("dve", lambda e: e.reduce_sum(sm.full[:, 0:1], lt.full[:, 0:64], AX.X), reads=[lt.d], writes=[sm.d])
        P.op("dve", lambda e: e.reduce_sum(sm.full[:, 1:2], lt.full[:, 64:128], AX.X), reads=[lt.d, sm.d], writes=[sm.d])
        P.op("act", lambda e: e.activation(sm.full[:, 2:4], sm.full[:, 0:2], AF.Exp), reads=[sm.d], writes=[sm.d])
        P.op("dve", lambda e: e.tensor_tensor(sm.full[:, 4:5], sm.full[:, 3:4], sm.full[:, 2:3], ALU.subtract), reads=[sm.d], writes=[sm.d])
        P.op("dve", lambda e: e.tensor_scalar(sm.full[:, 5:6], sm.full[:, 4:5], -LAMBDA_INIT, None, ALU.add), reads=[sm.d], writes=[sm.d])
        P.op("dve", lambda e: e.tensor_scalar(sgb.full, bc.full[:, 272:400], 1.0 - LAMBDA_INIT, None, ALU.mult), reads=[bc.d], writes=[sgb.d])
        neglam = sm.full[:, 5:6]
        qTs = [P.sb("qT%d" % i, [128, S], BF16) for i in range(2)]
        kTs = [P.sb("kT%d" % i, [128, S], BF16) for i in range(2)]
        vTs = [P.sb("vT%d" % i, [128, S], BF16) for i in range(2)]
        Vas = [P.sb("Va%d" % i, [128, NQB, 132], BF16) for i in range(2)]
        ysts = [P.sb("yst%d" % i, [128, S], BF16) for i in range(2)]
        STs = [P.psum("ST%d" % i) for i in range(2)]
        Os = [P.psum("O%d" % i) for i in range(4)]
        tpb = P.psum("tpb", BF16)
        tp2 = P.psum("tp2", BF16)
        PTs = [P.sb("PT%d" % i, [128, 512], BF16) for i in range(3)]
        tmps = [P.sb("tmp%d" % i, [128, 128], F32) for i in range(2)]
        rd = [P.sb("rd%d" % i, [128, 8], F32) for i in range(2)]
        o1s = [P.sb("o1_%d" % i, [128, 128], F32) for i in range(2)]
        oos = [P.sb("oo_%d" % i, [128, 128], F32) for i in range(2)]
        junk = P.sb("junk", [128, 128], F32)
        ons = [P.sb("on_%d" % i, [128, 128], BF16) for i in range(2)]
        rot = [0, 0, 0]
        for h in range(8):
            qT, kT, vT, Va, yst = qTs[h % 2], kTs[h % 2], vTs[h % 2], Vas[h % 2], ysts[h % 2]
            P.dma("sp", qT.full, dr["qkv"][h * 128:(h + 1) * 128, c0:c0 + S], writes=[qT.d])
            P.dma("sp", kT.full, dr["qkv"][1024 + h * 128:1024 + (h + 1) * 128, c0:c0 + S], writes=[kT.d])
            P.dma("sp", vT.full, dr["qkv"][2048 + h * 128:2048 + (h + 1) * 128, c0:c0 + S], writes=[vT.d])
            P.op("pool", lambda e, Va=Va: e.memset(Va.full, 1.0), writes=[Va.d])
            for t0 in range(0, NQB, 8):
                n = min(8, NQB - t0)
                for i in range(n):
                    tb = t0 + i
                    P.op("pe", lambda e, i=i, tb=tb, vT=vT: e.transpose(tpb.full[:, i * 128:(i + 1) * 128],
                                                                        vT.full[:, tb * 128:(tb + 1) * 128], r.ident),
                         reads=[vT.d, r.cb.d], writes=[tpb.d])
                P.op("dve", lambda e, t0=t0, n=n, Va=Va: e.tensor_copy(
                    Va.full[:, t0:t0 + n, 0:128], tpb.full[:, 0:n * 128].rearrange("p (a b) -> p a b", b=128)),
                    reads=[tpb.d], writes=[Va.d])
            for qb in range(NQB):
                Opair = (Os[(qb % 2) * 2], Os[(qb % 2) * 2 + 1])
                for j in range(2):
                    m = 2 * h + j
                    O = Opair[j]
                    for g0 in range(0, qb + 1, 4):
                        kbs = list(range(g0, min(g0 + 4, qb + 1)))
                        STb = STs[rot[0] % 2]
                        rot[0] += 1
                        PT = PTs[rot[1] % 3]
                        rot[1] += 1
                        for i, kb in enumerate(kbs):
                            P.op("pe", lambda e, STb=STb, i=i, kb=kb, j=j, qb=qb, kT=kT, qT=qT: e.matmul(
                                STb.full[:, i * 128:(i + 1) * 128], kT.full[64 * j:64 * j + 64, kb * 128:(kb + 1) * 128],
                                qT.full[64 * j:64 * j + 64, qb * 128:(qb + 1) * 128], start=True, stop=True),
                                reads=[kT.d, qT.d], writes=[STb.d])
                        nfar = len([kb for kb in kbs if kb <= qb - 2])
                        if nfar:
                            P.op("act", lambda e, PT=PT, STb=STb, nfar=nfar, m=m: e.activation(
                                PT.full[:, 0:nfar * 128], STb.full[:, 0:nfar * 128], AF.Exp, bias=bc.full[:, m:m + 1], scale=0.125),
                                reads=[bc.d], writes=[PT.d, STb.d])
                        for i, kb in enumerate(kbs):
                            if kb <= qb - 2:
                                continue
                            ty = 0 if kb == qb else 1
                            tmp = tmps[rot[2] % 2]
                            rot[2] += 1
                            bo = (ty * 16 + m) * 128
                            P.op("dve", lambda e, tmp=tmp, STb=STb, i=i, bo=bo: e.scalar_tensor_tensor(
                                tmp.full, STb.full[:, i * 128:(i + 1) * 128], 0.125, biasT.full[:, bo:bo + 128], ALU.mult, ALU.add),
                                reads=[biasT.d], writes=[tmp.d, STb.d])
                            P.op("act", lambda e, tmp=tmp, PT=PT, i=i: e.activation(PT.full[:, i * 128:(i + 1) * 128], tmp.full, AF.Exp),
                                 reads=[tmp.d], writes=[PT.d])
                            if kb == qb:
                                P.op("pool", lambda e, PT=PT, i=i: e.memset(PT.full[64:128, i * 128:i * 128 + 64], 0.0),
                                     reads=[PT.d], writes=[PT.d])
                        for i, kb in enumerate(kbs):
                            P.op("pe", lambda e, O=O, PT=PT, i=i, kb=kb, qb=qb, Va=Va: e.matmul(
                                O.full[:, 0:129], PT.full[:, i * 128:(i + 1) * 128], Va.full[:, kb, 0:129],
                                start=(kb == 0), stop=(kb == qb)), reads=[PT.d, Va.d], writes=[O.d])
                O0, O1 = Opair
                rdt, o1, oo, on = rd[qb % 2], o1s[qb % 2], oos[qb % 2], ons[qb % 2]
                P.op("dve", lambda e, rdt=rdt, O0=O0: e.reciprocal(rdt.full[:, 0:1], O0.full[:, 128:129]), reads=[O0.d], writes=[rdt.d])
                P.op("dve", lambda e, rdt=rdt, O1=O1: e.reciprocal(rdt.full[:, 1:2], O1.full[:, 128:129]), reads=[O1.d, rdt.d], writes=[rdt.d])
                P.op("dve", lambda e, rdt=rdt: e.tensor_tensor(rdt.full[:, 2:3], rdt.full[:, 1:2], neglam, ALU.mult), reads=[rdt.d, sm.d], writes=[rdt.d])
                P.op("dve", lambda e, rdt=rdt, O0=O0, o1=o1: e.tensor_scalar(o1.full, O0.full[:, 0:128], rdt.full[:, 0:1], None, ALU.mult),
                     reads=[O0.d, rdt.d], writes=[o1.d])
                P.op("dve", lambda e, rdt=rdt, O1=O1, o1=o1, oo=oo: e.scalar_tensor_tensor(
                    oo.full, O1.full[:, 0:128], rdt.full[:, 2:3], o1.full, ALU.mult, ALU.add), reads=[O1.d, rdt.d, o1.d], writes=[oo.d])
                P.op("dve", lambda e, oo=oo: e.tensor_tensor(junk.full, oo.full, oo.full, ALU.mult), reads=[oo.d], writes=[junk.d])
                P.op("dve", lambda e, rdt=rdt: e.reduce_sum(rdt.full[:, 3:4], junk.full, AX.X), reads=[junk.d, rdt.d], writes=[rdt.d])
                P.op("act", lambda e, rdt=rdt: e.activation(rdt.full[:, 4:5], rdt.full[:, 3:4], AF.Sqrt, bias=SUBLN_EPS, scale=1.0 / 128),
                     reads=[rdt.d], writes=[rdt.d])
                P.op("dve", lambda e, rdt=rdt: e.reciprocal(rdt.full[:, 5:6], rdt.full[:, 4:5]), reads=[rdt.d], writes=[rdt.d])
                P.op("dve", lambda e, rdt=rdt, oo=oo, on=on: e.scalar_tensor_tensor(
                    on.full, oo.full, rdt.full[:, 5:6], sgb.full, ALU.mult, ALU.mult), reads=[oo.d, rdt.d, sgb.d], writes=[on.d])
                P.op("pe", lambda e, on=on, qb=qb: e.transpose(tp2.full[:, (qb % 8) * 128:(qb % 8 + 1) * 128], on.full, r.ident),
                     reads=[on.d, r.cb.d], writes=[tp2.d])
                P.op("act", lambda e, qb=qb, yst=yst: e.activation(yst.full[:, qb * 128:(qb + 1) * 128],
                                                                   tp2.full[:, (qb % 8) * 128:(qb % 8 + 1) * 128], AF.Copy),
                     reads=[tp2.d], writes=[yst.d])
            P.dma("pool", dr["yB"][h * 128:(h + 1) * 128, c0:c0 + S], yst.full, reads=[yst.d])
        P.emit()
    nc.all_engine_barrier()


def phase_B(nc, cfg, dr, b):
    S = cfg.S
    c0 = b * S
    NSEG = S // 512
    with ExitStack() as st:
        P = Prog(nc, st)
        r = load_consts(P, dr, ["vecs", "cb", "cf"])
        Yseg = P.sb("Yseg", [128, 8, 512], F32)

        class _V:
            pass
        wl_f = _V()
        wl_f.full = Yseg.full.rearrange("p (a b) t -> p a (b t)", b=2)
        wl_f.d = Yseg.d
        wl = P.sb("wl", [128, 4, 1024], BF16)
        P.dma("sp", wl_f.full[0:96, 0, :], dr["w_up"], writes=[wl_f.d])
        P.dma("sp", wl_f.full[0:96, 1, :], dr["a_up"], writes=[wl_f.d])
        P.dma("sp", wl_f.full[:, 2:4, :], dr["g_up"].rearrange("(a p) n -> p a n", p=128), writes=[wl_f.d])
        P.op("dve", lambda e: e.tensor_copy(wl.full[0:96, 0:2, :], wl_f.full[0:96, 0:2, :]), reads=[wl_f.d], writes=[wl.d])
        P.op("dve", lambda e: e.tensor_copy(wl.full[:, 2:4, :], wl_f.full[:, 2:4, :]), reads=[wl_f.d, wl.d], writes=[wl.d])
        Hf = P.sb("Hf", [128, 512], F32)
        Hbs = [P.sb("Hb%d" % i, [128, 512], BF16) for i in range(2)]
        P.op("dve", lambda e: e.memset(Hf.full, 0.0), writes=[Hf.d])
        P.op("dve", lambda e: e.memset(Hbs[0].full, 0.0), writes=[Hbs[0].d])
        pA = [P.psum("pA%d" % i) for i in range(2)]
        pX = [P.psum("pX%d" % i) for i in range(2)]
        tpb = P.psum("tpb", BF16)
        pS = [P.psum("pS%d" % i) for i in range(3)]
        rot = {"pA": 0, "pX": 0, "t": 0}

        def f32t(name, n=1):
            return [P.sb("%s%d" % (name, i), [128, 512], F32) for i in range(n)]

        def bf16t(name, n=1):
            return [P.sb("%s%d" % (name, i), [128, 512], BF16) for i in range(n)]

        zt = [P.sb("zt%d" % i, [128, 513], F32) for i in range(2)]
        dtmp = f32t("dtmp")[0]
        twd = P.sb("twd", [128, 512], BF16)
        lad = P.sb("lad", [128, 512], BF16)
        sgd = P.sb("sgd", [128, 2, 512], BF16)
        lin = f32t("lin")[0]
        rl, kl, vl, sig, aa, kkr, sqk, rn, kk, kp, bb, cw, cwx, E1 = [f32t(n)[0] for n in
            ["rl", "kl", "vl", "sig", "aa", "kkr", "sqk", "rn", "kk", "kp", "bb", "cw", "cwx", "E1"]]
        t1, E0, Ei, rk = rn, cwx, cw, sqk
        bT, kTt, vb = bf16t("bT")[0], bf16t("kTt")[0], bf16t("vb")[0]
        aT = bf16t("aT", 8)
        rT = bf16t("rT", 8)
        bktm = [P.sb("bktm%d" % i, [128, 1024], BF16) for i in range(8)]
        vtm = bf16t("vtm", 8)
        AakT = bf16t("AakT", 8)
        ArbT = bf16t("ArbT", 8)
        ArkT = bf16t("ArkT", 8)
        TT = bf16t("TT", 8)
        gT = bf16t("gT", 8)
        bon = f32t("bon", 8)
        Wc = P.sb("Wc", [128, 8, 8], F32)
        ptmp = bf16t("ptmp", 6)
        Zb = bf16t("Zb")[0]
        Ub = bf16t("Ub")[0]
        tmpH = f32t("tmpH")[0]
        yn = P.sb("yn", [128, 8, 512], BF16)
        gst = P.sb("gst", [128, 256], F32)
        t3 = f32t("t3")[0]
        yo = bf16t("yo", 2)

        def blocks(fn):
            for j in range(2):
                for c in range(8):
                    fn(j, c, slice(64 * j, 64 * j + 64), slice(64 * c, 64 * c + 64))

        def prod(L, R, dst, mask=None, addend=None, eng="dve"):
            ps = pX[rot["pX"] % 2]
            rot["pX"] += 1
            blocks(lambda j, c, pj, cc: P.op("pe", lambda e: e.matmul(ps.full[pj, cc], L.full[pj, cc], R.full[pj, cc], start=True, stop=True),
                                             reads=[L.d, R.d], writes=[ps.d]))
            if mask is not None:
                P.op("dve", lambda e: e.tensor_tensor(dst.full, ps.full, mask, ALU.mult), reads=[ps.d, r.cb.d], writes=[dst.d])
            elif addend is not None:
                P.op("dve", lambda e: e.tensor_tensor(dst.full, ps.full, addend.full, ALU.add), reads=[ps.d, addend.d], writes=[dst.d])
            else:
                P.op("act", lambda e: e.activation(dst.full, ps.full, AF.Copy), reads=[ps.d], writes=[dst.d])

        def load_shift(row0, nrows, sg, mucol, dst_fn):
            z = zt[rot["t"] % 2]
            rot["t"] += 1
            t0 = c0 + sg * 512
            if sg == 0:
                P.op("pool", lambda e: e.memset(z.full[:, 0:1], 0.0), writes=[z.d])
                P.dma("sp", z.full[0:nrows, 1:513], dr["zA"][row0:row0 + nrows, t0:t0 + 512], writes=[z.d])
            else:
                P.dma("sp", z.full[0:nrows, 0:513], dr["zA"][row0:row0 + nrows, t0 - 1:t0 + 512], writes=[z.d])
            P.op("dve", lambda e: e.tensor_tensor(dtmp.full[0:nrows, :], z.full[0:nrows, 0:512], z.full[0:nrows, 1:513], ALU.subtract),
                 reads=[z.d], writes=[dtmp.d])
            dst_fn(z)

        def lerp_to(dst, nrows, mucol):
            def fn(z):
                P.op("dve", lambda e: e.scalar_tensor_tensor(dst.full[0:nrows, :], dtmp.full[0:nrows, :], mucol[0:nrows, :],
                                                             z.full[0:nrows, 1:513], ALU.mult, ALU.add),
                     reads=[dtmp.d, z.d, r.vecs.d], writes=[dst.d])
            return fn

        for sg in range(NSEG):
            t0 = c0 + sg * 512
            load_shift(3072, 96, sg, None, lerp_to(lin, 96, vcol(r, "mu_wd")))
            P.op("act", lambda e: e.activation(twd.full[0:96, :], lin.full[0:96, :], AF.Tanh), reads=[lin.d], writes=[twd.d])
            load_shift(3168, 96, sg, None, lerp_to(lin, 96, vcol(r, "mu_ad")))
            P.op("act", lambda e: e.activation(lad.full[0:96, :], lin.full[0:96, :], AF.Copy), reads=[lin.d], writes=[lad.d])
            for a_ in range(2):
                load_shift(3264 + 128 * a_, 128, sg, None, lerp_to(lin, 128, vcol(r, "mu_gd", a_)))
                P.op("act", lambda e, a_=a_: e.activation(sgd.full[:, a_, :], lin.full, AF.Sigmoid), reads=[lin.d], writes=[sgd.d])
            for hp in range(8):
                cs = slice(hp * 128, hp * 128 + 128)
                load_shift(hp * 128, 128, sg, None, lerp_to(rl, 128, vcol(r, "mu_r", hp)))
                load_shift(1024 + hp * 128, 128, sg, None, lerp_to(kl, 128, vcol(r, "mu_k", hp)))
                load_shift(2048 + hp * 128, 128, sg, None, lerp_to(vl, 128, vcol(r, "mu_v", hp)))
                ps = pA[rot["pA"] % 2]; rot["pA"] += 1
                P.op("pe", lambda e, ps=ps, cs=cs: e.matmul(ps.full, wl.full[0:96, 0, cs], twd.full[0:96, :], start=True, stop=True),
                     reads=[wl.d, twd.d], writes=[ps.d])
                P.op("act", lambda e, ps=ps, hp=hp: e.activation(sig.full, ps.full, AF.Sigmoid, bias=vcol(r, "w0", hp)),
                     reads=[ps.d, r.vecs.d], writes=[sig.d])
                ps = pA[rot["pA"] % 2]; rot["pA"] += 1
                P.op("pe", lambda e, ps=ps, cs=cs: e.matmul(ps.full, wl.full[0:96, 1, cs], lad.full[0:96, :], start=True, stop=True),
                     reads=[wl.d, lad.d], writes=[ps.d])
                P.op("act", lambda e, ps=ps, hp=hp: e.activation(aa.full, ps.full, AF.Sigmoid, bias=vcol(r, "a0", hp)),
                     reads=[ps.d, r.vecs.d], writes=[aa.d])
                ps = pA[rot["pA"] % 2]; rot["pA"] += 1
                for a_ in range(2):
                    P.op("pe", lambda e, ps=ps, cs=cs, a_=a_: e.matmul(ps.full, wl.full[:, 2 + a_, cs], sgd.full[:, a_, :],
                                                                      start=(a_ == 0), stop=(a_ == 1)),
                         reads=[wl.d, sgd.d], writes=[ps.d])
                P.op("act", lambda e, ps=ps, hp=hp: e.activation(gT[hp].full, ps.full, AF.Copy), reads=[ps.d], writes=[gT[hp].d])
                P.op("dve", lambda e, hp=hp: e.tensor_scalar(kkr.full, kl.full, vcol(r, "k_k", hp), None, ALU.mult),
                     reads=[kl.d, r.vecs.d], writes=[kkr.d])
                P.op("dve", lambda e: e.tensor_tensor(sqk.full, kkr.full, kkr.full, ALU.mult), reads=[kkr.d], writes=[sqk.d])
                ps = pA[rot["pA"] % 2]; rot["pA"] += 1
                P.op("pe", lambda e, ps=ps: e.matmul(ps.full, r.blockones, sqk.full, start=True, stop=True),
                     reads=[sqk.d, r.cf.d], writes=[ps.d])
                P.op("dve", lambda e, ps=ps: e.tensor_scalar(rn.full, ps.full, 1e-24, None, ALU.max), reads=[ps.d], writes=[rn.d])
                P.op("act", lambda e: e.activation(rn.full, rn.full, AF.Sqrt), reads=[rn.d], writes=[rn.d])
                P.op("dve", lambda e: e.reciprocal(rn.full, rn.full), reads=[rn.d], writes=[rn.d])
                P.op("dve", lambda e: e.tensor_tensor(kk.full, kkr.full, rn.full, ALU.mult), reads=[kkr.d, rn.d], writes=[kk.d])
                P.op("dve", lambda e, hp=hp: e.tensor_scalar(t1.full, aa.full, -1.0, vcol(r, "k_a", hp), ALU.add, ALU.mult),
                     reads=[aa.d, r.vecs.d], writes=[t1.d])
                P.op("dve", lambda e: e.scalar_tensor_tensor(kp.full, t1.full, 1.0, kl.full, ALU.add, ALU.mult),
                     reads=[t1.d, kl.d], writes=[kp.d])
                P.op("dve", lambda e: e.tensor_tensor(bb.full, kk.full, aa.full, ALU.mult), reads=[kk.d, aa.d], writes=[bb.d])
                P.op("dve", lambda e: e.tensor_tensor_scan(cw.full, r.scanmask, sig.full, 0.0, ALU.mult, ALU.add),
                     reads=[sig.d, r.cf.d], writes=[cw.d])
                P.op("dve", lambda e: e.tensor_tensor(cwx.full, cw.full, sig.full, ALU.subtract), reads=[cw.d, sig.d], writes=[cwx.d])
                P.op("act", lambda e: e.activation(E1.full, cw.full, AF.Exp, scale=-C0), reads=[cw.d], writes=[E1.d])
                P.op("act", lambda e: e.activation(E0.full, cwx.full, AF.Exp, scale=-C0), reads=[cwx.d], writes=[E0.d])
                P.op("act", lambda e: e.activation(Ei.full, cw.full, AF.Exp, scale=C0), reads=[cw.d], writes=[Ei.d])
                P.op("dve", lambda e, hp=hp: e.scalar_tensor_tensor(aT[hp].full, kk.full, -1.0, E0.full, ALU.mult, ALU.mult),
                     reads=[kk.d, E0.d], writes=[aT[hp].d])
                P.op("dve", lambda e, hp=hp: e.tensor_tensor(rT[hp].full, rl.full, E1.full, ALU.mult), reads=[rl.d, E1.d], writes=[rT[hp].d])
                P.op("dve", lambda e: e.tensor_tensor(bT.full, bb.full, Ei.full, ALU.mult), reads=[bb.d, Ei.d], writes=[bT.d])
                P.op("dve", lambda e: e.tensor_tensor(kTt.full, kp.full, Ei.full, ALU.mult), reads=[kp.d, Ei.d], writes=[kTt.d])
                P.op("dve", lambda e, hp=hp: e.tensor_copy(Wc.full[:, hp, :], E1.full.rearrange("p (c t) -> p c t", t=64)[:, :, 63]),
                     reads=[E1.d], writes=[Wc.d])
                P.op("act", lambda e: e.activation(vb.full, vl.full, AF.Copy), reads=[vl.d], writes=[vb.d])
                P.op("dve", lambda e, hp=hp: e.scalar_tensor_tensor(rk.full, rl.full, vcol(r, "r_k", hp), kp.full, ALU.mult, ALU.mult),
                     reads=[rl.d, kp.d, r.vecs.d], writes=[rk.d])
                ps = pA[rot["pA"] % 2]; rot["pA"] += 1
                P.op("pe", lambda e, ps=ps: e.matmul(ps.full, r.blockones, rk.full, start=True, stop=True),
                     reads=[rk.d, r.cf.d], writes=[ps.d])
                P.op("dve", lambda e, ps=ps, hp=hp: e.tensor_tensor(bon[hp].full, ps.full, vl.full, ALU.mult),
                     reads=[ps.d, vl.d], writes=[bon[hp].d])
                for X, off in ((bT, 0), (kTt, 512)):
                    blocks(lambda j, c, pj, cc, X=X, off=off: P.op("pe", lambda e: e.transpose(
                        tpb.full[pj, off + 64 * c:off + 64 * c + 64], X.full[pj, cc], r.ident[pj, pj]),
                        reads=[X.d, r.cb.d], writes=[tpb.d]))
                P.op("dve", lambda e, hp=hp: e.tensor_copy(bktm[hp].full, tpb.full), reads=[tpb.d], writes=[bktm[hp].d])
                blocks(lambda j, c, pj, cc: P.op("pe", lambda e: e.transpose(tpb.full[pj, cc], vb.full[pj, cc], r.ident[pj, pj]),
                                                 reads=[vb.d, r.cb.d], writes=[tpb.d]))
                P.op("dve", lambda e, hp=hp: e.tensor_copy(vtm[hp].full, tpb.full[:, 0:512]), reads=[tpb.d], writes=[vtm[hp].d])
                P0, P0T = ptmp[0], ptmp[1]
                prod(aT[hp], bT, P0, mask=r.ML)
                prod(bT, aT[hp], P0T, mask=r.MU)
                prod(kTt, aT[hp], AakT[hp], mask=r.MU)
                prod(bT, rT[hp], ArbT[hp], mask=r.MUI)
                prod(kTt, rT[hp], ArkT[hp], mask=r.MUI)
                P.op("dve", lambda e, hp=hp: e.tensor_tensor(TT[hp].full, P0T.full, r.IB, ALU.add), reads=[P0T.d, r.cb.d], writes=[TT[hp].d])
                Pc, PcT = P0, P0T
                free = [ptmp[2], ptmp[3], ptmp[4], ptmp[5]]
                for lvl in range(1, 6):
                    Pn = free.pop(0)
                    prod(PcT, Pc, Pn)
                    PnT = None
                    if lvl < 5:
                        PnT = free.pop(0)
                        prod(Pc, PcT, PnT)
                    prod(Pn, TT[hp], TT[hp], addend=TT[hp])
                    free.append(Pc)
                    free.append(PcT)
                    Pc, PcT = Pn, PnT
            for c in range(8):
                gc = sg * 8 + c
                Hc, Hn = Hbs[gc % 2], Hbs[(gc + 1) % 2]
                cc = slice(64 * c, 64 * c + 64)

                def heads(fn):
                    for hp in range(8):
                        for j in range(2):
                            fn(hp, slice(64 * j, 64 * j + 64), slice(64 * hp, 64 * hp + 64))

                heads(lambda hp, pj, hh: (
                    P.op("pe", lambda e, cc=cc, Hc=Hc, c=c: e.matmul(pS[0].full[pj, hh], aT[hp].full[pj, cc], Hc.full[pj, hh], start=True, stop=False),
                         reads=[aT[hp].d, Hc.d], writes=[pS[0].d]),
                    P.op("pe", lambda e, cc=cc, Hc=Hc, c=c: e.matmul(pS[0].full[pj, hh], AakT[hp].full[pj, cc], vtm[hp].full[pj, cc], start=False, stop=True),
                         reads=[AakT[hp].d, vtm[hp].d], writes=[pS[0].d])))
                P.op("act", lambda e: e.activation(Zb.full, pS[0].full, AF.Copy), reads=[pS[0].d], writes=[Zb.d])
                heads(lambda hp, pj, hh: P.op("pe", lambda e, cc=cc, Hc=Hc, c=c: e.matmul(pS[1].full[pj, hh], TT[hp].full[pj, cc], Zb.full[pj, hh], start=True, stop=True),
                                              reads=[TT[hp].d, Zb.d], writes=[pS[1].d]))
                P.op("dve", lambda e: e.tensor_copy(Ub.full, pS[1].full), reads=[pS[1].d], writes=[Ub.d])
                heads(lambda hp, pj, hh: (
                    P.op("pe", lambda e, cc=cc, Hc=Hc, c=c: e.matmul(pS[2].full[pj, hh], rT[hp].full[pj, cc], Hc.full[pj, hh], start=True, stop=False),
                         reads=[rT[hp].d, Hc.d], writes=[pS[2].d]),
                    P.op("pe", lambda e, cc=cc, Hc=Hc, c=c: e.matmul(pS[2].full[pj, hh], ArbT[hp].full[pj, cc], Ub.full[pj, hh], start=False, stop=False),
                         reads=[ArbT[hp].d, Ub.d], writes=[pS[2].d]),
                    P.op("pe", lambda e, cc=cc, Hc=Hc, c=c: e.matmul(pS[2].full[pj, hh], ArkT[hp].full[pj, cc], vtm[hp].full[pj, cc], start=False, stop=True),
                         reads=[ArkT[hp].d, vtm[hp].d], writes=[pS[2].d])))
                P.op("act", lambda e, c=c: e.activation(Yseg.full[:, c, :], pS[2].full, AF.Copy), reads=[pS[2].d], writes=[Yseg.d])
                heads(lambda hp, pj, hh: (
                    P.op("pe", lambda e, cc=cc, Hc=Hc, c=c: e.matmul(pS[0].full[pj, hh], bktm[hp].full[pj, cc], Ub.full[pj, hh], start=True, stop=False),
                         reads=[bktm[hp].d, Ub.d], writes=[pS[0].d]),
                    P.op("pe", lambda e, cc=cc, Hc=Hc, c=c: e.matmul(pS[0].full[pj, hh], bktm[hp].full[pj, 512 + 64 * c:512 + 64 * c + 64], vtm[hp].full[pj, cc],
                                                  start=False, stop=True),
                         reads=[bktm[hp].d, vtm[hp].d], writes=[pS[0].d])))
                P.op("dve", lambda e: e.tensor_tensor(tmpH.full, pS[0].full, Hf.full, ALU.add), reads=[pS[0].d, Hf.d], writes=[tmpH.d])
                P.op("dve", lambda e, c=c: e.tensor_tensor(
                    Hf.full.rearrange("p (h v) -> p h v", v=64), tmpH.full.rearrange("p (h v) -> p h v", v=64),
                    Wc.full[:, :, c:c + 1].broadcast_to([128, 8, 64]), ALU.mult), reads=[tmpH.d, Wc.d], writes=[Hf.d])
                P.op("act", lambda e, Hn=Hn: e.activation(Hn.full, Hf.full, AF.Copy), reads=[Hf.d], writes=[Hn.d])
            Yv = Yseg.full.rearrange("p c (h v) -> p (c h) v", v=64)
            Nv = yn.full.rearrange("p c (h v) -> p (c h) v", v=64)
            P.op("dve", lambda e: e.reduce_sum(gst.full[:, 0:64], Yv, AX.X), reads=[Yseg.d], writes=[gst.d])
            P.op("dve", lambda e: e.tensor_scalar(gst.full[:, 64:128], gst.full[:, 0:64], 1.0 / 64, None, ALU.mult), reads=[gst.d], writes=[gst.d])
            P.op("dve", lambda e: e.tensor_tensor(Yv, Yv, gst.full[:, 64:128].unsqueeze(2).broadcast_to([128, 64, 64]), ALU.subtract),
                 reads=[Yseg.d, gst.d], writes=[Yseg.d])
            P.op("dve", lambda e: e.tensor_tensor(Nv, Yv, Yv, ALU.mult), reads=[Yseg.d], writes=[yn.d])
            P.op("dve", lambda e: e.reduce_sum(gst.full[:, 128:192], Nv, AX.X), reads=[yn.d, gst.d], writes=[gst.d])
            P.op("act", lambda e: e.activation(gst.full[:, 192:256], gst.full[:, 128:192], AF.Sqrt, bias=GN_EPS, scale=1.0 / 64),
                 reads=[gst.d], writes=[gst.d])
            P.op("dve", lambda e: e.reciprocal(gst.full[:, 192:256], gst.full[:, 192:256]), reads=[gst.d], writes=[gst.d])
            P.op("dve", lambda e: e.tensor_tensor(Nv, Yv, gst.full[:, 192:256].unsqueeze(2).broadcast_to([128, 64, 64]), ALU.mult),
                 reads=[Yseg.d, gst.d], writes=[yn.d])
            for hp in range(8):
                blocks(lambda j, c, pj, cc, hp=hp: P.op("pe", lambda e: e.transpose(
                    tpb.full[pj, cc], yn.full[pj, c, 64 * hp:64 * hp + 64], r.ident[pj, pj]), reads=[yn.d, r.cb.d], writes=[tpb.d]))
                y_ = yo[hp % 2]
                P.op("dve", lambda e, hp=hp: e.tensor_scalar(t3.full, tpb.full[:, 0:512], vcol(r, "lnx_g", hp), vcol(r, "lnx_b", hp),
                                                            ALU.mult, ALU.add), reads=[tpb.d, r.vecs.d], writes=[t3.d])
                P.op("dve", lambda e, hp=hp: e.tensor_tensor(t3.full, t3.full, bon[hp].full, ALU.add), reads=[t3.d, bon[hp].d], writes=[t3.d])
                P.op("dve", lambda e, hp=hp, y_=y_: e.tensor_tensor(y_.full, t3.full, gT[hp].full, ALU.mult), reads=[t3.d, gT[hp].d], writes=[y_.d])
                P.dma("pool", dr["yA"][hp * 128:(hp + 1) * 128, t0:t0 + 512], y_.full, reads=[y_.d])
        P.emit()
    nc.all_engine_barrier()
```

```python
import numpy as np
import ml_dtypes
import concourse.bass as bass
import concourse.mybir as mybir
from concourse.bass_utils import run_bass_kernel_spmd

F32 = mybir.dt.float32
BF16 = mybir.dt.bfloat16
ALU = mybir.AluOpType
AF = mybir.ActivationFunctionType
AX = mybir.AxisListType


class Dep:
    __slots__ = ("name", "w", "rs")

    def __init__(self, name):
        self.name = name
        self.w = None
        self.rs = []


class Op:
    __slots__ = ("eng", "fn", "deps", "is_dma", "sem", "semval", "need_sig", "sigidx", "prev_same_sem")

    def __init__(self, eng, fn, is_dma):
        self.eng = eng
        self.fn = fn
        self.deps = []
        self.is_dma = is_dma
        self.sem = None
        self.semval = 0
        self.need_sig = False
        self.sigidx = 0
        self.prev_same_sem = None


class SB:
    def __init__(self, handle, dep):
        self.h = handle
        self.full = handle.ap()
        self.d = dep

    def __getitem__(self, k):
        return self.full[k]


ENGS = ("pe", "act", "dve", "pool", "sp")
N_DMA_SEMS = 40


class Prog:
    G = None

    @staticmethod
    def init_global(nc, st):
        g = {}
        g["sems"] = {e: st.enter_context(nc.semaphore("gs_" + e)) for e in ENGS}
        g["dsems"] = [st.enter_context(nc.semaphore("gd_%d" % i)) for i in range(N_DMA_SEMS)]
        g["cnt"] = {e: 0 for e in ENGS}
        g["n_dma"] = 0
        g["dma_last"] = [None] * N_DMA_SEMS
        g["dma_cnt"] = [0] * N_DMA_SEMS
        Prog.G = g

    def __init__(self, nc, st):
        self.nc = nc
        self.st = st
        self.ops = []
        self.deps = {}

    def dep(self, name):
        d = self.deps.get(name)
        if d is None:
            d = Dep(name)
            self.deps[name] = d
        return d

    _uid = [0]

    def sb(self, name, shape, dtype):
        Prog._uid[0] += 1
        h = self.st.enter_context(self.nc.sbuf_tensor("s%d_%s" % (Prog._uid[0], name), list(shape), dtype))
        return SB(h, Dep(name))

    def psum(self, name, dtype=F32):
        n = 512 if dtype == F32 else 1024
        Prog._uid[0] += 1
        h = self.st.enter_context(self.nc.psum_tensor("p%d_%s" % (Prog._uid[0], name), [128, n], dtype))
        return SB(h, Dep(name))

    def _track(self, op, reads, writes):
        ds = set()
        for b in reads:
            if b.w is not None:
                ds.add(b.w)
        for b in writes:
            if b.w is not None:
                ds.add(b.w)
            for r in b.rs:
                ds.add(r)
        ds.discard(op)
        for d in ds:
            if d.eng == "pe" and op.eng == "pe" and not d.is_dma and not op.is_dma:
                continue
            op.deps.append(d)
            d.need_sig = True
        for b in reads:
            b.rs.append(op)
        for b in writes:
            b.w = op
            b.rs = []

    def op(self, eng, fn, reads=(), writes=()):
        o = Op(eng, fn, False)
        self._track(o, reads, writes)
        self.ops.append(o)
        return o

    def dma(self, queue, out, in_, reads=(), writes=(), **kw):
        o = Op(queue, lambda e: e.dma_start(out=out, in_=in_, **kw), True)
        g = Prog.G
        s = g["n_dma"] % N_DMA_SEMS
        g["n_dma"] += 1
        o.sem = s
        g["dma_cnt"][s] += 16
        o.semval = g["dma_cnt"][s]
        o.prev_same_sem = g["dma_last"][s]
        g["dma_last"][s] = o
        self._track(o, reads, writes)
        self.ops.append(o)
        return o

    def emit(self):
        nc = self.nc
        g = Prog.G
        sems = g["sems"]
        dsems = g["dsems"]
        from contextlib import ExitStack
        with ExitStack() as st:
            cnt = g["cnt"]
            for o in self.ops:
                if not o.is_dma and o.need_sig:
                    cnt[o.eng] += 1
                    o.sigidx = cnt[o.eng]
            per_eng = {e: [] for e in ENGS}
            for o in self.ops:
                per_eng[o.eng].append(o)
            block = st.enter_context(nc.Block())

            def run(engname, eng):
                seen = {}

                def wait(sem, val, key):
                    if seen.get(key, 0) >= val:
                        return
                    seen[key] = val
                    eng.wait_ge(sem, val)

                for o in per_eng[engname]:
                    for d in o.deps:
                        if d.is_dma:
                            wait(dsems[d.sem], d.semval, ("d", d.sem))
                        else:
                            wait(sems[d.eng], d.sigidx, ("e", d.eng))
                    if o.is_dma:
                        p = o.prev_same_sem
                        if p is not None:
                            wait(dsems[p.sem], p.semval, ("d", p.sem))
                        o.fn(eng).then_inc(dsems[o.sem], 16)
                    else:
                        ins = o.fn(eng)
                        if o.need_sig:
                            ins.then_inc(sems[engname], 1)
                if engname == "sp":
                    for s in range(N_DMA_SEMS):
                        if g["dma_cnt"][s]:
                            wait(dsems[s], g["dma_cnt"][s], ("d", s))

            @block.tensor
            def _(eng):
                run("pe", eng)

            @block.scalar
            def _(eng):
                run("act", eng)

            @block.vector
            def _(eng):
                run("dve", eng)

            @block.gpsimd
            def _(eng):
                run("pool", eng)

            @block.sync
            def _(eng):
                run("sp", eng)
        self.ops = []


import math
from contextlib import ExitStack

D = 2048
AW = 1024
DL = 96
GL = 256
A_COLS = 3 * AW + 2 * DL + GL
QKC = 1024
IN_COLS = 10688
DFF = 8192
NCORES = 8
C0 = math.exp(-0.5)
LAMBDA_INIT = 0.8 - 0.6 * math.exp(-0.3 * 0)
GN_EPS = 64e-5
SUBLN_EPS = 1e-5
RMS_EPS = 1e-6

VC = {}
_o = 0
for _n, _w in [("g_mix", 16), ("g_mlp", 16), ("g_fin", 16), ("mu_r", 8), ("mu_k", 8), ("mu_v", 8),
               ("mu_wd", 1), ("mu_ad", 1), ("mu_gd", 2), ("w0", 8), ("a0", 8), ("k_k", 8), ("k_a", 8),
               ("r_k", 8), ("lnx_g", 8), ("lnx_b", 8)]:
    VC[_n] = _o
    _o += _w
NV = _o


class Cfg:
    def __init__(self, S=2048, NB=2, phases="ABCDE", dump=(), inject=()):
        self.S = S
        self.NB = NB
        self.T = S * NB
        self.phases = phases
        self.dump = set(dump)
        self.inject = set(inject)


def _pm(v, n):
    return np.ascontiguousarray(np.asarray(v, np.float32).reshape(n, 128).T)


class Res:
    pass


def load_consts(P, dr, names):
    r = Res()
    if "vecs" in names:
        r.vecs = P.sb("vecs", [128, NV], F32)
        P.dma("sp", r.vecs.full, dr["vecs"], writes=[r.vecs.d])
    if "cb" in names:
        r.cb = P.sb("cb", [128, 4 * 512 + 256], BF16)
        P.dma("sp", r.cb.full, dr["cb"], writes=[r.cb.d])
        r.ML = r.cb.full[:, 0:512]
        r.MU = r.cb.full[:, 512:1024]
        r.MUI = r.cb.full[:, 1024:1536]
        r.IB = r.cb.full[:, 1536:2048]
        r.ident = r.cb.full[:, 2048:2176]
        r.ones = r.cb.full[:, 2176:2304]
    if "cf" in names:
        r.cf = P.sb("cf", [128, 640], F32)
        P.dma("sp", r.cf.full, dr["cf"], writes=[r.cf.d])
        r.blockones = r.cf.full[:, 0:128]
        r.scanmask = r.cf.full[:, 128:640]
    return r


def vcol(r, name, i=0):
    c = VC[name] + i
    return r.vecs.full[:, c:c + 1]


class WStream:
    def __init__(self, P, nbuf=4):
        self.P = P
        self.bf = [P.sb("wbf%d" % i, [128, 16, 128], BF16) for i in range(nbuf)]
        self.i = 0

    def load(self, w_dram, k0, col0, width, nk=16):
        P = self.P
        bf = self.bf[self.i % len(self.bf)]
        self.i += 1
        src = w_dram[k0 * 128:(k0 + nk) * 128, col0:col0 + width].rearrange("(kc p) n -> p kc n", p=128)
        P.dma("pool", bf.full[:, 0:nk, 0:width], src, writes=[bf.d])
        return bf


def linear_fm(P, ws, w_dram, KC, blocks, xT, xdeps, NTG, banks, evac, tgw=512):
    kgs = min(16, KC)
    nkg = KC // kgs
    bk = 0
    for bi, (col0, width, tag) in enumerate(blocks):
        pss = []
        for tg in range(NTG):
            pss.append(banks[bk % len(banks)])
            bk += 1
        for kg in range(nkg):
            wb = ws.load(w_dram, kg * kgs, col0, width, kgs)
            for tg in range(NTG):
                ps = pss[tg]
                for kc in range(kgs):
                    kk = kg * kgs + kc
                    P.op("pe", lambda e, ps=ps, wb=wb, kc=kc, kk=kk, tg=tg, width=width:
                         e.matmul(ps.full[0:width, 0:tgw], wb.full[:, kc, 0:width], xT(kk, tg),
                                  start=(kk == 0), stop=(kk == KC - 1)),
                         reads=[wb.d] + xdeps(kk, tg), writes=[ps.d])
        for tg in range(NTG):
            evac(tag, bi, tg, pss[tg], width)


def rmsnorm_to_bf16(P, r, src_dram, c0, ntok, uT, gname, banks_small, xs_bufs, sq, rs, rinv, eps=RMS_EPS, q="act"):
    G = 256
    for gi in range(ntok // G):
        xs = xs_bufs[gi % len(xs_bufs)]
        src = src_dram[:, c0 + gi * G:c0 + (gi + 1) * G].rearrange("(kc p) t -> p kc t", p=128)
        P.dma(q, xs.full, src, writes=[xs.d])
        P.op("act", lambda e, xs=xs: e.activation(sq.full, xs.full, AF.Square), reads=[xs.d], writes=[sq.d])
        ps = banks_small[gi % len(banks_small)]
        for kc in range(16):
            P.op("pe", lambda e, ps=ps, kc=kc: e.matmul(ps.full[:, 0:G], r.ones, sq.full[:, kc, :],
                                                        start=(kc == 0), stop=(kc == 15)),
                 reads=[sq.d, r.cb.d], writes=[ps.d])
        P.op("act", lambda e, ps=ps: e.activation(rs.full, ps.full[:, 0:G], AF.Sqrt, bias=eps, scale=1.0 / D),
             reads=[ps.d], writes=[rs.d])
        P.op("dve", lambda e: e.reciprocal(rinv.full, rs.full), reads=[rs.d], writes=[rinv.d])
        for kc in range(16):
            P.op("dve", lambda e, xs=xs, kc=kc, gi=gi: e.scalar_tensor_tensor(
                uT.full[:, kc, gi * G:(gi + 1) * G], xs.full[:, kc, :], vcol(r, gname, kc), rinv.full,
                ALU.mult, ALU.mult),
                reads=[xs.d, rinv.d, r.vecs.d], writes=[uT.d])


def phase_A(nc, cfg, dr, b):
    S = cfg.S
    c0 = b * S
    NTG = S // 512
    with ExitStack() as st:
        P = Prog(nc, st)
        r = load_consts(P, dr, ["vecs", "cb"])
        uT = P.sb("uT", [128, 16, S], BF16)
        xs_bufs = [P.sb("xs%d" % i, [128, 16, 256], F32) for i in range(2)]
        sq = P.sb("sq", [128, 16, 256], BF16)
        rs = P.sb("rs", [128, 256], F32)
        rinv = P.sb("rinv", [128, 256], F32)
        banks = [P.psum("pm%d" % i) for i in range(6)]
        bsm = [P.psum("psm%d" % i) for i in range(2)]
        ws = WStream(P)
        stf = [P.sb("stf%d" % i, [128, S], F32) for i in range(2)]
        stb = [P.sb("stb%d" % i, [128, S], BF16) for i in range(2)]
        rmsnorm_to_bf16(P, r, dr["xT"], c0, S, uT, "g_mix", bsm, xs_bufs, sq, rs, rinv)

        blocks = []
        for i in range(24):
            blocks.append((i * 128, 128, ("zA", i * 128)))
        blocks.append((3072, 96, ("zA", 3072)))
        blocks.append((3168, 96, ("zA", 3168)))
        blocks.append((3264, 128, ("zA", 3264)))
        blocks.append((3392, 128, ("zA", 3392)))
        for i in range(24):
            blocks.append((A_COLS + i * 128, 128, ("qkv", i * 128)))
        for i in range(32):
            blocks.append((A_COLS + 3072 + i * 128, 128, ("gate", i * 128)))
        cnt = {"f": 0, "b": 0}

        def evac(tag, bi, tg, ps, width):
            kind, row0 = tag
            if kind == "zA":
                sbuf = stf[cnt["f"] % 2]
                P.op("act", lambda e: e.activation(sbuf.full[0:width, tg * 512:(tg + 1) * 512], ps.full[0:width, :], AF.Copy),
                     reads=[ps.d], writes=[sbuf.d])
                if tg == NTG - 1:
                    P.dma("act", dr["zA"][row0:row0 + width, c0:c0 + S], sbuf.full[0:width, :], reads=[sbuf.d])
                    cnt["f"] += 1
            else:
                sbuf = stb[cnt["b"] % 2]
                fn = AF.Copy if kind == "qkv" else AF.Sigmoid
                P.op("act", lambda e: e.activation(sbuf.full[0:width, tg * 512:(tg + 1) * 512], ps.full[0:width, :], fn),
                     reads=[ps.d], writes=[sbuf.d])
                if tg == NTG - 1:
                    dst = dr["qkv"] if kind == "qkv" else dr["gates"]
                    P.dma("act", dst[row0:row0 + width, c0:c0 + S], sbuf.full[0:width, :], reads=[sbuf.d])
                    cnt["b"] += 1

        import os
        if os.environ.get("DBG_A") == "norm":
            blocks = []
        elif os.environ.get("DBG_A"):
            blocks = blocks[:int(os.environ["DBG_A"])]
        linear_fm(P, ws, dr["w_in"], 16, blocks, lambda kk, tg: uT.full[:, kk, tg * 512:(tg + 1) * 512],
                  lambda kk, tg: [uT.d], NTG, banks, evac)
        P.emit()
    nc.all_engine_barrier()


SCRATCH = {
    "zA": ([A_COLS, None], F32),
    "qkv": ([3072, None], BF16),
    "gates": ([4096, None], BF16),
    "yA": ([AW, None], BF16),
    "yB": ([AW, None], BF16),
    "h": ([D, None], F32),
}


def build_nc(cfg):
    nc = bass.Bass("TRN2", target_bir_lowering=False)
    T = cfg.T
    dr = {}
    gst = ExitStack()
    Prog.init_global(nc, gst)

    def inp(name, shape, dt=F32):
        dr[name] = nc.dram_tensor(name, list(shape), dt, kind="ExternalInput").ap()

    inp("xT", [D, T])
    if "A" in cfg.phases:
        inp("w_in", [D, IN_COLS])
    if "D" in cfg.phases:
        inp("p_a", [AW, D])
        inp("p_b", [AW, D])
        inp("w_out", [D, D])
    if "E" in cfg.phases:
        inp("w_ff1", [D, DFF])
        inp("w_ff2", [DFF, D])
    if "B" in cfg.phases:
        inp("w_up", [DL, AW])
        inp("a_up", [DL, AW])
        inp("g_up", [GL, AW])
    inp("vecs", [128, NV])
    inp("bc", [128, 16 + 256 + 128])
    inp("biasT", [128, 2 * 16 * 128])
    inp("cb", [128, 4 * 512 + 256], BF16)
    inp("cf", [128, 640])
    for name, (shape, dt) in SCRATCH.items():
        shp = [shape[0], T]
        if name in cfg.inject:
            kind = "ExternalInput"
        elif name in cfg.dump:
            kind = "ExternalOutput"
        else:
            kind = "Internal"
        dr[name] = nc.dram_tensor(name, shp, dt, kind=kind).ap()
    dr["outT"] = nc.dram_tensor("outT", [D, T], F32, kind="ExternalOutput").ap()
    for b in range(cfg.NB):
        if "A" in cfg.phases:
            phase_A(nc, cfg, dr, b)
        if "B" in cfg.phases:
            phase_B(nc, cfg, dr, b)
        if "C" in cfg.phases:
            phase_C(nc, cfg, dr, b)
        if "D" in cfg.phases:
            phase_D(nc, cfg, dr, b)
        if "E" in cfg.phases:
            phase_E(nc, cfg, dr, b)
    return nc


def t5_bucket_np(rel):
    nb = 16
    max_exact = 8
    ret = np.where(rel > 0, nb, 0)
    n = np.abs(rel)
    nf = np.maximum(n, 1).astype(np.float32)
    large = max_exact + (np.log(nf / max_exact) / math.log(128 / max_exact) * (nb - max_exact)).astype(np.int32)
    large = np.minimum(large, nb - 1)
    return ret + np.where(n < max_exact, n, large)


def host_consts(inp):
    f = np.float32
    vecs = np.zeros((128, NV), f)

    def put(name, v, n):
        vecs[:, VC[name]:VC[name] + n] = _pm(v, n)

    put("g_mix", inp["norm_mix_g"][0], 16)
    put("g_mlp", inp["norm_mlp_g"][0], 16)
    put("g_fin", inp["norm_final_g"], 16)
    mu = np.asarray(inp["mu_shift"][0], f)
    put("mu_r", mu[0:1024], 8)
    put("mu_k", mu[1024:2048], 8)
    put("mu_v", mu[2048:3072], 8)
    vecs[0:96, VC["mu_wd"]] = mu[3072:3168]
    vecs[0:96, VC["mu_ad"]] = mu[3168:3264]
    put("mu_gd", mu[3264:3520], 2)
    put("w0", inp["w0"][0], 8)
    put("a0", inp["a0"][0], 8)
    put("k_k", inp["k_k"][0], 8)
    put("k_a", inp["k_a"][0], 8)
    put("r_k", np.asarray(inp["r_k"][0]).reshape(-1), 8)
    put("lnx_g", inp["lnx_g"][0], 8)
    put("lnx_b", inp["lnx_b"][0], 8)

    rb = np.asarray(inp["rel_bias"], f)
    bc = np.zeros((128, 16 + 256 + 128), f)
    bc[:, 0:16] = rb[15][None, :]
    for i, nme in enumerate(["lambda_q1", "lambda_k1", "lambda_q2", "lambda_k2"]):
        bc[:, 16 + 64 * i:16 + 64 * (i + 1)] = np.asarray(inp[nme][0], f)[None, :]
    bc[:, 272:400] = np.asarray(inp["subln_g"][0], f)[None, :]
    kl = np.arange(128)[:, None]
    ql = np.arange(128)[None, :]
    biasT = np.zeros((128, 2, 16, 128), f)
    for ty, off in enumerate([0, -128]):
        bidx = t5_bucket_np(off + kl - ql)
        biasT[:, ty, :, :] = np.transpose(rb[bidx], (0, 2, 1))
    biasT = biasT.reshape(128, -1)
    row = (np.arange(128) % 64)[:, None]
    col = (np.arange(512) % 64)[None, :]
    ML = (col < row).astype(f)
    MU = (row < col).astype(f)
    MUI = (row <= col).astype(f)
    IB = (row == col).astype(f)
    ident = np.eye(128, dtype=f)
    ones = np.ones((128, 128), f)
    cb = np.concatenate([ML, MU, MUI, IB, ident, ones], axis=1).astype(ml_dtypes.bfloat16)
    blockones = np.zeros((128, 128), f)
    blockones[0:64, 0:64] = 1
    blockones[64:, 64:] = 1
    scanmask = np.broadcast_to((np.arange(512) % 64 != 0).astype(f)[None, :], (128, 512))
    cf = np.concatenate([blockones, scanmask], axis=1).astype(f)
    out = {"vecs": vecs, "bc": bc, "biasT": np.ascontiguousarray(biasT), "cb": np.ascontiguousarray(cb),
           "cf": np.ascontiguousarray(cf)}
    for nme in ["w_in", "p_a", "p_b", "w_out", "w_ff1", "w_ff2", "w_up", "a_up", "g_up"]:
        out[nme] = np.ascontiguousarray(np.asarray(inp[nme][0], f))
    return out


def kernel(**inputs):
    cfg = Cfg()
    x = np.asarray(inputs["x"], np.float32)
    B, S, _ = x.shape
    nc = build_nc(cfg)
    shared = host_consts(inputs)
    in_maps = []
    for c in range(NCORES):
        m = dict(shared)
        xs = x[c * cfg.NB:(c + 1) * cfg.NB].reshape(cfg.T, D)
        m["xT"] = np.ascontiguousarray(xs.T)
        in_maps.append(m)
    res = run_bass_kernel_spmd(nc, in_maps, core_ids=list(range(NCORES)))
    out = np.empty((B, S, D), np.float32)
    for c in range(NCORES):
        o = res.results[c]["outT"]
        out[c * cfg.NB:(c + 1) * cfg.NB] = o.T.reshape(cfg.NB, S, D)
    return out


def phase_D(nc, cfg, dr, b):
    S = cfg.S
    c0 = b * S
    NTG = S // 512
    with ExitStack() as st:
        P = Prog(nc, st)
        yAs = P.sb("yAs", [128, 8, S], BF16)
        yBs = P.sb("yBs", [128, 8, S], BF16)
        mT = P.sb("mT", [128, 16, S], BF16)
        P.dma("act", yAs.full, dr["yA"][:, c0:c0 + S].rearrange("(kc p) t -> p kc t", p=128), writes=[yAs.d])
        P.dma("act", yBs.full, dr["yB"][:, c0:c0 + S].rearrange("(kc p) t -> p kc t", p=128), writes=[yBs.d])
        ws = WStream(P)
        banks = [P.psum("pm%d" % i) for i in range(8)]
        gts = [P.sb("gt%d" % i, [128, 2, S], BF16) for i in range(2)]
        t1 = P.sb("t1", [128, 512], F32)
        t2 = P.sb("t2", [128, 512], F32)
        xts = [P.sb("xt%d" % i, [128, S], F32) for i in range(2)]
        sth = [P.sb("sth%d" % i, [128, S], F32) for i in range(2)]
        for cb in range(16):
            gt = gts[cb % 2]
            P.dma("act", gt.full[:, 0, :], dr["gates"][cb * 128:(cb + 1) * 128, c0:c0 + S], writes=[gt.d])
            P.dma("act", gt.full[:, 1, :], dr["gates"][2048 + cb * 128:2048 + (cb + 1) * 128, c0:c0 + S], writes=[gt.d])
            for tg in range(NTG):
                psA = banks[(2 * (cb * NTG + tg)) % 8]
                psB = banks[(2 * (cb * NTG + tg) + 1) % 8]
                if tg == 0:
                    wa = ws.load(dr["p_a"], 0, cb * 128, 128, 8)
                    wb = ws.load(dr["p_b"], 0, cb * 128, 128, 8)
                for (ps, w, ys) in ((psA, wa, yAs), (psB, wb, yBs)):
                    for kc in range(8):
                        P.op("pe", lambda e, ps=ps, w=w, ys=ys, kc=kc, tg=tg: e.matmul(
                            ps.full[:, :], w.full[:, kc, :], ys.full[:, kc, tg * 512:(tg + 1) * 512],
                            start=(kc == 0), stop=(kc == 7)), reads=[w.d, ys.d], writes=[ps.d])
                P.op("dve", lambda e, psA=psA, gt=gt, tg=tg: e.tensor_tensor(
                    t1.full, psA.full, gt.full[:, 0, tg * 512:(tg + 1) * 512], ALU.mult), reads=[psA.d, gt.d], writes=[t1.d])
                P.op("dve", lambda e, psB=psB, gt=gt, tg=tg: e.tensor_tensor(
                    t2.full, psB.full, gt.full[:, 1, tg * 512:(tg + 1) * 512], ALU.mult), reads=[psB.d, gt.d], writes=[t2.d])
                P.op("dve", lambda e, cb=cb, tg=tg: e.tensor_tensor(
                    mT.full[:, cb, tg * 512:(tg + 1) * 512], t1.full, t2.full, ALU.add), reads=[t1.d, t2.d], writes=[mT.d])
        cnt = [0]

        def evac(tag, bi, tg, ps, width):
            xt = xts[bi % 2]
            sb_ = sth[bi % 2]
            if tg == 0:
                P.dma("act", xt.full, dr["xT"][bi * 128:(bi + 1) * 128, c0:c0 + S], writes=[xt.d])
            P.op("dve", lambda e: e.tensor_tensor(sb_.full[:, tg * 512:(tg + 1) * 512], ps.full,
                                                  xt.full[:, tg * 512:(tg + 1) * 512], ALU.add),
                 reads=[ps.d, xt.d], writes=[sb_.d])
            if tg == NTG - 1:
                P.dma("sp", dr["h"][bi * 128:(bi + 1) * 128, c0:c0 + S], sb_.full, reads=[sb_.d])

        linear_fm(P, ws, dr["w_out"], 16, [(i * 128, 128, None) for i in range(16)],
                  lambda kk, tg: mT.full[:, kk, tg * 512:(tg + 1) * 512], lambda kk, tg: [mT.d], NTG, banks[0:6], evac)
        P.emit()
    nc.all_engine_barrier()


def phase_E(nc, cfg, dr, b):
    S = cfg.S
    TF = min(1024, S)
    NTG = S // TF
    NH = TF // 512
    with ExitStack() as st:
        P = Prog(nc, st)
        r = load_consts(P, dr, ["vecs", "cb"])
        h2 = P.sb("h2", [128, 16, TF], F32)
        mT = P.sb("mT", [128, 16, TF], BF16)
        hids = [P.sb("hid%d" % i, [128, 16, TF], BF16) for i in range(2)]
        sq = P.sb("sq", [128, 16, 256], BF16)
        rs = P.sb("rs", [128, 256], F32)
        rinv = P.sb("rinv", [128, 256], F32)
        rls = [P.sb("rl%d" % i, [128, 512], BF16) for i in range(2)]
        ost = [P.sb("ost%d" % i, [128, TF], F32) for i in range(2)]
        banks = [P.psum("pm%d" % i) for i in range(6)]
        bsm = [P.psum("psm%d" % i) for i in range(2)]
        ws = WStream(P)
        bk = [0]

        def norm_from_h2(dst_fn, gname, G=256):
            for gi in range(TF // G):
                gs = slice(gi * G, (gi + 1) * G)
                P.op("act", lambda e, gs=gs: e.activation(sq.full, h2.full[:, :, gs], AF.Square), reads=[h2.d], writes=[sq.d])
                ps = bsm[gi % 2]
                for kc in range(16):
                    P.op("pe", lambda e, ps=ps, kc=kc: e.matmul(ps.full[:, 0:G], r.ones, sq.full[:, kc, :], start=(kc == 0), stop=(kc == 15)),
                         reads=[sq.d, r.cb.d], writes=[ps.d])
                P.op("act", lambda e, ps=ps: e.activation(rs.full, ps.full[:, 0:G], AF.Sqrt, bias=RMS_EPS, scale=1.0 / D),
                     reads=[ps.d], writes=[rs.d])
                P.op("dve", lambda e: e.reciprocal(rinv.full, rs.full), reads=[rs.d], writes=[rinv.d])
                for kc in range(16):
                    dst_fn(kc, gs, gi)

        for tg in range(NTG):
            c0 = b * S + tg * TF
            for kc in range(16):
                P.dma("sp", h2.full[:, kc, :], dr["h"][kc * 128:(kc + 1) * 128, c0:c0 + TF], writes=[h2.d])

            def to_m(kc, gs, gi):
                P.op("dve", lambda e: e.scalar_tensor_tensor(mT.full[:, kc, gs], h2.full[:, kc, gs], vcol(r, "g_mlp", kc), rinv.full,
                                                             ALU.mult, ALU.mult), reads=[h2.d, rinv.d, r.vecs.d], writes=[mT.d])

            norm_from_h2(to_m, "g_mlp")
            for g in range(4):
                hid = hids[g % 2]
                for blk in range(16):
                    wb = ws.load(dr["w_ff1"], 0, (g * 16 + blk) * 128, 128, 16)
                    for hf in range(NH):
                        ps = banks[bk[0] % 6]
                        bk[0] += 1
                        hs = slice(hf * 512, (hf + 1) * 512)
                        for kc in range(16):
                            P.op("pe", lambda e, ps=ps, wb=wb, kc=kc, hs=hs: e.matmul(ps.full, wb.full[:, kc, :], mT.full[:, kc, hs],
                                                                                      start=(kc == 0), stop=(kc == 15)),
                                 reads=[wb.d, mT.d], writes=[ps.d])
                        rl = rls[bk[0] % 2]
                        P.op("act", lambda e, ps=ps, rl=rl: e.activation(rl.full, ps.full, AF.Relu), reads=[ps.d], writes=[rl.d])
                        P.op("dve", lambda e, rl=rl, hid=hid, blk=blk, hs=hs: e.tensor_tensor(hid.full[:, blk, hs], rl.full, rl.full, ALU.mult),
                             reads=[rl.d], writes=[hid.d])
                for cb in range(16):
                    wb = ws.load(dr["w_ff2"], g * 16, cb * 128, 128, 16)
                    for hf in range(NH):
                        ps = banks[bk[0] % 6]
                        bk[0] += 1
                        hs = slice(hf * 512, (hf + 1) * 512)
                        for kc in range(16):
                            P.op("pe", lambda e, ps=ps, wb=wb, kc=kc, hs=hs, hid=hid: e.matmul(ps.full, wb.full[:, kc, :], hid.full[:, kc, hs],
                                                                                               start=(kc == 0), stop=(kc == 15)),
                                 reads=[wb.d, hid.d], writes=[ps.d])
                        P.op("dve", lambda e, ps=ps, cb=cb, hs=hs: e.tensor_tensor(h2.full[:, cb, hs], ps.full, h2.full[:, cb, hs], ALU.add),
                             reads=[ps.d, h2.d], writes=[h2.d])

            def to_out(kc, gs, gi):
                o = ost[kc % 2]
                P.op("dve", lambda e: e.scalar_tensor_tensor(o.full[:, gs], h2.full[:, kc, gs], vcol(r, "g_fin", kc), rinv.full,
                                                             ALU.mult, ALU.mult), reads=[h2.d, rinv.d, r.vecs.d], writes=[o.d])
                P.dma("sp", dr["outT"][kc * 128:(kc + 1) * 128, c0 + gs.start:c0 + gs.stop], o.full[:, gs], reads=[o.d])

            norm_from_h2(to_out, "g_fin")
        P.emit()
    nc.all_engine_barrier()


def phase_C(nc, cfg, dr, b):
    S = cfg.S
    c0 = b * S
    NQB = S // 128
    with ExitStack() as st:
        P = Prog(nc, st)
        r = load_consts(P, dr, ["cb"])
        bc = P.sb("bc", [128, 400], F32)
        P.dma("sp", bc.full, dr["bc"], writes=[bc.d])
        biasT = P.sb("biasT", [128, 2 * 16 * 128], F32)
        P.dma("sp", biasT.full, dr["biasT"], writes=[biasT.d])
        sm = P.sb("sm", [128, 16], F32)
        lt = P.sb("lt", [128, 128], F32)
        sgb = P.sb("sgb", [128, 128], F32)
        P.op("dve", lambda e: e.tensor_tensor(lt.full[:, 0:64], bc.full[:, 16:80], bc.full[:, 80:144], ALU.mult), reads=[bc.d], writes=[lt.d])
        P.op("dve", lambda e: e.tensor_tensor(lt.full[:, 64:128], bc.full[:, 144:208], bc.full[:, 208:272], ALU.mult), reads=[bc.d], writes=[lt.d])
        P.op("dve", lambda e: e.reduce_sum(sm.full[:, 0:1], lt.full[:, 0:64], AX.X), reads=[lt.d], writes=[sm.d])
        P.op("dve", lambda e: e.reduce_sum(sm.full[:, 1:2], lt.full[:, 64:128], AX.X), reads=[lt.d, sm.d], writes=[sm.d])
        P.op("act", lambda e: e.activation(sm.full[:, 2:4], sm.full[:, 0:2], AF.Exp), reads=[sm.d], writes=[sm.d])
        P.op("dve", lambda e: e.tensor_tensor(sm.full[:, 4:5], sm.full[:, 3:4], sm.full[:, 2:3], ALU.subtract), reads=[sm.d], writes=[sm.d])
        P.op("dve", lambda e: e.tensor_scalar(sm.full[:, 5:6], sm.full[:, 4:5], -LAMBDA_INIT, None, ALU.add), reads=[sm.d], writes=[sm.d])
        P.op("dve", lambda e: e.tensor_scalar(sgb.full, bc.full[:, 272:400], 1.0 - LAMBDA_INIT, None, ALU.mult), reads=[bc.d], writes=[sgb.d])
        neglam = sm.full[:, 5:6]
        qTs = [P.sb("qT%d" % i, [128, S], BF16) for i in range(2)]
        kTs = [P.sb("kT%d" % i, [128, S], BF16) for i in range(2)]
        vTs = [P.sb("vT%d" % i, [128, S], BF16) for i in range(2)]
        Vas = [P.sb("Va%d" % i, [128, NQB, 132], BF16) for i in range(2)]
        ysts = [P.sb("yst%d" % i, [128, S], BF16) for i in range(2)]
        STs = [P.psum("ST%d" % i) for i in range(2)]
        Os = [P.psum("O%d" % i) for i in range(4)]
        tpb = P.psum("tpb", BF16)
        tp2 = P.psum("tp2", BF16)
        PTs = [P.sb("PT%d" % i, [128, 512], BF16) for i in range(3)]
        tmps = [P.sb("tmp%d" % i, [128, 128], F32) for i in range(2)]
        rd = [P.sb("rd%d" % i, [128, 8], F32) for i in range(2)]
        o1s = [P.sb("o1_%d" % i, [128, 128], F32) for i in range(2)]
        oos = [P.sb("oo_%d" % i, [128, 128], F32) for i in range(2)]
        junk = P.sb("junk", [128, 128], F32)
        ons = [P.sb("on_%d" % i, [128, 128], BF16) for i in range(2)]
        rot = [0, 0, 0]
        for h in range(8):
            qT, kT, vT, Va, yst = qTs[h % 2], kTs[h % 2], vTs[h % 2], Vas[h % 2], ysts[h % 2]
            P.dma("sp", qT.full, dr["qkv"][h * 128:(h + 1) * 128, c0:c0 + S], writes=[qT.d])
            P.dma("sp", kT.full, dr["qkv"][1024 + h * 128:1024 + (h + 1) * 128, c0:c0 + S], writes=[kT.d])
            P.dma("sp", vT.full, dr["qkv"][2048 + h * 128:2048 + (h + 1) * 128, c0:c0 + S], writes=[vT.d])
            P.op("pool", lambda e, Va=Va: e.memset(Va.full, 1.0), writes=[Va.d])
            for t0 in range(0, NQB, 8):
                n = min(8, NQB - t0)
                for i in range(n):
                    tb = t0 + i
                    P.op("pe", lambda e, i=i, tb=tb, vT=vT: e.transpose(tpb.full[:, i * 128:(i + 1) * 128],
                                                                        vT.full[:, tb * 128:(tb + 1) * 128], r.ident),
                         reads=[vT.d, r.cb.d], writes=[tpb.d])
                P.op("dve", lambda e, t0=t0, n=n, Va=Va: e.tensor_copy(
                    Va.full[:, t0:t0 + n, 0:128], tpb.full[:, 0:n * 128].rearrange("p (a b) -> p a b", b=128)),
                    reads=[tpb.d], writes=[Va.d])
            for qb in range(NQB):
                Opair = (Os[(qb % 2) * 2], Os[(qb % 2) * 2 + 1])
                for j in range(2):
                    m = 2 * h + j
                    O = Opair[j]
                    for g0 in range(0, qb + 1, 4):
                        kbs = list(range(g0, min(g0 + 4, qb + 1)))
                        STb = STs[rot[0] % 2]
                        rot[0] += 1
                        PT = PTs[rot[1] % 3]
                        rot[1] += 1
                        for i, kb in enumerate(kbs):
                            P.op("pe", lambda e, STb=STb, i=i, kb=kb, j=j, qb=qb, kT=kT, qT=qT: e.matmul(
                                STb.full[:, i * 128:(i + 1) * 128], kT.full[64 * j:64 * j + 64, kb * 128:(kb + 1) * 128],
                                qT.full[64 * j:64 * j + 64, qb * 128:(qb + 1) * 128], start=True, stop=True),
                                reads=[kT.d, qT.d], writes=[STb.d])
                        nfar = len([kb for kb in kbs if kb <= qb - 2])
                        if nfar:
                            P.op("act", lambda e, PT=PT, STb=STb, nfar=nfar, m=m: e.activation(
                                PT.full[:, 0:nfar * 128], STb.full[:, 0:nfar * 128], AF.Exp, bias=bc.full[:, m:m + 1], scale=0.125),
                                reads=[bc.d], writes=[PT.d, STb.d])
                        for i, kb in enumerate(kbs):
                            if kb <= qb - 2:
                                continue
                            ty = 0 if kb == qb else 1
                            tmp = tmps[rot[2] % 2]
                            rot[2] += 1
                            bo = (ty * 16 + m) * 128
                            P.op("dve", lambda e, tmp=tmp, STb=STb, i=i, bo=bo: e.scalar_tensor_tensor(
                                tmp.full, STb.full[:, i * 128:(i + 1) * 128], 0.125, biasT.full[:, bo:bo + 128], ALU.mult, ALU.add),
                                reads=[biasT.d], writes=[tmp.d, STb.d])
                            P.op("act", lambda e, tmp=tmp, PT=PT, i=i: e.activation(PT.full[:, i * 128:(i + 1) * 128], tmp.full, AF.Exp),
                                 reads=[tmp.d], writes=[PT.d])
                            if kb == qb:
                                P.op("pool", lambda e, PT=PT, i=i: e.memset(PT.full[64:128, i * 128:i * 128 + 64], 0.0),
                                     reads=[PT.d], writes=[PT.d])
                        for i, kb in enumerate(kbs):
                            P.op("pe", lambda e, O=O, PT=PT, i=i, kb=kb, qb=qb, Va=Va: e.matmul(
                                O.full[:, 0:129], PT.full[:, i * 128:(i + 1) * 128], Va.full[:, kb, 0:129],
                                start=(kb == 0), stop=(kb == qb)), reads=[PT.d, Va.d], writes=[O.d])
                O0, O1 = Opair
                rdt, o1, oo, on = rd[qb % 2], o1s[qb % 2], oos[qb % 2], ons[qb % 2]
                P.op("dve", lambda e, rdt=rdt, O0=O0: e.reciprocal(rdt.full[:, 0:1], O0.full[:, 128:129]), reads=[O0.d], writes=[rdt.d])
                P.op("dve", lambda e, rdt=rdt, O1=O1: e.reciprocal(rdt.full[:, 1:2], O1.full[:, 128:129]), reads=[O1.d, rdt.d], writes=[rdt.d])
                P.op("dve", lambda e, rdt=rdt: e.tensor_tensor(rdt.full[:, 2:3], rdt.full[:, 1:2], neglam, ALU.mult), reads=[rdt.d, sm.d], writes=[rdt.d])
                P.op("dve", lambda e, rdt=rdt, O0=O0, o1=o1: e.tensor_scalar(o1.full, O0.full[:, 0:128], rdt.full[:, 0:1], None, ALU.mult),
                     reads=[O0.d, rdt.d], writes=[o1.d])
                P.op("dve", lambda e, rdt=rdt, O1=O1, o1=o1, oo=oo: e.scalar_tensor_tensor(
                    oo.full, O1.full[:, 0:128], rdt.full[:, 2:3], o1.full, ALU.mult, ALU.add), reads=[O1.d, rdt.d, o1.d], writes=[oo.d])
                P.op("dve", lambda e, oo=oo: e.tensor_tensor(junk.full, oo.full, oo.full, ALU.mult), reads=[oo.d], writes=[junk.d])
                P.op("dve", lambda e, rdt=rdt: e.reduce_sum(rdt.full[:, 3:4], junk.full, AX.X), reads=[junk.d, rdt.d], writes=[rdt.d])
                P.op("act", lambda e, rdt=rdt: e.activation(rdt.full[:, 4:5], rdt.full[:, 3:4], AF.Sqrt, bias=SUBLN_EPS, scale=1.0 / 128),
                     reads=[rdt.d], writes=[rdt.d])
                P.op("dve", lambda e, rdt=rdt: e.reciprocal(rdt.full[:, 5:6], rdt.full[:, 4:5]), reads=[rdt.d], writes=[rdt.d])
                P.op("dve", lambda e, rdt=rdt, oo=oo, on=on: e.scalar_tensor_tensor(
                    on.full, oo.full, rdt.full[:, 5:6], sgb.full, ALU.mult, ALU.mult), reads=[oo.d, rdt.d, sgb.d], writes=[on.d])
                P.op("pe", lambda e, on=on, qb=qb: e.transpose(tp2.full[:, (qb % 8) * 128:(qb % 8 + 1) * 128], on.full, r.ident),
                     reads=[on.d, r.cb.d], writes=[tp2.d])
                P.op("act", lambda e, qb=qb, yst=yst: e.activation(yst.full[:, qb * 128:(qb + 1) * 128],
                                                                   tp2.full[:, (qb % 8) * 128:(qb % 8 + 1) * 128], AF.Copy),
                     reads=[tp2.d], writes=[yst.d])
            P.dma("act", dr["yB"][h * 128:(h + 1) * 128, c0:c0 + S], yst.full, reads=[yst.d])
        P.emit()
    nc.all_engine_barrier()


def phase_B(nc, cfg, dr, b):
    S = cfg.S
    c0 = b * S
    NSEG = S // 512
    with ExitStack() as st:
        P = Prog(nc, st)
        r = load_consts(P, dr, ["vecs", "cb", "cf"])
        Yseg = P.sb("Yseg", [128, 8, 512], F32)

        class _V:
            pass
        wl_f = _V()
        wl_f.full = Yseg.full.rearrange("p (a b) t -> p a (b t)", b=2)
        wl_f.d = Yseg.d
        wl = P.sb("wl", [128, 4, 1024], BF16)
        P.dma("sp", wl_f.full[0:96, 0, :], dr["w_up"], writes=[wl_f.d])
        P.dma("sp", wl_f.full[0:96, 1, :], dr["a_up"], writes=[wl_f.d])
        P.dma("sp", wl_f.full[:, 2:4, :], dr["g_up"].rearrange("(a p) n -> p a n", p=128), writes=[wl_f.d])
        P.op("dve", lambda e: e.tensor_copy(wl.full[0:96, 0:2, :], wl_f.full[0:96, 0:2, :]), reads=[wl_f.d], writes=[wl.d])
        P.op("dve", lambda e: e.tensor_copy(wl.full[:, 2:4, :], wl_f.full[:, 2:4, :]), reads=[wl_f.d, wl.d], writes=[wl.d])
        Hf = P.sb("Hf", [128, 512], F32)
        Hbs = [P.sb("Hb%d" % i, [128, 512], BF16) for i in range(2)]
        P.op("dve", lambda e: e.memset(Hf.full, 0.0), writes=[Hf.d])
        P.op("dve", lambda e: e.memset(Hbs[0].full, 0.0), writes=[Hbs[0].d])
        pA = [P.psum("pA%d" % i) for i in range(2)]
        pX = [P.psum("pX%d" % i) for i in range(2)]
        tpb = P.psum("tpb", BF16)
        pS = [P.psum("pS%d" % i) for i in range(3)]
        rot = {"pA": 0, "pX": 0, "t": 0}

        def f32t(name, n=1):
            return [P.sb("%s%d" % (name, i), [128, 512], F32) for i in range(n)]

        def bf16t(name, n=1):
            return [P.sb("%s%d" % (name, i), [128, 512], BF16) for i in range(n)]

        zt = [P.sb("zt%d" % i, [128, 513], F32) for i in range(2)]
        dtmp = f32t("dtmp")[0]
        twd = P.sb("twd", [128, 512], BF16)
        lad = P.sb("lad", [128, 512], BF16)
        sgd = P.sb("sgd", [128, 2, 512], BF16)
        lin = f32t("lin")[0]
        rl, kl, vl, sig, aa, kkr, sqk, rn, kk, kp, bb, cw, cwx, E1 = [f32t(n)[0] for n in
            ["rl", "kl", "vl", "sig", "aa", "kkr", "sqk", "rn", "kk", "kp", "bb", "cw", "cwx", "E1"]]
        t1, E0, Ei, rk = rn, cwx, cw, sqk
        bT, kTt, vb = bf16t("bT")[0], bf16t("kTt")[0], bf16t("vb")[0]
        aT = bf16t("aT", 8)
        rT = bf16t("rT", 8)
        bktm = [P.sb("bktm%d" % i, [128, 1024], BF16) for i in range(8)]
        vtm = bf16t("vtm", 8)
        AakT = bf16t("AakT", 8)
        ArbT = bf16t("ArbT", 8)
        ArkT = bf16t("ArkT", 8)
        TT = bf16t("TT", 8)
        gT = bf16t("gT", 8)
        bon = f32t("bon", 8)
        Wc = P.sb("Wc", [128, 8, 8], F32)
        ptmp = bf16t("ptmp", 6)
        Zb = bf16t("Zb")[0]
        Ub = bf16t("Ub")[0]
        tmpH = f32t("tmpH")[0]
        yn = P.sb("yn", [128, 8, 512], BF16)
        gst = P.sb("gst", [128, 256], F32)
        t3 = f32t("t3")[0]
        yo = bf16t("yo", 2)

        def blocks(fn):
            for j in range(2):
                for c in range(8):
                    fn(j, c, slice(64 * j, 64 * j + 64), slice(64 * c, 64 * c + 64))

        def prod(L, R, dst, mask=None, addend=None, eng="dve"):
            ps = pX[rot["pX"] % 2]
            rot["pX"] += 1
            blocks(lambda j, c, pj, cc: P.op("pe", lambda e: e.matmul(ps.full[pj, cc], L.full[pj, cc], R.full[pj, cc], start=True, stop=True),
                                             reads=[L.d, R.d], writes=[ps.d]))
            if mask is not None:
                P.op("dve", lambda e: e.tensor_tensor(dst.full, ps.full, mask, ALU.mult), reads=[ps.d, r.cb.d], writes=[dst.d])
            elif addend is not None:
                P.op("dve", lambda e: e.tensor_tensor(dst.full, ps.full, addend.full, ALU.add), reads=[ps.d, addend.d], writes=[dst.d])
            else:
                P.op("act", lambda e: e.activation(dst.full, ps.full, AF.Copy), reads=[ps.d], writes=[dst.d])

        def load_shift(row0, nrows, sg, mucol, dst_fn):
            z = zt[rot["t"] % 2]
            rot["t"] += 1
            t0 = c0 + sg * 512
            if sg == 0:
                P.op("pool", lambda e: e.memset(z.full[:, 0:1], 0.0), writes=[z.d])
                P.dma("sp", z.full[0:nrows, 1:513], dr["zA"][row0:row0 + nrows, t0:t0 + 512], writes=[z.d])
            else:
                P.dma("sp", z.full[0:nrows, 0:513], dr["zA"][row0:row0 + nrows, t0 - 1:t0 + 512], writes=[z.d])
            P.op("dve", lambda e: e.tensor_tensor(dtmp.full[0:nrows, :], z.full[0:nrows, 0:512], z.full[0:nrows, 1:513], ALU.subtract),
                 reads=[z.d], writes=[dtmp.d])
            dst_fn(z)

        def lerp_to(dst, nrows, mucol):
            def fn(z):
                P.op("dve", lambda e: e.scalar_tensor_tensor(dst.full[0:nrows, :], dtmp.full[0:nrows, :], mucol[0:nrows, :],
                                                             z.full[0:nrows, 1:513], ALU.mult, ALU.add),
                     reads=[dtmp.d, z.d, r.vecs.d], writes=[dst.d])
            return fn

        for sg in range(NSEG):
            t0 = c0 + sg * 512
            load_shift(3072, 96, sg, None, lerp_to(lin, 96, vcol(r, "mu_wd")))
            P.op("act", lambda e: e.activation(twd.full[0:96, :], lin.full[0:96, :], AF.Tanh), reads=[lin.d], writes=[twd.d])
            load_shift(3168, 96, sg, None, lerp_to(lin, 96, vcol(r, "mu_ad")))
            P.op("act", lambda e: e.activation(lad.full[0:96, :], lin.full[0:96, :], AF.Copy), reads=[lin.d], writes=[lad.d])
            for a_ in range(2):
                load_shift(3264 + 128 * a_, 128, sg, None, lerp_to(lin, 128, vcol(r, "mu_gd", a_)))
                P.op("act", lambda e, a_=a_: e.activation(sgd.full[:, a_, :], lin.full, AF.Sigmoid), reads=[lin.d], writes=[sgd.d])
            for hp in range(8):
                cs = slice(hp * 128, hp * 128 + 128)
                load_shift(hp * 128, 128, sg, None, lerp_to(rl, 128, vcol(r, "mu_r", hp)))
                load_shift(1024 + hp * 128, 128, sg, None, lerp_to(kl, 128, vcol(r, "mu_k", hp)))
                load_shift(2048 + hp * 128, 128, sg, None, lerp_to(vl, 128, vcol(r, "mu_v", hp)))
                ps = pA[rot["pA"] % 2]; rot["pA"] += 1
                P.op("pe", lambda e, ps=ps, cs=cs: e.matmul(ps.full, wl.full[0:96, 0, cs], twd.full[0:96, :], start=True, stop=True),
                     reads=[wl.d, twd.d], writes=[ps.d])
                P.op("act", lambda e, ps=ps, hp=hp: e.activation(sig.full, ps.full, AF.Sigmoid, bias=vcol(r, "w0", hp)),
                     reads=[ps.d, r.vecs.d], writes=[sig.d])
                ps = pA[rot["pA"] % 2]; rot["pA"] += 1
                P.op("pe", lambda e, ps=ps, cs=cs: e.matmul(ps.full, wl.full[0:96, 1, cs], lad.full[0:96, :], start=True, stop=True),
                     reads=[wl.d, lad.d], writes=[ps.d])
                P.op("act", lambda e, ps=ps, hp=hp: e.activation(aa.full, ps.full, AF.Sigmoid, bias=vcol(r, "a0", hp)),
                     reads=[ps.d, r.vecs.d], writes=[aa.d])
                ps = pA[rot["pA"] % 2]; rot["pA"] += 1
                for a_ in range(2):
                    P.op("pe", lambda e, ps=ps, cs=cs, a_=a_: e.matmul(ps.full, wl.full[:, 2 + a_, cs], sgd.full[:, a_, :],
                                                                      start=(a_ == 0), stop=(a_ == 1)),
                         reads=[wl.d, sgd.d], writes=[ps.d])
                P.op("act", lambda e, ps=ps, hp=hp: e.activation(gT[hp].full, ps.full, AF.Copy), reads=[ps.d], writes=[gT[hp].d])
                P.op("dve", lambda e, hp=hp: e.tensor_scalar(kkr.full, kl.full, vcol(r, "k_k", hp), None, ALU.mult),
                     reads=[kl.d, r.vecs.d], writes=[kkr.d])
                P.op("dve", lambda e: e.tensor_tensor(sqk.full, kkr.full, kkr.full, ALU.mult), reads=[kkr.d], writes=[sqk.d])
                ps = pA[rot["pA"] % 2]; rot["pA"] += 1
                P.op("pe", lambda e, ps=ps: e.matmul(ps.full, r.blockones, sqk.full, start=True, stop=True),
                     reads=[sqk.d, r.cf.d], writes=[ps.d])
                P.op("dve", lambda e, ps=ps: e.tensor_scalar(rn.full, ps.full, 1e-24, None, ALU.max), reads=[ps.d], writes=[rn.d])
                P.op("act", lambda e: e.activation(rn.full, rn.full, AF.Sqrt), reads=[rn.d], writes=[rn.d])
                P.op("dve", lambda e: e.reciprocal(rn.full, rn.full), reads=[rn.d], writes=[rn.d])
                P.op("dve", lambda e: e.tensor_tensor(kk.full, kkr.full, rn.full, ALU.mult), reads=[kkr.d, rn.d], writes=[kk.d])
                P.op("dve", lambda e, hp=hp: e.tensor_scalar(t1.full, aa.full, -1.0, vcol(r, "k_a", hp), ALU.add, ALU.mult),
                     reads=[aa.d, r.vecs.d], writes=[t1.d])
                P.op("dve", lambda e: e.scalar_tensor_tensor(kp.full, t1.full, 1.0, kl.full, ALU.add, ALU.mult),
                     reads=[t1.d, kl.d], writes=[kp.d])
                P.op("dve", lambda e: e.tensor_tensor(bb.full, kk.full, aa.full, ALU.mult), reads=[kk.d, aa.d], writes=[bb.d])
                P.op("dve", lambda e: e.tensor_tensor_scan(cw.full, r.scanmask, sig.full, 0.0, ALU.mult, ALU.add),
                     reads=[sig.d, r.cf.d], writes=[cw.d])
                P.op("dve", lambda e: e.tensor_tensor(cwx.full, cw.full, sig.full, ALU.subtract), reads=[cw.d, sig.d], writes=[cwx.d])
                P.op("act", lambda e: e.activation(E1.full, cw.full, AF.Exp, scale=-C0), reads=[cw.d], writes=[E1.d])
                P.op("act", lambda e: e.activation(E0.full, cwx.full, AF.Exp, scale=-C0), reads=[cwx.d], writes=[E0.d])
                P.op("act", lambda e: e.activation(Ei.full, cw.full, AF.Exp, scale=C0), reads=[cw.d], writes=[Ei.d])
                P.op("dve", lambda e, hp=hp: e.scalar_tensor_tensor(aT[hp].full, kk.full, -1.0, E0.full, ALU.mult, ALU.mult),
                     reads=[kk.d, E0.d], writes=[aT[hp].d])
                P.op("dve", lambda e, hp=hp: e.tensor_tensor(rT[hp].full, rl.full, E1.full, ALU.mult), reads=[rl.d, E1.d], writes=[rT[hp].d])
                P.op("dve", lambda e: e.tensor_tensor(bT.full, bb.full, Ei.full, ALU.mult), reads=[bb.d, Ei.d], writes=[bT.d])
                P.op("dve", lambda e: e.tensor_tensor(kTt.full, kp.full, Ei.full, ALU.mult), reads=[kp.d, Ei.d], writes=[kTt.d])
                P.op("dve", lambda e, hp=hp: e.tensor_copy(Wc.full[:, hp, :], E1.full.rearrange("p (c t) -> p c t", t=64)[:, :, 63]),
                     reads=[E1.d], writes=[Wc.d])
                P.op("act", lambda e: e.activation(vb.full, vl.full, AF.Copy), reads=[vl.d], writes=[vb.d])
                P.op("dve", lambda e, hp=hp: e.scalar_tensor_tensor(rk.full, rl.full, vcol(r, "r_k", hp), kp.full, ALU.mult, ALU.mult),
                     reads=[rl.d, kp.d, r.vecs.d], writes=[rk.d])
                ps = pA[rot["pA"] % 2]; rot["pA"] += 1
                P.op("pe", lambda e, ps=ps: e.matmul(ps.full, r.blockones, rk.full, start=True, stop=True),
                     reads=[rk.d, r.cf.d], writes=[ps.d])
                P.op("dve", lambda e, ps=ps, hp=hp: e.tensor_tensor(bon[hp].full, ps.full, vl.full, ALU.mult),
                     reads=[ps.d, vl.d], writes=[bon[hp].d])
                for X, off in ((bT, 0), (kTt, 512)):
                    blocks(lambda j, c, pj, cc, X=X, off=off: P.op("pe", lambda e: e.transpose(
                        tpb.full[pj, off + 64 * c:off + 64 * c + 64], X.full[pj, cc], r.ident[pj, pj]),
                        reads=[X.d, r.cb.d], writes=[tpb.d]))
                P.op("dve", lambda e, hp=hp: e.tensor_copy(bktm[hp].full, tpb.full), reads=[tpb.d], writes=[bktm[hp].d])
                blocks(lambda j, c, pj, cc: P.op("pe", lambda e: e.transpose(tpb.full[pj, cc], vb.full[pj, cc], r.ident[pj, pj]),
                                                 reads=[vb.d, r.cb.d], writes=[tpb.d]))
                P.op("dve", lambda e, hp=hp: e.tensor_copy(vtm[hp].full, tpb.full[:, 0:512]), reads=[tpb.d], writes=[vtm[hp].d])
                P0, P0T = ptmp[0], ptmp[1]
                prod(aT[hp], bT, P0, mask=r.ML)
                prod(bT, aT[hp], P0T, mask=r.MU)
                prod(kTt, aT[hp], AakT[hp], mask=r.MU)
                prod(bT, rT[hp], ArbT[hp], mask=r.MUI)
                prod(kTt, rT[hp], ArkT[hp], mask=r.MUI)
                P.op("dve", lambda e, hp=hp: e.tensor_tensor(TT[hp].full, P0T.full, r.IB, ALU.add), reads=[P0T.d, r.cb.d], writes=[TT[hp].d])
                Pc, PcT = P0, P0T
                free = [ptmp[2], ptmp[3], ptmp[4], ptmp[5]]
                for lvl in range(1, 6):
                    Pn = free.pop(0)
                    prod(PcT, Pc, Pn)
                    PnT = None
                    if lvl < 5:
                        PnT = free.pop(0)
                        prod(Pc, PcT, PnT)
                    prod(Pn, TT[hp], TT[hp], addend=TT[hp])
                    free.append(Pc)
                    free.append(PcT)
                    Pc, PcT = Pn, PnT
            for c in range(8):
                gc = sg * 8 + c
                Hc, Hn = Hbs[gc % 2], Hbs[(gc + 1) % 2]
                cc = slice(64 * c, 64 * c + 64)

                def heads(fn):
                    for hp in range(8):
                        for j in range(2):
                            fn(hp, slice(64 * j, 64 * j + 64), slice(64 * hp, 64 * hp + 64))

                heads(lambda hp, pj, hh: (
                    P.op("pe", lambda e, cc=cc, Hc=Hc, c=c: e.matmul(pS[0].full[pj, hh], aT[hp].full[pj, cc], Hc.full[pj, hh], start=True, stop=False),
                         reads=[aT[hp].d, Hc.d], writes=[pS[0].d]),
                    P.op("pe", lambda e, cc=cc, Hc=Hc, c=c: e.matmul(pS[0].full[pj, hh], AakT[hp].full[pj, cc], vtm[hp].full[pj, cc], start=False, stop=True),
                         reads=[AakT[hp].d, vtm[hp].d], writes=[pS[0].d])))
                P.op("act", lambda e: e.activation(Zb.full, pS[0].full, AF.Copy), reads=[pS[0].d], writes=[Zb.d])
                heads(lambda hp, pj, hh: P.op("pe", lambda e, cc=cc, Hc=Hc, c=c: e.matmul(pS[1].full[pj, hh], TT[hp].full[pj, cc], Zb.full[pj, hh], start=True, stop=True),
                                              reads=[TT[hp].d, Zb.d], writes=[pS[1].d]))
                P.op("dve", lambda e: e.tensor_copy(Ub.full, pS[1].full), reads=[pS[1].d], writes=[Ub.d])
                heads(lambda hp, pj, hh: (
                    P.op("pe", lambda e, cc=cc, Hc=Hc, c=c: e.matmul(pS[2].full[pj, hh], rT[hp].full[pj, cc], Hc.full[pj, hh], start=True, stop=False),
                         reads=[rT[hp].d, Hc.d], writes=[pS[2].d]),
                    P.op("pe", lambda e, cc=cc, Hc=Hc, c=c: e.matmul(pS[2].full[pj, hh], ArbT[hp].full[pj, cc], Ub.full[pj, hh], start=False, stop=False),
                         reads=[ArbT[hp].d, Ub.d], writes=[pS[2].d]),
                    P.op("pe", lambda e, cc=cc, Hc=Hc, c=c: e.matmul(pS[2].full[pj, hh], ArkT[hp].full[pj, cc], vtm[hp].full[pj, cc], start=False, stop=True),
                         reads=[ArkT[hp].d, vtm[hp].d], writes=[pS[2].d])))
                P.op("act", lambda e, c=c: e.activation(Yseg.full[:, c, :], pS[2].full, AF.Copy), reads=[pS[2].d], writes=[Yseg.d])
                heads(lambda hp, pj, hh: (
                    P.op("pe", lambda e, cc=cc, Hc=Hc, c=c: e.matmul(pS[0].full[pj, hh], bktm[hp].full[pj, cc], Ub.full[pj, hh], start=True, stop=False),
                         reads=[bktm[hp].d, Ub.d], writes=[pS[0].d]),
                    P.op("pe", lambda e, cc=cc, Hc=Hc, c=c: e.matmul(pS[0].full[pj, hh], bktm[hp].full[pj, 512 + 64 * c:512 + 64 * c + 64], vtm[hp].full[pj, cc],
                                                  start=False, stop=True),
                         reads=[bktm[hp].d, vtm[hp].d], writes=[pS[0].d])))
                P.op("dve", lambda e: e.tensor_tensor(tmpH.full, pS[0].full, Hf.full, ALU.add), reads=[pS[0].d, Hf.d], writes=[tmpH.d])
                P.op("dve", lambda e, c=c: e.tensor_tensor(
                    Hf.full.rearrange("p (h v) -> p h v", v=64), tmpH.full.rearrange("p (h v) -> p h v", v=64),
                    Wc.full[:, :, c:c + 1].broadcast_to([128, 8, 64]), ALU.mult), reads=[tmpH.d, Wc.d], writes=[Hf.d])
                P.op("act", lambda e, Hn=Hn: e.activation(Hn.full, Hf.full, AF.Copy), reads=[Hf.d], writes=[Hn.d])
            Yv = Yseg.full.rearrange("p c (h v) -> p (c h) v", v=64)
            Nv = yn.full.rearrange("p c (h v) -> p (c h) v", v=64)
            P.op("dve", lambda e: e.reduce_sum(gst.full[:, 0:64], Yv, AX.X), reads=[Yseg.d], writes=[gst.d])
            P.op("dve", lambda e: e.tensor_scalar(gst.full[:, 64:128], gst.full[:, 0:64], 1.0 / 64, None, ALU.mult), reads=[gst.d], writes=[gst.d])
            P.op("dve", lambda e: e.tensor_tensor(Yv, Yv, gst.full[:, 64:128].unsqueeze(2).broadcast_to([128, 64, 64]), ALU.subtract),
                 reads=[Yseg.d, gst.d], writes=[Yseg.d])
            P.op("dve", lambda e: e.tensor_tensor(Nv, Yv, Yv, ALU.mult), reads=[Yseg.d], writes=[yn.d])
            P.op("dve", lambda e: e.reduce_sum(gst.full[:, 128:192], Nv, AX.X), reads=[yn.d, gst.d], writes=[gst.d])
            P.op("act", lambda e: e.activation(gst.full[:, 192:256], gst.full[:, 128:192], AF.Sqrt, bias=GN_EPS, scale=1.0 / 64),
                 reads=[gst.d], writes=[gst.d])
            P.op("dve", lambda e: e.reciprocal(gst.full[:, 192:256], gst.full[:, 192:256]), reads=[gst.d], writes=[gst.d])
            P.op("dve", lambda e: e.tensor_tensor(Nv, Yv, gst.full[:, 192:256].unsqueeze(2).broadcast_to([128, 64, 64]), ALU.mult),
                 reads=[Yseg.d, gst.d], writes=[yn.d])
            for hp in range(8):
                blocks(lambda j, c, pj, cc, hp=hp: P.op("pe", lambda e: e.transpose(
                    tpb.full[pj, cc], yn.full[pj, c, 64 * hp:64 * hp + 64], r.ident[pj, pj]), reads=[yn.d, r.cb.d], writes=[tpb.d]))
                y_ = yo[hp % 2]
                P.op("dve", lambda e, hp=hp: e.tensor_scalar(t3.full, tpb.full[:, 0:512], vcol(r, "lnx_g", hp), vcol(r, "lnx_b", hp),
                                                            ALU.mult, ALU.add), reads=[tpb.d, r.vecs.d], writes=[t3.d])
                P.op("dve", lambda e, hp=hp: e.tensor_tensor(t3.full, t3.full, bon[hp].full, ALU.add), reads=[t3.d, bon[hp].d], writes=[t3.d])
                P.op("dve", lambda e, hp=hp, y_=y_: e.tensor_tensor(y_.full, t3.full, gT[hp].full, ALU.mult), reads=[t3.d, gT[hp].d], writes=[y_.d])
                P.dma("sp", dr["yA"][hp * 128:(hp + 1) * 128, t0:t0 + 512], y_.full, reads=[y_.d])
        P.emit()
    nc.all_engine_barrier()
```

```python
import numpy as np
import ml_dtypes
import concourse.bass as bass
import concourse.mybir as mybir
from concourse.bass_utils import run_bass_kernel_spmd

F32 = mybir.dt.float32
BF16 = mybir.dt.bfloat16
ALU = mybir.AluOpType
AF = mybir.ActivationFunctionType
AX = mybir.AxisListType


class Dep:
    __slots__ = ("name", "w", "rs")

    def __init__(self, name):
        self.name = name
        self.w = None
        self.rs = []


class Op:
    __slots__ = ("eng", "fn", "deps", "is_dma", "sem", "semval", "need_sig", "sigidx", "prev_same_sem")

    def __init__(self, eng, fn, is_dma):
        self.eng = eng
        self.fn = fn
        self.deps = []
        self.is_dma = is_dma
        self.sem = None
        self.semval = 0
        self.need_sig = False
        self.sigidx = 0
        self.prev_same_sem = None


class SB:
    def __init__(self, handle, dep):
        self.h = handle
        self.full = handle.ap()
        self.d = dep

    def __getitem__(self, k):
        return self.full[k]


ENGS = ("pe", "act", "dve", "pool", "sp")
N_DMA_SEMS = 40


class Prog:
    G = None

    @staticmethod
    def init_global(nc, st):
        g = {}
        g["sems"] = {e: st.enter_context(nc.semaphore("gs_" + e)) for e in ENGS}
        g["dsems"] = [st.enter_context(nc.semaphore("gd_%d" % i)) for i in range(N_DMA_SEMS)]
        g["cnt"] = {e: 0 for e in ENGS}
        g["n_dma"] = 0
        g["dma_last"] = [None] * N_DMA_SEMS
        g["dma_cnt"] = [0] * N_DMA_SEMS
        Prog.G = g

    def __init__(self, nc, st):
        self.nc = nc
        self.st = st
        self.ops = []
        self.deps = {}

    def dep(self, name):
        d = self.deps.get(name)
        if d is None:
            d = Dep(name)
            self.deps[name] = d
        return d

    _uid = [0]

    def sb(self, name, shape, dtype):
        Prog._uid[0] += 1
        h = self.st.enter_context(self.nc.sbuf_tensor("s%d_%s" % (Prog._uid[0], name), list(shape), dtype))
        return SB(h, Dep(name))

    def psum(self, name, dtype=F32):
        n = 512 if dtype == F32 else 1024
        Prog._uid[0] += 1
        h = self.st.enter_context(self.nc.psum_tensor("p%d_%s" % (Prog._uid[0], name), [128, n], dtype))
        return SB(h, Dep(name))

    def _track(self, op, reads, writes):
        ds = set()
        for b in reads:
            if b.w is not None:
                ds.add(b.w)
        for b in writes:
            if b.w is not None:
                ds.add(b.w)
            for r in b.rs:
                ds.add(r)
        ds.discard(op)
        for d in ds:
            if d.eng == "pe" and op.eng == "pe" and not d.is_dma and not op.is_dma:
                continue
            op.deps.append(d)
            d.need_sig = True
        for b in reads:
            b.rs.append(op)
        for b in writes:
            b.w = op
            b.rs = []

    def op(self, eng, fn, reads=(), writes=()):
        o = Op(eng, fn, False)
        self._track(o, reads, writes)
        self.ops.append(o)
        return o

    def dma(self, queue, out, in_, reads=(), writes=(), **kw):
        o = Op(queue, lambda e: e.dma_start(out=out, in_=in_, **kw), True)
        g = Prog.G
        s = g["n_dma"] % N_DMA_SEMS
        g["n_dma"] += 1
        o.sem = s
        g["dma_cnt"][s] += 16
        o.semval = g["dma_cnt"][s]
        o.prev_same_sem = g["dma_last"][s]
        g["dma_last"][s] = o
        self._track(o, reads, writes)
        self.ops.append(o)
        return o

    def emit(self):
        nc = self.nc
        g = Prog.G
        sems = g["sems"]
        dsems = g["dsems"]
        from contextlib import ExitStack
        with ExitStack() as st:
            cnt = g["cnt"]
            for o in self.ops:
                if not o.is_dma and o.need_sig:
                    cnt[o.eng] += 1
                    o.sigidx = cnt[o.eng]
            per_eng = {e: [] for e in ENGS}
            for o in self.ops:
                per_eng[o.eng].append(o)
            block = st.enter_context(nc.Block())

            def run(engname, eng):
                seen = {}

                def wait(sem, val, key):
                    if seen.get(key, 0) >= val:
                        return
                    seen[key] = val
                    eng.wait_ge(sem, val)

                for o in per_eng[engname]:
                    for d in o.deps:
                        if d.is_dma:
                            wait(dsems[d.sem], d.semval, ("d", d.sem))
                        else:
                            wait(sems[d.eng], d.sigidx, ("e", d.eng))
                    if o.is_dma:
                        p = o.prev_same_sem
                        if p is not None:
                            wait(dsems[p.sem], p.semval, ("d", p.sem))
                        o.fn(eng).then_inc(dsems[o.sem], 16)
                    else:
                        ins = o.fn(eng)
                        if o.need_sig:
                            ins.then_inc(sems[engname], 1)
                if engname == "sp":
                    for s in range(N_DMA_SEMS):
                        if g["dma_cnt"][s]:
                            wait(dsems[s], g["dma_cnt"][s], ("d", s))

            @block.tensor
            def _(eng):
                run("pe", eng)

            @block.scalar
            def _(eng):
                run("act", eng)

            @block.vector
            def _(eng):
                run("dve", eng)

            @block.gpsimd
            def _(eng):
                run("pool", eng)

            @block.sync
            def _(eng):
                run("sp", eng)
        self.ops = []


import math
from contextlib import ExitStack

D = 2048
AW = 1024
DL = 96
GL = 256
A_COLS = 3 * AW + 2 * DL + GL
QKC = 1024
IN_COLS = 10688
DFF = 8192
NCORES = 8
C0 = math.exp(-0.5)
LAMBDA_INIT = 0.8 - 0.6 * math.exp(-0.3 * 0)
GN_EPS = 64e-5
SUBLN_EPS = 1e-5
RMS_EPS = 1e-6

VC = {}
_o = 0
for _n, _w in [("g_mix", 16), ("g_mlp", 16), ("g_fin", 16), ("mu_r", 8), ("mu_k", 8), ("mu_v", 8),
               ("mu_wd", 1), ("mu_ad", 1), ("mu_gd", 2), ("w0", 8), ("a0", 8), ("k_k", 8), ("k_a", 8),
               ("r_k", 8), ("lnx_g", 8), ("lnx_b", 8)]:
    VC[_n] = _o
    _o += _w
NV = _o


class Cfg:
    def __init__(self, S=2048, NB=2, phases="ABCDE", dump=(), inject=()):
        self.S = S
        self.NB = NB
        self.T = S * NB
        self.phases = phases
        self.dump = set(dump)
        self.inject = set(inject)


def _pm(v, n):
    return np.ascontiguousarray(np.asarray(v, np.float32).reshape(n, 128).T)


class Res:
    pass


def load_consts(P, dr, names):
    r = Res()
    if "vecs" in names:
        r.vecs = P.sb("vecs", [128, NV], F32)
        P.dma("sp", r.vecs.full, dr["vecs"], writes=[r.vecs.d])
    if "cb" in names:
        r.cb = P.sb("cb", [128, 4 * 512 + 256], BF16)
        P.dma("sp", r.cb.full, dr["cb"], writes=[r.cb.d])
        r.ML = r.cb.full[:, 0:512]
        r.MU = r.cb.full[:, 512:1024]
        r.MUI = r.cb.full[:, 1024:1536]
        r.IB = r.cb.full[:, 1536:2048]
        r.ident = r.cb.full[:, 2048:2176]
        r.ones = r.cb.full[:, 2176:2304]
    if "cf" in names:
        r.cf = P.sb("cf", [128, 640], F32)
        P.dma("sp", r.cf.full, dr["cf"], writes=[r.cf.d])
        r.blockones = r.cf.full[:, 0:128]
        r.scanmask = r.cf.full[:, 128:640]
    return r


def vcol(r, name, i=0):
    c = VC[name] + i
    return r.vecs.full[:, c:c + 1]


class WStream:
    def __init__(self, P, nbuf=4):
        self.P = P
        self.bf = [P.sb("wbf%d" % i, [128, 16, 128], BF16) for i in range(nbuf)]
        self.i = 0

    def load(self, w_dram, k0, col0, width, nk=16):
        P = self.P
        bf = self.bf[self.i % len(self.bf)]
        self.i += 1
        src = w_dram[k0 * 128:(k0 + nk) * 128, col0:col0 + width].rearrange("(kc p) n -> p kc n", p=128)
        P.dma("pool", bf.full[:, 0:nk, 0:width], src, writes=[bf.d])
        return bf


def linear_fm(P, ws, w_dram, KC, blocks, xT, xdeps, NTG, banks, evac, tgw=512):
    kgs = min(16, KC)
    nkg = KC // kgs
    bk = 0
    for bi, (col0, width, tag) in enumerate(blocks):
        pss = []
        for tg in range(NTG):
            pss.append(banks[bk % len(banks)])
            bk += 1
        for kg in range(nkg):
            wb = ws.load(w_dram, kg * kgs, col0, width, kgs)
            for tg in range(NTG):
                ps = pss[tg]
                for kc in range(kgs):
                    kk = kg * kgs + kc
                    P.op("pe", lambda e, ps=ps, wb=wb, kc=kc, kk=kk, tg=tg, width=width:
                         e.matmul(ps.full[0:width, 0:tgw], wb.full[:, kc, 0:width], xT(kk, tg),
                                  start=(kk == 0), stop=(kk == KC - 1)),
                         reads=[wb.d] + xdeps(kk, tg), writes=[ps.d])
        for tg in range(NTG):
            evac(tag, bi, tg, pss[tg], width)


def rmsnorm_to_bf16(P, r, src_dram, c0, ntok, uT, gname, banks_small, xs_bufs, sq, rs, rinv, eps=RMS_EPS, q="act"):
    G = 256
    for gi in range(ntok // G):
        xs = xs_bufs[gi % len(xs_bufs)]
        src = src_dram[:, c0 + gi * G:c0 + (gi + 1) * G].rearrange("(kc p) t -> p kc t", p=128)
        P.dma(q, xs.full, src, writes=[xs.d])
        P.op("act", lambda e, xs=xs: e.activation(sq.full, xs.full, AF.Square), reads=[xs.d], writes=[sq.d])
        ps = banks_small[gi % len(banks_small)]
        for kc in range(16):
            P.op("pe", lambda e, ps=ps, kc=kc: e.matmul(ps.full[:, 0:G], r.ones, sq.full[:, kc, :],
                                                        start=(kc == 0), stop=(kc == 15)),
                 reads=[sq.d, r.cb.d], writes=[ps.d])
        P.op("act", lambda e, ps=ps: e.activation(rs.full, ps.full[:, 0:G], AF.Sqrt, bias=eps, scale=1.0 / D),
             reads=[ps.d], writes=[rs.d])
        P.op("dve", lambda e: e.reciprocal(rinv.full, rs.full), reads=[rs.d], writes=[rinv.d])
        for kc in range(16):
            P.op("dve", lambda e, xs=xs, kc=kc, gi=gi: e.scalar_tensor_tensor(
                uT.full[:, kc, gi * G:(gi + 1) * G], xs.full[:, kc, :], vcol(r, gname, kc), rinv.full,
                ALU.mult, ALU.mult),
                reads=[xs.d, rinv.d, r.vecs.d], writes=[uT.d])


def phase_A(nc, cfg, dr, b):
    S = cfg.S
    c0 = b * S
    NTG = S // 512
    with ExitStack() as st:
        P = Prog(nc, st)
        r = load_consts(P, dr, ["vecs", "cb"])
        uT = P.sb("uT", [128, 16, S], BF16)
        xs_bufs = [P.sb("xs%d" % i, [128, 16, 256], F32) for i in range(2)]
        sq = P.sb("sq", [128, 16, 256], BF16)
        rs = P.sb("rs", [128, 256], F32)
        rinv = P.sb("rinv", [128, 256], F32)
        banks = [P.psum("pm%d" % i) for i in range(6)]
        bsm = [P.psum("psm%d" % i) for i in range(2)]
        ws = WStream(P)
        stf = [P.sb("stf%d" % i, [128, S], F32) for i in range(2)]
        stb = [P.sb("stb%d" % i, [128, S], BF16) for i in range(2)]
        rmsnorm_to_bf16(P, r, dr["xT"], c0, S, uT, "g_mix", bsm, xs_bufs, sq, rs, rinv)

        blocks = []
        for i in range(24):
            blocks.append((i * 128, 128, ("zA", i * 128)))
        blocks.append((3072, 96, ("zA", 3072)))
        blocks.append((3168, 96, ("zA", 3168)))
        blocks.append((3264, 128, ("zA", 3264)))
        blocks.append((3392, 128, ("zA", 3392)))
        for i in range(24):
            blocks.append((A_COLS + i * 128, 128, ("qkv", i * 128)))
        for i in range(32):
            blocks.append((A_COLS + 3072 + i * 128, 128, ("gate", i * 128)))
        cnt = {"f": 0, "b": 0}

        def evac(tag, bi, tg, ps, width):
            kind, row0 = tag
            if kind == "zA":
                sbuf = stf[cnt["f"] % 2]
                P.op("act", lambda e: e.activation(sbuf.full[0:width, tg * 512:(tg + 1) * 512], ps.full[0:width, :], AF.Copy),
                     reads=[ps.d], writes=[sbuf.d])
                if tg == NTG - 1:
                    P.dma("act", dr["zA"][row0:row0 + width, c0:c0 + S], sbuf.full[0:width, :], reads=[sbuf.d])
                    cnt["f"] += 1
            else:
                sbuf = stb[cnt["b"] % 2]
                fn = AF.Copy if kind == "qkv" else AF.Sigmoid
                P.op("act", lambda e: e.activation(sbuf.full[0:width, tg * 512:(tg + 1) * 512], ps.full[0:width, :], fn),
                     reads=[ps.d], writes=[sbuf.d])
                if tg == NTG - 1:
                    dst = dr["qkv"] if kind == "qkv" else dr["gates"]
                    P.dma("act", dst[row0:row0 + width, c0:c0 + S], sbuf.full[0:width, :], reads=[sbuf.d])
                    cnt["b"] += 1

        import os
        if os.environ.get("DBG_A") == "norm":
            blocks = []
        elif os.environ.get("DBG_A"):
            blocks = blocks[:int(os.environ["DBG_A"])]
        linear_fm(P, ws, dr["w_in"], 16, blocks, lambda kk, tg: uT.full[:, kk, tg * 512:(tg + 1) * 512],
                  lambda kk, tg: [uT.d], NTG, banks, evac)
        P.emit()
    nc.all_engine_barrier()


SCRATCH = {
    "zA": ([A_COLS, None], F32),
    "qkv": ([3072, None], BF16),
    "gates": ([4096, None], BF16),
    "yA": ([AW, None], BF16),
    "yB": ([AW, None], BF16),
    "h": ([D, None], F32),
}


def build_nc(cfg):
    nc = bass.Bass("TRN2", target_bir_lowering=False)
    T = cfg.T
    dr = {}
    gst = ExitStack()
    Prog.init_global(nc, gst)

    def inp(name, shape, dt=F32):
        dr[name] = nc.dram_tensor(name, list(shape), dt, kind="ExternalInput").ap()

    inp("xT", [D, T])
    if "A" in cfg.phases:
        inp("w_in", [D, IN_COLS])
    if "D" in cfg.phases:
        inp("p_a", [AW, D])
        inp("p_b", [AW, D])
        inp("w_out", [D, D])
    if "E" in cfg.phases:
        inp("w_ff1", [D, DFF])
        inp("w_ff2", [DFF, D])
    if "B" in cfg.phases:
        inp("w_up", [DL, AW])
        inp("a_up", [DL, AW])
        inp("g_up", [GL, AW])
    inp("vecs", [128, NV])
    inp("bc", [128, 16 + 256 + 128])
    inp("biasT", [128, 2 * 16 * 128])
    inp("cb", [128, 4 * 512 + 256], BF16)
    inp("cf", [128, 640])
    for name, (shape, dt) in SCRATCH.items():
        shp = [shape[0], T]
        if name in cfg.inject:
            kind = "ExternalInput"
        elif name in cfg.dump:
            kind = "ExternalOutput"
        else:
            kind = "Internal"
        dr[name] = nc.dram_tensor(name, shp, dt, kind=kind).ap()
    dr["outT"] = nc.dram_tensor("outT", [D, T], F32, kind="ExternalOutput").ap()
    for b in range(cfg.NB):
        if "A" in cfg.phases:
            phase_A(nc, cfg, dr, b)
        if "B" in cfg.phases:
            phase_B(nc, cfg, dr, b)
        if "C" in cfg.phases:
            phase_C(nc, cfg, dr, b)
        if "D" in cfg.phases:
            phase_D(nc, cfg, dr, b)
        if "E" in cfg.phases:
            phase_E(nc, cfg, dr, b)
    return nc


def t5_bucket_np(rel):
    nb = 16
    max_exact = 8
    ret = np.where(rel > 0, nb, 0)
    n = np.abs(rel)
    nf = np.maximum(n, 1).astype(np.float32)
    large = max_exact + (np.log(nf / max_exact) / math.log(128 / max_exact) * (nb - max_exact)).astype(np.int32)
    large = np.minimum(large, nb - 1)
    return ret + np.where(n < max_exact, n, large)


def host_consts(inp):
    f = np.float32
    vecs = np.zeros((128, NV), f)

    def put(name, v, n):
        vecs[:, VC[name]:VC[name] + n] = _pm(v, n)

    put("g_mix", inp["norm_mix_g"][0], 16)
    put("g_mlp", inp["norm_mlp_g"][0], 16)
    put("g_fin", inp["norm_final_g"], 16)
    mu = np.asarray(inp["mu_shift"][0], f)
    put("mu_r", mu[0:1024], 8)
    put("mu_k", mu[1024:2048], 8)
    put("mu_v", mu[2048:3072], 8)
    vecs[0:96, VC["mu_wd"]] = mu[3072:3168]
    vecs[0:96, VC["mu_ad"]] = mu[3168:3264]
    put("mu_gd", mu[3264:3520], 2)
    put("w0", inp["w0"][0], 8)
    put("a0", inp["a0"][0], 8)
    put("k_k", inp["k_k"][0], 8)
    put("k_a", inp["k_a"][0], 8)
    put("r_k", np.asarray(inp["r_k"][0]).reshape(-1), 8)
    put("lnx_g", inp["lnx_g"][0], 8)
    put("lnx_b", inp["lnx_b"][0], 8)

    rb = np.asarray(inp["rel_bias"], f)
    bc = np.zeros((128, 16 + 256 + 128), f)
    bc[:, 0:16] = rb[15][None, :]
    for i, nme in enumerate(["lambda_q1", "lambda_k1", "lambda_q2", "lambda_k2"]):
        bc[:, 16 + 64 * i:16 + 64 * (i + 1)] = np.asarray(inp[nme][0], f)[None, :]
    bc[:, 272:400] = np.asarray(inp["subln_g"][0], f)[None, :]
    kl = np.arange(128)[:, None]
    ql = np.arange(128)[None, :]
    biasT = np.zeros((128, 2, 16, 128), f)
    for ty, off in enumerate([0, -128]):
        bidx = t5_bucket_np(off + kl - ql)
        biasT[:, ty, :, :] = np.transpose(rb[bidx], (0, 2, 1))
    biasT = biasT.reshape(128, -1)
    row = (np.arange(128) % 64)[:, None]
    col = (np.arange(512) % 64)[None, :]
    ML = (col < row).astype(f)
    MU = (row < col).astype(f)
    MUI = (row <= col).astype(f)
    IB = (row == col).astype(f)
    ident = np.eye(128, dtype=f)
    ones = np.ones((128, 128), f)
    cb = np.concatenate([ML, MU, MUI, IB, ident, ones], axis=1).astype(ml_dtypes.bfloat16)
    blockones = np.zeros((128, 128), f)
    blockones[0:64, 0:64] = 1
    blockones[64:, 64:] = 1
    scanmask = np.broadcast_to((np.arange(512) % 64 != 0).astype(f)[None, :], (128, 512))
    cf = np.concatenate([blockones, scanmask], axis=1).astype(f)
    out = {"vecs": vecs, "bc": bc, "biasT": np.ascontiguousarray(biasT), "cb": np.ascontiguousarray(cb),
           "cf": np.ascontiguousarray(cf)}
    for nme in ["w_in", "p_a", "p_b", "w_out", "w_ff1", "w_ff2", "w_up", "a_up", "g_up"]:
        out[nme] = np.ascontiguousarray(np.asarray(inp[nme][0], f))
    return out


def kernel(**inputs):
    cfg = Cfg()
    x = np.asarray(inputs["x"], np.float32)
    B, S, _ = x.shape
    nc = build_nc(cfg)
    shared = host_consts(inputs)
    in_maps = []
    for c in range(NCORES):
        m = dict(shared)
        xs = x[c * cfg.NB:(c + 1) * cfg.NB].reshape(cfg.T, D)
        m["xT"] = np.ascontiguousarray(xs.T)
        in_maps.append(m)
    res = run_bass_kernel_spmd(nc, in_maps, core_ids=list(range(NCORES)))
    out = np.empty((B, S, D), np.float32)
    for c in range(NCORES):
        o = res.results[c]["outT"]
        out[c * cfg.NB:(c + 1) * cfg.NB] = o.T.reshape(cfg.NB, S, D)
    return out


def phase_D(nc, cfg, dr, b):
    S = cfg.S
    c0 = b * S
    NTG = S // 512
    with ExitStack() as st:
        P = Prog(nc, st)
        yAs = P.sb("yAs", [128, 8, S], BF16)
        yBs = P.sb("yBs", [128, 8, S], BF16)
        mT = P.sb("mT", [128, 16, S], BF16)
        P.dma("act", yAs.full, dr["yA"][:, c0:c0 + S].rearrange("(kc p) t -> p kc t", p=128), writes=[yAs.d])
        P.dma("act", yBs.full, dr["yB"][:, c0:c0 + S].rearrange("(kc p) t -> p kc t", p=128), writes=[yBs.d])
        ws = WStream(P)
        banks = [P.psum("pm%d" % i) for i in range(8)]
        gts = [P.sb("gt%d" % i, [128, 2, S], BF16) for i in range(2)]
        t1 = P.sb("t1", [128, 512], F32)
        t2 = P.sb("t2", [128, 512], F32)
        xts = [P.sb("xt%d" % i, [128, S], F32) for i in range(2)]
        sth = [P.sb("sth%d" % i, [128, S], F32) for i in range(2)]
        for cb in range(16):
            gt = gts[cb % 2]
            P.dma("act", gt.full[:, 0, :], dr["gates"][cb * 128:(cb + 1) * 128, c0:c0 + S], writes=[gt.d])
            P.dma("act", gt.full[:, 1, :], dr["gates"][2048 + cb * 128:2048 + (cb + 1) * 128, c0:c0 + S], writes=[gt.d])
            for tg in range(NTG):
                psA = banks[(2 * (cb * NTG + tg)) % 8]
                psB = banks[(2 * (cb * NTG + tg) + 1) % 8]
                if tg == 0:
                    wa = ws.load(dr["p_a"], 0, cb * 128, 128, 8)
                    wb = ws.load(dr["p_b"], 0, cb * 128, 128, 8)
                for (ps, w, ys) in ((psA, wa, yAs), (psB, wb, yBs)):
                    for kc in range(8):
                        P.op("pe", lambda e, ps=ps, w=w, ys=ys, kc=kc, tg=tg: e.matmul(
                            ps.full[:, :], w.full[:, kc, :], ys.full[:, kc, tg * 512:(tg + 1) * 512],
                            start=(kc == 0), stop=(kc == 7)), reads=[w.d, ys.d], writes=[ps.d])
                P.op("dve", lambda e, psA=psA, gt=gt, tg=tg: e.tensor_tensor(
                    t1.full, psA.full, gt.full[:, 0, tg * 512:(tg + 1) * 512], ALU.mult), reads=[psA.d, gt.d], writes=[t1.d])
                P.op("dve", lambda e, psB=psB, gt=gt, tg=tg: e.tensor_tensor(
                    t2.full, psB.full, gt.full[:, 1, tg * 512:(tg + 1) * 512], ALU.mult), reads=[psB.d, gt.d], writes=[t2.d])
                P.op("dve", lambda e, cb=cb, tg=tg: e.tensor_tensor(
                    mT.full[:, cb, tg * 512:(tg + 1) * 512], t1.full, t2.full, ALU.add), reads=[t1.d, t2.d], writes=[mT.d])
        cnt = [0]

        def evac(tag, bi, tg, ps, width):
            xt = xts[bi % 2]
            sb_ = sth[bi % 2]
            if tg == 0:
                P.dma("act", xt.full, dr["xT"][bi * 128:(bi + 1) * 128, c0:c0 + S], writes=[xt.d])
            P.op("dve", lambda e: e.tensor_tensor(sb_.full[:, tg * 512:(tg + 1) * 512], ps.full,
                                                  xt.full[:, tg * 512:(tg + 1) * 512], ALU.add),
                 reads=[ps.d, xt.d], writes=[sb_.d])
            if tg == NTG - 1:
                P.dma("sp", dr["h"][bi * 128:(bi + 1) * 128, c0:c0 + S], sb_.full, reads=[sb_.d])

        linear_fm(P, ws, dr["w_out"], 16, [(i * 128, 128, None) for i in range(16)],
                  lambda kk, tg: mT.full[:, kk, tg * 512:(tg + 1) * 512], lambda kk, tg: [mT.d], NTG, banks[0:6], evac)
        P.emit()
    nc.all_engine_barrier()


def phase_E(nc, cfg, dr, b):
    S = cfg.S
    TF = min(1024, S)
    NTG = S // TF
    NH = TF // 512
    with ExitStack() as st:
        P = Prog(nc, st)
        r = load_consts(P, dr, ["vecs", "cb"])
        h2 = P.sb("h2", [128, 16, TF], F32)
        mT = P.sb("mT", [128, 16, TF], BF16)
        hids = [P.sb("hid%d" % i, [128, 16, TF], BF16) for i in range(2)]
        sq = P.sb("sq", [128, 16, 256], BF16)
        rs = P.sb("rs", [128, 256], F32)
        rinv = P.sb("rinv", [128, 256], F32)
        rls = [P.sb("rl%d" % i, [128, 512], BF16) for i in range(2)]
        ost = [P.sb("ost%d" % i, [128, TF], F32) for i in range(2)]
        banks = [P.psum("pm%d" % i) for i in range(6)]
        bsm = [P.psum("psm%d" % i) for i in range(2)]
        ws = WStream(P)
        bk = [0]

        def norm_from_h2(dst_fn, gname, G=256):
            for gi in range(TF // G):
                gs = slice(gi * G, (gi + 1) * G)
                P.op("act", lambda e, gs=gs: e.activation(sq.full, h2.full[:, :, gs], AF.Square), reads=[h2.d], writes=[sq.d])
                ps = bsm[gi % 2]
                for kc in range(16):
                    P.op("pe", lambda e, ps=ps, kc=kc: e.matmul(ps.full[:, 0:G], r.ones, sq.full[:, kc, :], start=(kc == 0), stop=(kc == 15)),
                         reads=[sq.d, r.cb.d], writes=[ps.d])
                P.op("act", lambda e, ps=ps: e.activation(rs.full, ps.full[:, 0:G], AF.Sqrt, bias=RMS_EPS, scale=1.0 / D),
                     reads=[ps.d], writes=[rs.d])
                P.op("dve", lambda e: e.reciprocal(rinv.full, rs.full), reads=[rs.d], writes=[rinv.d])
                for kc in range(16):
                    dst_fn(kc, gs, gi)

        for tg in range(NTG):
            c0 = b * S + tg * TF
            for kc in range(16):
                P.dma("sp", h2.full[:, kc, :], dr["h"][kc * 128:(kc + 1) * 128, c0:c0 + TF], writes=[h2.d])

            def to_m(kc, gs, gi):
                P.op("dve", lambda e: e.scalar_tensor_tensor(mT.full[:, kc, gs], h2.full[:, kc, gs], vcol(r, "g_mlp", kc), rinv.full,
                                                             ALU.mult, ALU.mult), reads=[h2.d, rinv.d, r.vecs.d], writes=[mT.d])

            norm_from_h2(to_m, "g_mlp")
            for g in range(4):
                hid = hids[g % 2]
                for blk in range(16):
                    wb = ws.load(dr["w_ff1"], 0, (g * 16 + blk) * 128, 128, 16)
                    for hf in range(NH):
                        ps = banks[bk[0] % 6]
                        bk[0] += 1
                        hs = slice(hf * 512, (hf + 1) * 512)
                        for kc in range(16):
                            P.op("pe", lambda e, ps=ps, wb=wb, kc=kc, hs=hs: e.matmul(ps.full, wb.full[:, kc, :], mT.full[:, kc, hs],
                                                                                      start=(kc == 0), stop=(kc == 15)),
                                 reads=[wb.d, mT.d], writes=[ps.d])
                        rl = rls[bk[0] % 2]
                        P.op("act", lambda e, ps=ps, rl=rl: e.activation(rl.full, ps.full, AF.Relu), reads=[ps.d], writes=[rl.d])
                        P.op("dve", lambda e, rl=rl, hid=hid, blk=blk, hs=hs: e.tensor_tensor(hid.full[:, blk, hs], rl.full, rl.full, ALU.mult),
                             reads=[rl.d], writes=[hid.d])
                for cb in range(16):
                    wb = ws.load(dr["w_ff2"], g * 16, cb * 128, 128, 16)
                    for hf in range(NH):
                        ps = banks[bk[0] % 6]
                        bk[0] += 1
                        hs = slice(hf * 512, (hf + 1) * 512)
                        for kc in range(16):
                            P.op("pe", lambda e, ps=ps, wb=wb, kc=kc, hs=hs, hid=hid: e.matmul(ps.full, wb.full[:, kc, :], hid.full[:, kc, hs],
                                                                                               start=(kc == 0), stop=(kc == 15)),
                                 reads=[wb.d, hid.d], writes=[ps.d])
                        P.op("dve", lambda e, ps=ps, cb=cb, hs=hs: e.tensor_tensor(h2.full[:, cb, hs], ps.full, h2.full[:, cb, hs], ALU.add),
                             reads=[ps.d, h2.d], writes=[h2.d])

            def to_out(kc, gs, gi):
                o = ost[kc % 2]
                P.op("dve", lambda e: e.scalar_tensor_tensor(o.full[:, gs], h2.full[:, kc, gs], vcol(r, "g_fin", kc), rinv.full,
                                                             ALU.mult, ALU.mult), reads=[h2.d, rinv.d, r.vecs.d], writes=[o.d])
                P.dma("sp", dr["outT"][kc * 128:(kc + 1) * 128, c0 + gs.start:c0 + gs.stop], o.full[:, gs], reads=[o.d])

            norm_from_h2(to_out, "g_fin")
        P.emit()
    nc.all_engine_barrier()


def phase_C(nc, cfg, dr, b):
    S = cfg.S
    c0 = b * S
    NQB = S // 128
    with ExitStack() as st:
        P = Prog(nc, st)
        r = load_consts(P, dr, ["cb"])
        bc = P.sb("bc", [128, 400], F32)
        P.dma("sp", bc.full, dr["bc"], writes=[bc.d])
        biasT = P.sb("biasT", [128, 2 * 16 * 128], F32)
        P.dma("sp", biasT.full, dr["biasT"], writes=[biasT.d])
        sm = P.sb("sm", [128, 16], F32)
        lt = P.sb("lt", [128, 128], F32)
        sgb = P.sb("sgb", [128, 128], F32)
        P.op("dve", lambda e: e.tensor_tensor(lt.full[:, 0:64], bc.full[:, 16:80], bc.full[:, 80:144], ALU.mult), reads=[bc.d], writes=[lt.d])
        P.op("dve", lambda e: e.tensor_tensor(lt.full[:, 64:128], bc.full[:, 144:208], bc.full[:, 208:272], ALU.mult), reads=[bc.d], writes=[lt.d])
        P.op("dve", lambda e: e.reduce_sum(sm.full[:, 0:1], lt.full[:, 0:64], AX.X), reads=[lt.d], writes=[sm.d])
        P.op("dve", lambda e: e.reduce_sum(sm.full[:, 1:2], lt.full[:, 64:128], AX.X), reads=[lt.d, sm.d], writes=[sm.d])
        P.op("act", lambda e: e.activation(sm.full[:, 2:4], sm.full[:, 0:2], AF.Exp), reads=[sm.d], writes=[sm.d])
        P.op("dve", lambda e: e.tensor_tensor(sm.full[:, 4:5], sm.full[:, 3:4], sm.full[:, 2:3], ALU.subtract), reads=[sm.d], writes=[sm.d])
        P.op("dve", lambda e: e.tensor_scalar(sm.full[:, 5:6], sm.full[:, 4:5], -LAMBDA_INIT, None, ALU.add), reads=[sm.d], writes=[sm.d])
        P.op("dve", lambda e: e.tensor_scalar(sgb.full, bc.full[:, 272:400], 1.0 - LAMBDA_INIT, None, ALU.mult), reads=[bc.d], writes=[sgb.d])
        neglam = sm.full[:, 5:6]
        qTs = [P.sb("qT%d" % i, [128, S], BF16) for i in range(2)]
        kTs = [P.sb("kT%d" % i, [128, S], BF16) for i in range(2)]
        vTs = [P.sb("vT%d" % i, [128, S], BF16) for i in range(2)]
        Vas = [P.sb("Va%d" % i, [128, NQB, 132], BF16) for i in range(2)]
        ysts = [P.sb("yst%d" % i, [128, S], BF16) for i in range(2)]
        STs = [P.psum("ST%d" % i) for i in range(2)]
        Os = [P.psum("O%d" % i) for i in range(4)]
        tpb = P.psum("tpb", BF16)
        tp2 = P.psum("tp2", BF16)
        PTs = [P.sb("PT%d" % i, [128, 512], BF16) for i in range(3)]
        tmps = [P.sb("tmp%d" % i, [128, 128], F32) for i in range(2)]
        rd = [P.sb("rd%d" % i, [128, 8], F32) for i in range(2)]
        o1s = [P.sb("o1_%d" % i, [128, 128], F32) for i in range(2)]
        oos = [P.sb("oo_%d" % i, [128, 128], F32) for i in range(2)]
        junk = P.sb("junk", [128, 128], F32)
        ons = [P.sb("on_%d" % i, [128, 128], BF16) for i in range(2)]
        rot = [0, 0, 0]
        ooa = [P.sb("ooa%d" % i, [128, NQB, 128], F32) for i in range(2)]
        ssa = [P.sb("ssa%d" % i, [128, 2 * NQB], F32) for i in range(2)]
        items = []

        def head_pre(h):
            qT, kT, vT, Va = qTs[h % 2], kTs[h % 2], vTs[h % 2], Vas[h % 2]
            P.dma("sp", qT.full, dr["qkv"][h * 128:(h + 1) * 128, c0:c0 + S], writes=[qT.d])
            P.dma("sp", kT.full, dr["qkv"][1024 + h * 128:1024 + (h + 1) * 128, c0:c0 + S], writes=[kT.d])
            P.dma("sp", vT.full, dr["qkv"][2048 + h * 128:2048 + (h + 1) * 128, c0:c0 + S], writes=[vT.d])
            P.op("pool", lambda e, Va=Va: e.memset(Va.full, 1.0), writes=[Va.d])
            for t0 in range(0, NQB, 8):
                n = min(8, NQB - t0)
                for i in range(n):
                    tb = t0 + i
                    P.op("pe", lambda e, i=i, tb=tb, vT=vT: e.transpose(tpb.full[:, i * 128:(i + 1) * 128],
                                                                        vT.full[:, tb * 128:(tb + 1) * 128], r.ident),
                         reads=[vT.d, r.cb.d], writes=[tpb.d])
                P.op("dve", lambda e, t0=t0, n=n, Va=Va: e.tensor_copy(
                    Va.full[:, t0:t0 + n, 0:128], tpb.full[:, 0:n * 128].rearrange("p (a b) -> p a b", b=128)),
                    reads=[tpb.d], writes=[Va.d])

        def qb_combine(h, qb, Opair):
            O0, O1 = Opair
            oa, sa = ooa[h % 2], ssa[h % 2]
            rdt, o1 = rd[qb % 2], o1s[qb % 2]
            P.op("dve", lambda e: e.reciprocal(rdt.full[:, 0:1], O0.full[:, 128:129]), reads=[O0.d], writes=[rdt.d])
            P.op("dve", lambda e: e.reciprocal(rdt.full[:, 1:2], O1.full[:, 128:129]), reads=[O1.d, rdt.d], writes=[rdt.d])
            P.op("dve", lambda e: e.tensor_tensor(rdt.full[:, 2:3], rdt.full[:, 1:2], neglam, ALU.mult), reads=[rdt.d, sm.d], writes=[rdt.d])
            P.op("dve", lambda e: e.tensor_scalar(o1.full, O0.full[:, 0:128], rdt.full[:, 0:1], None, ALU.mult),
                 reads=[O0.d, rdt.d], writes=[o1.d])
            P.op("dve", lambda e: e.scalar_tensor_tensor(oa.full[:, qb, :], O1.full[:, 0:128], rdt.full[:, 2:3], o1.full, ALU.mult, ALU.add),
                 reads=[O1.d, rdt.d, o1.d], writes=[oa.d])
            P.op("dve", lambda e: e.tensor_tensor(junk.full, oa.full[:, qb, :], oa.full[:, qb, :], ALU.mult), reads=[oa.d], writes=[junk.d])
            P.op("dve", lambda e: e.reduce_sum(sa.full[:, qb:qb + 1], junk.full, AX.X), reads=[junk.d, sa.d], writes=[sa.d])

        def head_post(h):
            oa, sa, yst = ooa[h % 2], ssa[h % 2], ysts[h % 2]
            P.op("act", lambda e: e.activation(sa.full[:, NQB:2 * NQB], sa.full[:, 0:NQB], AF.Sqrt, bias=SUBLN_EPS, scale=1.0 / 128),
                 reads=[sa.d], writes=[sa.d])
            P.op("dve", lambda e: e.reciprocal(sa.full[:, NQB:2 * NQB], sa.full[:, NQB:2 * NQB]), reads=[sa.d], writes=[sa.d])
            for qb in range(NQB):
                on = ons[qb % 2]
                P.op("dve", lambda e, qb=qb, on=on: e.scalar_tensor_tensor(
                    on.full, oa.full[:, qb, :], sa.full[:, NQB + qb:NQB + qb + 1], sgb.full, ALU.mult, ALU.mult),
                    reads=[oa.d, sa.d, sgb.d], writes=[on.d])
                P.op("pe", lambda e, on=on, qb=qb: e.transpose(tp2.full[:, (qb % 8) * 128:(qb % 8 + 1) * 128], on.full, r.ident),
                     reads=[on.d, r.cb.d], writes=[tp2.d])
                if qb % 8 == 7 or qb == NQB - 1:
                    q0 = (qb // 8) * 8
                    n = qb - q0 + 1
                    P.op("dve", lambda e, q0=q0, n=n: e.tensor_copy(yst.full[:, q0 * 128:(q0 + n) * 128], tp2.full[:, 0:n * 128]),
                         reads=[tp2.d], writes=[yst.d])
            P.dma("sp", dr["yB"][h * 128:(h + 1) * 128, c0:c0 + S], yst.full, reads=[yst.d])

        for h in range(8):
            qT, kT, vT, Va = qTs[h % 2], kTs[h % 2], vTs[h % 2], Vas[h % 2]
            first = True
            for qb in range(NQB):
                Opair = (Os[(qb % 2) * 2], Os[(qb % 2) * 2 + 1])
                for j in range(2):
                    m = 2 * h + j
                    O = Opair[j]
                    for g0 in range(0, qb + 1, 4):
                        kbs = list(range(g0, min(g0 + 4, qb + 1)))
                        STb = STs[rot[0] % 2]
                        rot[0] += 1
                        PT = PTs[rot[1] % 3]
                        rot[1] += 1

                        def s1(STb=STb, kbs=kbs, j=j, qb=qb, kT=kT, qT=qT):
                            for i, kb in enumerate(kbs):
                                P.op("pe", lambda e, i=i, kb=kb: e.matmul(
                                    STb.full[:, i * 128:(i + 1) * 128], kT.full[64 * j:64 * j + 64, kb * 128:(kb + 1) * 128],
                                    qT.full[64 * j:64 * j + 64, qb * 128:(qb + 1) * 128], start=True, stop=True),
                                    reads=[kT.d, qT.d], writes=[STb.d])

                        def s2(STb=STb, PT=PT, kbs=kbs, qb=qb, m=m, g0=g0):
                            nfar = len([kb for kb in kbs if kb <= qb - 2])
                            if nfar:
                                P.op("act", lambda e: e.activation(PT.full[:, 0:nfar * 128], STb.full[:, 0:nfar * 128], AF.Exp,
                                                                   bias=bc.full[:, m:m + 1], scale=0.125),
                                     reads=[bc.d], writes=[PT.d, STb.d])
                            for i, kb in enumerate(kbs):
                                if kb <= qb - 2:
                                    continue
                                ty = 0 if kb == qb else 1
                                tmp = tmps[rot[2] % 2]
                                rot[2] += 1
                                bo = (ty * 16 + m) * 128
                                P.op("dve", lambda e, tmp=tmp, i=i, bo=bo: e.scalar_tensor_tensor(
                                    tmp.full, STb.full[:, i * 128:(i + 1) * 128], 0.125, biasT.full[:, bo:bo + 128], ALU.mult, ALU.add),
                                    reads=[biasT.d], writes=[tmp.d, STb.d])
                                P.op("act", lambda e, tmp=tmp, i=i: e.activation(PT.full[:, i * 128:(i + 1) * 128], tmp.full, AF.Exp),
                                     reads=[tmp.d], writes=[PT.d])
                                if kb == qb:
                                    P.op("pool", lambda e, i=i: e.memset(PT.full[64:128, i * 128:i * 128 + 64], 0.0),
                                         reads=[PT.d], writes=[PT.d])

                        def s3(PT=PT, kbs=kbs, O=O, qb=qb, Va=Va):
                            for i, kb in enumerate(kbs):
                                P.op("pe", lambda e, i=i, kb=kb: e.matmul(
                                    O.full[:, 0:129], PT.full[:, i * 128:(i + 1) * 128], Va.full[:, kb, 0:129],
                                    start=(kb == 0), stop=(kb == qb)), reads=[PT.d, Va.d], writes=[O.d])

                        pre = (lambda h=h: head_pre(h)) if first else None
                        first = False
                        last_of_qb = (j == 1 and kbs[-1] == qb)
                        post = []
                        if last_of_qb:
                            post.append(lambda h=h, qb=qb, Opair=Opair: qb_combine(h, qb, Opair))
                            if qb == NQB - 1:
                                post.append(lambda h=h: head_post(h))
                        items.append((pre, s1, s2, s3, post))
        n = len(items)
        for i in range(n + 1):
            if i < n:
                pre, s1, s2, s3, post = items[i]
                if pre:
                    pre()
                s1()
                s2()
            if i >= 1:
                pre, s1, s2, s3, post = items[i - 1]
                s3()
                for f in post:
                    f()
        P.emit()
    nc.all_engine_barrier()


def phase_B(nc, cfg, dr, b):
    S = cfg.S
    c0 = b * S
    NSEG = S // 512
    with ExitStack() as st:
        P = Prog(nc, st)
        r = load_consts(P, dr, ["vecs", "cb", "cf"])
        Yseg = P.sb("Yseg", [128, 8, 512], F32)

        class _V:
            pass
        wl_f = _V()
        wl_f.full = Yseg.full.rearrange("p (a b) t -> p a (b t)", b=2)
        wl_f.d = Yseg.d
        wl = P.sb("wl", [128, 4, 1024], BF16)
        P.dma("sp", wl_f.full[0:96, 0, :], dr["w_up"], writes=[wl_f.d])
        P.dma("sp", wl_f.full[0:96, 1, :], dr["a_up"], writes=[wl_f.d])
        P.dma("sp", wl_f.full[:, 2:4, :], dr["g_up"].rearrange("(a p) n -> p a n", p=128), writes=[wl_f.d])
        P.op("dve", lambda e: e.tensor_copy(wl.full[0:96, 0:2, :], wl_f.full[0:96, 0:2, :]), reads=[wl_f.d], writes=[wl.d])
        P.op("dve", lambda e: e.tensor_copy(wl.full[:, 2:4, :], wl_f.full[:, 2:4, :]), reads=[wl_f.d, wl.d], writes=[wl.d])
        Hf = P.sb("Hf", [128, 512], F32)
        Hbs = [P.sb("Hb%d" % i, [128, 512], BF16) for i in range(2)]
        P.op("dve", lambda e: e.memset(Hf.full, 0.0), writes=[Hf.d])
        P.op("dve", lambda e: e.memset(Hbs[0].full, 0.0), writes=[Hbs[0].d])
        pA = [P.psum("pA%d" % i) for i in range(2)]
        pX = [P.psum("pX%d" % i) for i in range(2)]
        tpb = P.psum("tpb", BF16)
        pS = [P.psum("pS%d" % i) for i in range(3)]
        rot = {"pA": 0, "pX": 0, "t": 0}

        def f32t(name, n=1):
            return [P.sb("%s%d" % (name, i), [128, 512], F32) for i in range(n)]

        def bf16t(name, n=1):
            return [P.sb("%s%d" % (name, i), [128, 512], BF16) for i in range(n)]

        zt = [P.sb("zt%d" % i, [128, 513], F32) for i in range(2)]
        dtmp = f32t("dtmp")[0]
        twd = P.sb("twd", [128, 512], BF16)
        lad = P.sb("lad", [128, 512], BF16)
        sgd = P.sb("sgd", [128, 2, 512], BF16)
        lin = f32t("lin")[0]
        rl, kl, vl, sig, aa, kkr, sqk, rn, kk, kp, bb, cw, cwx, E1 = [f32t(n)[0] for n in
            ["rl", "kl", "vl", "sig", "aa", "kkr", "sqk", "rn", "kk", "kp", "bb", "cw", "cwx", "E1"]]
        t1, E0, Ei, rk = rn, cwx, cw, sqk
        bT, kTt, vb = bf16t("bT")[0], bf16t("kTt")[0], bf16t("vb")[0]
        aT = bf16t("aT", 8)
        rT = bf16t("rT", 8)
        bktm = [P.sb("bktm%d" % i, [128, 1024], BF16) for i in range(8)]
        vtm = bf16t("vtm", 8)
        AakT = bf16t("AakT", 8)
        ArbT = bf16t("ArbT", 8)
        ArkT = bf16t("ArkT", 8)
        TT = bf16t("TT", 8)
        gT = bf16t("gT", 8)
        bon = f32t("bon", 8)
        Wc = P.sb("Wc", [128, 8, 8], F32)
        ptmp = bf16t("ptmp", 6)
        Zb = bf16t("Zb")[0]
        Ub = bf16t("Ub")[0]
        tmpH = f32t("tmpH")[0]
        yn = P.sb("yn", [128, 8, 512], BF16)
        gst = P.sb("gst", [128, 256], F32)
        t3 = f32t("t3")[0]
        yo = bf16t("yo", 2)

        def blocks(fn):
            for c in range(8):
                for j in range(2):
                    fn(j, c, slice(64 * j, 64 * j + 64), slice(64 * c, 64 * c + 64))

        def prod(L, R, dst, mask=None, addend=None, eng="dve"):
            ps = pX[rot["pX"] % 2]
            rot["pX"] += 1
            blocks(lambda j, c, pj, cc: P.op("pe", lambda e: e.matmul(ps.full[pj, cc], L.full[pj, cc], R.full[pj, cc], start=True, stop=True),
                                             reads=[L.d, R.d], writes=[ps.d]))
            if mask is not None:
                P.op("dve", lambda e: e.tensor_tensor(dst.full, ps.full, mask, ALU.mult), reads=[ps.d, r.cb.d], writes=[dst.d])
            elif addend is not None:
                P.op("dve", lambda e: e.tensor_tensor(dst.full, ps.full, addend.full, ALU.add), reads=[ps.d, addend.d], writes=[dst.d])
            else:
                P.op("act", lambda e: e.activation(dst.full, ps.full, AF.Copy), reads=[ps.d], writes=[dst.d])

        def load_shift(row0, nrows, sg, mucol, dst_fn):
            z = zt[rot["t"] % 2]
            rot["t"] += 1
            t0 = c0 + sg * 512
            if sg == 0:
                P.op("pool", lambda e: e.memset(z.full[:, 0:1], 0.0), writes=[z.d])
                P.dma("sp", z.full[0:nrows, 1:513], dr["zA"][row0:row0 + nrows, t0:t0 + 512], writes=[z.d])
            else:
                P.dma("sp", z.full[0:nrows, 0:513], dr["zA"][row0:row0 + nrows, t0 - 1:t0 + 512], writes=[z.d])
            P.op("dve", lambda e: e.tensor_tensor(dtmp.full[0:nrows, :], z.full[0:nrows, 0:512], z.full[0:nrows, 1:513], ALU.subtract),
                 reads=[z.d], writes=[dtmp.d])
            dst_fn(z)

        def lerp_to(dst, nrows, mucol):
            def fn(z):
                P.op("dve", lambda e: e.scalar_tensor_tensor(dst.full[0:nrows, :], dtmp.full[0:nrows, :], mucol[0:nrows, :],
                                                             z.full[0:nrows, 1:513], ALU.mult, ALU.add),
                     reads=[dtmp.d, z.d, r.vecs.d], writes=[dst.d])
            return fn

        for sg in range(NSEG):
            t0 = c0 + sg * 512
            load_shift(3072, 96, sg, None, lerp_to(lin, 96, vcol(r, "mu_wd")))
            P.op("act", lambda e: e.activation(twd.full[0:96, :], lin.full[0:96, :], AF.Tanh), reads=[lin.d], writes=[twd.d])
            load_shift(3168, 96, sg, None, lerp_to(lin, 96, vcol(r, "mu_ad")))
            P.op("act", lambda e: e.activation(lad.full[0:96, :], lin.full[0:96, :], AF.Copy), reads=[lin.d], writes=[lad.d])
            for a_ in range(2):
                load_shift(3264 + 128 * a_, 128, sg, None, lerp_to(lin, 128, vcol(r, "mu_gd", a_)))
                P.op("act", lambda e, a_=a_: e.activation(sgd.full[:, a_, :], lin.full, AF.Sigmoid), reads=[lin.d], writes=[sgd.d])
            for hp in range(8):
                cs = slice(hp * 128, hp * 128 + 128)
                load_shift(hp * 128, 128, sg, None, lerp_to(rl, 128, vcol(r, "mu_r", hp)))
                load_shift(1024 + hp * 128, 128, sg, None, lerp_to(kl, 128, vcol(r, "mu_k", hp)))
                load_shift(2048 + hp * 128, 128, sg, None, lerp_to(vl, 128, vcol(r, "mu_v", hp)))
                ps = pA[rot["pA"] % 2]; rot["pA"] += 1
                P.op("pe", lambda e, ps=ps, cs=cs: e.matmul(ps.full, wl.full[0:96, 0, cs], twd.full[0:96, :], start=True, stop=True),
                     reads=[wl.d, twd.d], writes=[ps.d])
                P.op("act", lambda e, ps=ps, hp=hp: e.activation(sig.full, ps.full, AF.Sigmoid, bias=vcol(r, "w0", hp)),
                     reads=[ps.d, r.vecs.d], writes=[sig.d])
                ps = pA[rot["pA"] % 2]; rot["pA"] += 1
                P.op("pe", lambda e, ps=ps, cs=cs: e.matmul(ps.full, wl.full[0:96, 1, cs], lad.full[0:96, :], start=True, stop=True),
                     reads=[wl.d, lad.d], writes=[ps.d])
                P.op("act", lambda e, ps=ps, hp=hp: e.activation(aa.full, ps.full, AF.Sigmoid, bias=vcol(r, "a0", hp)),
                     reads=[ps.d, r.vecs.d], writes=[aa.d])
                ps = pA[rot["pA"] % 2]; rot["pA"] += 1
                for a_ in range(2):
                    P.op("pe", lambda e, ps=ps, cs=cs, a_=a_: e.matmul(ps.full, wl.full[:, 2 + a_, cs], sgd.full[:, a_, :],
                                                                      start=(a_ == 0), stop=(a_ == 1)),
                         reads=[wl.d, sgd.d], writes=[ps.d])
                P.op("act", lambda e, ps=ps, hp=hp: e.activation(gT[hp].full, ps.full, AF.Copy), reads=[ps.d], writes=[gT[hp].d])
                P.op("dve", lambda e, hp=hp: e.tensor_scalar(kkr.full, kl.full, vcol(r, "k_k", hp), None, ALU.mult),
                     reads=[kl.d, r.vecs.d], writes=[kkr.d])
                P.op("dve", lambda e: e.tensor_tensor(sqk.full, kkr.full, kkr.full, ALU.mult), reads=[kkr.d], writes=[sqk.d])
                ps = pA[rot["pA"] % 2]; rot["pA"] += 1
                P.op("pe", lambda e, ps=ps: e.matmul(ps.full, r.blockones, sqk.full, start=True, stop=True),
                     reads=[sqk.d, r.cf.d], writes=[ps.d])
                P.op("dve", lambda e, ps=ps: e.tensor_scalar(rn.full, ps.full, 1e-24, None, ALU.max), reads=[ps.d], writes=[rn.d])
                P.op("act", lambda e: e.activation(rn.full, rn.full, AF.Sqrt), reads=[rn.d], writes=[rn.d])
                P.op("dve", lambda e: e.reciprocal(rn.full, rn.full), reads=[rn.d], writes=[rn.d])
                P.op("dve", lambda e: e.tensor_tensor(kk.full, kkr.full, rn.full, ALU.mult), reads=[kkr.d, rn.d], writes=[kk.d])
                P.op("dve", lambda e, hp=hp: e.tensor_scalar(t1.full, aa.full, -1.0, vcol(r, "k_a", hp), ALU.add, ALU.mult),
                     reads=[aa.d, r.vecs.d], writes=[t1.d])
                P.op("dve", lambda e: e.scalar_tensor_tensor(kp.full, t1.full, 1.0, kl.full, ALU.add, ALU.mult),
                     reads=[t1.d, kl.d], writes=[kp.d])
                P.op("dve", lambda e: e.tensor_tensor(bb.full, kk.full, aa.full, ALU.mult), reads=[kk.d, aa.d], writes=[bb.d])
                P.op("dve", lambda e: e.tensor_tensor_scan(cw.full, r.scanmask, sig.full, 0.0, ALU.mult, ALU.add),
                     reads=[sig.d, r.cf.d], writes=[cw.d])
                P.op("dve", lambda e: e.tensor_tensor(cwx.full, cw.full, sig.full, ALU.subtract), reads=[cw.d, sig.d], writes=[cwx.d])
                P.op("act", lambda e: e.activation(E1.full, cw.full, AF.Exp, scale=-C0), reads=[cw.d], writes=[E1.d])
                P.op("act", lambda e: e.activation(E0.full, cwx.full, AF.Exp, scale=-C0), reads=[cwx.d], writes=[E0.d])
                P.op("act", lambda e: e.activation(Ei.full, cw.full, AF.Exp, scale=C0), reads=[cw.d], writes=[Ei.d])
                P.op("dve", lambda e, hp=hp: e.scalar_tensor_tensor(aT[hp].full, kk.full, -1.0, E0.full, ALU.mult, ALU.mult),
                     reads=[kk.d, E0.d], writes=[aT[hp].d])
                P.op("dve", lambda e, hp=hp: e.tensor_tensor(rT[hp].full, rl.full, E1.full, ALU.mult), reads=[rl.d, E1.d], writes=[rT[hp].d])
                P.op("dve", lambda e: e.tensor_tensor(bT.full, bb.full, Ei.full, ALU.mult), reads=[bb.d, Ei.d], writes=[bT.d])
                P.op("dve", lambda e: e.tensor_tensor(kTt.full, kp.full, Ei.full, ALU.mult), reads=[kp.d, Ei.d], writes=[kTt.d])
                P.op("dve", lambda e, hp=hp: e.tensor_copy(Wc.full[:, hp, :], E1.full.rearrange("p (c t) -> p c t", t=64)[:, :, 63]),
                     reads=[E1.d], writes=[Wc.d])
                P.op("act", lambda e: e.activation(vb.full, vl.full, AF.Copy), reads=[vl.d], writes=[vb.d])
                P.op("dve", lambda e, hp=hp: e.scalar_tensor_tensor(rk.full, rl.full, vcol(r, "r_k", hp), kp.full, ALU.mult, ALU.mult),
                     reads=[rl.d, kp.d, r.vecs.d], writes=[rk.d])
                ps = pA[rot["pA"] % 2]; rot["pA"] += 1
                P.op("pe", lambda e, ps=ps: e.matmul(ps.full, r.blockones, rk.full, start=True, stop=True),
                     reads=[rk.d, r.cf.d], writes=[ps.d])
                P.op("dve", lambda e, ps=ps, hp=hp: e.tensor_tensor(bon[hp].full, ps.full, vl.full, ALU.mult),
                     reads=[ps.d, vl.d], writes=[bon[hp].d])
                for X, off in ((bT, 0), (kTt, 512)):
                    blocks(lambda j, c, pj, cc, X=X, off=off: P.op("pe", lambda e: e.transpose(
                        tpb.full[pj, off + 64 * c:off + 64 * c + 64], X.full[pj, cc], r.ident[pj, pj]),
                        reads=[X.d, r.cb.d], writes=[tpb.d]))
                P.op("dve", lambda e, hp=hp: e.tensor_copy(bktm[hp].full, tpb.full), reads=[tpb.d], writes=[bktm[hp].d])
                blocks(lambda j, c, pj, cc: P.op("pe", lambda e: e.transpose(tpb.full[pj, cc], vb.full[pj, cc], r.ident[pj, pj]),
                                                 reads=[vb.d, r.cb.d], writes=[tpb.d]))
                P.op("dve", lambda e, hp=hp: e.tensor_copy(vtm[hp].full, tpb.full[:, 0:512]), reads=[tpb.d], writes=[vtm[hp].d])
                P0, P0T = ptmp[0], ptmp[1]
                prod(aT[hp], bT, P0, mask=r.ML)
                prod(bT, aT[hp], P0T, mask=r.MU)
                prod(kTt, aT[hp], AakT[hp], mask=r.MU)
                prod(bT, rT[hp], ArbT[hp], mask=r.MUI)
                prod(kTt, rT[hp], ArkT[hp], mask=r.MUI)
                P.op("dve", lambda e, hp=hp: e.tensor_tensor(TT[hp].full, P0T.full, r.IB, ALU.add), reads=[P0T.d, r.cb.d], writes=[TT[hp].d])
                Pc, PcT = P0, P0T
                free = [ptmp[2], ptmp[3], ptmp[4], ptmp[5]]
                for lvl in range(1, 6):
                    Pn = free.pop(0)
                    prod(PcT, Pc, Pn)
                    PnT = None
                    if lvl < 5:
                        PnT = free.pop(0)
                        prod(Pc, PcT, PnT)
                    prod(Pn, TT[hp], TT[hp], addend=TT[hp])
                    free.append(Pc)
                    free.append(PcT)
                    Pc, PcT = Pn, PnT
            for c in range(8):
                gc = sg * 8 + c
                Hc, Hn = Hbs[gc % 2], Hbs[(gc + 1) % 2]
                cc = slice(64 * c, 64 * c + 64)

                def heads(fn):
                    for hp in range(8):
                        for j in range(2):
                            fn(hp, slice(64 * j, 64 * j + 64), slice(64 * hp, 64 * hp + 64))

                heads(lambda hp, pj, hh: (
                    P.op("pe", lambda e, cc=cc, Hc=Hc, c=c: e.matmul(pS[0].full[pj, hh], aT[hp].full[pj, cc], Hc.full[pj, hh], start=True, stop=False),
                         reads=[aT[hp].d, Hc.d], writes=[pS[0].d]),
                    P.op("pe", lambda e, cc=cc, Hc=Hc, c=c: e.matmul(pS[0].full[pj, hh], AakT[hp].full[pj, cc], vtm[hp].full[pj, cc], start=False, stop=True),
                         reads=[AakT[hp].d, vtm[hp].d], writes=[pS[0].d])))
                P.op("act", lambda e: e.activation(Zb.full, pS[0].full, AF.Copy), reads=[pS[0].d], writes=[Zb.d])
                heads(lambda hp, pj, hh: P.op("pe", lambda e, cc=cc, Hc=Hc, c=c: e.matmul(pS[1].full[pj, hh], TT[hp].full[pj, cc], Zb.full[pj, hh], start=True, stop=True),
                                              reads=[TT[hp].d, Zb.d], writes=[pS[1].d]))
                P.op("dve", lambda e: e.tensor_copy(Ub.full, pS[1].full), reads=[pS[1].d], writes=[Ub.d])
                heads(lambda hp, pj, hh: (
                    P.op("pe", lambda e, cc=cc, Hc=Hc, c=c: e.matmul(pS[2].full[pj, hh], rT[hp].full[pj, cc], Hc.full[pj, hh], start=True, stop=False),
                         reads=[rT[hp].d, Hc.d], writes=[pS[2].d]),
                    P.op("pe", lambda e, cc=cc, Hc=Hc, c=c: e.matmul(pS[2].full[pj, hh], ArbT[hp].full[pj, cc], Ub.full[pj, hh], start=False, stop=False),
                         reads=[ArbT[hp].d, Ub.d], writes=[pS[2].d]),
                    P.op("pe", lambda e, cc=cc, Hc=Hc, c=c: e.matmul(pS[2].full[pj, hh], ArkT[hp].full[pj, cc], vtm[hp].full[pj, cc], start=False, stop=True),
                         reads=[ArkT[hp].d, vtm[hp].d], writes=[pS[2].d])))
                P.op("act", lambda e, c=c: e.activation(Yseg.full[:, c, :], pS[2].full, AF.Copy), reads=[pS[2].d], writes=[Yseg.d])
                heads(lambda hp, pj, hh: (
                    P.op("pe", lambda e, cc=cc, Hc=Hc, c=c: e.matmul(pS[0].full[pj, hh], bktm[hp].full[pj, cc], Ub.full[pj, hh], start=True, stop=False),
                         reads=[bktm[hp].d, Ub.d], writes=[pS[0].d]),
                    P.op("pe", lambda e, cc=cc, Hc=Hc, c=c: e.matmul(pS[0].full[pj, hh], bktm[hp].full[pj, 512 + 64 * c:512 + 64 * c + 64], vtm[hp].full[pj, cc],
                                                  start=False, stop=True),
                         reads=[bktm[hp].d, vtm[hp].d], writes=[pS[0].d])))
                P.op("dve", lambda e: e.tensor_tensor(tmpH.full, pS[0].full, Hf.full, ALU.add), reads=[pS[0].d, Hf.d], writes=[tmpH.d])
                P.op("dve", lambda e, c=c: e.tensor_tensor(
                    Hf.full.rearrange("p (h v) -> p h v", v=64), tmpH.full.rearrange("p (h v) -> p h v", v=64),
                    Wc.full[:, :, c:c + 1].broadcast_to([128, 8, 64]), ALU.mult), reads=[tmpH.d, Wc.d], writes=[Hf.d])
                P.op("act", lambda e, Hn=Hn: e.activation(Hn.full, Hf.full, AF.Copy), reads=[Hf.d], writes=[Hn.d])
            Yv = Yseg.full.rearrange("p c (h v) -> p (c h) v", v=64)
            Nv = yn.full.rearrange("p c (h v) -> p (c h) v", v=64)
            P.op("dve", lambda e: e.reduce_sum(gst.full[:, 0:64], Yv, AX.X), reads=[Yseg.d], writes=[gst.d])
            P.op("dve", lambda e: e.tensor_scalar(gst.full[:, 64:128], gst.full[:, 0:64], 1.0 / 64, None, ALU.mult), reads=[gst.d], writes=[gst.d])
            P.op("dve", lambda e: e.tensor_tensor(Yv, Yv, gst.full[:, 64:128].unsqueeze(2).broadcast_to([128, 64, 64]), ALU.subtract),
                 reads=[Yseg.d, gst.d], writes=[Yseg.d])
            P.op("dve", lambda e: e.tensor_tensor(Nv, Yv, Yv, ALU.mult), reads=[Yseg.d], writes=[yn.d])
            P.op("dve", lambda e: e.reduce_sum(gst.full[:, 128:192], Nv, AX.X), reads=[yn.d, gst.d], writes=[gst.d])
            P.op("act", lambda e: e.activation(gst.full[:, 192:256], gst.full[:, 128:192], AF.Sqrt, bias=GN_EPS, scale=1.0 / 64),
                 reads=[gst.d], writes=[gst.d])
            P.op("dve", lambda e: e.reciprocal(gst.full[:, 192:256], gst.full[:, 192:256]), reads=[gst.d], writes=[gst.d])
            P.op("dve", lambda e: e.tensor_tensor(Nv, Yv, gst.full[:, 192:256].unsqueeze(2).broadcast_to([128, 64, 64]), ALU.mult),
                 reads=[Yseg.d, gst.d], writes=[yn.d])
            for hp in range(8):
                blocks(lambda j, c, pj, cc, hp=hp: P.op("pe", lambda e: e.transpose(
                    tpb.full[pj, cc], yn.full[pj, c, 64 * hp:64 * hp + 64], r.ident[pj, pj]), reads=[yn.d, r.cb.d], writes=[tpb.d]))
                y_ = yo[hp % 2]
                P.op("dve", lambda e, hp=hp: e.tensor_scalar(t3.full, tpb.full[:, 0:512], vcol(r, "lnx_g", hp), vcol(r, "lnx_b", hp),
                                                            ALU.mult, ALU.add), reads=[tpb.d, r.vecs.d], writes=[t3.d])
                P.op("dve", lambda e, hp=hp: e.tensor_tensor(t3.full, t3.full, bon[hp].full, ALU.add), reads=[t3.d, bon[hp].d], writes=[t3.d])
                P.op("dve", lambda e, hp=hp, y_=y_: e.tensor_tensor(y_.full, t3.full, gT[hp].full, ALU.mult), reads=[t3.d, gT[hp].d], writes=[y_.d])
                P.dma("sp", dr["yA"][hp * 128:(hp + 1) * 128, t0:t0 + 512], y_.full, reads=[y_.d])
        P.emit()
    nc.all_engine_barrier()
```

```python
import numpy as np
import ml_dtypes
import concourse.bass as bass
import concourse.mybir as mybir
from concourse.bass_utils import run_bass_kernel_spmd

F32 = mybir.dt.float32
BF16 = mybir.dt.bfloat16
ALU = mybir.AluOpType
AF = mybir.ActivationFunctionType
AX = mybir.AxisListType


class Dep:
    __slots__ = ("name", "w", "rs")

    def __init__(self, name):
        self.name = name
        self.w = None
        self.rs = []


class Op:
    __slots__ = ("eng", "fn", "deps", "is_dma", "sem", "semval", "need_sig", "sigidx", "prev_same_sem")

    def __init__(self, eng, fn, is_dma):
        self.eng = eng
        self.fn = fn
        self.deps = []
        self.is_dma = is_dma
        self.sem = None
        self.semval = 0
        self.need_sig = False
        self.sigidx = 0
        self.prev_same_sem = None


class SB:
    def __init__(self, handle, dep):
        self.h = handle
        self.full = handle.ap()
        self.d = dep

    def __getitem__(self, k):
        return self.full[k]


ENGS = ("pe", "act", "dve", "pool", "sp")
N_DMA_SEMS = 40


class Prog:
    G = None

    @staticmethod
    def init_global(nc, st):
        g = {}
        g["sems"] = {e: st.enter_context(nc.semaphore("gs_" + e)) for e in ENGS}
        g["dsems"] = [st.enter_context(nc.semaphore("gd_%d" % i)) for i in range(N_DMA_SEMS)]
        g["cnt"] = {e: 0 for e in ENGS}
        g["n_dma"] = 0
        g["dma_last"] = [None] * N_DMA_SEMS
        g["dma_cnt"] = [0] * N_DMA_SEMS
        Prog.G = g

    def __init__(self, nc, st):
        self.nc = nc
        self.st = st
        self.ops = []
        self.deps = {}

    def dep(self, name):
        d = self.deps.get(name)
        if d is None:
            d = Dep(name)
            self.deps[name] = d
        return d

    _uid = [0]

    def sb(self, name, shape, dtype):
        Prog._uid[0] += 1
        h = self.st.enter_context(self.nc.sbuf_tensor("s%d_%s" % (Prog._uid[0], name), list(shape), dtype))
        return SB(h, Dep(name))

    def psum(self, name, dtype=F32):
        n = 512 if dtype == F32 else 1024
        Prog._uid[0] += 1
        h = self.st.enter_context(self.nc.psum_tensor("p%d_%s" % (Prog._uid[0], name), [128, n], dtype))
        return SB(h, Dep(name))

    def _track(self, op, reads, writes):
        ds = set()
        for b in reads:
            if b.w is not None:
                ds.add(b.w)
        for b in writes:
            if b.w is not None:
                ds.add(b.w)
            for r in b.rs:
                ds.add(r)
        ds.discard(op)
        for d in ds:
            if d.eng == "pe" and op.eng == "pe" and not d.is_dma and not op.is_dma:
                continue
            op.deps.append(d)
            d.need_sig = True
        for b in reads:
            b.rs.append(op)
        for b in writes:
            b.w = op
            b.rs = []

    def op(self, eng, fn, reads=(), writes=()):
        o = Op(eng, fn, False)
        self._track(o, reads, writes)
        self.ops.append(o)
        return o

    def dma(self, queue, out, in_, reads=(), writes=(), **kw):
        o = Op(queue, lambda e: e.dma_start(out=out, in_=in_, **kw), True)
        g = Prog.G
        s = g["n_dma"] % N_DMA_SEMS
        g["n_dma"] += 1
        o.sem = s
        g["dma_cnt"][s] += 16
        o.semval = g["dma_cnt"][s]
        o.prev_same_sem = g["dma_last"][s]
        g["dma_last"][s] = o
        self._track(o, reads, writes)
        self.ops.append(o)
        return o

    def emit(self):
        nc = self.nc
        g = Prog.G
        sems = g["sems"]
        dsems = g["dsems"]
        from contextlib import ExitStack
        with ExitStack() as st:
            cnt = g["cnt"]
            for o in self.ops:
                if not o.is_dma and o.need_sig:
                    cnt[o.eng] += 1
                    o.sigidx = cnt[o.eng]
            per_eng = {e: [] for e in ENGS}
            for o in self.ops:
                per_eng[o.eng].append(o)
            block = st.enter_context(nc.Block())

            def run(engname, eng):
                seen = {}

                def wait(sem, val, key):
                    if seen.get(key, 0) >= val:
                        return
                    seen[key] = val
                    eng.wait_ge(sem, val)

                for o in per_eng[engname]:
                    for d in o.deps:
                        if d.is_dma:
                            wait(dsems[d.sem], d.semval, ("d", d.sem))
                        else:
                            wait(sems[d.eng], d.sigidx, ("e", d.eng))
                    if o.is_dma:
                        p = o.prev_same_sem
                        if p is not None:
                            wait(dsems[p.sem], p.semval, ("d", p.sem))
                        o.fn(eng).then_inc(dsems[o.sem], 16)
                    else:
                        ins = o.fn(eng)
                        if o.need_sig:
                            ins.then_inc(sems[engname], 1)
                if engname == "sp":
                    for s in range(N_DMA_SEMS):
                        if g["dma_cnt"][s]:
                            wait(dsems[s], g["dma_cnt"][s], ("d", s))

            @block.tensor
            def _(eng):
                run("pe", eng)

            @block.scalar
            def _(eng):
                run("act", eng)

            @block.vector
            def _(eng):
                run("dve", eng)

            @block.gpsimd
            def _(eng):
                run("pool", eng)

            @block.sync
            def _(eng):
                run("sp", eng)
        self.ops = []


import math
from contextlib import ExitStack

D = 2048
AW = 1024
DL = 96
GL = 256
A_COLS = 3 * AW + 2 * DL + GL
QKC = 1024
IN_COLS = 10688
DFF = 8192
NCORES = 8
C0 = math.exp(-0.5)
LAMBDA_INIT = 0.8 - 0.6 * math.exp(-0.3 * 0)
GN_EPS = 64e-5
SUBLN_EPS = 1e-5
RMS_EPS = 1e-6

VC = {}
_o = 0
for _n, _w in [("g_mix", 16), ("g_mlp", 16), ("g_fin", 16), ("mu_r", 8), ("mu_k", 8), ("mu_v", 8),
               ("mu_wd", 1), ("mu_ad", 1), ("mu_gd", 2), ("w0", 8), ("a0", 8), ("k_k", 8), ("k_a", 8),
               ("r_k", 8), ("lnx_g", 8), ("lnx_b", 8)]:
    VC[_n] = _o
    _o += _w
NV = _o


class Cfg:
    def __init__(self, S=2048, NB=2, phases="ABCDE", dump=(), inject=()):
        self.S = S
        self.NB = NB
        self.T = S * NB
        self.phases = phases
        self.dump = set(dump)
        self.inject = set(inject)


def _pm(v, n):
    return np.ascontiguousarray(np.asarray(v, np.float32).reshape(n, 128).T)


class Res:
    pass


def load_consts(P, dr, names):
    r = Res()
    if "vecs" in names:
        r.vecs = P.sb("vecs", [128, NV], F32)
        P.dma("sp", r.vecs.full, dr["vecs"], writes=[r.vecs.d])
    if "cb" in names:
        r.cb = P.sb("cb", [128, 4 * 512 + 256], BF16)
        P.dma("sp", r.cb.full, dr["cb"], writes=[r.cb.d])
        r.ML = r.cb.full[:, 0:512]
        r.MU = r.cb.full[:, 512:1024]
        r.MUI = r.cb.full[:, 1024:1536]
        r.IB = r.cb.full[:, 1536:2048]
        r.ident = r.cb.full[:, 2048:2176]
        r.ones = r.cb.full[:, 2176:2304]
    if "cf" in names:
        r.cf = P.sb("cf", [128, 640], F32)
        P.dma("sp", r.cf.full, dr["cf"], writes=[r.cf.d])
        r.blockones = r.cf.full[:, 0:128]
        r.scanmask = r.cf.full[:, 128:640]
    return r


def vcol(r, name, i=0):
    c = VC[name] + i
    return r.vecs.full[:, c:c + 1]


class WStream:
    def __init__(self, P, nbuf=4):
        self.P = P
        self.bf = [P.sb("wbf%d" % i, [128, 16, 128], BF16) for i in range(nbuf)]
        self.i = 0

    def load(self, w_dram, k0, col0, width, nk=16):
        P = self.P
        bf = self.bf[self.i % len(self.bf)]
        self.i += 1
        src = w_dram[k0 * 128:(k0 + nk) * 128, col0:col0 + width].rearrange("(kc p) n -> p kc n", p=128)
        P.dma("pool", bf.full[:, 0:nk, 0:width], src, writes=[bf.d])
        return bf


def linear_fm(P, ws, w_dram, KC, blocks, xT, xdeps, NTG, banks, evac, tgw=512):
    kgs = min(16, KC)
    nkg = KC // kgs
    bk = 0
    for bi, (col0, width, tag) in enumerate(blocks):
        pss = []
        for tg in range(NTG):
            pss.append(banks[bk % len(banks)])
            bk += 1
        for kg in range(nkg):
            wb = ws.load(w_dram, kg * kgs, col0, width, kgs)
            for tg in range(NTG):
                ps = pss[tg]
                for kc in range(kgs):
                    kk = kg * kgs + kc
                    P.op("pe", lambda e, ps=ps, wb=wb, kc=kc, kk=kk, tg=tg, width=width:
                         e.matmul(ps.full[0:width, 0:tgw], wb.full[:, kc, 0:width], xT(kk, tg),
                                  start=(kk == 0), stop=(kk == KC - 1)),
                         reads=[wb.d] + xdeps(kk, tg), writes=[ps.d])
        for tg in range(NTG):
            evac(tag, bi, tg, pss[tg], width)


def rmsnorm_to_bf16(P, r, src_dram, c0, ntok, uT, gname, banks_small, xs_bufs, sq, rs, rinv, eps=RMS_EPS, q="act"):
    G = 256
    for gi in range(ntok // G):
        xs = xs_bufs[gi % len(xs_bufs)]
        src = src_dram[:, c0 + gi * G:c0 + (gi + 1) * G].rearrange("(kc p) t -> p kc t", p=128)
        P.dma(q, xs.full, src, writes=[xs.d])
        P.op("act", lambda e, xs=xs: e.activation(sq.full, xs.full, AF.Square), reads=[xs.d], writes=[sq.d])
        ps = banks_small[gi % len(banks_small)]
        for kc in range(16):
            P.op("pe", lambda e, ps=ps, kc=kc: e.matmul(ps.full[:, 0:G], r.ones, sq.full[:, kc, :],
                                                        start=(kc == 0), stop=(kc == 15)),
                 reads=[sq.d, r.cb.d], writes=[ps.d])
        P.op("act", lambda e, ps=ps: e.activation(rs.full, ps.full[:, 0:G], AF.Sqrt, bias=eps, scale=1.0 / D),
             reads=[ps.d], writes=[rs.d])
        P.op("dve", lambda e: e.reciprocal(rinv.full, rs.full), reads=[rs.d], writes=[rinv.d])
        for kc in range(16):
            P.op("dve", lambda e, xs=xs, kc=kc, gi=gi: e.scalar_tensor_tensor(
                uT.full[:, kc, gi * G:(gi + 1) * G], xs.full[:, kc, :], vcol(r, gname, kc), rinv.full,
                ALU.mult, ALU.mult),
                reads=[xs.d, rinv.d, r.vecs.d], writes=[uT.d])


def phase_A(nc, cfg, dr, b):
    S = cfg.S
    c0 = b * S
    NTG = S // 512
    with ExitStack() as st:
        P = Prog(nc, st)
        r = load_consts(P, dr, ["vecs", "cb"])
        uT = P.sb("uT", [128, 16, S], BF16)
        xs_bufs = [P.sb("xs%d" % i, [128, 16, 256], F32) for i in range(2)]
        sq = P.sb("sq", [128, 16, 256], BF16)
        rs = P.sb("rs", [128, 256], F32)
        rinv = P.sb("rinv", [128, 256], F32)
        banks = [P.psum("pm%d" % i) for i in range(6)]
        bsm = [P.psum("psm%d" % i) for i in range(2)]
        ws = WStream(P)
        stf = [P.sb("stf%d" % i, [128, S], F32) for i in range(2)]
        stb = [P.sb("stb%d" % i, [128, S], BF16) for i in range(2)]
        rmsnorm_to_bf16(P, r, dr["xT"], c0, S, uT, "g_mix", bsm, xs_bufs, sq, rs, rinv)

        blocks = []
        for i in range(24):
            blocks.append((i * 128, 128, ("zA", i * 128)))
        blocks.append((3072, 96, ("zA", 3072)))
        blocks.append((3168, 96, ("zA", 3168)))
        blocks.append((3264, 128, ("zA", 3264)))
        blocks.append((3392, 128, ("zA", 3392)))
        for i in range(24):
            blocks.append((A_COLS + i * 128, 128, ("qkv", i * 128)))
        for i in range(32):
            blocks.append((A_COLS + 3072 + i * 128, 128, ("gate", i * 128)))
        cnt = {"f": 0, "b": 0}

        def evac(tag, bi, tg, ps, width):
            kind, row0 = tag
            if kind == "zA":
                sbuf = stf[cnt["f"] % 2]
                P.op("act", lambda e: e.activation(sbuf.full[0:width, tg * 512:(tg + 1) * 512], ps.full[0:width, :], AF.Copy),
                     reads=[ps.d], writes=[sbuf.d])
                if tg == NTG - 1:
                    P.dma("act", dr["zA"][row0:row0 + width, c0:c0 + S], sbuf.full[0:width, :], reads=[sbuf.d])
                    cnt["f"] += 1
            else:
                sbuf = stb[cnt["b"] % 2]
                fn = AF.Copy if kind == "qkv" else AF.Sigmoid
                P.op("act", lambda e: e.activation(sbuf.full[0:width, tg * 512:(tg + 1) * 512], ps.full[0:width, :], fn),
                     reads=[ps.d], writes=[sbuf.d])
                if tg == NTG - 1:
                    dst = dr["qkv"] if kind == "qkv" else dr["gates"]
                    P.dma("act", dst[row0:row0 + width, c0:c0 + S], sbuf.full[0:width, :], reads=[sbuf.d])
                    cnt["b"] += 1

        import os
        if os.environ.get("DBG_A") == "norm":
            blocks = []
        elif os.environ.get("DBG_A"):
            blocks = blocks[:int(os.environ["DBG_A"])]
        linear_fm(P, ws, dr["w_in"], 16, blocks, lambda kk, tg: uT.full[:, kk, tg * 512:(tg + 1) * 512],
                  lambda kk, tg: [uT.d], NTG, banks, evac)
        P.emit()
    nc.all_engine_barrier()


SCRATCH = {
    "zA": ([A_COLS, None], F32),
    "qkv": ([3072, None], BF16),
    "gates": ([4096, None], BF16),
    "yA": ([AW, None], BF16),
    "yB": ([AW, None], BF16),
    "h": ([D, None], F32),
}


def build_nc(cfg):
    nc = bass.Bass("TRN2", target_bir_lowering=False)
    T = cfg.T
    dr = {}
    gst = ExitStack()
    Prog.init_global(nc, gst)

    def inp(name, shape, dt=F32):
        dr[name] = nc.dram_tensor(name, list(shape), dt, kind="ExternalInput").ap()

    inp("xT", [D, T])
    if "A" in cfg.phases:
        inp("w_in", [D, IN_COLS])
    if "D" in cfg.phases:
        inp("p_a", [AW, D])
        inp("p_b", [AW, D])
        inp("w_out", [D, D])
    if "E" in cfg.phases:
        inp("w_ff1", [D, DFF])
        inp("w_ff2", [DFF, D])
    if "B" in cfg.phases:
        inp("w_up", [DL, AW])
        inp("a_up", [DL, AW])
        inp("g_up", [GL, AW])
    inp("vecs", [128, NV])
    inp("bc", [128, 16 + 256 + 128])
    inp("biasT", [128, 2 * 16 * 128])
    inp("cb", [128, 4 * 512 + 256], BF16)
    inp("cf", [128, 640])
    for name, (shape, dt) in SCRATCH.items():
        shp = [shape[0], T]
        if name in cfg.inject:
            kind = "ExternalInput"
        elif name in cfg.dump:
            kind = "ExternalOutput"
        else:
            kind = "Internal"
        dr[name] = nc.dram_tensor(name, shp, dt, kind=kind).ap()
    dr["outT"] = nc.dram_tensor("outT", [D, T], F32, kind="ExternalOutput").ap()
    for b in range(cfg.NB):
        if "A" in cfg.phases:
            phase_A(nc, cfg, dr, b)
        if "B" in cfg.phases:
            phase_B(nc, cfg, dr, b)
        if "C" in cfg.phases:
            phase_C(nc, cfg, dr, b)
        if "D" in cfg.phases:
            phase_D(nc, cfg, dr, b)
        if "E" in cfg.phases:
            phase_E(nc, cfg, dr, b)
    return nc


def t5_bucket_np(rel):
    nb = 16
    max_exact = 8
    ret = np.where(rel > 0, nb, 0)
    n = np.abs(rel)
    nf = np.maximum(n, 1).astype(np.float32)
    large = max_exact + (np.log(nf / max_exact) / math.log(128 / max_exact) * (nb - max_exact)).astype(np.int32)
    large = np.minimum(large, nb - 1)
    return ret + np.where(n < max_exact, n, large)


def host_consts(inp):
    f = np.float32
    vecs = np.zeros((128, NV), f)

    def put(name, v, n):
        vecs[:, VC[name]:VC[name] + n] = _pm(v, n)

    put("g_mix", inp["norm_mix_g"][0], 16)
    put("g_mlp", inp["norm_mlp_g"][0], 16)
    put("g_fin", inp["norm_final_g"], 16)
    mu = np.asarray(inp["mu_shift"][0], f)
    put("mu_r", mu[0:1024], 8)
    put("mu_k", mu[1024:2048], 8)
    put("mu_v", mu[2048:3072], 8)
    vecs[0:96, VC["mu_wd"]] = mu[3072:3168]
    vecs[0:96, VC["mu_ad"]] = mu[3168:3264]
    put("mu_gd", mu[3264:3520], 2)
    put("w0", inp["w0"][0], 8)
    put("a0", inp["a0"][0], 8)
    put("k_k", inp["k_k"][0], 8)
    put("k_a", inp["k_a"][0], 8)
    put("r_k", np.asarray(inp["r_k"][0]).reshape(-1), 8)
    put("lnx_g", inp["lnx_g"][0], 8)
    put("lnx_b", inp["lnx_b"][0], 8)

    rb = np.asarray(inp["rel_bias"], f)
    bc = np.zeros((128, 16 + 256 + 128), f)
    bc[:, 0:16] = rb[15][None, :]
    for i, nme in enumerate(["lambda_q1", "lambda_k1", "lambda_q2", "lambda_k2"]):
        bc[:, 16 + 64 * i:16 + 64 * (i + 1)] = np.asarray(inp[nme][0], f)[None, :]
    bc[:, 272:400] = np.asarray(inp["subln_g"][0], f)[None, :]
    kl = np.arange(128)[:, None]
    ql = np.arange(128)[None, :]
    biasT = np.zeros((128, 2, 16, 128), f)
    for ty, off in enumerate([0, -128]):
        bidx = t5_bucket_np(off + kl - ql)
        biasT[:, ty, :, :] = np.transpose(rb[bidx], (0, 2, 1))
    biasT = biasT.reshape(128, -1)
    row = (np.arange(128) % 64)[:, None]
    col = (np.arange(512) % 64)[None, :]
    ML = (col < row).astype(f)
    MU = (row < col).astype(f)
    MUI = (row <= col).astype(f)
    IB = (row == col).astype(f)
    ident = np.eye(128, dtype=f)
    ones = np.ones((128, 128), f)
    cb = np.concatenate([ML, MU, MUI, IB, ident, ones], axis=1).astype(ml_dtypes.bfloat16)
    blockones = np.zeros((128, 128), f)
    blockones[0:64, 0:64] = 1
    blockones[64:, 64:] = 1
    scanmask = np.broadcast_to((np.arange(512) % 64 != 0).astype(f)[None, :], (128, 512))
    cf = np.concatenate([blockones, scanmask], axis=1).astype(f)
    out = {"vecs": vecs, "bc": bc, "biasT": np.ascontiguousarray(biasT), "cb": np.ascontiguousarray(cb),
           "cf": np.ascontiguousarray(cf)}
    for nme in ["w_in", "p_a", "p_b", "w_out", "w_ff1", "w_ff2", "w_up", "a_up", "g_up"]:
        out[nme] = np.ascontiguousarray(np.asarray(inp[nme][0], f))
    return out


def kernel(**inputs):
    cfg = Cfg()
    x = np.asarray(inputs["x"], np.float32)
    B, S, _ = x.shape
    nc = build_nc(cfg)
    shared = host_consts(inputs)
    in_maps = []
    for c in range(NCORES):
        m = dict(shared)
        xs = x[c * cfg.NB:(c + 1) * cfg.NB].reshape(cfg.T, D)
        m["xT"] = np.ascontiguousarray(xs.T)
        in_maps.append(m)
    res = run_bass_kernel_spmd(nc, in_maps, core_ids=list(range(NCORES)))
    out = np.empty((B, S, D), np.float32)
    for c in range(NCORES):
        o = res.results[c]["outT"]
        out[c * cfg.NB:(c + 1) * cfg.NB] = o.T.reshape(cfg.NB, S, D)
    return out


def phase_D(nc, cfg, dr, b):
    S = cfg.S
    c0 = b * S
    NTG = S // 512
    with ExitStack() as st:
        P = Prog(nc, st)
        yAs = P.sb("yAs", [128, 8, S], BF16)
        yBs = P.sb("yBs", [128, 8, S], BF16)
        mT = P.sb("mT", [128, 16, S], BF16)
        P.dma("act", yAs.full, dr["yA"][:, c0:c0 + S].rearrange("(kc p) t -> p kc t", p=128), writes=[yAs.d])
        P.dma("act", yBs.full, dr["yB"][:, c0:c0 + S].rearrange("(kc p) t -> p kc t", p=128), writes=[yBs.d])
        ws = WStream(P)
        banks = [P.psum("pm%d" % i) for i in range(8)]
        gts = [P.sb("gt%d" % i, [128, 2, S], BF16) for i in range(2)]
        t1 = P.sb("t1", [128, 512], F32)
        t2 = P.sb("t2", [128, 512], F32)
        xts = [P.sb("xt%d" % i, [128, S], F32) for i in range(2)]
        sth = [P.sb("sth%d" % i, [128, S], F32) for i in range(2)]
        for cb in range(16):
            gt = gts[cb % 2]
            P.dma("act", gt.full[:, 0, :], dr["gates"][cb * 128:(cb + 1) * 128, c0:c0 + S], writes=[gt.d])
            P.dma("act", gt.full[:, 1, :], dr["gates"][2048 + cb * 128:2048 + (cb + 1) * 128, c0:c0 + S], writes=[gt.d])
            for tg in range(NTG):
                psA = banks[(2 * (cb * NTG + tg)) % 8]
                psB = banks[(2 * (cb * NTG + tg) + 1) % 8]
                if tg == 0:
                    wa = ws.load(dr["p_a"], 0, cb * 128, 128, 8)
                    wb = ws.load(dr["p_b"], 0, cb * 128, 128, 8)
                for (ps, w, ys) in ((psA, wa, yAs), (psB, wb, yBs)):
                    for kc in range(8):
                        P.op("pe", lambda e, ps=ps, w=w, ys=ys, kc=kc, tg=tg: e.matmul(
                            ps.full[:, :], w.full[:, kc, :], ys.full[:, kc, tg * 512:(tg + 1) * 512],
                            start=(kc == 0), stop=(kc == 7)), reads=[w.d, ys.d], writes=[ps.d])
                P.op("dve", lambda e, psA=psA, gt=gt, tg=tg: e.tensor_tensor(
                    t1.full, psA.full, gt.full[:, 0, tg * 512:(tg + 1) * 512], ALU.mult), reads=[psA.d, gt.d], writes=[t1.d])
                P.op("dve", lambda e, psB=psB, gt=gt, tg=tg: e.tensor_tensor(
                    t2.full, psB.full, gt.full[:, 1, tg * 512:(tg + 1) * 512], ALU.mult), reads=[psB.d, gt.d], writes=[t2.d])
                P.op("dve", lambda e, cb=cb, tg=tg: e.tensor_tensor(
                    mT.full[:, cb, tg * 512:(tg + 1) * 512], t1.full, t2.full, ALU.add), reads=[t1.d, t2.d], writes=[mT.d])
        cnt = [0]

        def evac(tag, bi, tg, ps, width):
            xt = xts[bi % 2]
            sb_ = sth[bi % 2]
            if tg == 0:
                P.dma("act", xt.full, dr["xT"][bi * 128:(bi + 1) * 128, c0:c0 + S], writes=[xt.d])
            P.op("dve", lambda e: e.tensor_tensor(sb_.full[:, tg * 512:(tg + 1) * 512], ps.full,
                                                  xt.full[:, tg * 512:(tg + 1) * 512], ALU.add),
                 reads=[ps.d, xt.d], writes=[sb_.d])
            if tg == NTG - 1:
                P.dma("sp", dr["h"][bi * 128:(bi + 1) * 128, c0:c0 + S], sb_.full, reads=[sb_.d])

        linear_fm(P, ws, dr["w_out"], 16, [(i * 128, 128, None) for i in range(16)],
                  lambda kk, tg: mT.full[:, kk, tg * 512:(tg + 1) * 512], lambda kk, tg: [mT.d], NTG, banks[0:6], evac)
        P.emit()
    nc.all_engine_barrier()


def phase_E(nc, cfg, dr, b):
    S = cfg.S
    TF = min(1024, S)
    NTG = S // TF
    NH = TF // 512
    with ExitStack() as st:
        P = Prog(nc, st)
        r = load_consts(P, dr, ["vecs", "cb"])
        h2 = P.sb("h2", [128, 16, TF], F32)
        mT = P.sb("mT", [128, 16, TF], BF16)
        hids = [P.sb("hid%d" % i, [128, 16, TF], BF16) for i in range(2)]
        sq = P.sb("sq", [128, 16, 256], BF16)
        rs = P.sb("rs", [128, 256], F32)
        rinv = P.sb("rinv", [128, 256], F32)
        rls = [P.sb("rl%d" % i, [128, 512], BF16) for i in range(2)]
        ost = [P.sb("ost%d" % i, [128, TF], F32) for i in range(2)]
        banks = [P.psum("pm%d" % i) for i in range(6)]
        bsm = [P.psum("psm%d" % i) for i in range(2)]
        ws = WStream(P)
        bk = [0]

        def norm_from_h2(dst_fn, gname, G=256):
            for gi in range(TF // G):
                gs = slice(gi * G, (gi + 1) * G)
                P.op("act", lambda e, gs=gs: e.activation(sq.full, h2.full[:, :, gs], AF.Square), reads=[h2.d], writes=[sq.d])
                ps = bsm[gi % 2]
                for kc in range(16):
                    P.op("pe", lambda e, ps=ps, kc=kc: e.matmul(ps.full[:, 0:G], r.ones, sq.full[:, kc, :], start=(kc == 0), stop=(kc == 15)),
                         reads=[sq.d, r.cb.d], writes=[ps.d])
                P.op("act", lambda e, ps=ps: e.activation(rs.full, ps.full[:, 0:G], AF.Sqrt, bias=RMS_EPS, scale=1.0 / D),
                     reads=[ps.d], writes=[rs.d])
                P.op("dve", lambda e: e.reciprocal(rinv.full, rs.full), reads=[rs.d], writes=[rinv.d])
                for kc in range(16):
                    dst_fn(kc, gs, gi)

        for tg in range(NTG):
            c0 = b * S + tg * TF
            for kc in range(16):
                P.dma("sp", h2.full[:, kc, :], dr["h"][kc * 128:(kc + 1) * 128, c0:c0 + TF], writes=[h2.d])

            def to_m(kc, gs, gi):
                P.op("dve", lambda e: e.scalar_tensor_tensor(mT.full[:, kc, gs], h2.full[:, kc, gs], vcol(r, "g_mlp", kc), rinv.full,
                                                             ALU.mult, ALU.mult), reads=[h2.d, rinv.d, r.vecs.d], writes=[mT.d])

            norm_from_h2(to_m, "g_mlp")
            for g in range(4):
                hid = hids[g % 2]
                for blk in range(16):
                    wb = ws.load(dr["w_ff1"], 0, (g * 16 + blk) * 128, 128, 16)
                    for hf in range(NH):
                        ps = banks[bk[0] % 6]
                        bk[0] += 1
                        hs = slice(hf * 512, (hf + 1) * 512)
                        for kc in range(16):
                            P.op("pe", lambda e, ps=ps, wb=wb, kc=kc, hs=hs: e.matmul(ps.full, wb.full[:, kc, :], mT.full[:, kc, hs],
                                                                                      start=(kc == 0), stop=(kc == 15)),
                                 reads=[wb.d, mT.d], writes=[ps.d])
                        rl = rls[bk[0] % 2]
                        P.op("act", lambda e, ps=ps, rl=rl: e.activation(rl.full, ps.full, AF.Relu), reads=[ps.d], writes=[rl.d])
                        P.op("dve", lambda e, rl=rl, hid=hid, blk=blk, hs=hs: e.tensor_tensor(hid.full[:, blk, hs], rl.full, rl.full, ALU.mult),
                             reads=[rl.d], writes=[hid.d])
                for cb in range(16):
                    wb = ws.load(dr["w_ff2"], g * 16, cb * 128, 128, 16)
                    for hf in range(NH):
                        ps = banks[bk[0] % 6]
                        bk[0] += 1
                        hs = slice(hf * 512, (hf + 1) * 512)
                        for kc in range(16):
                            P.op("pe", lambda e, ps=ps, wb=wb, kc=kc, hs=hs, hid=hid: e.matmul(ps.full, wb.full[:, kc, :], hid.full[:, kc, hs],
                                                                                               start=(kc == 0), stop=(kc == 15)),
                                 reads=[wb.d, hid.d], writes=[ps.d])
                        P.op("dve", lambda e, ps=ps, cb=cb, hs=hs: e.tensor_tensor(h2.full[:, cb, hs], ps.full, h2.full[:, cb, hs], ALU.add),
                             reads=[ps.d, h2.d], writes=[h2.d])

            def to_out(kc, gs, gi):
                o = ost[kc % 2]
                P.op("dve", lambda e: e.scalar_tensor_tensor(o.full[:, gs], h2.full[:, kc, gs], vcol(r, "g_fin", kc), rinv.full,
                                                             ALU.mult, ALU.mult), reads=[h2.d, rinv.d, r.vecs.d], writes=[o.d])
                P.dma("sp", dr["outT"][kc * 128:(kc + 1) * 128, c0 + gs.start:c0 + gs.stop], o.full[:, gs], reads=[o.d])

            norm_from_h2(to_out, "g_fin")
        P.emit()
    nc.all_engine_barrier()


def phase_C(nc, cfg, dr, b):
    S = cfg.S
    c0 = b * S
    NQB = S // 128
    with ExitStack() as st:
        P = Prog(nc, st)
        r = load_consts(P, dr, ["cb"])
        bc = P.sb("bc", [128, 400], F32)
        P.dma("sp", bc.full, dr["bc"], writes=[bc.d])
        biasT = P.sb("biasT", [128, 2 * 16 * 128], F32)
        P.dma("sp", biasT.full, dr["biasT"], writes=[biasT.d])
        sm = P.sb("sm", [128, 16], F32)
        lt = P.sb("lt", [128, 128], F32)
        sgb = P.sb("sgb", [128, 128], F32)
        P.op("dve", lambda e: e.tensor_tensor(lt.full[:, 0:64], bc.full[:, 16:80], bc.full[:, 80:144], ALU.mult), reads=[bc.d], writes=[lt.d])
        P.op("dve", lambda e: e.tensor_tensor(lt.full[:, 64:128], bc.full[:, 144:208], bc.full[:, 208:272], ALU.mult), reads=[bc.d], writes=[lt.d])
        P.op("dve", lambda e: e.reduce_sum(sm.full[:, 0:1], lt.full[:, 0:64], AX.X), reads=[lt.d], writes=[sm.d])
        P.op("dve", lambda e: e.reduce_sum(sm.full[:, 1:2], lt.full[:, 64:128], AX.X), reads=[lt.d, sm.d], writes=[sm.d])
        P.op("act", lambda e: e.activation(sm.full[:, 2:4], sm.full[:, 0:2], AF.Exp), reads=[sm.d], writes=[sm.d])
        P.op("dve", lambda e: e.tensor_tensor(sm.full[:, 4:5], sm.full[:, 3:4], sm.full[:, 2:3], ALU.subtract), reads=[sm.d], writes=[sm.d])
        P.op("dve", lambda e: e.tensor_scalar(sm.full[:, 5:6], sm.full[:, 4:5], -LAMBDA_INIT, None, ALU.add), reads=[sm.d], writes=[sm.d])
        P.op("dve", lambda e: e.tensor_scalar(sgb.full, bc.full[:, 272:400], 1.0 - LAMBDA_INIT, None, ALU.mult), reads=[bc.d], writes=[sgb.d])
        neglam = sm.full[:, 5:6]
        qTs = [P.sb("qT%d" % i, [128, S], BF16) for i in range(2)]
        kTs = [P.sb("kT%d" % i, [128, S], BF16) for i in range(2)]
        vTs = [P.sb("vT%d" % i, [128, S], BF16) for i in range(2)]
        Vas = [P.sb("Va%d" % i, [128, NQB, 132], BF16) for i in range(2)]
        ysts = [P.sb("yst%d" % i, [128, S], BF16) for i in range(2)]
        STs = [P.psum("ST%d" % i) for i in range(2)]
        Os = [P.psum("O%d" % i) for i in range(4)]
        tpb = P.psum("tpb", BF16)
        tp2 = P.psum("tp2", BF16)
        PTs = [P.sb("PT%d" % i, [128, 512], BF16) for i in range(3)]
        tmps = [P.sb("tmp%d" % i, [128, 128], F32) for i in range(2)]
        rd = [P.sb("rd%d" % i, [128, 8], F32) for i in range(2)]
        o1s = [P.sb("o1_%d" % i, [128, 128], F32) for i in range(2)]
        oos = [P.sb("oo_%d" % i, [128, 128], F32) for i in range(2)]
        junk = P.sb("junk", [128, 128], F32)
        ons = [P.sb("on_%d" % i, [128, 128], BF16) for i in range(2)]
        rot = [0, 0, 0]
        ooa = [P.sb("ooa%d" % i, [128, NQB, 128], F32) for i in range(2)]
        ssa = [P.sb("ssa%d" % i, [128, 2 * NQB], F32) for i in range(2)]
        items = []

        def head_pre(h):
            qT, kT, vT, Va = qTs[h % 2], kTs[h % 2], vTs[h % 2], Vas[h % 2]
            P.dma("sp", qT.full, dr["qkv"][h * 128:(h + 1) * 128, c0:c0 + S], writes=[qT.d])
            P.dma("sp", kT.full, dr["qkv"][1024 + h * 128:1024 + (h + 1) * 128, c0:c0 + S], writes=[kT.d])
            P.dma("sp", vT.full, dr["qkv"][2048 + h * 128:2048 + (h + 1) * 128, c0:c0 + S], writes=[vT.d])
            P.op("pool", lambda e, Va=Va: e.memset(Va.full, 1.0), writes=[Va.d])
            for t0 in range(0, NQB, 8):
                n = min(8, NQB - t0)
                for i in range(n):
                    tb = t0 + i
                    P.op("pe", lambda e, i=i, tb=tb, vT=vT: e.transpose(tpb.full[:, i * 128:(i + 1) * 128],
                                                                        vT.full[:, tb * 128:(tb + 1) * 128], r.ident),
                         reads=[vT.d, r.cb.d], writes=[tpb.d])
                P.op("dve", lambda e, t0=t0, n=n, Va=Va: e.tensor_copy(
                    Va.full[:, t0:t0 + n, 0:128], tpb.full[:, 0:n * 128].rearrange("p (a b) -> p a b", b=128)),
                    reads=[tpb.d], writes=[Va.d])

        def qb_combine(h, qb, Opair):
            O0, O1 = Opair
            oa, sa = ooa[h % 2], ssa[h % 2]
            rdt, o1 = rd[qb % 2], o1s[qb % 2]
            P.op("dve", lambda e: e.reciprocal(rdt.full[:, 0:1], O0.full[:, 128:129]), reads=[O0.d], writes=[rdt.d])
            P.op("dve", lambda e: e.reciprocal(rdt.full[:, 1:2], O1.full[:, 128:129]), reads=[O1.d, rdt.d], writes=[rdt.d])
            P.op("dve", lambda e: e.tensor_tensor(rdt.full[:, 2:3], rdt.full[:, 1:2], neglam, ALU.mult), reads=[rdt.d, sm.d], writes=[rdt.d])
            P.op("dve", lambda e: e.tensor_scalar(o1.full, O0.full[:, 0:128], rdt.full[:, 0:1], None, ALU.mult),
                 reads=[O0.d, rdt.d], writes=[o1.d])
            P.op("dve", lambda e: e.scalar_tensor_tensor(oa.full[:, qb, :], O1.full[:, 0:128], rdt.full[:, 2:3], o1.full, ALU.mult, ALU.add),
                 reads=[O1.d, rdt.d, o1.d], writes=[oa.d])
            P.op("dve", lambda e: e.tensor_tensor(junk.full, oa.full[:, qb, :], oa.full[:, qb, :], ALU.mult), reads=[oa.d], writes=[junk.d])
            P.op("dve", lambda e: e.reduce_sum(sa.full[:, qb:qb + 1], junk.full, AX.X), reads=[junk.d, sa.d], writes=[sa.d])

        def head_post(h):
            oa, sa, yst = ooa[h % 2], ssa[h % 2], ysts[h % 2]
            P.op("act", lambda e: e.activation(sa.full[:, NQB:2 * NQB], sa.full[:, 0:NQB], AF.Sqrt, bias=SUBLN_EPS, scale=1.0 / 128),
                 reads=[sa.d], writes=[sa.d])
            P.op("dve", lambda e: e.reciprocal(sa.full[:, NQB:2 * NQB], sa.full[:, NQB:2 * NQB]), reads=[sa.d], writes=[sa.d])
            for qb in range(NQB):
                on = ons[qb % 2]
                P.op("dve", lambda e, qb=qb, on=on: e.scalar_tensor_tensor(
                    on.full, oa.full[:, qb, :], sa.full[:, NQB + qb:NQB + qb + 1], sgb.full, ALU.mult, ALU.mult),
                    reads=[oa.d, sa.d, sgb.d], writes=[on.d])
                P.op("pe", lambda e, on=on, qb=qb: e.transpose(tp2.full[:, (qb % 8) * 128:(qb % 8 + 1) * 128], on.full, r.ident),
                     reads=[on.d, r.cb.d], writes=[tp2.d])
                if qb % 8 == 7 or qb == NQB - 1:
                    q0 = (qb // 8) * 8
                    n = qb - q0 + 1
                    P.op("dve", lambda e, q0=q0, n=n: e.tensor_copy(yst.full[:, q0 * 128:(q0 + n) * 128], tp2.full[:, 0:n * 128]),
                         reads=[tp2.d], writes=[yst.d])
            P.dma("sp", dr["yB"][h * 128:(h + 1) * 128, c0:c0 + S], yst.full, reads=[yst.d])

        for h in range(8):
            qT, kT, vT, Va = qTs[h % 2], kTs[h % 2], vTs[h % 2], Vas[h % 2]
            first = True
            for qb in range(NQB):
                Opair = (Os[(qb % 2) * 2], Os[(qb % 2) * 2 + 1])
                for j in range(2):
                    m = 2 * h + j
                    O = Opair[j]
                    for g0 in range(0, qb + 1, 4):
                        kbs = list(range(g0, min(g0 + 4, qb + 1)))
                        STb = STs[rot[0] % 2]
                        rot[0] += 1
                        PT = PTs[rot[1] % 3]
                        rot[1] += 1

                        def s1(STb=STb, kbs=kbs, j=j, qb=qb, kT=kT, qT=qT):
                            for i, kb in enumerate(kbs):
                                P.op("pe", lambda e, i=i, kb=kb: e.matmul(
                                    STb.full[:, i * 128:(i + 1) * 128], kT.full[64 * j:64 * j + 64, kb * 128:(kb + 1) * 128],
                                    qT.full[64 * j:64 * j + 64, qb * 128:(qb + 1) * 128], start=True, stop=True),
                                    reads=[kT.d, qT.d], writes=[STb.d])

                        def s2(STb=STb, PT=PT, kbs=kbs, qb=qb, m=m, g0=g0):
                            nfar = len([kb for kb in kbs if kb <= qb - 2])
                            if nfar:
                                P.op("act", lambda e: e.activation(PT.full[:, 0:nfar * 128], STb.full[:, 0:nfar * 128], AF.Exp,
                                                                   bias=bc.full[:, m:m + 1], scale=0.125),
                                     reads=[bc.d], writes=[PT.d, STb.d])
                            for i, kb in enumerate(kbs):
                                if kb <= qb - 2:
                                    continue
                                ty = 0 if kb == qb else 1
                                tmp = tmps[rot[2] % 2]
                                rot[2] += 1
                                bo = (ty * 16 + m) * 128
                                P.op("dve", lambda e, tmp=tmp, i=i, bo=bo: e.scalar_tensor_tensor(
                                    tmp.full, STb.full[:, i * 128:(i + 1) * 128], 0.125, biasT.full[:, bo:bo + 128], ALU.mult, ALU.add),
                                    reads=[biasT.d], writes=[tmp.d, STb.d])
                                P.op("act", lambda e, tmp=tmp, i=i: e.activation(PT.full[:, i * 128:(i + 1) * 128], tmp.full, AF.Exp),
                                     reads=[tmp.d], writes=[PT.d])
                                if kb == qb:
                                    P.op("pool", lambda e, i=i: e.memset(PT.full[64:128, i * 128:i * 128 + 64], 0.0),
                                         reads=[PT.d], writes=[PT.d])

                        def s3(PT=PT, kbs=kbs, O=O, qb=qb, Va=Va):
                            for i, kb in enumerate(kbs):
                                P.op("pe", lambda e, i=i, kb=kb: e.matmul(
                                    O.full[:, 0:129], PT.full[:, i * 128:(i + 1) * 128], Va.full[:, kb, 0:129],
                                    start=(kb == 0), stop=(kb == qb)), reads=[PT.d, Va.d], writes=[O.d])

                        pre = (lambda h=h: head_pre(h)) if first else None
                        first = False
                        last_of_qb = (j == 1 and kbs[-1] == qb)
                        post = []
                        if last_of_qb:
                            post.append(lambda h=h, qb=qb, Opair=Opair: qb_combine(h, qb, Opair))
                            if qb == NQB - 1:
                                post.append(lambda h=h: head_post(h))
                        items.append((pre, s1, s2, s3, post))
        n = len(items)
        for i in range(n + 1):
            if i < n:
                pre, s1, s2, s3, post = items[i]
                if pre:
                    pre()
                s1()
                s2()
            if i >= 1:
                pre, s1, s2, s3, post = items[i - 1]
                s3()
                for f in post:
                    f()
        P.emit()
    nc.all_engine_barrier()


def phase_B(nc, cfg, dr, b):
    S = cfg.S
    c0 = b * S
    NSEG = S // 512
    with ExitStack() as st:
        P = Prog(nc, st)
        r = load_consts(P, dr, ["vecs", "cb", "cf"])
        Yseg = P.sb("Yseg", [128, 8, 512], F32)

        class _V:
            pass
        wl_f = _V()
        wl_f.full = Yseg.full.rearrange("p (a b) t -> p a (b t)", b=2)
        wl_f.d = Yseg.d
        wl = P.sb("wl", [128, 4, 1024], BF16)
        P.dma("sp", wl_f.full[0:96, 0, :], dr["w_up"], writes=[wl_f.d])
        P.dma("sp", wl_f.full[0:96, 1, :], dr["a_up"], writes=[wl_f.d])
        P.dma("sp", wl_f.full[:, 2:4, :], dr["g_up"].rearrange("(a p) n -> p a n", p=128), writes=[wl_f.d])
        P.op("dve", lambda e: e.tensor_copy(wl.full[0:96, 0:2, :], wl_f.full[0:96, 0:2, :]), reads=[wl_f.d], writes=[wl.d])
        P.op("dve", lambda e: e.tensor_copy(wl.full[:, 2:4, :], wl_f.full[:, 2:4, :]), reads=[wl_f.d, wl.d], writes=[wl.d])
        Hf = P.sb("Hf", [128, 512], F32)
        Hbs = [P.sb("Hb%d" % i, [128, 512], BF16) for i in range(2)]
        P.op("dve", lambda e: e.memset(Hf.full, 0.0), writes=[Hf.d])
        P.op("dve", lambda e: e.memset(Hbs[0].full, 0.0), writes=[Hbs[0].d])
        pA = [P.psum("pA%d" % i) for i in range(2)]
        pX = [P.psum("pX%d" % i) for i in range(2)]
        tpb = P.psum("tpb", BF16)
        pS = [P.psum("pS%d" % i) for i in range(3)]
        rot = {"pA": 0, "pX": 0, "t": 0}

        def f32t(name, n=1):
            return [P.sb("%s%d" % (name, i), [128, 512], F32) for i in range(n)]

        def bf16t(name, n=1):
            return [P.sb("%s%d" % (name, i), [128, 512], BF16) for i in range(n)]

        zt = [P.sb("zt%d" % i, [128, 513], F32) for i in range(2)]
        dtmp = f32t("dtmp")[0]
        twd = P.sb("twd", [128, 512], BF16)
        lad = P.sb("lad", [128, 512], BF16)
        sgd = P.sb("sgd", [128, 2, 512], BF16)
        lin = f32t("lin")[0]
        rl, kl, vl, sig, aa, kkr, sqk, rn, kk, kp, bb, cw, cwx, E1 = [f32t(n)[0] for n in
            ["rl", "kl", "vl", "sig", "aa", "kkr", "sqk", "rn", "kk", "kp", "bb", "cw", "cwx", "E1"]]
        t1, E0, Ei, rk = rn, cwx, cw, sqk
        bTs, kTts, vb = bf16t("bT", 2), bf16t("kTt", 2), bf16t("vb")[0]
        aT = bf16t("aT", 8)
        rT = bf16t("rT", 8)
        bktm = [P.sb("bktm%d" % i, [128, 1024], BF16) for i in range(8)]
        vtm = bf16t("vtm", 8)
        AakT = bf16t("AakT", 8)
        ArbT = bf16t("ArbT", 8)
        ArkT = bf16t("ArkT", 8)
        TT = bf16t("TT", 8)
        gT = bf16t("gT", 8)
        bon = f32t("bon", 8)
        Wc = P.sb("Wc", [128, 8, 8], F32)
        ptmp = bf16t("ptmp", 6)
        Zb = bf16t("Zb")[0]
        Ub = bf16t("Ub")[0]
        tmpH = f32t("tmpH")[0]
        yn = P.sb("yn", [128, 8, 512], BF16)
        gst = P.sb("gst", [128, 256], F32)
        t3 = f32t("t3")[0]
        yo = bf16t("yo", 2)

        def blocks(fn):
            for c in range(8):
                for j in range(2):
                    fn(j, c, slice(64 * j, 64 * j + 64), slice(64 * c, 64 * c + 64))

        def prod(L, R, dst, mask=None, addend=None, eng="dve"):
            ps = pX[rot["pX"] % 2]
            rot["pX"] += 1
            blocks(lambda j, c, pj, cc: P.op("pe", lambda e: e.matmul(ps.full[pj, cc], L.full[pj, cc], R.full[pj, cc], start=True, stop=True),
                                             reads=[L.d, R.d], writes=[ps.d]))
            if mask is not None:
                P.op("dve", lambda e: e.tensor_tensor(dst.full, ps.full, mask, ALU.mult), reads=[ps.d, r.cb.d], writes=[dst.d])
            elif addend is not None:
                P.op("dve", lambda e: e.tensor_tensor(dst.full, ps.full, addend.full, ALU.add), reads=[ps.d, addend.d], writes=[dst.d])
            else:
                P.op("act", lambda e: e.activation(dst.full, ps.full, AF.Copy), reads=[ps.d], writes=[dst.d])

        def load_shift(row0, nrows, sg, mucol, dst_fn):
            z = zt[rot["t"] % 2]
            rot["t"] += 1
            t0 = c0 + sg * 512
            if sg == 0:
                P.op("pool", lambda e: e.memset(z.full[:, 0:1], 0.0), writes=[z.d])
                P.dma("sp", z.full[0:nrows, 1:513], dr["zA"][row0:row0 + nrows, t0:t0 + 512], writes=[z.d])
            else:
                P.dma("sp", z.full[0:nrows, 0:513], dr["zA"][row0:row0 + nrows, t0 - 1:t0 + 512], writes=[z.d])
            P.op("dve", lambda e: e.tensor_tensor(dtmp.full[0:nrows, :], z.full[0:nrows, 0:512], z.full[0:nrows, 1:513], ALU.subtract),
                 reads=[z.d], writes=[dtmp.d])
            dst_fn(z)

        def lerp_to(dst, nrows, mucol):
            def fn(z):
                P.op("dve", lambda e: e.scalar_tensor_tensor(dst.full[0:nrows, :], dtmp.full[0:nrows, :], mucol[0:nrows, :],
                                                             z.full[0:nrows, 1:513], ALU.mult, ALU.add),
                     reads=[dtmp.d, z.d, r.vecs.d], writes=[dst.d])
            return fn

        for sg in range(NSEG):
            t0 = c0 + sg * 512
            load_shift(3072, 96, sg, None, lerp_to(lin, 96, vcol(r, "mu_wd")))
            P.op("act", lambda e: e.activation(twd.full[0:96, :], lin.full[0:96, :], AF.Tanh), reads=[lin.d], writes=[twd.d])
            load_shift(3168, 96, sg, None, lerp_to(lin, 96, vcol(r, "mu_ad")))
            P.op("act", lambda e: e.activation(lad.full[0:96, :], lin.full[0:96, :], AF.Copy), reads=[lin.d], writes=[lad.d])
            for a_ in range(2):
                load_shift(3264 + 128 * a_, 128, sg, None, lerp_to(lin, 128, vcol(r, "mu_gd", a_)))
                P.op("act", lambda e, a_=a_: e.activation(sgd.full[:, a_, :], lin.full, AF.Sigmoid), reads=[lin.d], writes=[sgd.d])
            def make_prep(hp, sg=sg):
                cs = slice(hp * 128, hp * 128 + 128)
                bT, kTt = bTs[hp % 2], kTts[hp % 2]
                steps = []

                def st0():
                    ps = None
                    load_shift(hp * 128, 128, sg, None, lerp_to(rl, 128, vcol(r, "mu_r", hp)))
                    load_shift(1024 + hp * 128, 128, sg, None, lerp_to(kl, 128, vcol(r, "mu_k", hp)))
                    load_shift(2048 + hp * 128, 128, sg, None, lerp_to(vl, 128, vcol(r, "mu_v", hp)))
                steps.append(st0)

                def st1():
                    ps = None
                    ps = pA[rot["pA"] % 2]; rot["pA"] += 1
                    P.op("pe", lambda e, ps=ps, cs=cs: e.matmul(ps.full, wl.full[0:96, 0, cs], twd.full[0:96, :], start=True, stop=True),
                         reads=[wl.d, twd.d], writes=[ps.d])
                    P.op("act", lambda e, ps=ps, hp=hp: e.activation(sig.full, ps.full, AF.Sigmoid, bias=vcol(r, "w0", hp)),
                         reads=[ps.d, r.vecs.d], writes=[sig.d])
                    ps = pA[rot["pA"] % 2]; rot["pA"] += 1
                    P.op("pe", lambda e, ps=ps, cs=cs: e.matmul(ps.full, wl.full[0:96, 1, cs], lad.full[0:96, :], start=True, stop=True),
                         reads=[wl.d, lad.d], writes=[ps.d])
                    P.op("act", lambda e, ps=ps, hp=hp: e.activation(aa.full, ps.full, AF.Sigmoid, bias=vcol(r, "a0", hp)),
                         reads=[ps.d, r.vecs.d], writes=[aa.d])
                    ps = pA[rot["pA"] % 2]; rot["pA"] += 1
                    for a_ in range(2):
                        P.op("pe", lambda e, ps=ps, cs=cs, a_=a_: e.matmul(ps.full, wl.full[:, 2 + a_, cs], sgd.full[:, a_, :],
                                                                          start=(a_ == 0), stop=(a_ == 1)),
                             reads=[wl.d, sgd.d], writes=[ps.d])
                    P.op("act", lambda e, ps=ps, hp=hp: e.activation(gT[hp].full, ps.full, AF.Copy), reads=[ps.d], writes=[gT[hp].d])
                steps.append(st1)

                def st2():
                    ps = None
                    P.op("dve", lambda e, hp=hp: e.tensor_scalar(kkr.full, kl.full, vcol(r, "k_k", hp), None, ALU.mult),
                         reads=[kl.d, r.vecs.d], writes=[kkr.d])
                    P.op("dve", lambda e: e.tensor_tensor(sqk.full, kkr.full, kkr.full, ALU.mult), reads=[kkr.d], writes=[sqk.d])
                    ps = pA[rot["pA"] % 2]; rot["pA"] += 1
                    P.op("pe", lambda e, ps=ps: e.matmul(ps.full, r.blockones, sqk.full, start=True, stop=True),
                         reads=[sqk.d, r.cf.d], writes=[ps.d])
                    P.op("dve", lambda e, ps=ps: e.tensor_scalar(rn.full, ps.full, 1e-24, None, ALU.max), reads=[ps.d], writes=[rn.d])
                    P.op("act", lambda e: e.activation(rn.full, rn.full, AF.Sqrt), reads=[rn.d], writes=[rn.d])
                    P.op("dve", lambda e: e.reciprocal(rn.full, rn.full), reads=[rn.d], writes=[rn.d])
                    P.op("dve", lambda e: e.tensor_tensor(kk.full, kkr.full, rn.full, ALU.mult), reads=[kkr.d, rn.d], writes=[kk.d])
                steps.append(st2)

                def st3():
                    ps = None
                    P.op("dve", lambda e, hp=hp: e.tensor_scalar(t1.full, aa.full, -1.0, vcol(r, "k_a", hp), ALU.add, ALU.mult),
                         reads=[aa.d, r.vecs.d], writes=[t1.d])
                    P.op("dve", lambda e: e.scalar_tensor_tensor(kp.full, t1.full, 1.0, kl.full, ALU.add, ALU.mult),
                         reads=[t1.d, kl.d], writes=[kp.d])
                    P.op("dve", lambda e: e.tensor_tensor(bb.full, kk.full, aa.full, ALU.mult), reads=[kk.d, aa.d], writes=[bb.d])
                steps.append(st3)

                def st4():
                    ps = None
                    P.op("dve", lambda e: e.tensor_tensor_scan(cw.full, r.scanmask, sig.full, 0.0, ALU.mult, ALU.add),
                         reads=[sig.d, r.cf.d], writes=[cw.d])
                    P.op("dve", lambda e: e.tensor_tensor(cwx.full, cw.full, sig.full, ALU.subtract), reads=[cw.d, sig.d], writes=[cwx.d])
                    P.op("act", lambda e: e.activation(E1.full, cw.full, AF.Exp, scale=-C0), reads=[cw.d], writes=[E1.d])
                    P.op("act", lambda e: e.activation(E0.full, cwx.full, AF.Exp, scale=-C0), reads=[cwx.d], writes=[E0.d])
                    P.op("act", lambda e: e.activation(Ei.full, cw.full, AF.Exp, scale=C0), reads=[cw.d], writes=[Ei.d])
                    P.op("dve", lambda e, hp=hp: e.scalar_tensor_tensor(aT[hp].full, kk.full, -1.0, E0.full, ALU.mult, ALU.mult),
                         reads=[kk.d, E0.d], writes=[aT[hp].d])
                    P.op("dve", lambda e, hp=hp: e.tensor_tensor(rT[hp].full, rl.full, E1.full, ALU.mult), reads=[rl.d, E1.d], writes=[rT[hp].d])
                    P.op("dve", lambda e: e.tensor_tensor(bT.full, bb.full, Ei.full, ALU.mult), reads=[bb.d, Ei.d], writes=[bT.d])
                    P.op("dve", lambda e: e.tensor_tensor(kTt.full, kp.full, Ei.full, ALU.mult), reads=[kp.d, Ei.d], writes=[kTt.d])
                    P.op("dve", lambda e, hp=hp: e.tensor_copy(Wc.full[:, hp, :], E1.full.rearrange("p (c t) -> p c t", t=64)[:, :, 63]),
                         reads=[E1.d], writes=[Wc.d])
                    P.op("act", lambda e: e.activation(vb.full, vl.full, AF.Copy), reads=[vl.d], writes=[vb.d])
                steps.append(st4)

                def st5():
                    ps = None
                    P.op("dve", lambda e, hp=hp: e.scalar_tensor_tensor(rk.full, rl.full, vcol(r, "r_k", hp), kp.full, ALU.mult, ALU.mult),
                         reads=[rl.d, kp.d, r.vecs.d], writes=[rk.d])
                    ps = pA[rot["pA"] % 2]; rot["pA"] += 1
                    P.op("pe", lambda e, ps=ps: e.matmul(ps.full, r.blockones, rk.full, start=True, stop=True),
                         reads=[rk.d, r.cf.d], writes=[ps.d])
                    P.op("dve", lambda e, ps=ps, hp=hp: e.tensor_tensor(bon[hp].full, ps.full, vl.full, ALU.mult),
                         reads=[ps.d, vl.d], writes=[bon[hp].d])
                steps.append(st5)

                def st6():
                    ps = None
                    for X, off in ((bT, 0), (kTt, 512)):
                        blocks(lambda j, c, pj, cc, X=X, off=off: P.op("pe", lambda e: e.transpose(
                            tpb.full[pj, off + 64 * c:off + 64 * c + 64], X.full[pj, cc], r.ident[pj, pj]),
                            reads=[X.d, r.cb.d], writes=[tpb.d]))
                    P.op("dve", lambda e, hp=hp: e.tensor_copy(bktm[hp].full, tpb.full), reads=[tpb.d], writes=[bktm[hp].d])
                    blocks(lambda j, c, pj, cc: P.op("pe", lambda e: e.transpose(tpb.full[pj, cc], vb.full[pj, cc], r.ident[pj, pj]),
                                                     reads=[vb.d, r.cb.d], writes=[tpb.d]))
                    P.op("dve", lambda e, hp=hp: e.tensor_copy(vtm[hp].full, tpb.full[:, 0:512]), reads=[tpb.d], writes=[vtm[hp].d])
                steps.append(st6)
                return steps

            def make_inv(hp):
                bT, kTt = bTs[hp % 2], kTts[hp % 2]
                steps = []
                P0, P0T = ptmp[0], ptmp[1]
                steps.append(lambda: prod(aT[hp], bT, P0, mask=r.ML))
                steps.append(lambda: prod(bT, aT[hp], P0T, mask=r.MU))
                steps.append(lambda: prod(kTt, aT[hp], AakT[hp], mask=r.MU))
                steps.append(lambda: prod(bT, rT[hp], ArbT[hp], mask=r.MUI))
                steps.append(lambda: prod(kTt, rT[hp], ArkT[hp], mask=r.MUI))
                steps.append(lambda: P.op("dve", lambda e: e.tensor_tensor(TT[hp].full, P0T.full, r.IB, ALU.add),
                                          reads=[P0T.d, r.cb.d], writes=[TT[hp].d]))
                Pc, PcT = P0, P0T
                free = [ptmp[2], ptmp[3], ptmp[4], ptmp[5]]
                for lvl in range(1, 6):
                    Pn = free.pop(0)
                    steps.append(lambda PcT=PcT, Pc=Pc, Pn=Pn: prod(PcT, Pc, Pn))
                    PnT = None
                    if lvl < 5:
                        PnT = free.pop(0)
                        steps.append(lambda PcT=PcT, Pc=Pc, PnT=PnT: prod(Pc, PcT, PnT))
                    steps.append(lambda Pn=Pn: prod(Pn, TT[hp], TT[hp], addend=TT[hp]))
                    free.append(Pc)
                    free.append(PcT)
                    Pc, PcT = Pn, PnT
                return steps

            def weave(a, b):
                na, nb = len(a), len(b)
                ia = ib = 0
                while ia < na or ib < nb:
                    if ib < nb and (ia >= na or ib * max(na, 1) <= ia * nb):
                        b[ib]()
                        ib += 1
                    else:
                        a[ia]()
                        ia += 1

            pend = []
            for hp in range(8):
                weave(make_prep(hp), pend)
                pend = make_inv(hp)
            weave([], pend)
            for c in range(8):
                gc = sg * 8 + c
                Hc, Hn = Hbs[gc % 2], Hbs[(gc + 1) % 2]
                cc = slice(64 * c, 64 * c + 64)

                def heads(fn):
                    for hp in range(8):
                        for j in range(2):
                            fn(hp, slice(64 * j, 64 * j + 64), slice(64 * hp, 64 * hp + 64))

                heads(lambda hp, pj, hh: (
                    P.op("pe", lambda e, cc=cc, Hc=Hc, c=c: e.matmul(pS[0].full[pj, hh], aT[hp].full[pj, cc], Hc.full[pj, hh], start=True, stop=False),
                         reads=[aT[hp].d, Hc.d], writes=[pS[0].d]),
                    P.op("pe", lambda e, cc=cc, Hc=Hc, c=c: e.matmul(pS[0].full[pj, hh], AakT[hp].full[pj, cc], vtm[hp].full[pj, cc], start=False, stop=True),
                         reads=[AakT[hp].d, vtm[hp].d], writes=[pS[0].d])))
                P.op("act", lambda e: e.activation(Zb.full, pS[0].full, AF.Copy), reads=[pS[0].d], writes=[Zb.d])
                heads(lambda hp, pj, hh: P.op("pe", lambda e, cc=cc, Hc=Hc, c=c: e.matmul(pS[1].full[pj, hh], TT[hp].full[pj, cc], Zb.full[pj, hh], start=True, stop=True),
                                              reads=[TT[hp].d, Zb.d], writes=[pS[1].d]))
                P.op("dve", lambda e: e.tensor_copy(Ub.full, pS[1].full), reads=[pS[1].d], writes=[Ub.d])
                heads(lambda hp, pj, hh: (
                    P.op("pe", lambda e, cc=cc, Hc=Hc, c=c: e.matmul(pS[2].full[pj, hh], rT[hp].full[pj, cc], Hc.full[pj, hh], start=True, stop=False),
                         reads=[rT[hp].d, Hc.d], writes=[pS[2].d]),
                    P.op("pe", lambda e, cc=cc, Hc=Hc, c=c: e.matmul(pS[2].full[pj, hh], ArbT[hp].full[pj, cc], Ub.full[pj, hh], start=False, stop=False),
                         reads=[ArbT[hp].d, Ub.d], writes=[pS[2].d]),
                    P.op("pe", lambda e, cc=cc, Hc=Hc, c=c: e.matmul(pS[2].full[pj, hh], ArkT[hp].full[pj, cc], vtm[hp].full[pj, cc], start=False, stop=True),
                         reads=[ArkT[hp].d, vtm[hp].d], writes=[pS[2].d])))
                P.op("act", lambda e, c=c: e.activation(Yseg.full[:, c, :], pS[2].full, AF.Copy), reads=[pS[2].d], writes=[Yseg.d])
                heads(lambda hp, pj, hh: (
                    P.op("pe", lambda e, cc=cc, Hc=Hc, c=c: e.matmul(pS[0].full[pj, hh], bktm[hp].full[pj, cc], Ub.full[pj, hh], start=True, stop=False),
                         reads=[bktm[hp].d, Ub.d], writes=[pS[0].d]),
                    P.op("pe", lambda e, cc=cc, Hc=Hc, c=c: e.matmul(pS[0].full[pj, hh], bktm[hp].full[pj, 512 + 64 * c:512 + 64 * c + 64], vtm[hp].full[pj, cc],
                                                  start=False, stop=True),
                         reads=[bktm[hp].d, vtm[hp].d], writes=[pS[0].d])))
                P.op("dve", lambda e: e.tensor_tensor(tmpH.full, pS[0].full, Hf.full, ALU.add), reads=[pS[0].d, Hf.d], writes=[tmpH.d])
                P.op("dve", lambda e, c=c: e.tensor_tensor(
                    Hf.full.rearrange("p (h v) -> p h v", v=64), tmpH.full.rearrange("p (h v) -> p h v", v=64),
                    Wc.full[:, :, c:c + 1].broadcast_to([128, 8, 64]), ALU.mult), reads=[tmpH.d, Wc.d], writes=[Hf.d])
                P.op("act", lambda e, Hn=Hn: e.activation(Hn.full, Hf.full, AF.Copy), reads=[Hf.d], writes=[Hn.d])
            Yv = Yseg.full.rearrange("p c (h v) -> p (c h) v", v=64)
            Nv = yn.full.rearrange("p c (h v) -> p (c h) v", v=64)
            P.op("dve", lambda e: e.reduce_sum(gst.full[:, 0:64], Yv, AX.X), reads=[Yseg.d], writes=[gst.d])
            P.op("dve", lambda e: e.tensor_scalar(gst.full[:, 64:128], gst.full[:, 0:64], 1.0 / 64, None, ALU.mult), reads=[gst.d], writes=[gst.d])
            P.op("dve", lambda e: e.tensor_tensor(Yv, Yv, gst.full[:, 64:128].unsqueeze(2).broadcast_to([128, 64, 64]), ALU.subtract),
                 reads=[Yseg.d, gst.d], writes=[Yseg.d])
            P.op("dve", lambda e: e.tensor_tensor(Nv, Yv, Yv, ALU.mult), reads=[Yseg.d], writes=[yn.d])
            P.op("dve", lambda e: e.reduce_sum(gst.full[:, 128:192], Nv, AX.X), reads=[yn.d, gst.d], writes=[gst.d])
            P.op("act", lambda e: e.activation(gst.full[:, 192:256], gst.full[:, 128:192], AF.Sqrt, bias=GN_EPS, scale=1.0 / 64),
                 reads=[gst.d], writes=[gst.d])
            P.op("dve", lambda e: e.reciprocal(gst.full[:, 192:256], gst.full[:, 192:256]), reads=[gst.d], writes=[gst.d])
            P.op("dve", lambda e: e.tensor_tensor(Nv, Yv, gst.full[:, 192:256].unsqueeze(2).broadcast_to([128, 64, 64]), ALU.mult),
                 reads=[Yseg.d, gst.d], writes=[yn.d])
            for hp in range(8):
                blocks(lambda j, c, pj, cc, hp=hp: P.op("pe", lambda e: e.transpose(
                    tpb.full[pj, cc], yn.full[pj, c, 64 * hp:64 * hp + 64], r.ident[pj, pj]), reads=[yn.d, r.cb.d], writes=[tpb.d]))
                y_ = yo[hp % 2]
                P.op("dve", lambda e, hp=hp: e.tensor_scalar(t3.full, tpb.full[:, 0:512], vcol(r, "lnx_g", hp), vcol(r, "lnx_b", hp),
                                                            ALU.mult, ALU.add), reads=[tpb.d, r.vecs.d], writes=[t3.d])
                P.op("dve", lambda e, hp=hp: e.tensor_tensor(t3.full, t3.full, bon[hp].full, ALU.add), reads=[t3.d, bon[hp].d], writes=[t3.d])
                P.op("dve", lambda e, hp=hp, y_=y_: e.tensor_tensor(y_.full, t3.full, gT[hp].full, ALU.mult), reads=[t3.d, gT[hp].d], writes=[y_.d])
                P.dma("sp", dr["yA"][hp * 128:(hp + 1) * 128, t0:t0 + 512], y_.full, reads=[y_.d])
        P.emit()
    nc.all_engine_barrier()
```

```python
import numpy as np
import ml_dtypes
import concourse.bass as bass
import concourse.mybir as mybir
from concourse.bass_utils import run_bass_kernel_spmd

F32 = mybir.dt.float32
BF16 = mybir.dt.bfloat16
ALU = mybir.AluOpType
AF = mybir.ActivationFunctionType
AX = mybir.AxisListType


class Dep:
    __slots__ = ("name", "w", "rs")

    def __init__(self, name):
        self.name = name
        self.w = None
        self.rs = []


class Op:
    __slots__ = ("eng", "fn", "deps", "is_dma", "sem", "semval", "need_sig", "sigidx", "prev_same_sem")

    def __init__(self, eng, fn, is_dma):
        self.eng = eng
        self.fn = fn
        self.deps = []
        self.is_dma = is_dma
        self.sem = None
        self.semval = 0
        self.need_sig = False
        self.sigidx = 0
        self.prev_same_sem = None


class SB:
    def __init__(self, handle, dep):
        self.h = handle
        self.full = handle.ap()
        self.d = dep

    def __getitem__(self, k):
        return self.full[k]


ENGS = ("pe", "act", "dve", "pool", "sp")
N_DMA_SEMS = 40


class Prog:
    G = None

    @staticmethod
    def init_global(nc, st):
        g = {}
        g["sems"] = {e: st.enter_context(nc.semaphore("gs_" + e)) for e in ENGS}
        g["dsems"] = [st.enter_context(nc.semaphore("gd_%d" % i)) for i in range(N_DMA_SEMS)]
        g["cnt"] = {e: 0 for e in ENGS}
        g["n_dma"] = 0
        g["dma_last"] = [None] * N_DMA_SEMS
        g["dma_cnt"] = [0] * N_DMA_SEMS
        Prog.G = g

    def __init__(self, nc, st):
        self.nc = nc
        self.st = st
        self.ops = []
        self.deps = {}

    def dep(self, name):
        d = self.deps.get(name)
        if d is None:
            d = Dep(name)
            self.deps[name] = d
        return d

    _uid = [0]

    def sb(self, name, shape, dtype):
        Prog._uid[0] += 1
        h = self.st.enter_context(self.nc.sbuf_tensor("s%d_%s" % (Prog._uid[0], name), list(shape), dtype))
        return SB(h, Dep(name))

    def psum(self, name, dtype=F32):
        n = 512 if dtype == F32 else 1024
        Prog._uid[0] += 1
        h = self.st.enter_context(self.nc.psum_tensor("p%d_%s" % (Prog._uid[0], name), [128, n], dtype))
        return SB(h, Dep(name))

    def _track(self, op, reads, writes):
        ds = set()
        for b in reads:
            if b.w is not None:
                ds.add(b.w)
        for b in writes:
            if b.w is not None:
                ds.add(b.w)
            for r in b.rs:
                ds.add(r)
        ds.discard(op)
        for d in ds:
            if d.eng == "pe" and op.eng == "pe" and not d.is_dma and not op.is_dma:
                continue
            op.deps.append(d)
            d.need_sig = True
        for b in reads:
            b.rs.append(op)
        for b in writes:
            b.w = op
            b.rs = []

    def op(self, eng, fn, reads=(), writes=()):
        o = Op(eng, fn, False)
        self._track(o, reads, writes)
        self.ops.append(o)
        return o

    def dma(self, queue, out, in_, reads=(), writes=(), **kw):
        o = Op(queue, lambda e: e.dma_start(out=out, in_=in_, **kw), True)
        g = Prog.G
        s = g["n_dma"] % N_DMA_SEMS
        g["n_dma"] += 1
        o.sem = s
        g["dma_cnt"][s] += 16
        o.semval = g["dma_cnt"][s]
        o.prev_same_sem = g["dma_last"][s]
        g["dma_last"][s] = o
        self._track(o, reads, writes)
        self.ops.append(o)
        return o

    def emit(self):
        nc = self.nc
        g = Prog.G
        sems = g["sems"]
        dsems = g["dsems"]
        from contextlib import ExitStack
        with ExitStack() as st:
            cnt = g["cnt"]
            for o in self.ops:
                if not o.is_dma and o.need_sig:
                    cnt[o.eng] += 1
                    o.sigidx = cnt[o.eng]
            per_eng = {e: [] for e in ENGS}
            for o in self.ops:
                per_eng[o.eng].append(o)
            block = st.enter_context(nc.Block())

            def run(engname, eng):
                seen = {}

                def wait(sem, val, key):
                    if seen.get(key, 0) >= val:
                        return
                    seen[key] = val
                    eng.wait_ge(sem, val)

                for o in per_eng[engname]:
                    for d in o.deps:
                        if d.is_dma:
                            wait(dsems[d.sem], d.semval, ("d", d.sem))
                        else:
                            wait(sems[d.eng], d.sigidx, ("e", d.eng))
                    if o.is_dma:
                        p = o.prev_same_sem
                        if p is not None:
                            wait(dsems[p.sem], p.semval, ("d", p.sem))
                        o.fn(eng).then_inc(dsems[o.sem], 16)
                    else:
                        ins = o.fn(eng)
                        if o.need_sig:
                            ins.then_inc(sems[engname], 1)
                if engname == "sp":
                    for s in range(N_DMA_SEMS):
                        if g["dma_cnt"][s]:
                            wait(dsems[s], g["dma_cnt"][s], ("d", s))

            @block.tensor
            def _(eng):
                run("pe", eng)

            @block.scalar
            def _(eng):
                run("act", eng)

            @block.vector
            def _(eng):
                run("dve", eng)

            @block.gpsimd
            def _(eng):
                run("pool", eng)

            @block.sync
            def _(eng):
                run("sp", eng)
        self.ops = []


import math
from contextlib import ExitStack

D = 2048
AW = 1024
DL = 96
GL = 256
A_COLS = 3 * AW + 2 * DL + GL
QKC = 1024
IN_COLS = 10688
DFF = 8192
NCORES = 8
C0 = math.exp(-0.5)
LAMBDA_INIT = 0.8 - 0.6 * math.exp(-0.3 * 0)
GN_EPS = 64e-5
SUBLN_EPS = 1e-5
RMS_EPS = 1e-6

VC = {}
_o = 0
for _n, _w in [("g_mix", 16), ("g_mlp", 16), ("g_fin", 16), ("mu_r", 8), ("mu_k", 8), ("mu_v", 8),
               ("mu_wd", 1), ("mu_ad", 1), ("mu_gd", 2), ("w0", 8), ("a0", 8), ("k_k", 8), ("k_a", 8),
               ("r_k", 8), ("lnx_g", 8), ("lnx_b", 8)]:
    VC[_n] = _o
    _o += _w
NV = _o


class Cfg:
    def __init__(self, S=2048, NB=2, phases="ABCDE", dump=(), inject=()):
        self.S = S
        self.NB = NB
        self.T = S * NB
        self.phases = phases
        self.dump = set(dump)
        self.inject = set(inject)


def _pm(v, n):
    return np.ascontiguousarray(np.asarray(v, np.float32).reshape(n, 128).T)


class Res:
    pass


def load_consts(P, dr, names):
    r = Res()
    if "vecs" in names:
        r.vecs = P.sb("vecs", [128, NV], F32)
        P.dma("sp", r.vecs.full, dr["vecs"], writes=[r.vecs.d])
    if "cb" in names:
        r.cb = P.sb("cb", [128, 4 * 512 + 256], BF16)
        P.dma("sp", r.cb.full, dr["cb"], writes=[r.cb.d])
        r.ML = r.cb.full[:, 0:512]
        r.MU = r.cb.full[:, 512:1024]
        r.MUI = r.cb.full[:, 1024:1536]
        r.IB = r.cb.full[:, 1536:2048]
        r.ident = r.cb.full[:, 2048:2176]
        r.ones = r.cb.full[:, 2176:2304]
    if "cf" in names:
        r.cf = P.sb("cf", [128, 640], F32)
        P.dma("sp", r.cf.full, dr["cf"], writes=[r.cf.d])
        r.blockones = r.cf.full[:, 0:128]
        r.scanmask = r.cf.full[:, 128:640]
    return r


def vcol(r, name, i=0):
    c = VC[name] + i
    return r.vecs.full[:, c:c + 1]


class WStream:
    def __init__(self, P, nbuf=4):
        self.P = P
        self.bf = [P.sb("wbf%d" % i, [128, 16, 128], BF16) for i in range(nbuf)]
        self.i = 0

    def load(self, w_dram, k0, col0, width, nk=16):
        P = self.P
        bf = self.bf[self.i % len(self.bf)]
        self.i += 1
        src = w_dram[k0 * 128:(k0 + nk) * 128, col0:col0 + width].rearrange("(kc p) n -> p kc n", p=128)
        P.dma("pool", bf.full[:, 0:nk, 0:width], src, writes=[bf.d])
        return bf


def linear_fm(P, ws, w_dram, KC, blocks, xT, xdeps, NTG, banks, evac, tgw=512):
    kgs = min(16, KC)
    nkg = KC // kgs
    bk = 0
    for bi, (col0, width, tag) in enumerate(blocks):
        pss = []
        for tg in range(NTG):
            pss.append(banks[bk % len(banks)])
            bk += 1
        for kg in range(nkg):
            wb = ws.load(w_dram, kg * kgs, col0, width, kgs)
            for tg in range(NTG):
                ps = pss[tg]
                for kc in range(kgs):
                    kk = kg * kgs + kc
                    P.op("pe", lambda e, ps=ps, wb=wb, kc=kc, kk=kk, tg=tg, width=width:
                         e.matmul(ps.full[0:width, 0:tgw], wb.full[:, kc, 0:width], xT(kk, tg),
                                  start=(kk == 0), stop=(kk == KC - 1)),
                         reads=[wb.d] + xdeps(kk, tg), writes=[ps.d])
        for tg in range(NTG):
            evac(tag, bi, tg, pss[tg], width)


def rmsnorm_to_bf16(P, r, src_dram, c0, ntok, uT, gname, banks_small, xs_bufs, sq, rs, rinv, eps=RMS_EPS, q="act"):
    G = 256
    for gi in range(ntok // G):
        xs = xs_bufs[gi % len(xs_bufs)]
        src = src_dram[:, c0 + gi * G:c0 + (gi + 1) * G].rearrange("(kc p) t -> p kc t", p=128)
        P.dma(q, xs.full, src, writes=[xs.d])
        P.op("act", lambda e, xs=xs: e.activation(sq.full, xs.full, AF.Square), reads=[xs.d], writes=[sq.d])
        ps = banks_small[gi % len(banks_small)]
        for kc in range(16):
            P.op("pe", lambda e, ps=ps, kc=kc: e.matmul(ps.full[:, 0:G], r.ones, sq.full[:, kc, :],
                                                        start=(kc == 0), stop=(kc == 15)),
                 reads=[sq.d, r.cb.d], writes=[ps.d])
        P.op("act", lambda e, ps=ps: e.activation(rs.full, ps.full[:, 0:G], AF.Sqrt, bias=eps, scale=1.0 / D),
             reads=[ps.d], writes=[rs.d])
        P.op("dve", lambda e: e.reciprocal(rinv.full, rs.full), reads=[rs.d], writes=[rinv.d])
        for kc in range(16):
            P.op("dve", lambda e, xs=xs, kc=kc, gi=gi: e.scalar_tensor_tensor(
                uT.full[:, kc, gi * G:(gi + 1) * G], xs.full[:, kc, :], vcol(r, gname, kc), rinv.full,
                ALU.mult, ALU.mult),
                reads=[xs.d, rinv.d, r.vecs.d], writes=[uT.d])


def phase_A(nc, cfg, dr, b):
    S = cfg.S
    c0 = b * S
    NTG = S // 512
    with ExitStack() as st:
        P = Prog(nc, st)
        r = load_consts(P, dr, ["vecs", "cb"])
        uT = P.sb("uT", [128, 16, S], BF16)
        xs_bufs = [P.sb("xs%d" % i, [128, 16, 256], F32) for i in range(2)]
        sq = P.sb("sq", [128, 16, 256], BF16)
        rs = P.sb("rs", [128, 256], F32)
        rinv = P.sb("rinv", [128, 256], F32)
        banks = [P.psum("pm%d" % i) for i in range(6)]
        bsm = [P.psum("psm%d" % i) for i in range(2)]
        ws = WStream(P)
        stf = [P.sb("stf%d" % i, [128, S], F32) for i in range(2)]
        stb = [P.sb("stb%d" % i, [128, S], BF16) for i in range(2)]
        rmsnorm_to_bf16(P, r, dr["xT"], c0, S, uT, "g_mix", bsm, xs_bufs, sq, rs, rinv)

        blocks = []
        for i in range(24):
            blocks.append((i * 128, 128, ("zA", i * 128)))
        blocks.append((3072, 96, ("zA", 3072)))
        blocks.append((3168, 96, ("zA", 3168)))
        blocks.append((3264, 128, ("zA", 3264)))
        blocks.append((3392, 128, ("zA", 3392)))
        for i in range(24):
            blocks.append((A_COLS + i * 128, 128, ("qkv", i * 128)))
        for i in range(32):
            blocks.append((A_COLS + 3072 + i * 128, 128, ("gate", i * 128)))
        cnt = {"f": 0, "b": 0}

        def evac(tag, bi, tg, ps, width):
            kind, row0 = tag
            if kind == "zA":
                sbuf = stf[cnt["f"] % 2]
                P.op("act", lambda e: e.activation(sbuf.full[0:width, tg * 512:(tg + 1) * 512], ps.full[0:width, :], AF.Copy),
                     reads=[ps.d], writes=[sbuf.d])
                if tg == NTG - 1:
                    P.dma("act", dr["zA"][row0:row0 + width, c0:c0 + S], sbuf.full[0:width, :], reads=[sbuf.d])
                    cnt["f"] += 1
            else:
                sbuf = stb[cnt["b"] % 2]
                fn = AF.Copy if kind == "qkv" else AF.Sigmoid
                P.op("act", lambda e: e.activation(sbuf.full[0:width, tg * 512:(tg + 1) * 512], ps.full[0:width, :], fn),
                     reads=[ps.d], writes=[sbuf.d])
                if tg == NTG - 1:
                    dst = dr["qkv"] if kind == "qkv" else dr["gates"]
                    P.dma("act", dst[row0:row0 + width, c0:c0 + S], sbuf.full[0:width, :], reads=[sbuf.d])
                    cnt["b"] += 1

        import os
        if os.environ.get("DBG_A") == "norm":
            blocks = []
        elif os.environ.get("DBG_A"):
            blocks = blocks[:int(os.environ["DBG_A"])]
        linear_fm(P, ws, dr["w_in"], 16, blocks, lambda kk, tg: uT.full[:, kk, tg * 512:(tg + 1) * 512],
                  lambda kk, tg: [uT.d], NTG, banks, evac)
        P.emit()
    nc.all_engine_barrier()


SCRATCH = {
    "zA": ([A_COLS, None], F32),
    "qkv": ([3072, None], BF16),
    "gates": ([4096, None], BF16),
    "yA": ([AW, None], BF16),
    "yB": ([AW, None], BF16),
    "h": ([D, None], F32),
}


def build_nc(cfg):
    nc = bass.Bass("TRN2", target_bir_lowering=False)
    T = cfg.T
    dr = {}
    gst = ExitStack()
    Prog.init_global(nc, gst)

    def inp(name, shape, dt=F32):
        dr[name] = nc.dram_tensor(name, list(shape), dt, kind="ExternalInput").ap()

    inp("xT", [D, T])
    if "A" in cfg.phases:
        inp("w_in", [D, IN_COLS])
    if "D" in cfg.phases:
        inp("p_a", [AW, D])
        inp("p_b", [AW, D])
        inp("w_out", [D, D])
    if "E" in cfg.phases:
        inp("w_ff1", [D, DFF])
        inp("w_ff2", [DFF, D])
    if "B" in cfg.phases:
        inp("w_up", [DL, AW])
        inp("a_up", [DL, AW])
        inp("g_up", [GL, AW])
    inp("vecs", [128, NV])
    inp("bc", [128, 16 + 256 + 128])
    inp("biasT", [128, 2 * 16 * 128])
    inp("cb", [128, 4 * 512 + 256], BF16)
    inp("cf", [128, 640])
    for name, (shape, dt) in SCRATCH.items():
        shp = [shape[0], T]
        if name in cfg.inject:
            kind = "ExternalInput"
        elif name in cfg.dump:
            kind = "ExternalOutput"
        else:
            kind = "Internal"
        dr[name] = nc.dram_tensor(name, shp, dt, kind=kind).ap()
    dr["outT"] = nc.dram_tensor("outT", [D, T], F32, kind="ExternalOutput").ap()
    for b in range(cfg.NB):
        if "A" in cfg.phases:
            phase_A(nc, cfg, dr, b)
        if "B" in cfg.phases:
            phase_B(nc, cfg, dr, b)
        if "C" in cfg.phases:
            phase_C(nc, cfg, dr, b)
        if "D" in cfg.phases:
            phase_D(nc, cfg, dr, b)
        if "E" in cfg.phases:
            phase_E(nc, cfg, dr, b)
    return nc


def t5_bucket_np(rel):
    nb = 16
    max_exact = 8
    ret = np.where(rel > 0, nb, 0)
    n = np.abs(rel)
    nf = np.maximum(n, 1).astype(np.float32)
    large = max_exact + (np.log(nf / max_exact) / math.log(128 / max_exact) * (nb - max_exact)).astype(np.int32)
    large = np.minimum(large, nb - 1)
    return ret + np.where(n < max_exact, n, large)


def host_consts(inp):
    f = np.float32
    vecs = np.zeros((128, NV), f)

    def put(name, v, n):
        vecs[:, VC[name]:VC[name] + n] = _pm(v, n)

    put("g_mix", inp["norm_mix_g"][0], 16)
    put("g_mlp", inp["norm_mlp_g"][0], 16)
    put("g_fin", inp["norm_final_g"], 16)
    mu = np.asarray(inp["mu_shift"][0], f)
    put("mu_r", mu[0:1024], 8)
    put("mu_k", mu[1024:2048], 8)
    put("mu_v", mu[2048:3072], 8)
    vecs[0:96, VC["mu_wd"]] = mu[3072:3168]
    vecs[0:96, VC["mu_ad"]] = mu[3168:3264]
    put("mu_gd", mu[3264:3520], 2)
    put("w0", inp["w0"][0], 8)
    put("a0", inp["a0"][0], 8)
    put("k_k", inp["k_k"][0], 8)
    put("k_a", inp["k_a"][0], 8)
    put("r_k", np.asarray(inp["r_k"][0]).reshape(-1), 8)
    put("lnx_g", inp["lnx_g"][0], 8)
    put("lnx_b", inp["lnx_b"][0], 8)

    rb = np.asarray(inp["rel_bias"], f)
    bc = np.zeros((128, 16 + 256 + 128), f)
    bc[:, 0:16] = rb[15][None, :]
    for i, nme in enumerate(["lambda_q1", "lambda_k1", "lambda_q2", "lambda_k2"]):
        bc[:, 16 + 64 * i:16 + 64 * (i + 1)] = np.asarray(inp[nme][0], f)[None, :]
    bc[:, 272:400] = np.asarray(inp["subln_g"][0], f)[None, :]
    kl = np.arange(128)[:, None]
    ql = np.arange(128)[None, :]
    biasT = np.zeros((128, 2, 16, 128), f)
    for ty, off in enumerate([0, -128]):
        bidx = t5_bucket_np(off + kl - ql)
        biasT[:, ty, :, :] = np.transpose(rb[bidx], (0, 2, 1))
    biasT = biasT.reshape(128, -1)
    row = (np.arange(128) % 64)[:, None]
    col = (np.arange(512) % 64)[None, :]
    ML = (col < row).astype(f)
    MU = (row < col).astype(f)
    MUI = (row <= col).astype(f)
    IB = (row == col).astype(f)
    ident = np.eye(128, dtype=f)
    ones = np.ones((128, 128), f)
    cb = np.concatenate([ML, MU, MUI, IB, ident, ones], axis=1).astype(ml_dtypes.bfloat16)
    blockones = np.zeros((128, 128), f)
    blockones[0:64, 0:64] = 1
    blockones[64:, 64:] = 1
    scanmask = np.broadcast_to((np.arange(512) % 64 != 0).astype(f)[None, :], (128, 512))
    cf = np.concatenate([blockones, scanmask], axis=1).astype(f)
    out = {"vecs": vecs, "bc": bc, "biasT": np.ascontiguousarray(biasT), "cb": np.ascontiguousarray(cb),
           "cf": np.ascontiguousarray(cf)}
    for nme in ["w_in", "p_a", "p_b", "w_out", "w_ff1", "w_ff2", "w_up", "a_up", "g_up"]:
        out[nme] = np.ascontiguousarray(np.asarray(inp[nme][0], f))
    return out


def kernel(**inputs):
    cfg = Cfg()
    x = np.asarray(inputs["x"], np.float32)
    B, S, _ = x.shape
    nc = build_nc(cfg)
    shared = host_consts(inputs)
    in_maps = []
    for c in range(NCORES):
        m = dict(shared)
        xs = x[c * cfg.NB:(c + 1) * cfg.NB].reshape(cfg.T, D)
        m["xT"] = np.ascontiguousarray(xs.T)
        in_maps.append(m)
    res = run_bass_kernel_spmd(nc, in_maps, core_ids=list(range(NCORES)))
    out = np.empty((B, S, D), np.float32)
    for c in range(NCORES):
        o = res.results[c]["outT"]
        out[c * cfg.NB:(c + 1) * cfg.NB] = o.T.reshape(cfg.NB, S, D)
    return out


def phase_D(nc, cfg, dr, b):
    S = cfg.S
    c0 = b * S
    NTG = S // 512
    with ExitStack() as st:
        P = Prog(nc, st)
        yAs = P.sb("yAs", [128, 8, S], BF16)
        yBs = P.sb("yBs", [128, 8, S], BF16)
        mT = P.sb("mT", [128, 16, S], BF16)
        P.dma("act", yAs.full, dr["yA"][:, c0:c0 + S].rearrange("(kc p) t -> p kc t", p=128), writes=[yAs.d])
        P.dma("act", yBs.full, dr["yB"][:, c0:c0 + S].rearrange("(kc p) t -> p kc t", p=128), writes=[yBs.d])
        ws = WStream(P)
        banks = [P.psum("pm%d" % i) for i in range(8)]
        gts = [P.sb("gt%d" % i, [128, 2, S], BF16) for i in range(2)]
        t1 = P.sb("t1", [128, 512], F32)
        t2 = P.sb("t2", [128, 512], F32)
        xts = [P.sb("xt%d" % i, [128, S], F32) for i in range(2)]
        sth = [P.sb("sth%d" % i, [128, S], F32) for i in range(2)]
        for cb in range(16):
            gt = gts[cb % 2]
            P.dma("act", gt.full[:, 0, :], dr["gates"][cb * 128:(cb + 1) * 128, c0:c0 + S], writes=[gt.d])
            P.dma("act", gt.full[:, 1, :], dr["gates"][2048 + cb * 128:2048 + (cb + 1) * 128, c0:c0 + S], writes=[gt.d])
            for tg in range(NTG):
                psA = banks[(2 * (cb * NTG + tg)) % 8]
                psB = banks[(2 * (cb * NTG + tg) + 1) % 8]
                if tg == 0:
                    wa = ws.load(dr["p_a"], 0, cb * 128, 128, 8)
                    wb = ws.load(dr["p_b"], 0, cb * 128, 128, 8)
                for (ps, w, ys) in ((psA, wa, yAs), (psB, wb, yBs)):
                    for kc in range(8):
                        P.op("pe", lambda e, ps=ps, w=w, ys=ys, kc=kc, tg=tg: e.matmul(
                            ps.full[:, :], w.full[:, kc, :], ys.full[:, kc, tg * 512:(tg + 1) * 512],
                            start=(kc == 0), stop=(kc == 7)), reads=[w.d, ys.d], writes=[ps.d])
                P.op("dve", lambda e, psA=psA, gt=gt, tg=tg: e.tensor_tensor(
                    t1.full, psA.full, gt.full[:, 0, tg * 512:(tg + 1) * 512], ALU.mult), reads=[psA.d, gt.d], writes=[t1.d])
                P.op("dve", lambda e, psB=psB, gt=gt, tg=tg: e.tensor_tensor(
                    t2.full, psB.full, gt.full[:, 1, tg * 512:(tg + 1) * 512], ALU.mult), reads=[psB.d, gt.d], writes=[t2.d])
                P.op("dve", lambda e, cb=cb, tg=tg: e.tensor_tensor(
                    mT.full[:, cb, tg * 512:(tg + 1) * 512], t1.full, t2.full, ALU.add), reads=[t1.d, t2.d], writes=[mT.d])
        cnt = [0]

        def evac(tag, bi, tg, ps, width):
            xt = xts[bi % 2]
            sb_ = sth[bi % 2]
            if tg == 0:
                P.dma("act", xt.full, dr["xT"][bi * 128:(bi + 1) * 128, c0:c0 + S], writes=[xt.d])
            P.op("dve", lambda e: e.tensor_tensor(sb_.full[:, tg * 512:(tg + 1) * 512], ps.full,
                                                  xt.full[:, tg * 512:(tg + 1) * 512], ALU.add),
                 reads=[ps.d, xt.d], writes=[sb_.d])
            if tg == NTG - 1:
                P.dma("sp", dr["h"][bi * 128:(bi + 1) * 128, c0:c0 + S], sb_.full, reads=[sb_.d])

        linear_fm(P, ws, dr["w_out"], 16, [(i * 128, 128, None) for i in range(16)],
                  lambda kk, tg: mT.full[:, kk, tg * 512:(tg + 1) * 512], lambda kk, tg: [mT.d], NTG, banks[0:6], evac)
        P.emit()
    nc.all_engine_barrier()


def phase_E(nc, cfg, dr, b):
    S = cfg.S
    TF = min(1024, S)
    NTG = S // TF
    NH = TF // 512
    with ExitStack() as st:
        P = Prog(nc, st)
        r = load_consts(P, dr, ["vecs", "cb"])
        h2 = P.sb("h2", [128, 16, TF], F32)
        mT = P.sb("mT", [128, 16, TF], BF16)
        hids = [P.sb("hid%d" % i, [128, 16, TF], BF16) for i in range(2)]
        sq = P.sb("sq", [128, 16, 256], BF16)
        rs = P.sb("rs", [128, 256], F32)
        rinv = P.sb("rinv", [128, 256], F32)
        rls = [P.sb("rl%d" % i, [128, 512], BF16) for i in range(2)]
        ost = [P.sb("ost%d" % i, [128, TF], F32) for i in range(2)]
        banks = [P.psum("pm%d" % i) for i in range(6)]
        bsm = [P.psum("psm%d" % i) for i in range(2)]
        ws = WStream(P)
        bk = [0]
        NG = TF // 256
        h2g = [Dep("h2g%d" % i) for i in range(NG)]
        mTh = [Dep("mTh%d" % i) for i in range(NH)]

        def norm_from_h2(dst_fn, gname, G=256):
            for gi in range(TF // G):
                gs = slice(gi * G, (gi + 1) * G)
                P.op("act", lambda e, gs=gs: e.activation(sq.full, h2.full[:, :, gs], AF.Square), reads=[h2g[gi]], writes=[sq.d])
                ps = bsm[gi % 2]
                for kc in range(16):
                    P.op("pe", lambda e, ps=ps, kc=kc: e.matmul(ps.full[:, 0:G], r.ones, sq.full[:, kc, :], start=(kc == 0), stop=(kc == 15)),
                         reads=[sq.d, r.cb.d], writes=[ps.d])
                P.op("act", lambda e, ps=ps: e.activation(rs.full, ps.full[:, 0:G], AF.Sqrt, bias=RMS_EPS, scale=1.0 / D),
                     reads=[ps.d], writes=[rs.d])
                P.op("dve", lambda e: e.reciprocal(rinv.full, rs.full), reads=[rs.d], writes=[rinv.d])
                for kc in range(16):
                    dst_fn(kc, gs, gi)

        for tg in range(NTG):
            c0 = b * S + tg * TF
            for gi in range(NG):
                P.dma("sp", h2.full[:, :, gi * 256:(gi + 1) * 256],
                      dr["h"][:, c0 + gi * 256:c0 + (gi + 1) * 256].rearrange("(kc p) t -> p kc t", p=128), writes=[h2g[gi]])

            def to_m(kc, gs, gi):
                P.op("dve", lambda e: e.scalar_tensor_tensor(mT.full[:, kc, gs], h2.full[:, kc, gs], vcol(r, "g_mlp", kc), rinv.full,
                                                             ALU.mult, ALU.mult), reads=[h2g[gi], rinv.d, r.vecs.d], writes=[mTh[gi // 2]])

            norm_from_h2(to_m, "g_mlp")
            for g in range(4):
                hid = hids[g % 2]
                for blk in range(16):
                    wb = ws.load(dr["w_ff1"], 0, (g * 16 + blk) * 128, 128, 16)
                    for hf in range(NH):
                        ps = banks[bk[0] % 6]
                        bk[0] += 1
                        hs = slice(hf * 512, (hf + 1) * 512)
                        for kc in range(16):
                            P.op("pe", lambda e, ps=ps, wb=wb, kc=kc, hs=hs: e.matmul(ps.full, wb.full[:, kc, :], mT.full[:, kc, hs],
                                                                                      start=(kc == 0), stop=(kc == 15)),
                                 reads=[wb.d, mTh[hf]], writes=[ps.d])
                        rl = rls[bk[0] % 2]
                        P.op("act", lambda e, ps=ps, rl=rl: e.activation(rl.full, ps.full, AF.Relu), reads=[ps.d], writes=[rl.d])
                        P.op("dve", lambda e, rl=rl, hid=hid, blk=blk, hs=hs: e.tensor_tensor(hid.full[:, blk, hs], rl.full, rl.full, ALU.mult),
                             reads=[rl.d], writes=[hid.d])
                for cb in range(16):
                    wb = ws.load(dr["w_ff2"], g * 16, cb * 128, 128, 16)
                    for hf in range(NH):
                        ps = banks[bk[0] % 6]
                        bk[0] += 1
                        hs = slice(hf * 512, (hf + 1) * 512)
                        for kc in range(16):
                            P.op("pe", lambda e, ps=ps, wb=wb, kc=kc, hs=hs, hid=hid: e.matmul(ps.full, wb.full[:, kc, :], hid.full[:, kc, hs],
                                                                                               start=(kc == 0), stop=(kc == 15)),
                                 reads=[wb.d, hid.d], writes=[ps.d])
                        P.op("dve", lambda e, ps=ps, cb=cb, hs=hs: e.tensor_tensor(h2.full[:, cb, hs], ps.full, h2.full[:, cb, hs], ALU.add),
                             reads=[ps.d, h2g[2 * hf], h2g[2 * hf + 1]], writes=[h2g[2 * hf], h2g[2 * hf + 1]])

            def to_out(kc, gs, gi):
                o = ost[kc % 2]
                P.op("dve", lambda e: e.scalar_tensor_tensor(o.full[:, gs], h2.full[:, kc, gs], vcol(r, "g_fin", kc), rinv.full,
                                                             ALU.mult, ALU.mult), reads=[h2g[gi], rinv.d, r.vecs.d], writes=[o.d])
                P.dma("sp", dr["outT"][kc * 128:(kc + 1) * 128, c0 + gs.start:c0 + gs.stop], o.full[:, gs], reads=[o.d])

            norm_from_h2(to_out, "g_fin")
        P.emit()
    nc.all_engine_barrier()


def phase_C(nc, cfg, dr, b):
    S = cfg.S
    c0 = b * S
    NQB = S // 128
    with ExitStack() as st:
        P = Prog(nc, st)
        r = load_consts(P, dr, ["cb"])
        bc = P.sb("bc", [128, 400], F32)
        P.dma("sp", bc.full, dr["bc"], writes=[bc.d])
        biasT = P.sb("biasT", [128, 2 * 16 * 128], F32)
        P.dma("sp", biasT.full, dr["biasT"], writes=[biasT.d])
        sm = P.sb("sm", [128, 16], F32)
        lt = P.sb("lt", [128, 128], F32)
        sgb = P.sb("sgb", [128, 128], F32)
        P.op("dve", lambda e: e.tensor_tensor(lt.full[:, 0:64], bc.full[:, 16:80], bc.full[:, 80:144], ALU.mult), reads=[bc.d], writes=[lt.d])
        P.op("dve", lambda e: e.tensor_tensor(lt.full[:, 64:128], bc.full[:, 144:208], bc.full[:, 208:272], ALU.mult), reads=[bc.d], writes=[lt.d])
        P.op("dve", lambda e: e.reduce_sum(sm.full[:, 0:1], lt.full[:, 0:64], AX.X), reads=[lt.d], writes=[sm.d])
        P.op("dve", lambda e: e.reduce_sum(sm.full[:, 1:2], lt.full[:, 64:128], AX.X), reads=[lt.d, sm.d], writes=[sm.d])
        P.op("act", lambda e: e.activation(sm.full[:, 2:4], sm.full[:, 0:2], AF.Exp), reads=[sm.d], writes=[sm.d])
        P.op("dve", lambda e: e.tensor_tensor(sm.full[:, 4:5], sm.full[:, 3:4], sm.full[:, 2:3], ALU.subtract), reads=[sm.d], writes=[sm.d])
        P.op("dve", lambda e: e.tensor_scalar(sm.full[:, 5:6], sm.full[:, 4:5], -LAMBDA_INIT, None, ALU.add), reads=[sm.d], writes=[sm.d])
        P.op("dve", lambda e: e.tensor_scalar(sgb.full, bc.full[:, 272:400], 1.0 - LAMBDA_INIT, None, ALU.mult), reads=[bc.d], writes=[sgb.d])
        neglam = sm.full[:, 5:6]
        qTs = [P.sb("qT%d" % i, [128, S], BF16) for i in range(2)]
        kTs = [P.sb("kT%d" % i, [128, S], BF16) for i in range(2)]
        vTs = [P.sb("vT%d" % i, [128, S], BF16) for i in range(2)]
        Vas = [P.sb("Va%d" % i, [128, NQB, 132], BF16) for i in range(2)]
        ysts = [P.sb("yst%d" % i, [128, S], BF16) for i in range(2)]
        STs = [P.psum("ST%d" % i) for i in range(3)]
        Os = [P.psum("O%d" % i) for i in range(4)]
        tpb = P.psum("tpb", BF16)
        tp2 = tpb
        PTs = [P.sb("PT%d" % i, [128, 512], BF16) for i in range(5)]
        tmps = [P.sb("tmp%d" % i, [128, 128], F32) for i in range(2)]
        rd = [P.sb("rd%d" % i, [128, 8], F32) for i in range(2)]
        o1s = [P.sb("o1_%d" % i, [128, 128], F32) for i in range(2)]
        oos = [P.sb("oo_%d" % i, [128, 128], F32) for i in range(2)]
        junk = P.sb("junk", [128, 128], F32)
        ons = [P.sb("on_%d" % i, [128, 128], BF16) for i in range(2)]
        rot = [0, 0, 0]
        ooa = [P.sb("ooa%d" % i, [128, NQB, 128], F32) for i in range(2)]
        ssa = [P.sb("ssa%d" % i, [128, 2 * NQB], F32) for i in range(2)]
        items = []

        def head_pre(h):
            qT, kT, vT, Va = qTs[h % 2], kTs[h % 2], vTs[h % 2], Vas[h % 2]
            P.dma("sp", qT.full, dr["qkv"][h * 128:(h + 1) * 128, c0:c0 + S], writes=[qT.d])
            P.dma("sp", kT.full, dr["qkv"][1024 + h * 128:1024 + (h + 1) * 128, c0:c0 + S], writes=[kT.d])
            P.dma("sp", vT.full, dr["qkv"][2048 + h * 128:2048 + (h + 1) * 128, c0:c0 + S], writes=[vT.d])
            P.op("pool", lambda e, Va=Va: e.memset(Va.full, 1.0), writes=[Va.d])
            for t0 in range(0, NQB, 8):
                n = min(8, NQB - t0)
                for i in range(n):
                    tb = t0 + i
                    P.op("pe", lambda e, i=i, tb=tb, vT=vT: e.transpose(tpb.full[:, i * 128:(i + 1) * 128],
                                                                        vT.full[:, tb * 128:(tb + 1) * 128], r.ident),
                         reads=[vT.d, r.cb.d], writes=[tpb.d])
                P.op("dve", lambda e, t0=t0, n=n, Va=Va: e.tensor_copy(
                    Va.full[:, t0:t0 + n, 0:128], tpb.full[:, 0:n * 128].rearrange("p (a b) -> p a b", b=128)),
                    reads=[tpb.d], writes=[Va.d])

        def qb_combine(h, qb, Opair):
            O0, O1 = Opair
            oa, sa = ooa[h % 2], ssa[h % 2]
            rdt, o1 = rd[qb % 2], o1s[qb % 2]
            P.op("dve", lambda e: e.reciprocal(rdt.full[:, 0:1], O0.full[:, 128:129]), reads=[O0.d], writes=[rdt.d])
            P.op("dve", lambda e: e.reciprocal(rdt.full[:, 1:2], O1.full[:, 128:129]), reads=[O1.d, rdt.d], writes=[rdt.d])
            P.op("dve", lambda e: e.tensor_tensor(rdt.full[:, 2:3], rdt.full[:, 1:2], neglam, ALU.mult), reads=[rdt.d, sm.d], writes=[rdt.d])
            P.op("dve", lambda e: e.tensor_scalar(o1.full, O0.full[:, 0:128], rdt.full[:, 0:1], None, ALU.mult),
                 reads=[O0.d, rdt.d], writes=[o1.d])
            P.op("dve", lambda e: e.scalar_tensor_tensor(oa.full[:, qb, :], O1.full[:, 0:128], rdt.full[:, 2:3], o1.full, ALU.mult, ALU.add),
                 reads=[O1.d, rdt.d, o1.d], writes=[oa.d])
            P.op("dve", lambda e: e.tensor_tensor(junk.full, oa.full[:, qb, :], oa.full[:, qb, :], ALU.mult), reads=[oa.d], writes=[junk.d])
            P.op("dve", lambda e: e.reduce_sum(sa.full[:, qb:qb + 1], junk.full, AX.X), reads=[junk.d, sa.d], writes=[sa.d])

        def head_post(h):
            oa, sa, yst = ooa[h % 2], ssa[h % 2], ysts[h % 2]
            P.op("act", lambda e: e.activation(sa.full[:, NQB:2 * NQB], sa.full[:, 0:NQB], AF.Sqrt, bias=SUBLN_EPS, scale=1.0 / 128),
                 reads=[sa.d], writes=[sa.d])
            P.op("dve", lambda e: e.reciprocal(sa.full[:, NQB:2 * NQB], sa.full[:, NQB:2 * NQB]), reads=[sa.d], writes=[sa.d])
            for qb in range(NQB):
                on = ons[qb % 2]
                P.op("dve", lambda e, qb=qb, on=on: e.scalar_tensor_tensor(
                    on.full, oa.full[:, qb, :], sa.full[:, NQB + qb:NQB + qb + 1], sgb.full, ALU.mult, ALU.mult),
                    reads=[oa.d, sa.d, sgb.d], writes=[on.d])
                P.op("pe", lambda e, on=on, qb=qb: e.transpose(tp2.full[:, (qb % 8) * 128:(qb % 8 + 1) * 128], on.full, r.ident),
                     reads=[on.d, r.cb.d], writes=[tp2.d])
                if qb % 8 == 7 or qb == NQB - 1:
                    q0 = (qb // 8) * 8
                    n = qb - q0 + 1
                    P.op("dve", lambda e, q0=q0, n=n: e.tensor_copy(yst.full[:, q0 * 128:(q0 + n) * 128], tp2.full[:, 0:n * 128]),
                         reads=[tp2.d], writes=[yst.d])
            P.dma("sp", dr["yB"][h * 128:(h + 1) * 128, c0:c0 + S], yst.full, reads=[yst.d])

        for h in range(8):
            qT, kT, vT, Va = qTs[h % 2], kTs[h % 2], vTs[h % 2], Vas[h % 2]
            first = True
            for qb in range(NQB):
                Opair = (Os[(qb % 2) * 2], Os[(qb % 2) * 2 + 1])
                for j in range(2):
                    m = 2 * h + j
                    O = Opair[j]
                    for g0 in range(0, qb + 1, 4):
                        kbs = list(range(g0, min(g0 + 4, qb + 1)))
                        STb = STs[rot[0] % 3]
                        rot[0] += 1
                        PT = PTs[rot[1] % 5]
                        rot[1] += 1

                        def s1(STb=STb, kbs=kbs, j=j, qb=qb, kT=kT, qT=qT):
                            for i, kb in enumerate(kbs):
                                P.op("pe", lambda e, i=i, kb=kb: e.matmul(
                                    STb.full[:, i * 128:(i + 1) * 128], kT.full[64 * j:64 * j + 64, kb * 128:(kb + 1) * 128],
                                    qT.full[64 * j:64 * j + 64, qb * 128:(qb + 1) * 128], start=True, stop=True),
                                    reads=[kT.d, qT.d], writes=[STb.d])

                        def s2(STb=STb, PT=PT, kbs=kbs, qb=qb, m=m, g0=g0):
                            nfar = len([kb for kb in kbs if kb <= qb - 2])
                            if nfar:
                                P.op("act", lambda e: e.activation(PT.full[:, 0:nfar * 128], STb.full[:, 0:nfar * 128], AF.Exp,
                                                                   bias=bc.full[:, m:m + 1], scale=0.125),
                                     reads=[bc.d], writes=[PT.d, STb.d])
                            for i, kb in enumerate(kbs):
                                if kb <= qb - 2:
                                    continue
                                ty = 0 if kb == qb else 1
                                tmp = tmps[rot[2] % 2]
                                rot[2] += 1
                                bo = (ty * 16 + m) * 128
                                P.op("dve", lambda e, tmp=tmp, i=i, bo=bo: e.scalar_tensor_tensor(
                                    tmp.full, STb.full[:, i * 128:(i + 1) * 128], 0.125, biasT.full[:, bo:bo + 128], ALU.mult, ALU.add),
                                    reads=[biasT.d], writes=[tmp.d, STb.d])
                                P.op("act", lambda e, tmp=tmp, i=i: e.activation(PT.full[:, i * 128:(i + 1) * 128], tmp.full, AF.Exp),
                                     reads=[tmp.d], writes=[PT.d])
                                if kb == qb:
                                    P.op("pool", lambda e, i=i: e.memset(PT.full[64:128, i * 128:i * 128 + 64], 0.0),
                                         reads=[PT.d], writes=[PT.d])

                        def s3(PT=PT, kbs=kbs, O=O, qb=qb, Va=Va):
                            for i, kb in enumerate(kbs):
                                P.op("pe", lambda e, i=i, kb=kb: e.matmul(
                                    O.full[:, 0:129], PT.full[:, i * 128:(i + 1) * 128], Va.full[:, kb, 0:129],
                                    start=(kb == 0), stop=(kb == qb)), reads=[PT.d, Va.d], writes=[O.d])

                        pre = (lambda h=h: head_pre(h)) if first else None
                        first = False
                        last_of_qb = (j == 1 and kbs[-1] == qb)
                        post = []
                        if last_of_qb:
                            post.append(lambda h=h, qb=qb, Opair=Opair: qb_combine(h, qb, Opair))
                            if qb == NQB - 1:
                                post.append(lambda h=h: head_post(h))
                        items.append((pre, s1, s2, s3, post))
        n = len(items)
        SK = 2
        for i in range(n + SK):
            if i < n:
                pre, s1, s2, s3, post = items[i]
                if pre:
                    pre()
                s1()
                s2()
            if i >= SK:
                pre, s1, s2, s3, post = items[i - SK]
                s3()
                for f in post:
                    f()
        P.emit()
    nc.all_engine_barrier()


def phase_B(nc, cfg, dr, b):
    S = cfg.S
    c0 = b * S
    NSEG = S // 512
    with ExitStack() as st:
        P = Prog(nc, st)
        r = load_consts(P, dr, ["vecs", "cb", "cf"])
        Yseg = P.sb("Yseg", [128, 8, 512], F32)

        class _V:
            pass
        wl_f = _V()
        wl_f.full = Yseg.full.rearrange("p (a b) t -> p a (b t)", b=2)
        wl_f.d = Yseg.d
        wl = P.sb("wl", [128, 4, 1024], BF16)
        P.dma("sp", wl_f.full[0:96, 0, :], dr["w_up"], writes=[wl_f.d])
        P.dma("sp", wl_f.full[0:96, 1, :], dr["a_up"], writes=[wl_f.d])
        P.dma("sp", wl_f.full[:, 2:4, :], dr["g_up"].rearrange("(a p) n -> p a n", p=128), writes=[wl_f.d])
        P.op("dve", lambda e: e.tensor_copy(wl.full[0:96, 0:2, :], wl_f.full[0:96, 0:2, :]), reads=[wl_f.d], writes=[wl.d])
        P.op("dve", lambda e: e.tensor_copy(wl.full[:, 2:4, :], wl_f.full[:, 2:4, :]), reads=[wl_f.d, wl.d], writes=[wl.d])
        Hf = P.sb("Hf", [128, 512], F32)
        Hbs = [P.sb("Hb%d" % i, [128, 512], BF16) for i in range(2)]
        P.op("dve", lambda e: e.memset(Hf.full, 0.0), writes=[Hf.d])
        P.op("dve", lambda e: e.memset(Hbs[0].full, 0.0), writes=[Hbs[0].d])
        pA = [P.psum("pA%d" % i) for i in range(2)]
        pX = [P.psum("pX%d" % i) for i in range(2)]
        tpb = P.psum("tpb", BF16)
        pS = [P.psum("pS%d" % i) for i in range(3)]
        rot = {"pA": 0, "pX": 0, "t": 0}

        def f32t(name, n=1):
            return [P.sb("%s%d" % (name, i), [128, 512], F32) for i in range(n)]

        def bf16t(name, n=1):
            return [P.sb("%s%d" % (name, i), [128, 512], BF16) for i in range(n)]

        zt = [P.sb("zt%d" % i, [128, 513], F32) for i in range(2)]
        dtmp = f32t("dtmp")[0]
        twd = P.sb("twd", [128, 512], BF16)
        lad = P.sb("lad", [128, 512], BF16)
        sgd = P.sb("sgd", [128, 2, 512], BF16)
        lin = f32t("lin")[0]
        rl, kl, vl, sig, aa, kkr, sqk, rn, kk, kp, bb, cw, cwx, E1 = [f32t(n)[0] for n in
            ["rl", "kl", "vl", "sig", "aa", "kkr", "sqk", "rn", "kk", "kp", "bb", "cw", "cwx", "E1"]]
        t1, E0, Ei, rk = rn, cwx, cw, sqk
        bTs, kTts, vb = bf16t("bT", 2), bf16t("kTt", 2), bf16t("vb")[0]
        aT = bf16t("aT", 8)
        rT = bf16t("rT", 8)
        bktm = [P.sb("bktm%d" % i, [128, 1024], BF16) for i in range(8)]
        vtm = bf16t("vtm", 8)
        AakT = bf16t("AakT", 8)
        ArbT = bf16t("ArbT", 8)
        ArkT = bf16t("ArkT", 8)
        TT = bf16t("TT", 8)
        gT = bf16t("gT", 8)
        bon = f32t("bon", 8)
        Wc = P.sb("Wc", [128, 8, 8], F32)
        ptmp = bf16t("ptmp", 6)
        Zb = bf16t("Zb")[0]
        Ub = bf16t("Ub")[0]
        tmpH = f32t("tmpH")[0]
        yn = P.sb("yn", [128, 8, 512], BF16)
        gst = P.sb("gst", [128, 256], F32)
        t3 = f32t("t3")[0]
        yo = bf16t("yo", 2)

        def blocks(fn):
            for c in range(8):
                for j in range(2):
                    fn(j, c, slice(64 * j, 64 * j + 64), slice(64 * c, 64 * c + 64))

        def prod(L, R, dst, mask=None, addend=None, eng="dve"):
            ps = pX[rot["pX"] % 2]
            rot["pX"] += 1
            blocks(lambda j, c, pj, cc: P.op("pe", lambda e: e.matmul(ps.full[pj, cc], L.full[pj, cc], R.full[pj, cc], start=True, stop=True),
                                             reads=[L.d, R.d], writes=[ps.d]))
            if mask is not None:
                P.op("dve", lambda e: e.tensor_tensor(dst.full, ps.full, mask, ALU.mult), reads=[ps.d, r.cb.d], writes=[dst.d])
            elif addend is not None:
                P.op("dve", lambda e: e.tensor_tensor(dst.full, ps.full, addend.full, ALU.add), reads=[ps.d, addend.d], writes=[dst.d])
            else:
                P.op("act", lambda e: e.activation(dst.full, ps.full, AF.Copy), reads=[ps.d], writes=[dst.d])

        def load_shift(row0, nrows, sg, mucol, dst_fn):
            z = zt[rot["t"] % 2]
            rot["t"] += 1
            t0 = c0 + sg * 512
            if sg == 0:
                P.op("pool", lambda e: e.memset(z.full[:, 0:1], 0.0), writes=[z.d])
                P.dma("sp", z.full[0:nrows, 1:513], dr["zA"][row0:row0 + nrows, t0:t0 + 512], writes=[z.d])
            else:
                P.dma("sp", z.full[0:nrows, 0:513], dr["zA"][row0:row0 + nrows, t0 - 1:t0 + 512], writes=[z.d])
            P.op("dve", lambda e: e.tensor_tensor(dtmp.full[0:nrows, :], z.full[0:nrows, 0:512], z.full[0:nrows, 1:513], ALU.subtract),
                 reads=[z.d], writes=[dtmp.d])
            dst_fn(z)

        def lerp_to(dst, nrows, mucol):
            def fn(z):
                P.op("dve", lambda e: e.scalar_tensor_tensor(dst.full[0:nrows, :], dtmp.full[0:nrows, :], mucol[0:nrows, :],
                                                             z.full[0:nrows, 1:513], ALU.mult, ALU.add),
                     reads=[dtmp.d, z.d, r.vecs.d], writes=[dst.d])
            return fn

        for sg in range(NSEG):
            t0 = c0 + sg * 512
            load_shift(3072, 96, sg, None, lerp_to(lin, 96, vcol(r, "mu_wd")))
            P.op("act", lambda e: e.activation(twd.full[0:96, :], lin.full[0:96, :], AF.Tanh), reads=[lin.d], writes=[twd.d])
            load_shift(3168, 96, sg, None, lerp_to(lin, 96, vcol(r, "mu_ad")))
            P.op("act", lambda e: e.activation(lad.full[0:96, :], lin.full[0:96, :], AF.Copy), reads=[lin.d], writes=[lad.d])
            for a_ in range(2):
                load_shift(3264 + 128 * a_, 128, sg, None, lerp_to(lin, 128, vcol(r, "mu_gd", a_)))
                P.op("act", lambda e, a_=a_: e.activation(sgd.full[:, a_, :], lin.full, AF.Sigmoid), reads=[lin.d], writes=[sgd.d])
            def make_prep(hp, sg=sg):
                cs = slice(hp * 128, hp * 128 + 128)
                bT, kTt = bTs[hp % 2], kTts[hp % 2]
                steps = []

                def st0():
                    ps = None
                    load_shift(hp * 128, 128, sg, None, lerp_to(rl, 128, vcol(r, "mu_r", hp)))
                    load_shift(1024 + hp * 128, 128, sg, None, lerp_to(kl, 128, vcol(r, "mu_k", hp)))
                    load_shift(2048 + hp * 128, 128, sg, None, lerp_to(vl, 128, vcol(r, "mu_v", hp)))
                steps.append(st0)

                def st1():
                    ps = None
                    ps = pA[rot["pA"] % 2]; rot["pA"] += 1
                    P.op("pe", lambda e, ps=ps, cs=cs: e.matmul(ps.full, wl.full[0:96, 0, cs], twd.full[0:96, :], start=True, stop=True),
                         reads=[wl.d, twd.d], writes=[ps.d])
                    P.op("act", lambda e, ps=ps, hp=hp: e.activation(sig.full, ps.full, AF.Sigmoid, bias=vcol(r, "w0", hp)),
                         reads=[ps.d, r.vecs.d], writes=[sig.d])
                    ps = pA[rot["pA"] % 2]; rot["pA"] += 1
                    P.op("pe", lambda e, ps=ps, cs=cs: e.matmul(ps.full, wl.full[0:96, 1, cs], lad.full[0:96, :], start=True, stop=True),
                         reads=[wl.d, lad.d], writes=[ps.d])
                    P.op("act", lambda e, ps=ps, hp=hp: e.activation(aa.full, ps.full, AF.Sigmoid, bias=vcol(r, "a0", hp)),
                         reads=[ps.d, r.vecs.d], writes=[aa.d])
                    ps = pA[rot["pA"] % 2]; rot["pA"] += 1
                    for a_ in range(2):
                        P.op("pe", lambda e, ps=ps, cs=cs, a_=a_: e.matmul(ps.full, wl.full[:, 2 + a_, cs], sgd.full[:, a_, :],
                                                                          start=(a_ == 0), stop=(a_ == 1)),
                             reads=[wl.d, sgd.d], writes=[ps.d])
                    P.op("act", lambda e, ps=ps, hp=hp: e.activation(gT[hp].full, ps.full, AF.Copy), reads=[ps.d], writes=[gT[hp].d])
                steps.append(st1)

                def st2():
                    ps = None
                    P.op("dve", lambda e, hp=hp: e.tensor_scalar(kkr.full, kl.full, vcol(r, "k_k", hp), None, ALU.mult),
                         reads=[kl.d, r.vecs.d], writes=[kkr.d])
                    P.op("dve", lambda e: e.tensor_tensor(sqk.full, kkr.full, kkr.full, ALU.mult), reads=[kkr.d], writes=[sqk.d])
                    ps = pA[rot["pA"] % 2]; rot["pA"] += 1
                    P.op("pe", lambda e, ps=ps: e.matmul(ps.full, r.blockones, sqk.full, start=True, stop=True),
                         reads=[sqk.d, r.cf.d], writes=[ps.d])
                    P.op("dve", lambda e, ps=ps: e.tensor_scalar(rn.full, ps.full, 1e-24, None, ALU.max), reads=[ps.d], writes=[rn.d])
                    P.op("act", lambda e: e.activation(rn.full, rn.full, AF.Sqrt), reads=[rn.d], writes=[rn.d])
                    P.op("dve", lambda e: e.reciprocal(rn.full, rn.full), reads=[rn.d], writes=[rn.d])
                    P.op("dve", lambda e: e.tensor_tensor(kk.full, kkr.full, rn.full, ALU.mult), reads=[kkr.d, rn.d], writes=[kk.d])
                steps.append(st2)

                def st3():
                    ps = None
                    P.op("dve", lambda e, hp=hp: e.tensor_scalar(t1.full, aa.full, -1.0, vcol(r, "k_a", hp), ALU.add, ALU.mult),
                         reads=[aa.d, r.vecs.d], writes=[t1.d])
                    P.op("dve", lambda e: e.scalar_tensor_tensor(kp.full, t1.full, 1.0, kl.full, ALU.add, ALU.mult),
                         reads=[t1.d, kl.d], writes=[kp.d])
                    P.op("dve", lambda e: e.tensor_tensor(bb.full, kk.full, aa.full, ALU.mult), reads=[kk.d, aa.d], writes=[bb.d])
                steps.append(st3)

                def st4():
                    ps = None
                    P.op("dve", lambda e: e.tensor_tensor_scan(cw.full, r.scanmask, sig.full, 0.0, ALU.mult, ALU.add),
                         reads=[sig.d, r.cf.d], writes=[cw.d])
                    P.op("dve", lambda e: e.tensor_tensor(cwx.full, cw.full, sig.full, ALU.subtract), reads=[cw.d, sig.d], writes=[cwx.d])
                    P.op("act", lambda e: e.activation(E1.full, cw.full, AF.Exp, scale=-C0), reads=[cw.d], writes=[E1.d])
                    P.op("act", lambda e: e.activation(E0.full, cwx.full, AF.Exp, scale=-C0), reads=[cwx.d], writes=[E0.d])
                    P.op("act", lambda e: e.activation(Ei.full, cw.full, AF.Exp, scale=C0), reads=[cw.d], writes=[Ei.d])
                    P.op("dve", lambda e, hp=hp: e.scalar_tensor_tensor(aT[hp].full, kk.full, -1.0, E0.full, ALU.mult, ALU.mult),
                         reads=[kk.d, E0.d], writes=[aT[hp].d])
                    P.op("dve", lambda e, hp=hp: e.tensor_tensor(rT[hp].full, rl.full, E1.full, ALU.mult), reads=[rl.d, E1.d], writes=[rT[hp].d])
                    P.op("dve", lambda e: e.tensor_tensor(bT.full, bb.full, Ei.full, ALU.mult), reads=[bb.d, Ei.d], writes=[bT.d])
                    P.op("dve", lambda e: e.tensor_tensor(kTt.full, kp.full, Ei.full, ALU.mult), reads=[kp.d, Ei.d], writes=[kTt.d])
                    P.op("dve", lambda e, hp=hp: e.tensor_copy(Wc.full[:, hp, :], E1.full.rearrange("p (c t) -> p c t", t=64)[:, :, 63]),
                         reads=[E1.d], writes=[Wc.d])
                    P.op("act", lambda e: e.activation(vb.full, vl.full, AF.Copy), reads=[vl.d], writes=[vb.d])
                steps.append(st4)

                def st5():
                    ps = None
                    P.op("dve", lambda e, hp=hp: e.scalar_tensor_tensor(rk.full, rl.full, vcol(r, "r_k", hp), kp.full, ALU.mult, ALU.mult),
                         reads=[rl.d, kp.d, r.vecs.d], writes=[rk.d])
                    ps = pA[rot["pA"] % 2]; rot["pA"] += 1
                    P.op("pe", lambda e, ps=ps: e.matmul(ps.full, r.blockones, rk.full, start=True, stop=True),
                         reads=[rk.d, r.cf.d], writes=[ps.d])
                    P.op("dve", lambda e, ps=ps, hp=hp: e.tensor_tensor(bon[hp].full, ps.full, vl.full, ALU.mult),
                         reads=[ps.d, vl.d], writes=[bon[hp].d])
                steps.append(st5)

                def st6():
                    ps = None
                    for X, off in ((bT, 0), (kTt, 512)):
                        blocks(lambda j, c, pj, cc, X=X, off=off: P.op("pe", lambda e: e.transpose(
                            tpb.full[pj, off + 64 * c:off + 64 * c + 64], X.full[pj, cc], r.ident[pj, pj]),
                            reads=[X.d, r.cb.d], writes=[tpb.d]))
                    P.op("dve", lambda e, hp=hp: e.tensor_copy(bktm[hp].full, tpb.full), reads=[tpb.d], writes=[bktm[hp].d])
                    blocks(lambda j, c, pj, cc: P.op("pe", lambda e: e.transpose(tpb.full[pj, cc], vb.full[pj, cc], r.ident[pj, pj]),
                                                     reads=[vb.d, r.cb.d], writes=[tpb.d]))
                    P.op("dve", lambda e, hp=hp: e.tensor_copy(vtm[hp].full, tpb.full[:, 0:512]), reads=[tpb.d], writes=[vtm[hp].d])
                steps.append(st6)
                return steps

            def make_inv(hp):
                bT, kTt = bTs[hp % 2], kTts[hp % 2]
                steps = []
                P0, P0T = ptmp[0], ptmp[1]
                steps.append(lambda: prod(aT[hp], bT, P0, mask=r.ML))
                steps.append(lambda: prod(bT, aT[hp], P0T, mask=r.MU))
                steps.append(lambda: prod(kTt, aT[hp], AakT[hp], mask=r.MU))
                steps.append(lambda: prod(bT, rT[hp], ArbT[hp], mask=r.MUI))
                steps.append(lambda: prod(kTt, rT[hp], ArkT[hp], mask=r.MUI))
                steps.append(lambda: P.op("dve", lambda e: e.tensor_tensor(TT[hp].full, P0T.full, r.IB, ALU.add),
                                          reads=[P0T.d, r.cb.d], writes=[TT[hp].d]))
                Pc, PcT = P0, P0T
                free = [ptmp[2], ptmp[3], ptmp[4], ptmp[5]]
                for lvl in range(1, 6):
                    Pn = free.pop(0)
                    steps.append(lambda PcT=PcT, Pc=Pc, Pn=Pn: prod(PcT, Pc, Pn))
                    PnT = None
                    if lvl < 5:
                        PnT = free.pop(0)
                        steps.append(lambda PcT=PcT, Pc=Pc, PnT=PnT: prod(Pc, PcT, PnT))
                    steps.append(lambda Pn=Pn: prod(Pn, TT[hp], TT[hp], addend=TT[hp]))
                    free.append(Pc)
                    free.append(PcT)
                    Pc, PcT = Pn, PnT
                return steps

            def weave(a, b):
                na, nb = len(a), len(b)
                ia = ib = 0
                while ia < na or ib < nb:
                    if ib < nb and (ia >= na or ib * max(na, 1) <= ia * nb):
                        b[ib]()
                        ib += 1
                    else:
                        a[ia]()
                        ia += 1

            pend = []
            for hp in range(8):
                weave(make_prep(hp), pend)
                pend = make_inv(hp)
            weave([], pend)
            for c in range(8):
                gc = sg * 8 + c
                Hc, Hn = Hbs[gc % 2], Hbs[(gc + 1) % 2]
                cc = slice(64 * c, 64 * c + 64)

                def heads(fn):
                    for hp in range(8):
                        for j in range(2):
                            fn(hp, slice(64 * j, 64 * j + 64), slice(64 * hp, 64 * hp + 64))

                heads(lambda hp, pj, hh: (
                    P.op("pe", lambda e, cc=cc, Hc=Hc, c=c: e.matmul(pS[0].full[pj, hh], aT[hp].full[pj, cc], Hc.full[pj, hh], start=True, stop=False),
                         reads=[aT[hp].d, Hc.d], writes=[pS[0].d]),
                    P.op("pe", lambda e, cc=cc, Hc=Hc, c=c: e.matmul(pS[0].full[pj, hh], AakT[hp].full[pj, cc], vtm[hp].full[pj, cc], start=False, stop=True),
                         reads=[AakT[hp].d, vtm[hp].d], writes=[pS[0].d])))
                P.op("act", lambda e: e.activation(Zb.full, pS[0].full, AF.Copy), reads=[pS[0].d], writes=[Zb.d])
                heads(lambda hp, pj, hh: P.op("pe", lambda e, cc=cc, Hc=Hc, c=c: e.matmul(pS[1].full[pj, hh], TT[hp].full[pj, cc], Zb.full[pj, hh], start=True, stop=True),
                                              reads=[TT[hp].d, Zb.d], writes=[pS[1].d]))
                P.op("dve", lambda e: e.tensor_copy(Ub.full, pS[1].full), reads=[pS[1].d], writes=[Ub.d])
                heads(lambda hp, pj, hh: (
                    P.op("pe", lambda e, cc=cc, Hc=Hc, c=c: e.matmul(pS[2].full[pj, hh], rT[hp].full[pj, cc], Hc.full[pj, hh], start=True, stop=False),
                         reads=[rT[hp].d, Hc.d], writes=[pS[2].d]),
                    P.op("pe", lambda e, cc=cc, Hc=Hc, c=c: e.matmul(pS[2].full[pj, hh], ArbT[hp].full[pj, cc], Ub.full[pj, hh], start=False, stop=False),
                         reads=[ArbT[hp].d, Ub.d], writes=[pS[2].d]),
                    P.op("pe", lambda e, cc=cc, Hc=Hc, c=c: e.matmul(pS[2].full[pj, hh], ArkT[hp].full[pj, cc], vtm[hp].full[pj, cc], start=False, stop=True),
                         reads=[ArkT[hp].d, vtm[hp].d], writes=[pS[2].d])))
                P.op("act", lambda e, c=c: e.activation(Yseg.full[:, c, :], pS[2].full, AF.Copy), reads=[pS[2].d], writes=[Yseg.d])
                heads(lambda hp, pj, hh: (
                    P.op("pe", lambda e, cc=cc, Hc=Hc, c=c: e.matmul(pS[0].full[pj, hh], bktm[hp].full[pj, cc], Ub.full[pj, hh], start=True, stop=False),
                         reads=[bktm[hp].d, Ub.d], writes=[pS[0].d]),
                    P.op("pe", lambda e, cc=cc, Hc=Hc, c=c: e.matmul(pS[0].full[pj, hh], bktm[hp].full[pj, 512 + 64 * c:512 + 64 * c + 64], vtm[hp].full[pj, cc],
                                                  start=False, stop=True),
                         reads=[bktm[hp].d, vtm[hp].d], writes=[pS[0].d])))
                P.op("dve", lambda e: e.tensor_tensor(tmpH.full, pS[0].full, Hf.full, ALU.add), reads=[pS[0].d, Hf.d], writes=[tmpH.d])
                P.op("dve", lambda e, c=c: e.tensor_tensor(
                    Hf.full.rearrange("p (h v) -> p h v", v=64), tmpH.full.rearrange("p (h v) -> p h v", v=64),
                    Wc.full[:, :, c:c + 1].broadcast_to([128, 8, 64]), ALU.mult), reads=[tmpH.d, Wc.d], writes=[Hf.d])
                P.op("act", lambda e, Hn=Hn: e.activation(Hn.full, Hf.full, AF.Copy), reads=[Hf.d], writes=[Hn.d])
            Yv = Yseg.full.rearrange("p c (h v) -> p (c h) v", v=64)
            Nv = yn.full.rearrange("p c (h v) -> p (c h) v", v=64)
            P.op("dve", lambda e: e.reduce_sum(gst.full[:, 0:64], Yv, AX.X), reads=[Yseg.d], writes=[gst.d])
            P.op("dve", lambda e: e.tensor_scalar(gst.full[:, 64:128], gst.full[:, 0:64], 1.0 / 64, None, ALU.mult), reads=[gst.d], writes=[gst.d])
            P.op("dve", lambda e: e.tensor_tensor(Yv, Yv, gst.full[:, 64:128].unsqueeze(2).broadcast_to([128, 64, 64]), ALU.subtract),
                 reads=[Yseg.d, gst.d], writes=[Yseg.d])
            P.op("dve", lambda e: e.tensor_tensor(Nv, Yv, Yv, ALU.mult), reads=[Yseg.d], writes=[yn.d])
            P.op("dve", lambda e: e.reduce_sum(gst.full[:, 128:192], Nv, AX.X), reads=[yn.d, gst.d], writes=[gst.d])
            P.op("act", lambda e: e.activation(gst.full[:, 192:256], gst.full[:, 128:192], AF.Sqrt, bias=GN_EPS, scale=1.0 / 64),
                 reads=[gst.d], writes=[gst.d])
            P.op("dve", lambda e: e.reciprocal(gst.full[:, 192:256], gst.full[:, 192:256]), reads=[gst.d], writes=[gst.d])
            P.op("dve", lambda e: e.tensor_tensor(Nv, Yv, gst.full[:, 192:256].unsqueeze(2).broadcast_to([128, 64, 64]), ALU.mult),
                 reads=[Yseg.d, gst.d], writes=[yn.d])
            for hp in range(8):
                blocks(lambda j, c, pj, cc, hp=hp: P.op("pe", lambda e: e.transpose(
                    tpb.full[pj, cc], yn.full[pj, c, 64 * hp:64 * hp + 64], r.ident[pj, pj]), reads=[yn.d, r.cb.d], writes=[tpb.d]))
                y_ = yo[hp % 2]
                P.op("dve", lambda e, hp=hp: e.tensor_scalar(t3.full, tpb.full[:, 0:512], vcol(r, "lnx_g", hp), vcol(r, "lnx_b", hp),
                                                            ALU.mult, ALU.add), reads=[tpb.d, r.vecs.d], writes=[t3.d])
                P.op("dve", lambda e, hp=hp: e.tensor_tensor(t3.full, t3.full, bon[hp].full, ALU.add), reads=[t3.d, bon[hp].d], writes=[t3.d])
                P.op("dve", lambda e, hp=hp, y_=y_: e.tensor_tensor(y_.full, t3.full, gT[hp].full, ALU.mult), reads=[t3.d, gT[hp].d], writes=[y_.d])
                P.dma("sp", dr["yA"][hp * 128:(hp + 1) * 128, t0:t0 + 512], y_.full, reads=[y_.d])
        P.emit()
    nc.all_engine_barrier()
```

```python
import numpy as np
import ml_dtypes
import concourse.bass as bass
import concourse.mybir as mybir
from concourse.bass_utils import run_bass_kernel_spmd

F32 = mybir.dt.float32
BF16 = mybir.dt.bfloat16
ALU = mybir.AluOpType
AF = mybir.ActivationFunctionType
AX = mybir.AxisListType


class Dep:
    __slots__ = ("name", "w", "rs")

    def __init__(self, name):
        self.name = name
        self.w = None
        self.rs = []


class Op:
    __slots__ = ("eng", "fn", "deps", "is_dma", "sem", "semval", "need_sig", "sigidx", "prev_same_sem")

    def __init__(self, eng, fn, is_dma):
        self.eng = eng
        self.fn = fn
        self.deps = []
        self.is_dma = is_dma
        self.sem = None
        self.semval = 0
        self.need_sig = False
        self.sigidx = 0
        self.prev_same_sem = None


class SB:
    def __init__(self, handle, dep):
        self.h = handle
        self.full = handle.ap()
        self.d = dep

    def __getitem__(self, k):
        return self.full[k]


ENGS = ("pe", "act", "dve", "pool", "sp")
N_DMA_SEMS = 40


class Prog:
    G = None

    @staticmethod
    def init_global(nc, st):
        g = {}
        g["sems"] = {e: st.enter_context(nc.semaphore("gs_" + e)) for e in ENGS}
        g["dsems"] = [st.enter_context(nc.semaphore("gd_%d" % i)) for i in range(N_DMA_SEMS)]
        g["cnt"] = {e: 0 for e in ENGS}
        g["n_dma"] = 0
        g["dma_last"] = [None] * N_DMA_SEMS
        g["dma_cnt"] = [0] * N_DMA_SEMS
        Prog.G = g

    def __init__(self, nc, st):
        self.nc = nc
        self.st = st
        self.ops = []
        self.deps = {}

    def dep(self, name):
        d = self.deps.get(name)
        if d is None:
            d = Dep(name)
            self.deps[name] = d
        return d

    _uid = [0]

    def sb(self, name, shape, dtype):
        Prog._uid[0] += 1
        h = self.st.enter_context(self.nc.sbuf_tensor("s%d_%s" % (Prog._uid[0], name), list(shape), dtype))
        return SB(h, Dep(name))

    def psum(self, name, dtype=F32):
        n = 512 if dtype == F32 else 1024
        Prog._uid[0] += 1
        h = self.st.enter_context(self.nc.psum_tensor("p%d_%s" % (Prog._uid[0], name), [128, n], dtype))
        return SB(h, Dep(name))

    def _track(self, op, reads, writes):
        ds = set()
        for b in reads:
            if b.w is not None:
                ds.add(b.w)
        for b in writes:
            if b.w is not None:
                ds.add(b.w)
            for r in b.rs:
                ds.add(r)
        ds.discard(op)
        for d in ds:
            if d.eng == "pe" and op.eng == "pe" and not d.is_dma and not op.is_dma:
                continue
            op.deps.append(d)
            d.need_sig = True
        for b in reads:
            b.rs.append(op)
        for b in writes:
            b.w = op
            b.rs = []

    def op(self, eng, fn, reads=(), writes=()):
        o = Op(eng, fn, False)
        self._track(o, reads, writes)
        self.ops.append(o)
        return o

    def dma(self, queue, out, in_, reads=(), writes=(), **kw):
        o = Op(queue, lambda e: e.dma_start(out=out, in_=in_, **kw), True)
        g = Prog.G
        s = g["n_dma"] % N_DMA_SEMS
        g["n_dma"] += 1
        o.sem = s
        g["dma_cnt"][s] += 16
        o.semval = g["dma_cnt"][s]
        o.prev_same_sem = g["dma_last"][s]
        g["dma_last"][s] = o
        self._track(o, reads, writes)
        self.ops.append(o)
        return o

    def emit(self):
        nc = self.nc
        g = Prog.G
        sems = g["sems"]
        dsems = g["dsems"]
        from contextlib import ExitStack
        with ExitStack() as st:
            cnt = g["cnt"]
            for o in self.ops:
                if not o.is_dma and o.need_sig:
                    cnt[o.eng] += 1
                    o.sigidx = cnt[o.eng]
            per_eng = {e: [] for e in ENGS}
            for o in self.ops:
                per_eng[o.eng].append(o)
            block = st.enter_context(nc.Block())

            def run(engname, eng):
                seen = {}

                def wait(sem, val, key):
                    if seen.get(key, 0) >= val:
                        return
                    seen[key] = val
                    eng.wait_ge(sem, val)

                for o in per_eng[engname]:
                    for d in o.deps:
                        if d.is_dma:
                            wait(dsems[d.sem], d.semval, ("d", d.sem))
                        else:
                            wait(sems[d.eng], d.sigidx, ("e", d.eng))
                    if o.is_dma:
                        p = o.prev_same_sem
                        if p is not None:
                            wait(dsems[p.sem], p.semval, ("d", p.sem))
                        o.fn(eng).then_inc(dsems[o.sem], 16)
                    else:
                        ins = o.fn(eng)
                        if o.need_sig:
                            ins.then_inc(sems[engname], 1)
                if engname == "sp":
                    for s in range(N_DMA_SEMS):
                        if g["dma_cnt"][s]:
                            wait(dsems[s], g["dma_cnt"][s], ("d", s))

            @block.tensor
            def _(eng):
                run("pe", eng)

            @block.scalar
            def _(eng):
                run("act", eng)

            @block.vector
            def _(eng):
                run("dve", eng)

            @block.gpsimd
            def _(eng):
                run("pool", eng)

            @block.sync
            def _(eng):
                run("sp", eng)
        self.ops = []


import math
from contextlib import ExitStack

D = 2048
AW = 1024
DL = 96
GL = 256
A_COLS = 3 * AW + 2 * DL + GL
QKC = 1024
IN_COLS = 10688
DFF = 8192
NCORES = 8
C0 = math.exp(-0.5)
LAMBDA_INIT = 0.8 - 0.6 * math.exp(-0.3 * 0)
GN_EPS = 64e-5
SUBLN_EPS = 1e-5
RMS_EPS = 1e-6

VC = {}
_o = 0
for _n, _w in [("g_mix", 16), ("g_mlp", 16), ("g_fin", 16), ("mu_r", 8), ("mu_k", 8), ("mu_v", 8),
               ("mu_wd", 1), ("mu_ad", 1), ("mu_gd", 2), ("w0", 8), ("a0", 8), ("k_k", 8), ("k_a", 8),
               ("r_k", 8), ("lnx_g", 8), ("lnx_b", 8)]:
    VC[_n] = _o
    _o += _w
NV = _o


class Cfg:
    def __init__(self, S=2048, NB=2, phases="ABCDE", dump=(), inject=()):
        self.S = S
        self.NB = NB
        self.T = S * NB
        self.phases = phases
        self.dump = set(dump)
        self.inject = set(inject)


def _pm(v, n):
    return np.ascontiguousarray(np.asarray(v, np.float32).reshape(n, 128).T)


class Res:
    pass


def load_consts(P, dr, names):
    r = Res()
    if "vecs" in names:
        r.vecs = P.sb("vecs", [128, NV], F32)
        P.dma("sp", r.vecs.full, dr["vecs"], writes=[r.vecs.d])
    if "cb" in names:
        r.cb = P.sb("cb", [128, 4 * 512 + 256], BF16)
        P.dma("sp", r.cb.full, dr["cb"], writes=[r.cb.d])
        r.ML = r.cb.full[:, 0:512]
        r.MU = r.cb.full[:, 512:1024]
        r.MUI = r.cb.full[:, 1024:1536]
        r.IB = r.cb.full[:, 1536:2048]
        r.ident = r.cb.full[:, 2048:2176]
        r.ones = r.cb.full[:, 2176:2304]
    if "cf" in names:
        r.cf = P.sb("cf", [128, 640], F32)
        P.dma("sp", r.cf.full, dr["cf"], writes=[r.cf.d])
        r.blockones = r.cf.full[:, 0:128]
        r.scanmask = r.cf.full[:, 128:640]
    return r


def vcol(r, name, i=0):
    c = VC[name] + i
    return r.vecs.full[:, c:c + 1]


class WStream:
    def __init__(self, P, nbuf=4):
        self.P = P
        self.bf = [P.sb("wbf%d" % i, [128, 16, 128], BF16) for i in range(nbuf)]
        self.i = 0

    def load(self, w_dram, k0, col0, width, nk=16):
        P = self.P
        bf = self.bf[self.i % len(self.bf)]
        self.i += 1
        src = w_dram[k0 * 128:(k0 + nk) * 128, col0:col0 + width].rearrange("(kc p) n -> p kc n", p=128)
        P.dma("pool", bf.full[:, 0:nk, 0:width], src, writes=[bf.d])
        return bf


def linear_fm(P, ws, w_dram, KC, blocks, xT, xdeps, NTG, banks, evac, tgw=512):
    kgs = min(16, KC)
    nkg = KC // kgs
    bk = 0
    for bi, (col0, width, tag) in enumerate(blocks):
        pss = []
        for tg in range(NTG):
            pss.append(banks[bk % len(banks)])
            bk += 1
        for kg in range(nkg):
            wb = ws.load(w_dram, kg * kgs, col0, width, kgs)
            for tg in range(NTG):
                ps = pss[tg]
                for kc in range(kgs):
                    kk = kg * kgs + kc
                    P.op("pe", lambda e, ps=ps, wb=wb, kc=kc, kk=kk, tg=tg, width=width:
                         e.matmul(ps.full[0:width, 0:tgw], wb.full[:, kc, 0:width], xT(kk, tg),
                                  start=(kk == 0), stop=(kk == KC - 1)),
                         reads=[wb.d] + xdeps(kk, tg), writes=[ps.d])
        for tg in range(NTG):
            evac(tag, bi, tg, pss[tg], width)


def rmsnorm_to_bf16(P, r, src_dram, c0, ntok, uT, gname, banks_small, xs_bufs, sq, rs, rinv, eps=RMS_EPS, q="act", utd=None):
    G = 256
    for gi in range(ntok // G):
        xs = xs_bufs[gi % len(xs_bufs)]
        src = src_dram[:, c0 + gi * G:c0 + (gi + 1) * G].rearrange("(kc p) t -> p kc t", p=128)
        P.dma(q, xs.full, src, writes=[xs.d])
        P.op("act", lambda e, xs=xs: e.activation(sq.full, xs.full, AF.Square), reads=[xs.d], writes=[sq.d])
        ps = banks_small[gi % len(banks_small)]
        for kc in range(16):
            P.op("pe", lambda e, ps=ps, kc=kc: e.matmul(ps.full[:, 0:G], r.ones, sq.full[:, kc, :],
                                                        start=(kc == 0), stop=(kc == 15)),
                 reads=[sq.d, r.cb.d], writes=[ps.d])
        P.op("act", lambda e, ps=ps: e.activation(rs.full, ps.full[:, 0:G], AF.Sqrt, bias=eps, scale=1.0 / D),
             reads=[ps.d], writes=[rs.d])
        P.op("dve", lambda e: e.reciprocal(rinv.full, rs.full), reads=[rs.d], writes=[rinv.d])
        for kc in range(16):
            P.op("dve", lambda e, xs=xs, kc=kc, gi=gi: e.scalar_tensor_tensor(
                uT.full[:, kc, gi * G:(gi + 1) * G], xs.full[:, kc, :], vcol(r, gname, kc), rinv.full,
                ALU.mult, ALU.mult),
                reads=[xs.d, rinv.d, r.vecs.d], writes=[uT.d if utd is None else utd[gi // 2]])


def phase_A(nc, cfg, dr, b):
    S = cfg.S
    c0 = b * S
    NTG = S // 512
    with ExitStack() as st:
        P = Prog(nc, st)
        r = load_consts(P, dr, ["vecs", "cb"])
        uT = P.sb("uT", [128, 16, S], BF16)
        xs_bufs = [P.sb("xs%d" % i, [128, 16, 256], F32) for i in range(2)]
        sq = P.sb("sq", [128, 16, 256], BF16)
        rs = P.sb("rs", [128, 256], F32)
        rinv = P.sb("rinv", [128, 256], F32)
        banks = [P.psum("pm%d" % i) for i in range(6)]
        bsm = [P.psum("psm%d" % i) for i in range(2)]
        ws = WStream(P)
        stf = [P.sb("stf%d" % i, [128, S], F32) for i in range(2)]
        stb = [P.sb("stb%d" % i, [128, S], BF16) for i in range(2)]
        utd = [Dep("utd%d" % i) for i in range(NTG)]
        rmsnorm_to_bf16(P, r, dr["xT"], c0, S, uT, "g_mix", bsm, xs_bufs, sq, rs, rinv, utd=utd)

        blocks = []
        for i in range(24):
            blocks.append((i * 128, 128, ("zA", i * 128)))
        blocks.append((3072, 96, ("zA", 3072)))
        blocks.append((3168, 96, ("zA", 3168)))
        blocks.append((3264, 128, ("zA", 3264)))
        blocks.append((3392, 128, ("zA", 3392)))
        for i in range(24):
            blocks.append((A_COLS + i * 128, 128, ("qkv", i * 128)))
        for i in range(32):
            blocks.append((A_COLS + 3072 + i * 128, 128, ("gate", i * 128)))
        cnt = {"f": 0, "b": 0}

        def evac(tag, bi, tg, ps, width):
            kind, row0 = tag
            if kind == "zA":
                sbuf = stf[cnt["f"] % 2]
                P.op("act", lambda e: e.activation(sbuf.full[0:width, tg * 512:(tg + 1) * 512], ps.full[0:width, :], AF.Copy),
                     reads=[ps.d], writes=[sbuf.d])
                if tg == NTG - 1:
                    P.dma("act", dr["zA"][row0:row0 + width, c0:c0 + S], sbuf.full[0:width, :], reads=[sbuf.d])
                    cnt["f"] += 1
            else:
                sbuf = stb[cnt["b"] % 2]
                fn = AF.Copy if kind == "qkv" else AF.Sigmoid
                P.op("act", lambda e: e.activation(sbuf.full[0:width, tg * 512:(tg + 1) * 512], ps.full[0:width, :], fn),
                     reads=[ps.d], writes=[sbuf.d])
                if tg == NTG - 1:
                    dst = dr["qkv"] if kind == "qkv" else dr["gates"]
                    P.dma("act", dst[row0:row0 + width, c0:c0 + S], sbuf.full[0:width, :], reads=[sbuf.d])
                    cnt["b"] += 1

        import os
        if os.environ.get("DBG_A") == "norm":
            blocks = []
        elif os.environ.get("DBG_A"):
            blocks = blocks[:int(os.environ["DBG_A"])]
        linear_fm(P, ws, dr["w_in"], 16, blocks, lambda kk, tg: uT.full[:, kk, tg * 512:(tg + 1) * 512],
                  lambda kk, tg: [utd[tg]], NTG, banks, evac)
        P.emit()
    nc.all_engine_barrier()


SCRATCH = {
    "zA": ([A_COLS, None], F32),
    "qkv": ([3072, None], BF16),
    "gates": ([4096, None], BF16),
    "yA": ([AW, None], BF16),
    "yB": ([AW, None], BF16),
    "h": ([D, None], F32),
}


def build_nc(cfg):
    nc = bass.Bass("TRN2", target_bir_lowering=False)
    T = cfg.T
    dr = {}
    gst = ExitStack()
    Prog.init_global(nc, gst)

    def inp(name, shape, dt=F32):
        dr[name] = nc.dram_tensor(name, list(shape), dt, kind="ExternalInput").ap()

    inp("xT", [D, T])
    if "A" in cfg.phases:
        inp("w_in", [D, IN_COLS])
    if "D" in cfg.phases:
        inp("p_a", [AW, D])
        inp("p_b", [AW, D])
        inp("w_out", [D, D])
    if "E" in cfg.phases:
        inp("w_ff1", [D, DFF])
        inp("w_ff2", [DFF, D])
    if "B" in cfg.phases:
        inp("w_up", [DL, AW])
        inp("a_up", [DL, AW])
        inp("g_up", [GL, AW])
    inp("vecs", [128, NV])
    inp("bc", [128, 16 + 256 + 128])
    inp("biasT", [128, 2 * 16 * 128])
    inp("cb", [128, 4 * 512 + 256], BF16)
    inp("cf", [128, 640])
    for name, (shape, dt) in SCRATCH.items():
        shp = [shape[0], T]
        if name in cfg.inject:
            kind = "ExternalInput"
        elif name in cfg.dump:
            kind = "ExternalOutput"
        else:
            kind = "Internal"
        dr[name] = nc.dram_tensor(name, shp, dt, kind=kind).ap()
    dr["outT"] = nc.dram_tensor("outT", [D, T], F32, kind="ExternalOutput").ap()
    for b in range(cfg.NB):
        if "A" in cfg.phases:
            phase_A(nc, cfg, dr, b)
        if "B" in cfg.phases:
            phase_B(nc, cfg, dr, b)
        if "C" in cfg.phases:
            phase_C(nc, cfg, dr, b)
        if "D" in cfg.phases:
            phase_D(nc, cfg, dr, b)
        if "E" in cfg.phases:
            phase_E(nc, cfg, dr, b)
    return nc


def t5_bucket_np(rel):
    nb = 16
    max_exact = 8
    ret = np.where(rel > 0, nb, 0)
    n = np.abs(rel)
    nf = np.maximum(n, 1).astype(np.float32)
    large = max_exact + (np.log(nf / max_exact) / math.log(128 / max_exact) * (nb - max_exact)).astype(np.int32)
    large = np.minimum(large, nb - 1)
    return ret + np.where(n < max_exact, n, large)


def host_consts(inp):
    f = np.float32
    vecs = np.zeros((128, NV), f)

    def put(name, v, n):
        vecs[:, VC[name]:VC[name] + n] = _pm(v, n)

    put("g_mix", inp["norm_mix_g"][0], 16)
    put("g_mlp", inp["norm_mlp_g"][0], 16)
    put("g_fin", inp["norm_final_g"], 16)
    mu = np.asarray(inp["mu_shift"][0], f)
    put("mu_r", mu[0:1024], 8)
    put("mu_k", mu[1024:2048], 8)
    put("mu_v", mu[2048:3072], 8)
    vecs[0:96, VC["mu_wd"]] = mu[3072:3168]
    vecs[0:96, VC["mu_ad"]] = mu[3168:3264]
    put("mu_gd", mu[3264:3520], 2)
    put("w0", inp["w0"][0], 8)
    put("a0", inp["a0"][0], 8)
    put("k_k", inp["k_k"][0], 8)
    put("k_a", inp["k_a"][0], 8)
    put("r_k", np.asarray(inp["r_k"][0]).reshape(-1), 8)
    put("lnx_g", inp["lnx_g"][0], 8)
    put("lnx_b", inp["lnx_b"][0], 8)

    rb = np.asarray(inp["rel_bias"], f)
    bc = np.zeros((128, 16 + 256 + 128), f)
    bc[:, 0:16] = rb[15][None, :]
    for i, nme in enumerate(["lambda_q1", "lambda_k1", "lambda_q2", "lambda_k2"]):
        bc[:, 16 + 64 * i:16 + 64 * (i + 1)] = np.asarray(inp[nme][0], f)[None, :]
    bc[:, 272:400] = np.asarray(inp["subln_g"][0], f)[None, :]
    kl = np.arange(128)[:, None]
    ql = np.arange(128)[None, :]
    biasT = np.zeros((128, 2, 16, 128), f)
    for ty, off in enumerate([0, -128]):
        bidx = t5_bucket_np(off + kl - ql)
        biasT[:, ty, :, :] = np.transpose(rb[bidx], (0, 2, 1))
    biasT = biasT.reshape(128, -1)
    row = (np.arange(128) % 64)[:, None]
    col = (np.arange(512) % 64)[None, :]
    ML = (col < row).astype(f)
    MU = (row < col).astype(f)
    MUI = (row <= col).astype(f)
    IB = (row == col).astype(f)
    ident = np.eye(128, dtype=f)
    ones = np.ones((128, 128), f)
    cb = np.concatenate([ML, MU, MUI, IB, ident, ones], axis=1).astype(ml_dtypes.bfloat16)
    blockones = np.zeros((128, 128), f)
    blockones[0:64, 0:64] = 1
    blockones[64:, 64:] = 1
    scanmask = np.broadcast_to((np.arange(512) % 64 != 0).astype(f)[None, :], (128, 512))
    cf = np.concatenate([blockones, scanmask], axis=1).astype(f)
    out = {"vecs": vecs, "bc": bc, "biasT": np.ascontiguousarray(biasT), "cb": np.ascontiguousarray(cb),
           "cf": np.ascontiguousarray(cf)}
    for nme in ["w_in", "p_a", "p_b", "w_out", "w_ff1", "w_ff2", "w_up", "a_up", "g_up"]:
        out[nme] = np.ascontiguousarray(np.asarray(inp[nme][0], f))
    return out


def kernel(**inputs):
    cfg = Cfg()
    x = np.asarray(inputs["x"], np.float32)
    B, S, _ = x.shape
    nc = build_nc(cfg)
    shared = host_consts(inputs)
    in_maps = []
    for c in range(NCORES):
        m = dict(shared)
        xs = x[c * cfg.NB:(c + 1) * cfg.NB].reshape(cfg.T, D)
        m["xT"] = np.ascontiguousarray(xs.T)
        in_maps.append(m)
    res = run_bass_kernel_spmd(nc, in_maps, core_ids=list(range(NCORES)))
    out = np.empty((B, S, D), np.float32)
    for c in range(NCORES):
        o = res.results[c]["outT"]
        out[c * cfg.NB:(c + 1) * cfg.NB] = o.T.reshape(cfg.NB, S, D)
    return out


def phase_D(nc, cfg, dr, b):
    S = cfg.S
    c0 = b * S
    NTG = S // 512
    with ExitStack() as st:
        P = Prog(nc, st)
        yAs = P.sb("yAs", [128, 8, S], BF16)
        yBs = P.sb("yBs", [128, 8, S], BF16)
        mT = P.sb("mT", [128, 16, S], BF16)
        P.dma("act", yAs.full, dr["yA"][:, c0:c0 + S].rearrange("(kc p) t -> p kc t", p=128), writes=[yAs.d])
        P.dma("act", yBs.full, dr["yB"][:, c0:c0 + S].rearrange("(kc p) t -> p kc t", p=128), writes=[yBs.d])
        ws = WStream(P)
        banks = [P.psum("pm%d" % i) for i in range(8)]
        gts = [P.sb("gt%d" % i, [128, 2, S], BF16) for i in range(2)]
        t1 = P.sb("t1", [128, 512], F32)
        t2 = P.sb("t2", [128, 512], F32)
        xts = [P.sb("xt%d" % i, [128, S], F32) for i in range(2)]
        sth = [P.sb("sth%d" % i, [128, S], F32) for i in range(2)]
        for cb in range(16):
            gt = gts[cb % 2]
            P.dma("act", gt.full[:, 0, :], dr["gates"][cb * 128:(cb + 1) * 128, c0:c0 + S], writes=[gt.d])
            P.dma("act", gt.full[:, 1, :], dr["gates"][2048 + cb * 128:2048 + (cb + 1) * 128, c0:c0 + S], writes=[gt.d])
            for tg in range(NTG):
                psA = banks[(2 * (cb * NTG + tg)) % 8]
                psB = banks[(2 * (cb * NTG + tg) + 1) % 8]
                if tg == 0:
                    wa = ws.load(dr["p_a"], 0, cb * 128, 128, 8)
                    wb = ws.load(dr["p_b"], 0, cb * 128, 128, 8)
                for (ps, w, ys) in ((psA, wa, yAs), (psB, wb, yBs)):
                    for kc in range(8):
                        P.op("pe", lambda e, ps=ps, w=w, ys=ys, kc=kc, tg=tg: e.matmul(
                            ps.full[:, :], w.full[:, kc, :], ys.full[:, kc, tg * 512:(tg + 1) * 512],
                            start=(kc == 0), stop=(kc == 7)), reads=[w.d, ys.d], writes=[ps.d])
                P.op("dve", lambda e, psA=psA, gt=gt, tg=tg: e.tensor_tensor(
                    t1.full, psA.full, gt.full[:, 0, tg * 512:(tg + 1) * 512], ALU.mult), reads=[psA.d, gt.d], writes=[t1.d])
                P.op("dve", lambda e, psB=psB, gt=gt, tg=tg: e.tensor_tensor(
                    t2.full, psB.full, gt.full[:, 1, tg * 512:(tg + 1) * 512], ALU.mult), reads=[psB.d, gt.d], writes=[t2.d])
                P.op("dve", lambda e, cb=cb, tg=tg: e.tensor_tensor(
                    mT.full[:, cb, tg * 512:(tg + 1) * 512], t1.full, t2.full, ALU.add), reads=[t1.d, t2.d], writes=[mT.d])
        cnt = [0]

        def evac(tag, bi, tg, ps, width):
            xt = xts[bi % 2]
            sb_ = sth[bi % 2]
            if tg == 0:
                P.dma("act", xt.full, dr["xT"][bi * 128:(bi + 1) * 128, c0:c0 + S], writes=[xt.d])
            P.op("dve", lambda e: e.tensor_tensor(sb_.full[:, tg * 512:(tg + 1) * 512], ps.full,
                                                  xt.full[:, tg * 512:(tg + 1) * 512], ALU.add),
                 reads=[ps.d, xt.d], writes=[sb_.d])
            if tg == NTG - 1:
                P.dma("sp", dr["h"][bi * 128:(bi + 1) * 128, c0:c0 + S], sb_.full, reads=[sb_.d])

        linear_fm(P, ws, dr["w_out"], 16, [(i * 128, 128, None) for i in range(16)],
                  lambda kk, tg: mT.full[:, kk, tg * 512:(tg + 1) * 512], lambda kk, tg: [mT.d], NTG, banks[0:6], evac)
        P.emit()
    nc.all_engine_barrier()


def phase_E(nc, cfg, dr, b):
    S = cfg.S
    TF = min(1024, S)
    NTG = S // TF
    NH = TF // 512
    with ExitStack() as st:
        P = Prog(nc, st)
        r = load_consts(P, dr, ["vecs", "cb"])
        h2 = P.sb("h2", [128, 16, TF], F32)
        mT = P.sb("mT", [128, 16, TF], BF16)
        hids = [P.sb("hid%d" % i, [128, 16, TF], BF16) for i in range(2)]
        sq = P.sb("sq", [128, 16, 256], BF16)
        rs = P.sb("rs", [128, 256], F32)
        rinv = P.sb("rinv", [128, 256], F32)
        rls = [P.sb("rl%d" % i, [128, 512], BF16) for i in range(2)]
        ost = [P.sb("ost%d" % i, [128, TF], F32) for i in range(2)]
        banks = [P.psum("pm%d" % i) for i in range(6)]
        bsm = [P.psum("psm%d" % i) for i in range(2)]
        ws = WStream(P)
        bk = [0]
        NG = TF // 256
        h2g = [Dep("h2g%d" % i) for i in range(NG)]
        mTh = [Dep("mTh%d" % i) for i in range(NH)]

        def norm_from_h2(dst_fn, gname, G=256):
            for gi in range(TF // G):
                gs = slice(gi * G, (gi + 1) * G)
                P.op("act", lambda e, gs=gs: e.activation(sq.full, h2.full[:, :, gs], AF.Square), reads=[h2g[gi]], writes=[sq.d])
                ps = bsm[gi % 2]
                for kc in range(16):
                    P.op("pe", lambda e, ps=ps, kc=kc: e.matmul(ps.full[:, 0:G], r.ones, sq.full[:, kc, :], start=(kc == 0), stop=(kc == 15)),
                         reads=[sq.d, r.cb.d], writes=[ps.d])
                P.op("act", lambda e, ps=ps: e.activation(rs.full, ps.full[:, 0:G], AF.Sqrt, bias=RMS_EPS, scale=1.0 / D),
                     reads=[ps.d], writes=[rs.d])
                P.op("dve", lambda e: e.reciprocal(rinv.full, rs.full), reads=[rs.d], writes=[rinv.d])
                for kc in range(16):
                    dst_fn(kc, gs, gi)

        for tg in range(NTG):
            c0 = b * S + tg * TF
            for gi in range(NG):
                P.dma("sp", h2.full[:, :, gi * 256:(gi + 1) * 256],
                      dr["h"][:, c0 + gi * 256:c0 + (gi + 1) * 256].rearrange("(kc p) t -> p kc t", p=128), writes=[h2g[gi]])

            def to_m(kc, gs, gi):
                P.op("dve", lambda e: e.scalar_tensor_tensor(mT.full[:, kc, gs], h2.full[:, kc, gs], vcol(r, "g_mlp", kc), rinv.full,
                                                             ALU.mult, ALU.mult), reads=[h2g[gi], rinv.d, r.vecs.d], writes=[mTh[gi // 2]])

            norm_from_h2(to_m, "g_mlp")
            for g in range(4):
                hid = hids[g % 2]
                for blk in range(16):
                    wb = ws.load(dr["w_ff1"], 0, (g * 16 + blk) * 128, 128, 16)
                    for hf in range(NH):
                        ps = banks[bk[0] % 6]
                        bk[0] += 1
                        hs = slice(hf * 512, (hf + 1) * 512)
                        for kc in range(16):
                            P.op("pe", lambda e, ps=ps, wb=wb, kc=kc, hs=hs: e.matmul(ps.full, wb.full[:, kc, :], mT.full[:, kc, hs],
                                                                                      start=(kc == 0), stop=(kc == 15)),
                                 reads=[wb.d, mTh[hf]], writes=[ps.d])
                        rl = rls[bk[0] % 2]
                        P.op("act", lambda e, ps=ps, rl=rl: e.activation(rl.full, ps.full, AF.Relu), reads=[ps.d], writes=[rl.d])
                        P.op("dve", lambda e, rl=rl, hid=hid, blk=blk, hs=hs: e.tensor_tensor(hid.full[:, blk, hs], rl.full, rl.full, ALU.mult),
                             reads=[rl.d], writes=[hid.d])
                for cb in range(16):
                    wb = ws.load(dr["w_ff2"], g * 16, cb * 128, 128, 16)
                    for hf in range(NH):
                        ps = banks[bk[0] % 6]
                        bk[0] += 1
                        hs = slice(hf * 512, (hf + 1) * 512)
                        for kc in range(16):
                            P.op("pe", lambda e, ps=ps, wb=wb, kc=kc, hs=hs, hid=hid: e.matmul(ps.full, wb.full[:, kc, :], hid.full[:, kc, hs],
                                                                                               start=(kc == 0), stop=(kc == 15)),
                                 reads=[wb.d, hid.d], writes=[ps.d])
                        P.op("dve", lambda e, ps=ps, cb=cb, hs=hs: e.tensor_tensor(h2.full[:, cb, hs], ps.full, h2.full[:, cb, hs], ALU.add),
                             reads=[ps.d, h2g[2 * hf], h2g[2 * hf + 1]], writes=[h2g[2 * hf], h2g[2 * hf + 1]])

            def to_out(kc, gs, gi):
                o = ost[kc % 2]
                P.op("dve", lambda e: e.scalar_tensor_tensor(o.full[:, gs], h2.full[:, kc, gs], vcol(r, "g_fin", kc), rinv.full,
                                                             ALU.mult, ALU.mult), reads=[h2g[gi], rinv.d, r.vecs.d], writes=[o.d])
                P.dma("sp", dr["outT"][kc * 128:(kc + 1) * 128, c0 + gs.start:c0 + gs.stop], o.full[:, gs], reads=[o.d])

            norm_from_h2(to_out, "g_fin")
        P.emit()
    nc.all_engine_barrier()


def phase_C(nc, cfg, dr, b):
    S = cfg.S
    c0 = b * S
    NQB = S // 128
    with ExitStack() as st:
        P = Prog(nc, st)
        r = load_consts(P, dr, ["cb"])
        bc = P.sb("bc", [128, 400], F32)
        P.dma("sp", bc.full, dr["bc"], writes=[bc.d])
        biasT = P.sb("biasT", [128, 2 * 16 * 128], F32)
        P.dma("sp", biasT.full, dr["biasT"], writes=[biasT.d])
        sm = P.sb("sm", [128, 16], F32)
        lt = P.sb("lt", [128, 128], F32)
        sgb = P.sb("sgb", [128, 128], F32)
        P.op("dve", lambda e: e.tensor_tensor(lt.full[:, 0:64], bc.full[:, 16:80], bc.full[:, 80:144], ALU.mult), reads=[bc.d], writes=[lt.d])
        P.op("dve", lambda e: e.tensor_tensor(lt.full[:, 64:128], bc.full[:, 144:208], bc.full[:, 208:272], ALU.mult), reads=[bc.d], writes=[lt.d])
        P.op("dve", lambda e: e.reduce_sum(sm.full[:, 0:1], lt.full[:, 0:64], AX.X), reads=[lt.d], writes=[sm.d])
        P.op("dve", lambda e: e.reduce_sum(sm.full[:, 1:2], lt.full[:, 64:128], AX.X), reads=[lt.d, sm.d], writes=[sm.d])
        P.op("act", lambda e: e.activation(sm.full[:, 2:4], sm.full[:, 0:2], AF.Exp), reads=[sm.d], writes=[sm.d])
        P.op("dve", lambda e: e.tensor_tensor(sm.full[:, 4:5], sm.full[:, 3:4], sm.full[:, 2:3], ALU.subtract), reads=[sm.d], writes=[sm.d])
        P.op("dve", lambda e: e.tensor_scalar(sm.full[:, 5:6], sm.full[:, 4:5], -LAMBDA_INIT, None, ALU.add), reads=[sm.d], writes=[sm.d])
        P.op("dve", lambda e: e.tensor_scalar(sgb.full, bc.full[:, 272:400], 1.0 - LAMBDA_INIT, None, ALU.mult), reads=[bc.d], writes=[sgb.d])
        neglam = sm.full[:, 5:6]
        qTs = [P.sb("qT%d" % i, [128, S], BF16) for i in range(2)]
        kTs = [P.sb("kT%d" % i, [128, S], BF16) for i in range(2)]
        vTs = [P.sb("vT%d" % i, [128, S], BF16) for i in range(2)]
        Vas = [P.sb("Va%d" % i, [128, NQB, 132], BF16) for i in range(2)]
        ysts = [P.sb("yst%d" % i, [128, S], BF16) for i in range(2)]
        STs = [P.psum("ST%d" % i) for i in range(3)]
        Os = [P.psum("O%d" % i) for i in range(4)]
        tpb = P.psum("tpb", BF16)
        tp2 = tpb
        PTs = [P.sb("PT%d" % i, [128, 512], BF16) for i in range(5)]
        tmps = [P.sb("tmp%d" % i, [128, 128], F32) for i in range(2)]
        rd = [P.sb("rd%d" % i, [128, 8], F32) for i in range(2)]
        o1s = [P.sb("o1_%d" % i, [128, 128], F32) for i in range(2)]
        oos = [P.sb("oo_%d" % i, [128, 128], F32) for i in range(2)]
        junk = P.sb("junk", [128, 128], F32)
        ons = [P.sb("on_%d" % i, [128, 128], BF16) for i in range(2)]
        rot = [0, 0, 0]
        ooa = [P.sb("ooa%d" % i, [128, NQB, 128], F32) for i in range(2)]
        ssa = [P.sb("ssa%d" % i, [128, 2 * NQB], F32) for i in range(2)]
        items = []

        def head_pre(h):
            qT, kT, vT, Va = qTs[h % 2], kTs[h % 2], vTs[h % 2], Vas[h % 2]
            P.dma("sp", qT.full, dr["qkv"][h * 128:(h + 1) * 128, c0:c0 + S], writes=[qT.d])
            P.dma("sp", kT.full, dr["qkv"][1024 + h * 128:1024 + (h + 1) * 128, c0:c0 + S], writes=[kT.d])
            P.dma("sp", vT.full, dr["qkv"][2048 + h * 128:2048 + (h + 1) * 128, c0:c0 + S], writes=[vT.d])
            P.op("pool", lambda e, Va=Va: e.memset(Va.full, 1.0), writes=[Va.d])
            for t0 in range(0, NQB, 8):
                n = min(8, NQB - t0)
                for i in range(n):
                    tb = t0 + i
                    P.op("pe", lambda e, i=i, tb=tb, vT=vT: e.transpose(tpb.full[:, i * 128:(i + 1) * 128],
                                                                        vT.full[:, tb * 128:(tb + 1) * 128], r.ident),
                         reads=[vT.d, r.cb.d], writes=[tpb.d])
                P.op("dve", lambda e, t0=t0, n=n, Va=Va: e.tensor_copy(
                    Va.full[:, t0:t0 + n, 0:128], tpb.full[:, 0:n * 128].rearrange("p (a b) -> p a b", b=128)),
                    reads=[tpb.d], writes=[Va.d])

        def qb_combine(h, qb, Opair):
            O0, O1 = Opair
            oa, sa = ooa[h % 2], ssa[h % 2]
            rdt, o1 = rd[qb % 2], o1s[qb % 2]
            P.op("dve", lambda e: e.reciprocal(rdt.full[:, 0:1], O0.full[:, 128:129]), reads=[O0.d], writes=[rdt.d])
            P.op("dve", lambda e: e.reciprocal(rdt.full[:, 1:2], O1.full[:, 128:129]), reads=[O1.d, rdt.d], writes=[rdt.d])
            P.op("dve", lambda e: e.tensor_tensor(rdt.full[:, 2:3], rdt.full[:, 1:2], neglam, ALU.mult), reads=[rdt.d, sm.d], writes=[rdt.d])
            P.op("dve", lambda e: e.tensor_scalar(o1.full, O0.full[:, 0:128], rdt.full[:, 0:1], None, ALU.mult),
                 reads=[O0.d, rdt.d], writes=[o1.d])
            P.op("dve", lambda e: e.scalar_tensor_tensor(oa.full[:, qb, :], O1.full[:, 0:128], rdt.full[:, 2:3], o1.full, ALU.mult, ALU.add),
                 reads=[O1.d, rdt.d, o1.d], writes=[oa.d])
            P.op("dve", lambda e: e.tensor_tensor(junk.full, oa.full[:, qb, :], oa.full[:, qb, :], ALU.mult), reads=[oa.d], writes=[junk.d])
            P.op("dve", lambda e: e.reduce_sum(sa.full[:, qb:qb + 1], junk.full, AX.X), reads=[junk.d, sa.d], writes=[sa.d])

        def head_post(h):
            oa, sa, yst = ooa[h % 2], ssa[h % 2], ysts[h % 2]
            P.op("act", lambda e: e.activation(sa.full[:, NQB:2 * NQB], sa.full[:, 0:NQB], AF.Sqrt, bias=SUBLN_EPS, scale=1.0 / 128),
                 reads=[sa.d], writes=[sa.d])
            P.op("dve", lambda e: e.reciprocal(sa.full[:, NQB:2 * NQB], sa.full[:, NQB:2 * NQB]), reads=[sa.d], writes=[sa.d])
            for qb in range(NQB):
                on = ons[qb % 2]
                P.op("dve", lambda e, qb=qb, on=on: e.scalar_tensor_tensor(
                    on.full, oa.full[:, qb, :], sa.full[:, NQB + qb:NQB + qb + 1], sgb.full, ALU.mult, ALU.mult),
                    reads=[oa.d, sa.d, sgb.d], writes=[on.d])
                P.op("pe", lambda e, on=on, qb=qb: e.transpose(tp2.full[:, (qb % 8) * 128:(qb % 8 + 1) * 128], on.full, r.ident),
                     reads=[on.d, r.cb.d], writes=[tp2.d])
                if qb % 8 == 7 or qb == NQB - 1:
                    q0 = (qb // 8) * 8
                    n = qb - q0 + 1
                    P.op("dve", lambda e, q0=q0, n=n: e.tensor_copy(yst.full[:, q0 * 128:(q0 + n) * 128], tp2.full[:, 0:n * 128]),
                         reads=[tp2.d], writes=[yst.d])
            P.dma("sp", dr["yB"][h * 128:(h + 1) * 128, c0:c0 + S], yst.full, reads=[yst.d])

        for h in range(8):
            qT, kT, vT, Va = qTs[h % 2], kTs[h % 2], vTs[h % 2], Vas[h % 2]
            first = True
            for qb in range(NQB):
                Opair = (Os[(qb % 2) * 2], Os[(qb % 2) * 2 + 1])
                for j in range(2):
                    m = 2 * h + j
                    O = Opair[j]
                    for g0 in range(0, qb + 1, 4):
                        kbs = list(range(g0, min(g0 + 4, qb + 1)))
                        STb = STs[rot[0] % 3]
                        rot[0] += 1
                        PT = PTs[rot[1] % 5]
                        rot[1] += 1

                        def s1(STb=STb, kbs=kbs, j=j, qb=qb, kT=kT, qT=qT):
                            for i, kb in enumerate(kbs):
                                P.op("pe", lambda e, i=i, kb=kb: e.matmul(
                                    STb.full[:, i * 128:(i + 1) * 128], kT.full[64 * j:64 * j + 64, kb * 128:(kb + 1) * 128],
                                    qT.full[64 * j:64 * j + 64, qb * 128:(qb + 1) * 128], start=True, stop=True),
                                    reads=[kT.d, qT.d], writes=[STb.d])

                        def s2(STb=STb, PT=PT, kbs=kbs, qb=qb, m=m, g0=g0):
                            nfar = len([kb for kb in kbs if kb <= qb - 2])
                            if nfar:
                                P.op("act", lambda e: e.activation(PT.full[:, 0:nfar * 128], STb.full[:, 0:nfar * 128], AF.Exp,
                                                                   bias=bc.full[:, m:m + 1], scale=0.125),
                                     reads=[bc.d], writes=[PT.d, STb.d])
                            for i, kb in enumerate(kbs):
                                if kb <= qb - 2:
                                    continue
                                ty = 0 if kb == qb else 1
                                tmp = tmps[rot[2] % 2]
                                rot[2] += 1
                                bo = (ty * 16 + m) * 128
                                P.op("dve", lambda e, tmp=tmp, i=i, bo=bo: e.scalar_tensor_tensor(
                                    tmp.full, STb.full[:, i * 128:(i + 1) * 128], 0.125, biasT.full[:, bo:bo + 128], ALU.mult, ALU.add),
                                    reads=[biasT.d], writes=[tmp.d, STb.d])
                                P.op("act", lambda e, tmp=tmp, i=i: e.activation(PT.full[:, i * 128:(i + 1) * 128], tmp.full, AF.Exp),
                                     reads=[tmp.d], writes=[PT.d])
                                if kb == qb:
                                    P.op("pool", lambda e, i=i: e.memset(PT.full[64:128, i * 128:i * 128 + 64], 0.0),
                                         reads=[PT.d], writes=[PT.d])

                        def s3(PT=PT, kbs=kbs, O=O, qb=qb, Va=Va):
                            for i, kb in enumerate(kbs):
                                P.op("pe", lambda e, i=i, kb=kb: e.matmul(
                                    O.full[:, 0:129], PT.full[:, i * 128:(i + 1) * 128], Va.full[:, kb, 0:129],
                                    start=(kb == 0), stop=(kb == qb)), reads=[PT.d, Va.d], writes=[O.d])

                        pre = (lambda h=h: head_pre(h)) if first else None
                        first = False
                        last_of_qb = (j == 1 and kbs[-1] == qb)
                        post = []
                        if last_of_qb:
                            post.append(lambda h=h, qb=qb, Opair=Opair: qb_combine(h, qb, Opair))
                            if qb == NQB - 1:
                                post.append(lambda h=h: head_post(h))
                        items.append((pre, s1, s2, s3, post))
        n = len(items)
        SK = 2
        for i in range(n + SK):
            if i < n:
                pre, s1, s2, s3, post = items[i]
                if pre:
                    pre()
                s1()
                s2()
            if i >= SK:
                pre, s1, s2, s3, post = items[i - SK]
                s3()
                for f in post:
                    f()
        P.emit()
    nc.all_engine_barrier()


def phase_B(nc, cfg, dr, b):
    S = cfg.S
    c0 = b * S
    NSEG = S // 512
    with ExitStack() as st:
        P = Prog(nc, st)
        r = load_consts(P, dr, ["vecs", "cb", "cf"])
        Yseg = P.sb("Yseg", [128, 8, 512], F32)

        class _V:
            pass
        wl_f = _V()
        wl_f.full = Yseg.full.rearrange("p (a b) t -> p a (b t)", b=2)
        wl_f.d = Yseg.d
        wl = P.sb("wl", [128, 4, 1024], BF16)
        P.dma("sp", wl_f.full[0:96, 0, :], dr["w_up"], writes=[wl_f.d])
        P.dma("sp", wl_f.full[0:96, 1, :], dr["a_up"], writes=[wl_f.d])
        P.dma("sp", wl_f.full[:, 2:4, :], dr["g_up"].rearrange("(a p) n -> p a n", p=128), writes=[wl_f.d])
        P.op("dve", lambda e: e.tensor_copy(wl.full[0:96, 0:2, :], wl_f.full[0:96, 0:2, :]), reads=[wl_f.d], writes=[wl.d])
        P.op("dve", lambda e: e.tensor_copy(wl.full[:, 2:4, :], wl_f.full[:, 2:4, :]), reads=[wl_f.d, wl.d], writes=[wl.d])
        Hf = P.sb("Hf", [128, 512], F32)
        Hbs = [P.sb("Hb%d" % i, [128, 512], BF16) for i in range(2)]
        P.op("dve", lambda e: e.memset(Hf.full, 0.0), writes=[Hf.d])
        P.op("dve", lambda e: e.memset(Hbs[0].full, 0.0), writes=[Hbs[0].d])
        pA = [P.psum("pA%d" % i) for i in range(2)]
        pX = [P.psum("pX%d" % i) for i in range(2)]
        tpb = P.psum("tpb", BF16)
        pS = [P.psum("pS%d" % i) for i in range(3)]
        rot = {"pA": 0, "pX": 0, "t": 0}

        def f32t(name, n=1):
            return [P.sb("%s%d" % (name, i), [128, 512], F32) for i in range(n)]

        def bf16t(name, n=1):
            return [P.sb("%s%d" % (name, i), [128, 512], BF16) for i in range(n)]

        zt = [P.sb("zt%d" % i, [128, 513], F32) for i in range(2)]
        dtmp = f32t("dtmp")[0]
        twd = P.sb("twd", [128, 512], BF16)
        lad = P.sb("lad", [128, 512], BF16)
        sgd = P.sb("sgd", [128, 2, 512], BF16)
        lin = f32t("lin")[0]
        rl, kl, vl, sig, aa, kkr, sqk, rn, kk, kp, bb, cw, cwx, E1 = [f32t(n)[0] for n in
            ["rl", "kl", "vl", "sig", "aa", "kkr", "sqk", "rn", "kk", "kp", "bb", "cw", "cwx", "E1"]]
        t1, E0, Ei, rk = rn, cwx, cw, sqk
        bTs, kTts, vb = bf16t("bT", 2), bf16t("kTt", 2), bf16t("vb")[0]
        aT = bf16t("aT", 8)
        rT = bf16t("rT", 8)
        bktm = [P.sb("bktm%d" % i, [128, 1024], BF16) for i in range(8)]
        vtm = bf16t("vtm", 8)
        AakT = bf16t("AakT", 8)
        ArbT = bf16t("ArbT", 8)
        ArkT = bf16t("ArkT", 8)
        TT = bf16t("TT", 8)
        gT = bf16t("gT", 8)
        bon = f32t("bon", 8)
        Wc = P.sb("Wc", [128, 8, 8], F32)
        ptmp = bf16t("ptmp", 6)
        Zb = bf16t("Zb")[0]
        Ub = bf16t("Ub")[0]
        tmpH = f32t("tmpH")[0]
        yn = P.sb("yn", [128, 8, 512], BF16)
        gst = P.sb("gst", [128, 256], F32)
        t3 = f32t("t3")[0]
        yo = bf16t("yo", 2)

        def blocks(fn):
            for c in range(8):
                for j in range(2):
                    fn(j, c, slice(64 * j, 64 * j + 64), slice(64 * c, 64 * c + 64))

        def prod(L, R, dst, mask=None, addend=None, eng="dve"):
            ps = pX[rot["pX"] % 2]
            rot["pX"] += 1
            blocks(lambda j, c, pj, cc: P.op("pe", lambda e: e.matmul(ps.full[pj, cc], L.full[pj, cc], R.full[pj, cc], start=True, stop=True),
                                             reads=[L.d, R.d], writes=[ps.d]))
            if mask is not None:
                P.op("dve", lambda e: e.tensor_tensor(dst.full, ps.full, mask, ALU.mult), reads=[ps.d, r.cb.d], writes=[dst.d])
            elif addend is not None:
                P.op("dve", lambda e: e.tensor_tensor(dst.full, ps.full, addend.full, ALU.add), reads=[ps.d, addend.d], writes=[dst.d])
            else:
                P.op("act", lambda e: e.activation(dst.full, ps.full, AF.Copy), reads=[ps.d], writes=[dst.d])

        def load_shift(row0, nrows, sg, mucol, dst_fn):
            z = zt[rot["t"] % 2]
            rot["t"] += 1
            t0 = c0 + sg * 512
            if sg == 0:
                P.op("pool", lambda e: e.memset(z.full[:, 0:1], 0.0), writes=[z.d])
                P.dma("sp", z.full[0:nrows, 1:513], dr["zA"][row0:row0 + nrows, t0:t0 + 512], writes=[z.d])
            else:
                P.dma("sp", z.full[0:nrows, 0:513], dr["zA"][row0:row0 + nrows, t0 - 1:t0 + 512], writes=[z.d])
            P.op("dve", lambda e: e.tensor_tensor(dtmp.full[0:nrows, :], z.full[0:nrows, 0:512], z.full[0:nrows, 1:513], ALU.subtract),
                 reads=[z.d], writes=[dtmp.d])
            dst_fn(z)

        def lerp_to(dst, nrows, mucol):
            def fn(z):
                P.op("dve", lambda e: e.scalar_tensor_tensor(dst.full[0:nrows, :], dtmp.full[0:nrows, :], mucol[0:nrows, :],
                                                             z.full[0:nrows, 1:513], ALU.mult, ALU.add),
                     reads=[dtmp.d, z.d, r.vecs.d], writes=[dst.d])
            return fn

        for sg in range(NSEG):
            t0 = c0 + sg * 512
            load_shift(3072, 96, sg, None, lerp_to(lin, 96, vcol(r, "mu_wd")))
            P.op("act", lambda e: e.activation(twd.full[0:96, :], lin.full[0:96, :], AF.Tanh), reads=[lin.d], writes=[twd.d])
            load_shift(3168, 96, sg, None, lerp_to(lin, 96, vcol(r, "mu_ad")))
            P.op("act", lambda e: e.activation(lad.full[0:96, :], lin.full[0:96, :], AF.Copy), reads=[lin.d], writes=[lad.d])
            for a_ in range(2):
                load_shift(3264 + 128 * a_, 128, sg, None, lerp_to(lin, 128, vcol(r, "mu_gd", a_)))
                P.op("act", lambda e, a_=a_: e.activation(sgd.full[:, a_, :], lin.full, AF.Sigmoid), reads=[lin.d], writes=[sgd.d])
            def make_prep(hp, sg=sg):
                cs = slice(hp * 128, hp * 128 + 128)
                bT, kTt = bTs[hp % 2], kTts[hp % 2]
                steps = []

                def st0():
                    ps = None
                    load_shift(hp * 128, 128, sg, None, lerp_to(rl, 128, vcol(r, "mu_r", hp)))
                    load_shift(1024 + hp * 128, 128, sg, None, lerp_to(kl, 128, vcol(r, "mu_k", hp)))
                    load_shift(2048 + hp * 128, 128, sg, None, lerp_to(vl, 128, vcol(r, "mu_v", hp)))
                steps.append(st0)

                def st1():
                    ps = None
                    ps = pA[rot["pA"] % 2]; rot["pA"] += 1
                    P.op("pe", lambda e, ps=ps, cs=cs: e.matmul(ps.full, wl.full[0:96, 0, cs], twd.full[0:96, :], start=True, stop=True),
                         reads=[wl.d, twd.d], writes=[ps.d])
                    P.op("act", lambda e, ps=ps, hp=hp: e.activation(sig.full, ps.full, AF.Sigmoid, bias=vcol(r, "w0", hp)),
                         reads=[ps.d, r.vecs.d], writes=[sig.d])
                    ps = pA[rot["pA"] % 2]; rot["pA"] += 1
                    P.op("pe", lambda e, ps=ps, cs=cs: e.matmul(ps.full, wl.full[0:96, 1, cs], lad.full[0:96, :], start=True, stop=True),
                         reads=[wl.d, lad.d], writes=[ps.d])
                    P.op("act", lambda e, ps=ps, hp=hp: e.activation(aa.full, ps.full, AF.Sigmoid, bias=vcol(r, "a0", hp)),
                         reads=[ps.d, r.vecs.d], writes=[aa.d])
                    ps = pA[rot["pA"] % 2]; rot["pA"] += 1
                    for a_ in range(2):
                        P.op("pe", lambda e, ps=ps, cs=cs, a_=a_: e.matmul(ps.full, wl.full[:, 2 + a_, cs], sgd.full[:, a_, :],
                                                                          start=(a_ == 0), stop=(a_ == 1)),
                             reads=[wl.d, sgd.d], writes=[ps.d])
                    P.op("act", lambda e, ps=ps, hp=hp: e.activation(gT[hp].full, ps.full, AF.Copy), reads=[ps.d], writes=[gT[hp].d])
                steps.append(st1)

                def st2():
                    ps = None
                    P.op("dve", lambda e, hp=hp: e.tensor_scalar(kkr.full, kl.full, vcol(r, "k_k", hp), None, ALU.mult),
                         reads=[kl.d, r.vecs.d], writes=[kkr.d])
                    P.op("dve", lambda e: e.tensor_tensor(sqk.full, kkr.full, kkr.full, ALU.mult), reads=[kkr.d], writes=[sqk.d])
                    ps = pA[rot["pA"] % 2]; rot["pA"] += 1
                    P.op("pe", lambda e, ps=ps: e.matmul(ps.full, r.blockones, sqk.full, start=True, stop=True),
                         reads=[sqk.d, r.cf.d], writes=[ps.d])
                    P.op("dve", lambda e, ps=ps: e.tensor_scalar(rn.full, ps.full, 1e-24, None, ALU.max), reads=[ps.d], writes=[rn.d])
                    P.op("act", lambda e: e.activation(rn.full, rn.full, AF.Sqrt), reads=[rn.d], writes=[rn.d])
                    P.op("dve", lambda e: e.reciprocal(rn.full, rn.full), reads=[rn.d], writes=[rn.d])
                    P.op("dve", lambda e: e.tensor_tensor(kk.full, kkr.full, rn.full, ALU.mult), reads=[kkr.d, rn.d], writes=[kk.d])
                steps.append(st2)

                def st3():
                    ps = None
                    P.op("dve", lambda e, hp=hp: e.tensor_scalar(t1.full, aa.full, -1.0, vcol(r, "k_a", hp), ALU.add, ALU.mult),
                         reads=[aa.d, r.vecs.d], writes=[t1.d])
                    P.op("dve", lambda e: e.scalar_tensor_tensor(kp.full, t1.full, 1.0, kl.full, ALU.add, ALU.mult),
                         reads=[t1.d, kl.d], writes=[kp.d])
                    P.op("dve", lambda e: e.tensor_tensor(bb.full, kk.full, aa.full, ALU.mult), reads=[kk.d, aa.d], writes=[bb.d])
                steps.append(st3)

                def st4():
                    ps = None
                    P.op("dve", lambda e: e.tensor_tensor_scan(cw.full, r.scanmask, sig.full, 0.0, ALU.mult, ALU.add),
                         reads=[sig.d, r.cf.d], writes=[cw.d])
                    P.op("dve", lambda e: e.tensor_tensor(cwx.full, cw.full, sig.full, ALU.subtract), reads=[cw.d, sig.d], writes=[cwx.d])
                    P.op("act", lambda e: e.activation(E1.full, cw.full, AF.Exp, scale=-C0), reads=[cw.d], writes=[E1.d])
                    P.op("act", lambda e: e.activation(E0.full, cwx.full, AF.Exp, scale=-C0), reads=[cwx.d], writes=[E0.d])
                    P.op("act", lambda e: e.activation(Ei.full, cw.full, AF.Exp, scale=C0), reads=[cw.d], writes=[Ei.d])
                    P.op("dve", lambda e, hp=hp: e.scalar_tensor_tensor(aT[hp].full, kk.full, -1.0, E0.full, ALU.mult, ALU.mult),
                         reads=[kk.d, E0.d], writes=[aT[hp].d])
                    P.op("dve", lambda e, hp=hp: e.tensor_tensor(rT[hp].full, rl.full, E1.full, ALU.mult), reads=[rl.d, E1.d], writes=[rT[hp].d])
                    P.op("dve", lambda e: e.tensor_tensor(bT.full, bb.full, Ei.full, ALU.mult), reads=[bb.d, Ei.d], writes=[bT.d])
                    P.op("dve", lambda e: e.tensor_tensor(kTt.full, kp.full, Ei.full, ALU.mult), reads=[kp.d, Ei.d], writes=[kTt.d])
                    P.op("dve", lambda e, hp=hp: e.tensor_copy(Wc.full[:, hp, :], E1.full.rearrange("p (c t) -> p c t", t=64)[:, :, 63]),
                         reads=[E1.d], writes=[Wc.d])
                    P.op("act", lambda e: e.activation(vb.full, vl.full, AF.Copy), reads=[vl.d], writes=[vb.d])
                steps.append(st4)

                def st5():
                    ps = None
                    P.op("dve", lambda e, hp=hp: e.scalar_tensor_tensor(rk.full, rl.full, vcol(r, "r_k", hp), kp.full, ALU.mult, ALU.mult),
                         reads=[rl.d, kp.d, r.vecs.d], writes=[rk.d])
                    ps = pA[rot["pA"] % 2]; rot["pA"] += 1
                    P.op("pe", lambda e, ps=ps: e.matmul(ps.full, r.blockones, rk.full, start=True, stop=True),
                         reads=[rk.d, r.cf.d], writes=[ps.d])
                    P.op("dve", lambda e, ps=ps, hp=hp: e.tensor_tensor(bon[hp].full, ps.full, vl.full, ALU.mult),
                         reads=[ps.d, vl.d], writes=[bon[hp].d])
                steps.append(st5)

                def st6():
                    ps = None
                    for X, off in ((bT, 0), (kTt, 512)):
                        blocks(lambda j, c, pj, cc, X=X, off=off: P.op("pe", lambda e: e.transpose(
                            tpb.full[pj, off + 64 * c:off + 64 * c + 64], X.full[pj, cc], r.ident[pj, pj]),
                            reads=[X.d, r.cb.d], writes=[tpb.d]))
                    P.op("dve", lambda e, hp=hp: e.tensor_copy(bktm[hp].full, tpb.full), reads=[tpb.d], writes=[bktm[hp].d])
                    blocks(lambda j, c, pj, cc: P.op("pe", lambda e: e.transpose(tpb.full[pj, cc], vb.full[pj, cc], r.ident[pj, pj]),
                                                     reads=[vb.d, r.cb.d], writes=[tpb.d]))
                    P.op("dve", lambda e, hp=hp: e.tensor_copy(vtm[hp].full, tpb.full[:, 0:512]), reads=[tpb.d], writes=[vtm[hp].d])
                steps.append(st6)
                return steps

            def make_inv(hp):
                bT, kTt = bTs[hp % 2], kTts[hp % 2]
                steps = []
                P0, P0T = ptmp[0], ptmp[1]
                steps.append(lambda: prod(aT[hp], bT, P0, mask=r.ML))
                steps.append(lambda: prod(bT, aT[hp], P0T, mask=r.MU))
                steps.append(lambda: prod(kTt, aT[hp], AakT[hp], mask=r.MU))
                steps.append(lambda: prod(bT, rT[hp], ArbT[hp], mask=r.MUI))
                steps.append(lambda: prod(kTt, rT[hp], ArkT[hp], mask=r.MUI))
                steps.append(lambda: P.op("dve", lambda e: e.tensor_tensor(TT[hp].full, P0T.full, r.IB, ALU.add),
                                          reads=[P0T.d, r.cb.d], writes=[TT[hp].d]))
                Pc, PcT = P0, P0T
                free = [ptmp[2], ptmp[3], ptmp[4], ptmp[5]]
                for lvl in range(1, 6):
                    Pn = free.pop(0)
                    steps.append(lambda PcT=PcT, Pc=Pc, Pn=Pn: prod(PcT, Pc, Pn))
                    PnT = None
                    if lvl < 5:
                        PnT = free.pop(0)
                        steps.append(lambda PcT=PcT, Pc=Pc, PnT=PnT: prod(Pc, PcT, PnT))
                    steps.append(lambda Pn=Pn: prod(Pn, TT[hp], TT[hp], addend=TT[hp]))
                    free.append(Pc)
                    free.append(PcT)
                    Pc, PcT = Pn, PnT
                return steps

            def weave(a, b):
                na, nb = len(a), len(b)
                ia = ib = 0
                while ia < na or ib < nb:
                    if ib < nb and (ia >= na or ib * max(na, 1) <= ia * nb):
                        b[ib]()
                        ib += 1
                    else:
                        a[ia]()
                        ia += 1

            pend = []
            for hp in range(8):
                weave(make_prep(hp), pend)
                pend = make_inv(hp)
            weave([], pend)
            for c in range(8):
                gc = sg * 8 + c
                Hc, Hn = Hbs[gc % 2], Hbs[(gc + 1) % 2]
                cc = slice(64 * c, 64 * c + 64)

                def mmseq(ps, terms, c=c, cc=cc):
                    for hp in range(8):
                        hh = slice(64 * hp, 64 * hp + 64)
                        for ti, (lf, rf) in enumerate(terms):
                            for j in range(2):
                                pj = slice(64 * j, 64 * j + 64)
                                L, lc = lf(hp)
                                R, rc = rf(hp)
                                rc = hh if rc is None else rc
                                P.op("pe", lambda e, L=L, lc=lc, R=R, rc=rc, pj=pj, hh=hh, ti=ti, n=len(terms): e.matmul(
                                    ps.full[pj, hh], L.full[pj, lc], R.full[pj, rc], start=(ti == 0), stop=(ti == n - 1)),
                                    reads=[L.d, R.d], writes=[ps.d])

                k2 = slice(512 + 64 * c, 512 + 64 * c + 64)
                mmseq(pS[0], [(lambda hp: (aT[hp], cc), lambda hp, Hc=Hc: (Hc, None)),
                              (lambda hp: (AakT[hp], cc), lambda hp: (vtm[hp], cc))])
                P.op("act", lambda e: e.activation(Zb.full, pS[0].full, AF.Copy), reads=[pS[0].d], writes=[Zb.d])
                mmseq(pS[1], [(lambda hp: (TT[hp], cc), lambda hp: (Zb, None))])
                P.op("dve", lambda e: e.tensor_copy(Ub.full, pS[1].full), reads=[pS[1].d], writes=[Ub.d])
                mmseq(pS[2], [(lambda hp: (rT[hp], cc), lambda hp, Hc=Hc: (Hc, None)),
                              (lambda hp: (ArbT[hp], cc), lambda hp: (Ub, None)),
                              (lambda hp: (ArkT[hp], cc), lambda hp: (vtm[hp], cc))])
                P.op("act", lambda e, c=c: e.activation(Yseg.full[:, c, :], pS[2].full, AF.Copy), reads=[pS[2].d], writes=[Yseg.d])
                mmseq(pS[0], [(lambda hp: (bktm[hp], cc), lambda hp: (Ub, None)),
                              (lambda hp, k2=k2: (bktm[hp], k2), lambda hp: (vtm[hp], cc))])
                P.op("dve", lambda e: e.tensor_tensor(tmpH.full, pS[0].full, Hf.full, ALU.add), reads=[pS[0].d, Hf.d], writes=[tmpH.d])
                P.op("dve", lambda e, c=c: e.tensor_tensor(
                    Hf.full.rearrange("p (h v) -> p h v", v=64), tmpH.full.rearrange("p (h v) -> p h v", v=64),
                    Wc.full[:, :, c:c + 1].broadcast_to([128, 8, 64]), ALU.mult), reads=[tmpH.d, Wc.d], writes=[Hf.d])
                P.op("act", lambda e, Hn=Hn: e.activation(Hn.full, Hf.full, AF.Copy), reads=[Hf.d], writes=[Hn.d])
            Yv = Yseg.full.rearrange("p c (h v) -> p (c h) v", v=64)
            Nv = yn.full.rearrange("p c (h v) -> p (c h) v", v=64)
            P.op("dve", lambda e: e.reduce_sum(gst.full[:, 0:64], Yv, AX.X), reads=[Yseg.d], writes=[gst.d])
            P.op("dve", lambda e: e.tensor_scalar(gst.full[:, 64:128], gst.full[:, 0:64], 1.0 / 64, None, ALU.mult), reads=[gst.d], writes=[gst.d])
            P.op("dve", lambda e: e.tensor_tensor(Yv, Yv, gst.full[:, 64:128].unsqueeze(2).broadcast_to([128, 64, 64]), ALU.subtract),
                 reads=[Yseg.d, gst.d], writes=[Yseg.d])
            P.op("dve", lambda e: e.tensor_tensor(Nv, Yv, Yv, ALU.mult), reads=[Yseg.d], writes=[yn.d])
            P.op("dve", lambda e: e.reduce_sum(gst.full[:, 128:192], Nv, AX.X), reads=[yn.d, gst.d], writes=[gst.d])
            P.op("act", lambda e: e.activation(gst.full[:, 192:256], gst.full[:, 128:192], AF.Sqrt, bias=GN_EPS, scale=1.0 / 64),
                 reads=[gst.d], writes=[gst.d])
            P.op("dve", lambda e: e.reciprocal(gst.full[:, 192:256], gst.full[:, 192:256]), reads=[gst.d], writes=[gst.d])
            P.op("dve", lambda e: e.tensor_tensor(Nv, Yv, gst.full[:, 192:256].unsqueeze(2).broadcast_to([128, 64, 64]), ALU.mult),
                 reads=[Yseg.d, gst.d], writes=[yn.d])
            for hp in range(8):
                blocks(lambda j, c, pj, cc, hp=hp: P.op("pe", lambda e: e.transpose(
                    tpb.full[pj, cc], yn.full[pj, c, 64 * hp:64 * hp + 64], r.ident[pj, pj]), reads=[yn.d, r.cb.d], writes=[tpb.d]))
                y_ = yo[hp % 2]
                P.op("dve", lambda e, hp=hp: e.tensor_scalar(t3.full, tpb.full[:, 0:512], vcol(r, "lnx_g", hp), vcol(r, "lnx_b", hp),
                                                            ALU.mult, ALU.add), reads=[tpb.d, r.vecs.d], writes=[t3.d])
                P.op("dve", lambda e, hp=hp: e.tensor_tensor(t3.full, t3.full, bon[hp].full, ALU.add), reads=[t3.d, bon[hp].d], writes=[t3.d])
                P.op("dve", lambda e, hp=hp, y_=y_: e.tensor_tensor(y_.full, t3.full, gT[hp].full, ALU.mult), reads=[t3.d, gT[hp].d], writes=[y_.d])
                P.dma("sp", dr["yA"][hp * 128:(hp + 1) * 128, t0:t0 + 512], y_.full, reads=[y_.d])
        P.emit()
    nc.all_engine_barrier()
```

```python
import numpy as np
import ml_dtypes
import concourse.bass as bass
import concourse.mybir as mybir
from concourse.bass_utils import run_bass_kernel_spmd

F32 = mybir.dt.float32
BF16 = mybir.dt.bfloat16
ALU = mybir.AluOpType
AF = mybir.ActivationFunctionType
AX = mybir.AxisListType


class Dep:
    __slots__ = ("name", "w", "rs")

    def __init__(self, name):
        self.name = name
        self.w = None
        self.rs = []


class Op:
    __slots__ = ("eng", "fn", "deps", "is_dma", "sem", "semval", "need_sig", "sigidx", "prev_same_sem")

    def __init__(self, eng, fn, is_dma):
        self.eng = eng
        self.fn = fn
        self.deps = []
        self.is_dma = is_dma
        self.sem = None
        self.semval = 0
        self.need_sig = False
        self.sigidx = 0
        self.prev_same_sem = None


class SB:
    def __init__(self, handle, dep):
        self.h = handle
        self.full = handle.ap()
        self.d = dep

    def __getitem__(self, k):
        return self.full[k]


ENGS = ("pe", "act", "dve", "pool", "sp")
N_DMA_SEMS = 40


class Prog:
    G = None

    @staticmethod
    def init_global(nc, st):
        g = {}
        g["sems"] = {e: st.enter_context(nc.semaphore("gs_" + e)) for e in ENGS}
        g["dsems"] = [st.enter_context(nc.semaphore("gd_%d" % i)) for i in range(N_DMA_SEMS)]
        g["cnt"] = {e: 0 for e in ENGS}
        g["n_dma"] = 0
        g["dma_last"] = [None] * N_DMA_SEMS
        g["dma_cnt"] = [0] * N_DMA_SEMS
        Prog.G = g

    def __init__(self, nc, st):
        self.nc = nc
        self.st = st
        self.ops = []
        self.deps = {}

    def dep(self, name):
        d = self.deps.get(name)
        if d is None:
            d = Dep(name)
            self.deps[name] = d
        return d

    _uid = [0]

    def sb(self, name, shape, dtype):
        Prog._uid[0] += 1
        h = self.st.enter_context(self.nc.sbuf_tensor("s%d_%s" % (Prog._uid[0], name), list(shape), dtype))
        return SB(h, Dep(name))

    def psum(self, name, dtype=F32):
        n = 512 if dtype == F32 else 1024
        Prog._uid[0] += 1
        h = self.st.enter_context(self.nc.psum_tensor("p%d_%s" % (Prog._uid[0], name), [128, n], dtype))
        return SB(h, Dep(name))

    def _track(self, op, reads, writes):
        ds = set()
        for b in reads:
            if b.w is not None:
                ds.add(b.w)
        for b in writes:
            if b.w is not None:
                ds.add(b.w)
            for r in b.rs:
                ds.add(r)
        ds.discard(op)
        for d in ds:
            if d.eng == "pe" and op.eng == "pe" and not d.is_dma and not op.is_dma:
                continue
            op.deps.append(d)
            d.need_sig = True
        for b in reads:
            b.rs.append(op)
        for b in writes:
            b.w = op
            b.rs = []

    def op(self, eng, fn, reads=(), writes=()):
        o = Op(eng, fn, False)
        self._track(o, reads, writes)
        self.ops.append(o)
        return o

    def dma(self, queue, out, in_, reads=(), writes=(), **kw):
        o = Op(queue, lambda e: e.dma_start(out=out, in_=in_, **kw), True)
        g = Prog.G
        s = g["n_dma"] % N_DMA_SEMS
        g["n_dma"] += 1
        o.sem = s
        g["dma_cnt"][s] += 16
        o.semval = g["dma_cnt"][s]
        o.prev_same_sem = g["dma_last"][s]
        g["dma_last"][s] = o
        self._track(o, reads, writes)
        self.ops.append(o)
        return o

    def emit(self):
        nc = self.nc
        g = Prog.G
        sems = g["sems"]
        dsems = g["dsems"]
        from contextlib import ExitStack
        with ExitStack() as st:
            cnt = g["cnt"]
            for o in self.ops:
                if not o.is_dma and o.need_sig:
                    cnt[o.eng] += 1
                    o.sigidx = cnt[o.eng]
            per_eng = {e: [] for e in ENGS}
            for o in self.ops:
                per_eng[o.eng].append(o)
            block = st.enter_context(nc.Block())

            def run(engname, eng):
                seen = {}

                def wait(sem, val, key):
                    if seen.get(key, 0) >= val:
                        return
                    seen[key] = val
                    eng.wait_ge(sem, val)

                for o in per_eng[engname]:
                    for d in o.deps:
                        if d.is_dma:
                            wait(dsems[d.sem], d.semval, ("d", d.sem))
                        else:
                            wait(sems[d.eng], d.sigidx, ("e", d.eng))
                    if o.is_dma:
                        p = o.prev_same_sem
                        if p is not None:
                            wait(dsems[p.sem], p.semval, ("d", p.sem))
                        o.fn(eng).then_inc(dsems[o.sem], 16)
                    else:
                        ins = o.fn(eng)
                        if o.need_sig:
                            ins.then_inc(sems[engname], 1)
                if engname == "sp":
                    for s in range(N_DMA_SEMS):
                        if g["dma_cnt"][s]:
                            wait(dsems[s], g["dma_cnt"][s], ("d", s))

            @block.tensor
            def _(eng):
                run("pe", eng)

            @block.scalar
            def _(eng):
                run("act", eng)

            @block.vector
            def _(eng):
                run("dve", eng)

            @block.gpsimd
            def _(eng):
                run("pool", eng)

            @block.sync
            def _(eng):
                run("sp", eng)
        self.ops = []


import math
from contextlib import ExitStack

D = 2048
AW = 1024
DL = 96
GL = 256
A_COLS = 3 * AW + 2 * DL + GL
QKC = 1024
IN_COLS = 10688
DFF = 8192
NCORES = 8
C0 = math.exp(-0.5)
LAMBDA_INIT = 0.8 - 0.6 * math.exp(-0.3 * 0)
GN_EPS = 64e-5
SUBLN_EPS = 1e-5
RMS_EPS = 1e-6

VC = {}
_o = 0
for _n, _w in [("g_mix", 16), ("g_mlp", 16), ("g_fin", 16), ("mu_r", 8), ("mu_k", 8), ("mu_v", 8),
               ("mu_wd", 1), ("mu_ad", 1), ("mu_gd", 2), ("w0", 8), ("a0", 8), ("k_k", 8), ("k_a", 8),
               ("r_k", 8), ("lnx_g", 8), ("lnx_b", 8)]:
    VC[_n] = _o
    _o += _w
NV = _o


class Cfg:
    def __init__(self, S=2048, NB=2, phases="ABCDE", dump=(), inject=()):
        self.S = S
        self.NB = NB
        self.T = S * NB
        self.phases = phases
        self.dump = set(dump)
        self.inject = set(inject)


def _pm(v, n):
    return np.ascontiguousarray(np.asarray(v, np.float32).reshape(n, 128).T)


class Res:
    pass


def load_consts(P, dr, names):
    r = Res()
    if "vecs" in names:
        r.vecs = P.sb("vecs", [128, NV], F32)
        P.dma("sp", r.vecs.full, dr["vecs"], writes=[r.vecs.d])
    if "cb" in names:
        r.cb = P.sb("cb", [128, 4 * 512 + 256], BF16)
        P.dma("sp", r.cb.full, dr["cb"], writes=[r.cb.d])
        r.ML = r.cb.full[:, 0:512]
        r.MU = r.cb.full[:, 512:1024]
        r.MUI = r.cb.full[:, 1024:1536]
        r.IB = r.cb.full[:, 1536:2048]
        r.ident = r.cb.full[:, 2048:2176]
        r.ones = r.cb.full[:, 2176:2304]
    if "cf" in names:
        r.cf = P.sb("cf", [128, 640], F32)
        P.dma("sp", r.cf.full, dr["cf"], writes=[r.cf.d])
        r.blockones = r.cf.full[:, 0:128]
        r.scanmask = r.cf.full[:, 128:640]
    return r


def vcol(r, name, i=0):
    c = VC[name] + i
    return r.vecs.full[:, c:c + 1]


class WStream:
    def __init__(self, P, nbuf=4):
        self.P = P
        self.bf = [P.sb("wbf%d" % i, [128, 16, 128], BF16) for i in range(nbuf)]
        self.i = 0

    def load(self, w_dram, k0, col0, width, nk=16):
        P = self.P
        bf = self.bf[self.i % len(self.bf)]
        self.i += 1
        src = w_dram[k0 * 128:(k0 + nk) * 128, col0:col0 + width].rearrange("(kc p) n -> p kc n", p=128)
        P.dma("pool", bf.full[:, 0:nk, 0:width], src, writes=[bf.d])
        return bf


def linear_fm(P, ws, w_dram, KC, blocks, xT, xdeps, NTG, banks, evac, tgw=512):
    kgs = min(16, KC)
    nkg = KC // kgs
    bk = 0
    for bi, (col0, width, tag) in enumerate(blocks):
        pss = []
        for tg in range(NTG):
            pss.append(banks[bk % len(banks)])
            bk += 1
        for kg in range(nkg):
            wb = ws.load(w_dram, kg * kgs, col0, width, kgs)
            for tg in range(NTG):
                ps = pss[tg]
                for kc in range(kgs):
                    kk = kg * kgs + kc
                    P.op("pe", lambda e, ps=ps, wb=wb, kc=kc, kk=kk, tg=tg, width=width:
                         e.matmul(ps.full[0:width, 0:tgw], wb.full[:, kc, 0:width], xT(kk, tg),
                                  start=(kk == 0), stop=(kk == KC - 1)),
                         reads=[wb.d] + xdeps(kk, tg), writes=[ps.d])
        for tg in range(NTG):
            evac(tag, bi, tg, pss[tg], width)


def rmsnorm_to_bf16(P, r, src_dram, c0, ntok, uT, gname, banks_small, xs_bufs, sq, rs, rinv, eps=RMS_EPS, q="act", utd=None):
    G = 256
    for gi in range(ntok // G):
        xs = xs_bufs[gi % len(xs_bufs)]
        src = src_dram[:, c0 + gi * G:c0 + (gi + 1) * G].rearrange("(kc p) t -> p kc t", p=128)
        P.dma(q, xs.full, src, writes=[xs.d])
        P.op("act", lambda e, xs=xs: e.activation(sq.full, xs.full, AF.Square), reads=[xs.d], writes=[sq.d])
        ps = banks_small[gi % len(banks_small)]
        for kc in range(16):
            P.op("pe", lambda e, ps=ps, kc=kc: e.matmul(ps.full[:, 0:G], r.ones, sq.full[:, kc, :],
                                                        start=(kc == 0), stop=(kc == 15)),
                 reads=[sq.d, r.cb.d], writes=[ps.d])
        P.op("act", lambda e, ps=ps: e.activation(rs.full, ps.full[:, 0:G], AF.Sqrt, bias=eps, scale=1.0 / D),
             reads=[ps.d], writes=[rs.d])
        P.op("dve", lambda e: e.reciprocal(rinv.full, rs.full), reads=[rs.d], writes=[rinv.d])
        for kc in range(16):
            P.op("dve", lambda e, xs=xs, kc=kc, gi=gi: e.scalar_tensor_tensor(
                uT.full[:, kc, gi * G:(gi + 1) * G], xs.full[:, kc, :], vcol(r, gname, kc), rinv.full,
                ALU.mult, ALU.mult),
                reads=[xs.d, rinv.d, r.vecs.d], writes=[uT.d if utd is None else utd[gi // 2]])


def phase_A(nc, cfg, dr, b):
    S = cfg.S
    c0 = b * S
    NTG = S // 512
    with ExitStack() as st:
        P = Prog(nc, st)
        r = load_consts(P, dr, ["vecs", "cb"])
        uT = P.sb("uT", [128, 16, S], BF16)
        xs_bufs = [P.sb("xs%d" % i, [128, 16, 256], F32) for i in range(2)]
        sq = P.sb("sq", [128, 16, 256], BF16)
        rs = P.sb("rs", [128, 256], F32)
        rinv = P.sb("rinv", [128, 256], F32)
        banks = [P.psum("pm%d" % i) for i in range(6)]
        bsm = [P.psum("psm%d" % i) for i in range(2)]
        ws = WStream(P)
        stf = [P.sb("stf%d" % i, [128, S], F32) for i in range(2)]
        stb = [P.sb("stb%d" % i, [128, S], BF16) for i in range(2)]
        utd = [Dep("utd%d" % i) for i in range(NTG)]
        rmsnorm_to_bf16(P, r, dr["xT"], c0, S, uT, "g_mix", bsm, xs_bufs, sq, rs, rinv, utd=utd)

        blocks = []
        for i in range(24):
            blocks.append((i * 128, 128, ("zA", i * 128)))
        blocks.append((3072, 96, ("zA", 3072)))
        blocks.append((3168, 96, ("zA", 3168)))
        blocks.append((3264, 128, ("zA", 3264)))
        blocks.append((3392, 128, ("zA", 3392)))
        for i in range(24):
            blocks.append((A_COLS + i * 128, 128, ("qkv", i * 128)))
        for i in range(32):
            blocks.append((A_COLS + 3072 + i * 128, 128, ("gate", i * 128)))
        cnt = {"f": 0, "b": 0}

        def evac(tag, bi, tg, ps, width):
            kind, row0 = tag
            if kind == "zA":
                sbuf = stf[cnt["f"] % 2]
                P.op("act", lambda e: e.activation(sbuf.full[0:width, tg * 512:(tg + 1) * 512], ps.full[0:width, :], AF.Copy),
                     reads=[ps.d], writes=[sbuf.d])
                if tg == NTG - 1:
                    P.dma("act", dr["zA"][row0:row0 + width, c0:c0 + S], sbuf.full[0:width, :], reads=[sbuf.d])
                    cnt["f"] += 1
            else:
                sbuf = stb[cnt["b"] % 2]
                fn = AF.Copy if kind == "qkv" else AF.Sigmoid
                P.op("act", lambda e: e.activation(sbuf.full[0:width, tg * 512:(tg + 1) * 512], ps.full[0:width, :], fn),
                     reads=[ps.d], writes=[sbuf.d])
                if tg == NTG - 1:
                    dst = dr["qkv"] if kind == "qkv" else dr["gates"]
                    P.dma("act", dst[row0:row0 + width, c0:c0 + S], sbuf.full[0:width, :], reads=[sbuf.d])
                    cnt["b"] += 1

        import os
        if os.environ.get("DBG_A") == "norm":
            blocks = []
        elif os.environ.get("DBG_A"):
            blocks = blocks[:int(os.environ["DBG_A"])]
        linear_fm(P, ws, dr["w_in"], 16, blocks, lambda kk, tg: uT.full[:, kk, tg * 512:(tg + 1) * 512],
                  lambda kk, tg: [utd[tg]], NTG, banks, evac)
        P.emit()
    nc.all_engine_barrier()


SCRATCH = {
    "zA": ([A_COLS, None], F32),
    "qkv": ([3072, None], BF16),
    "gates": ([4096, None], BF16),
    "yA": ([AW, None], BF16),
    "yB": ([AW, None], BF16),
    "h": ([D, None], F32),
}


def build_nc(cfg):
    nc = bass.Bass("TRN2", target_bir_lowering=False)
    T = cfg.T
    dr = {}
    gst = ExitStack()
    Prog.init_global(nc, gst)

    def inp(name, shape, dt=F32):
        dr[name] = nc.dram_tensor(name, list(shape), dt, kind="ExternalInput").ap()

    inp("xT", [D, T])
    if "A" in cfg.phases:
        inp("w_in", [D, IN_COLS])
    if "D" in cfg.phases:
        inp("p_a", [AW, D])
        inp("p_b", [AW, D])
        inp("w_out", [D, D])
    if "E" in cfg.phases:
        inp("w_ff1", [D, DFF])
        inp("w_ff2", [DFF, D])
    if "B" in cfg.phases:
        inp("w_up", [DL, AW])
        inp("a_up", [DL, AW])
        inp("g_up", [GL, AW])
    inp("vecs", [128, NV])
    inp("bc", [128, 16 + 256 + 128])
    inp("biasT", [128, 2 * 16 * 128])
    inp("cb", [128, 4 * 512 + 256], BF16)
    inp("cf", [128, 640])
    for name, (shape, dt) in SCRATCH.items():
        shp = [shape[0], T]
        if name in cfg.inject:
            kind = "ExternalInput"
        elif name in cfg.dump:
            kind = "ExternalOutput"
        else:
            kind = "Internal"
        dr[name] = nc.dram_tensor(name, shp, dt, kind=kind).ap()
    dr["outT"] = nc.dram_tensor("outT", [D, T], F32, kind="ExternalOutput").ap()
    for b in range(cfg.NB):
        if "A" in cfg.phases:
            phase_A(nc, cfg, dr, b)
        if "B" in cfg.phases:
            phase_B(nc, cfg, dr, b)
        if "C" in cfg.phases:
            phase_C(nc, cfg, dr, b)
        if "D" in cfg.phases:
            phase_D(nc, cfg, dr, b)
        if "E" in cfg.phases:
            phase_E(nc, cfg, dr, b)
    return nc


def t5_bucket_np(rel):
    nb = 16
    max_exact = 8
    ret = np.where(rel > 0, nb, 0)
    n = np.abs(rel)
    nf = np.maximum(n, 1).astype(np.float32)
    large = max_exact + (np.log(nf / max_exact) / math.log(128 / max_exact) * (nb - max_exact)).astype(np.int32)
    large = np.minimum(large, nb - 1)
    return ret + np.where(n < max_exact, n, large)


def host_consts(inp):
    f = np.float32
    vecs = np.zeros((128, NV), f)

    def put(name, v, n):
        vecs[:, VC[name]:VC[name] + n] = _pm(v, n)

    put("g_mix", inp["norm_mix_g"][0], 16)
    put("g_mlp", inp["norm_mlp_g"][0], 16)
    put("g_fin", inp["norm_final_g"], 16)
    mu = np.asarray(inp["mu_shift"][0], f)
    put("mu_r", mu[0:1024], 8)
    put("mu_k", mu[1024:2048], 8)
    put("mu_v", mu[2048:3072], 8)
    vecs[0:96, VC["mu_wd"]] = mu[3072:3168]
    vecs[0:96, VC["mu_ad"]] = mu[3168:3264]
    put("mu_gd", mu[3264:3520], 2)
    put("w0", inp["w0"][0], 8)
    put("a0", inp["a0"][0], 8)
    put("k_k", inp["k_k"][0], 8)
    put("k_a", inp["k_a"][0], 8)
    put("r_k", np.asarray(inp["r_k"][0]).reshape(-1), 8)
    put("lnx_g", inp["lnx_g"][0], 8)
    put("lnx_b", inp["lnx_b"][0], 8)

    rb = np.asarray(inp["rel_bias"], f)
    bc = np.zeros((128, 16 + 256 + 128), f)
    bc[:, 0:16] = rb[15][None, :]
    for i, nme in enumerate(["lambda_q1", "lambda_k1", "lambda_q2", "lambda_k2"]):
        bc[:, 16 + 64 * i:16 + 64 * (i + 1)] = np.asarray(inp[nme][0], f)[None, :]
    bc[:, 272:400] = np.asarray(inp["subln_g"][0], f)[None, :]
    kl = np.arange(128)[:, None]
    ql = np.arange(128)[None, :]
    biasT = np.zeros((128, 2, 16, 128), f)
    for ty, off in enumerate([0, -128]):
        bidx = t5_bucket_np(off + kl - ql)
        biasT[:, ty, :, :] = np.transpose(rb[bidx], (0, 2, 1))
    biasT = biasT.reshape(128, -1)
    row = (np.arange(128) % 64)[:, None]
    col = (np.arange(512) % 64)[None, :]
    ML = (col < row).astype(f)
    MU = (row < col).astype(f)
    MUI = (row <= col).astype(f)
    IB = (row == col).astype(f)
    ident = np.eye(128, dtype=f)
    ones = np.ones((128, 128), f)
    cb = np.concatenate([ML, MU, MUI, IB, ident, ones], axis=1).astype(ml_dtypes.bfloat16)
    blockones = np.zeros((128, 128), f)
    blockones[0:64, 0:64] = 1
    blockones[64:, 64:] = 1
    scanmask = np.broadcast_to((np.arange(512) % 64 != 0).astype(f)[None, :], (128, 512))
    cf = np.concatenate([blockones, scanmask], axis=1).astype(f)
    out = {"vecs": vecs, "bc": bc, "biasT": np.ascontiguousarray(biasT), "cb": np.ascontiguousarray(cb),
           "cf": np.ascontiguousarray(cf)}
    for nme in ["w_in", "p_a", "p_b", "w_out", "w_ff1", "w_ff2", "w_up", "a_up", "g_up"]:
        out[nme] = np.ascontiguousarray(np.asarray(inp[nme][0], f))
    return out


def kernel(**inputs):
    cfg = Cfg()
    x = np.asarray(inputs["x"], np.float32)
    B, S, _ = x.shape
    nc = build_nc(cfg)
    shared = host_consts(inputs)
    in_maps = []
    for c in range(NCORES):
        m = dict(shared)
        xs = x[c * cfg.NB:(c + 1) * cfg.NB].reshape(cfg.T, D)
        m["xT"] = np.ascontiguousarray(xs.T)
        in_maps.append(m)
    res = run_bass_kernel_spmd(nc, in_maps, core_ids=list(range(NCORES)))
    out = np.empty((B, S, D), np.float32)
    for c in range(NCORES):
        o = res.results[c]["outT"]
        out[c * cfg.NB:(c + 1) * cfg.NB] = o.T.reshape(cfg.NB, S, D)
    return out


def phase_D(nc, cfg, dr, b):
    S = cfg.S
    c0 = b * S
    NTG = S // 512
    with ExitStack() as st:
        P = Prog(nc, st)
        yAs = P.sb("yAs", [128, 8, S], BF16)
        yBs = P.sb("yBs", [128, 8, S], BF16)
        mT = P.sb("mT", [128, 16, S], BF16)
        P.dma("act", yAs.full, dr["yA"][:, c0:c0 + S].rearrange("(kc p) t -> p kc t", p=128), writes=[yAs.d])
        P.dma("act", yBs.full, dr["yB"][:, c0:c0 + S].rearrange("(kc p) t -> p kc t", p=128), writes=[yBs.d])
        ws = WStream(P)
        banks = [P.psum("pm%d" % i) for i in range(8)]
        gts = [P.sb("gt%d" % i, [128, 2, S], BF16) for i in range(2)]
        t1 = P.sb("t1", [128, 512], F32)
        t2 = P.sb("t2", [128, 512], F32)
        xts = [P.sb("xt%d" % i, [128, S], F32) for i in range(2)]
        sth = [P.sb("sth%d" % i, [128, S], F32) for i in range(2)]
        for cb in range(16):
            gt = gts[cb % 2]
            P.dma("act", gt.full[:, 0, :], dr["gates"][cb * 128:(cb + 1) * 128, c0:c0 + S], writes=[gt.d])
            P.dma("act", gt.full[:, 1, :], dr["gates"][2048 + cb * 128:2048 + (cb + 1) * 128, c0:c0 + S], writes=[gt.d])
            for tg in range(NTG):
                psA = banks[(2 * (cb * NTG + tg)) % 8]
                psB = banks[(2 * (cb * NTG + tg) + 1) % 8]
                if tg == 0:
                    wa = ws.load(dr["p_a"], 0, cb * 128, 128, 8)
                    wb = ws.load(dr["p_b"], 0, cb * 128, 128, 8)
                for (ps, w, ys) in ((psA, wa, yAs), (psB, wb, yBs)):
                    for kc in range(8):
                        P.op("pe", lambda e, ps=ps, w=w, ys=ys, kc=kc, tg=tg: e.matmul(
                            ps.full[:, :], w.full[:, kc, :], ys.full[:, kc, tg * 512:(tg + 1) * 512],
                            start=(kc == 0), stop=(kc == 7)), reads=[w.d, ys.d], writes=[ps.d])
                P.op("dve", lambda e, psA=psA, gt=gt, tg=tg: e.tensor_tensor(
                    t1.full, psA.full, gt.full[:, 0, tg * 512:(tg + 1) * 512], ALU.mult), reads=[psA.d, gt.d], writes=[t1.d])
                P.op("dve", lambda e, psB=psB, gt=gt, tg=tg: e.tensor_tensor(
                    t2.full, psB.full, gt.full[:, 1, tg * 512:(tg + 1) * 512], ALU.mult), reads=[psB.d, gt.d], writes=[t2.d])
                P.op("dve", lambda e, cb=cb, tg=tg: e.tensor_tensor(
                    mT.full[:, cb, tg * 512:(tg + 1) * 512], t1.full, t2.full, ALU.add), reads=[t1.d, t2.d], writes=[mT.d])
        cnt = [0]

        def evac(tag, bi, tg, ps, width):
            xt = xts[bi % 2]
            sb_ = sth[bi % 2]
            if tg == 0:
                P.dma("act", xt.full, dr["xT"][bi * 128:(bi + 1) * 128, c0:c0 + S], writes=[xt.d])
            P.op("dve", lambda e: e.tensor_tensor(sb_.full[:, tg * 512:(tg + 1) * 512], ps.full,
                                                  xt.full[:, tg * 512:(tg + 1) * 512], ALU.add),
                 reads=[ps.d, xt.d], writes=[sb_.d])
            if tg == NTG - 1:
                P.dma("sp", dr["h"][bi * 128:(bi + 1) * 128, c0:c0 + S], sb_.full, reads=[sb_.d])

        linear_fm(P, ws, dr["w_out"], 16, [(i * 128, 128, None) for i in range(16)],
                  lambda kk, tg: mT.full[:, kk, tg * 512:(tg + 1) * 512], lambda kk, tg: [mT.d], NTG, banks[0:6], evac)
        P.emit()
    nc.all_engine_barrier()


def phase_E(nc, cfg, dr, b):
    S = cfg.S
    TF = min(1024, S)
    NTG = S // TF
    NH = TF // 512
    with ExitStack() as st:
        P = Prog(nc, st)
        r = load_consts(P, dr, ["vecs", "cb"])
        h2 = P.sb("h2", [128, 16, TF], F32)
        mT = P.sb("mT", [128, 16, TF], BF16)
        hids = [P.sb("hid%d" % i, [128, 16, TF], BF16) for i in range(2)]
        sq = P.sb("sq", [128, 16, 256], BF16)
        rs = P.sb("rs", [128, 256], F32)
        rinv = P.sb("rinv", [128, 256], F32)
        rls = [P.sb("rl%d" % i, [128, 512], BF16) for i in range(2)]
        ost = [P.sb("ost%d" % i, [128, TF], F32) for i in range(2)]
        banks = [P.psum("pm%d" % i) for i in range(6)]
        bsm = [P.psum("psm%d" % i) for i in range(2)]
        ws = WStream(P)
        bk = [0]
        NG = TF // 256
        h2g = [Dep("h2g%d" % i) for i in range(NG)]
        mTh = [Dep("mTh%d" % i) for i in range(NH)]

        def norm_from_h2(dst_fn, gname, G=256):
            for gi in range(TF // G):
                gs = slice(gi * G, (gi + 1) * G)
                P.op("act", lambda e, gs=gs: e.activation(sq.full, h2.full[:, :, gs], AF.Square), reads=[h2g[gi]], writes=[sq.d])
                ps = bsm[gi % 2]
                for kc in range(16):
                    P.op("pe", lambda e, ps=ps, kc=kc: e.matmul(ps.full[:, 0:G], r.ones, sq.full[:, kc, :], start=(kc == 0), stop=(kc == 15)),
                         reads=[sq.d, r.cb.d], writes=[ps.d])
                P.op("act", lambda e, ps=ps: e.activation(rs.full, ps.full[:, 0:G], AF.Sqrt, bias=RMS_EPS, scale=1.0 / D),
                     reads=[ps.d], writes=[rs.d])
                P.op("dve", lambda e: e.reciprocal(rinv.full, rs.full), reads=[rs.d], writes=[rinv.d])
                for kc in range(16):
                    dst_fn(kc, gs, gi)

        for tg in range(NTG):
            c0 = b * S + tg * TF
            for gi in range(NG):
                P.dma("sp", h2.full[:, :, gi * 256:(gi + 1) * 256],
                      dr["h"][:, c0 + gi * 256:c0 + (gi + 1) * 256].rearrange("(kc p) t -> p kc t", p=128), writes=[h2g[gi]])

            def to_m(kc, gs, gi):
                P.op("dve", lambda e: e.scalar_tensor_tensor(mT.full[:, kc, gs], h2.full[:, kc, gs], vcol(r, "g_mlp", kc), rinv.full,
                                                             ALU.mult, ALU.mult), reads=[h2g[gi], rinv.d, r.vecs.d], writes=[mTh[gi // 2]])

            norm_from_h2(to_m, "g_mlp")
            for g in range(4):
                hid = hids[g % 2]
                for blk in range(16):
                    wb = ws.load(dr["w_ff1"], 0, (g * 16 + blk) * 128, 128, 16)
                    for hf in range(NH):
                        ps = banks[bk[0] % 6]
                        bk[0] += 1
                        hs = slice(hf * 512, (hf + 1) * 512)
                        for kc in range(16):
                            P.op("pe", lambda e, ps=ps, wb=wb, kc=kc, hs=hs: e.matmul(ps.full, wb.full[:, kc, :], mT.full[:, kc, hs],
                                                                                      start=(kc == 0), stop=(kc == 15)),
                                 reads=[wb.d, mTh[hf]], writes=[ps.d])
                        rl = rls[bk[0] % 2]
                        P.op("act", lambda e, ps=ps, rl=rl: e.activation(rl.full, ps.full, AF.Relu), reads=[ps.d], writes=[rl.d])
                        P.op("dve", lambda e, rl=rl, hid=hid, blk=blk, hs=hs: e.tensor_tensor(hid.full[:, blk, hs], rl.full, rl.full, ALU.mult),
                             reads=[rl.d], writes=[hid.d])
                for cb in range(16):
                    wb = ws.load(dr["w_ff2"], g * 16, cb * 128, 128, 16)
                    for hf in range(NH):
                        ps = banks[bk[0] % 6]
                        bk[0] += 1
                        hs = slice(hf * 512, (hf + 1) * 512)
                        for kc in range(16):
                            P.op("pe", lambda e, ps=ps, wb=wb, kc=kc, hs=hs, hid=hid: e.matmul(ps.full, wb.full[:, kc, :], hid.full[:, kc, hs],
                                                                                               start=(kc == 0), stop=(kc == 15)),
                                 reads=[wb.d, hid.d], writes=[ps.d])
                        P.op("dve", lambda e, ps=ps, cb=cb, hs=hs: e.tensor_tensor(h2.full[:, cb, hs], ps.full, h2.full[:, cb, hs], ALU.add),
                             reads=[ps.d, h2g[2 * hf], h2g[2 * hf + 1]], writes=[h2g[2 * hf], h2g[2 * hf + 1]])

            def to_out(kc, gs, gi):
                o = ost[kc % 2]
                P.op("dve", lambda e: e.scalar_tensor_tensor(o.full[:, gs], h2.full[:, kc, gs], vcol(r, "g_fin", kc), rinv.full,
                                                             ALU.mult, ALU.mult), reads=[h2g[gi], rinv.d, r.vecs.d], writes=[o.d])
                P.dma("sp", dr["outT"][kc * 128:(kc + 1) * 128, c0 + gs.start:c0 + gs.stop], o.full[:, gs], reads=[o.d])

            norm_from_h2(to_out, "g_fin")
        P.emit()
    nc.all_engine_barrier()


def phase_C(nc, cfg, dr, b):
    S = cfg.S
    c0 = b * S
    NQB = S // 128
    with ExitStack() as st:
        P = Prog(nc, st)
        r = load_consts(P, dr, ["cb"])
        bc = P.sb("bc", [128, 400], F32)
        P.dma("sp", bc.full, dr["bc"], writes=[bc.d])
        biasT = P.sb("biasT", [128, 2 * 16 * 128], F32)
        P.dma("sp", biasT.full, dr["biasT"], writes=[biasT.d])
        sm = P.sb("sm", [128, 16], F32)
        lt = P.sb("lt", [128, 128], F32)
        sgb = P.sb("sgb", [128, 128], F32)
        P.op("dve", lambda e: e.tensor_tensor(lt.full[:, 0:64], bc.full[:, 16:80], bc.full[:, 80:144], ALU.mult), reads=[bc.d], writes=[lt.d])
        P.op("dve", lambda e: e.tensor_tensor(lt.full[:, 64:128], bc.full[:, 144:208], bc.full[:, 208:272], ALU.mult), reads=[bc.d], writes=[lt.d])
        P.op("dve", lambda e: e.reduce_sum(sm.full[:, 0:1], lt.full[:, 0:64], AX.X), reads=[lt.d], writes=[sm.d])
        P.op("dve", lambda e: e.reduce_sum(sm.full[:, 1:2], lt.full[:, 64:128], AX.X), reads=[lt.d, sm.d], writes=[sm.d])
        P.op("act", lambda e: e.activation(sm.full[:, 2:4], sm.full[:, 0:2], AF.Exp), reads=[sm.d], writes=[sm.d])
        P.op("dve", lambda e: e.tensor_tensor(sm.full[:, 4:5], sm.full[:, 3:4], sm.full[:, 2:3], ALU.subtract), reads=[sm.d], writes=[sm.d])
        P.op("dve", lambda e: e.tensor_scalar(sm.full[:, 5:6], sm.full[:, 4:5], -LAMBDA_INIT, None, ALU.add), reads=[sm.d], writes=[sm.d])
        P.op("dve", lambda e: e.tensor_scalar(sgb.full, bc.full[:, 272:400], 1.0 - LAMBDA_INIT, None, ALU.mult), reads=[bc.d], writes=[sgb.d])
        neglam = sm.full[:, 5:6]
        qTs = [P.sb("qT%d" % i, [128, S], BF16) for i in range(2)]
        kTs = [P.sb("kT%d" % i, [128, S], BF16) for i in range(2)]
        vTs = [P.sb("vT%d" % i, [128, S], BF16) for i in range(2)]
        Vas = [P.sb("Va%d" % i, [128, NQB, 132], BF16) for i in range(2)]
        ysts = [P.sb("yst%d" % i, [128, S], BF16) for i in range(2)]
        STs = [P.psum("ST%d" % i) for i in range(3)]
        Os = [P.psum("O%d" % i) for i in range(4)]
        tpb = P.psum("tpb", BF16)
        tp2 = tpb
        PTs = [P.sb("PT%d" % i, [128, 512], BF16) for i in range(5)]
        tmps = [P.sb("tmp%d" % i, [128, 128], F32) for i in range(2)]
        rd = [P.sb("rd%d" % i, [128, 8], F32) for i in range(2)]
        o1s = [P.sb("o1_%d" % i, [128, 128], F32) for i in range(2)]
        oos = [P.sb("oo_%d" % i, [128, 128], F32) for i in range(2)]
        junk = P.sb("junk", [128, 128], F32)
        ons = [P.sb("on_%d" % i, [128, 128], BF16) for i in range(2)]
        rot = [0, 0, 0]
        ooa = [P.sb("ooa%d" % i, [128, NQB, 128], F32) for i in range(2)]
        ssa = [P.sb("ssa%d" % i, [128, 2 * NQB], F32) for i in range(2)]
        items = []

        def head_pre(h):
            qT, kT, vT, Va = qTs[h % 2], kTs[h % 2], vTs[h % 2], Vas[h % 2]
            P.dma("sp", qT.full, dr["qkv"][h * 128:(h + 1) * 128, c0:c0 + S], writes=[qT.d])
            P.dma("sp", kT.full, dr["qkv"][1024 + h * 128:1024 + (h + 1) * 128, c0:c0 + S], writes=[kT.d])
            P.dma("sp", vT.full, dr["qkv"][2048 + h * 128:2048 + (h + 1) * 128, c0:c0 + S], writes=[vT.d])
            P.op("pool", lambda e, Va=Va: e.memset(Va.full, 1.0), writes=[Va.d])
            for t0 in range(0, NQB, 8):
                n = min(8, NQB - t0)
                for i in range(n):
                    tb = t0 + i
                    P.op("pe", lambda e, i=i, tb=tb, vT=vT: e.transpose(tpb.full[:, i * 128:(i + 1) * 128],
                                                                        vT.full[:, tb * 128:(tb + 1) * 128], r.ident),
                         reads=[vT.d, r.cb.d], writes=[tpb.d])
                P.op("dve", lambda e, t0=t0, n=n, Va=Va: e.tensor_copy(
                    Va.full[:, t0:t0 + n, 0:128], tpb.full[:, 0:n * 128].rearrange("p (a b) -> p a b", b=128)),
                    reads=[tpb.d], writes=[Va.d])

        def qb_combine(h, qb, Opair):
            O0, O1 = Opair
            oa, sa = ooa[h % 2], ssa[h % 2]
            rdt, o1 = rd[qb % 2], o1s[qb % 2]
            P.op("dve", lambda e: e.reciprocal(rdt.full[:, 0:1], O0.full[:, 128:129]), reads=[O0.d], writes=[rdt.d])
            P.op("dve", lambda e: e.reciprocal(rdt.full[:, 1:2], O1.full[:, 128:129]), reads=[O1.d, rdt.d], writes=[rdt.d])
            P.op("dve", lambda e: e.tensor_tensor(rdt.full[:, 2:3], rdt.full[:, 1:2], neglam, ALU.mult), reads=[rdt.d, sm.d], writes=[rdt.d])
            P.op("dve", lambda e: e.tensor_scalar(o1.full, O0.full[:, 0:128], rdt.full[:, 0:1], None, ALU.mult),
                 reads=[O0.d, rdt.d], writes=[o1.d])
            P.op("dve", lambda e: e.scalar_tensor_tensor(oa.full[:, qb, :], O1.full[:, 0:128], rdt.full[:, 2:3], o1.full, ALU.mult, ALU.add),
                 reads=[O1.d, rdt.d, o1.d], writes=[oa.d])
            P.op("dve", lambda e: e.tensor_tensor(junk.full, oa.full[:, qb, :], oa.full[:, qb, :], ALU.mult), reads=[oa.d], writes=[junk.d])
            P.op("dve", lambda e: e.reduce_sum(sa.full[:, qb:qb + 1], junk.full, AX.X), reads=[junk.d, sa.d], writes=[sa.d])

        def head_post(h):
            oa, sa, yst = ooa[h % 2], ssa[h % 2], ysts[h % 2]
            P.op("act", lambda e: e.activation(sa.full[:, NQB:2 * NQB], sa.full[:, 0:NQB], AF.Sqrt, bias=SUBLN_EPS, scale=1.0 / 128),
                 reads=[sa.d], writes=[sa.d])
            P.op("dve", lambda e: e.reciprocal(sa.full[:, NQB:2 * NQB], sa.full[:, NQB:2 * NQB]), reads=[sa.d], writes=[sa.d])
            for qb in range(NQB):
                on = ons[qb % 2]
                P.op("dve", lambda e, qb=qb, on=on: e.scalar_tensor_tensor(
                    on.full, oa.full[:, qb, :], sa.full[:, NQB + qb:NQB + qb + 1], sgb.full, ALU.mult, ALU.mult),
                    reads=[oa.d, sa.d, sgb.d], writes=[on.d])
                P.op("pe", lambda e, on=on, qb=qb: e.transpose(tp2.full[:, (qb % 8) * 128:(qb % 8 + 1) * 128], on.full, r.ident),
                     reads=[on.d, r.cb.d], writes=[tp2.d])
                if qb % 8 == 7 or qb == NQB - 1:
                    q0 = (qb // 8) * 8
                    n = qb - q0 + 1
                    P.op("dve", lambda e, q0=q0, n=n: e.tensor_copy(yst.full[:, q0 * 128:(q0 + n) * 128], tp2.full[:, 0:n * 128]),
                         reads=[tp2.d], writes=[yst.d])
            P.dma("sp", dr["yB"][h * 128:(h + 1) * 128, c0:c0 + S], yst.full, reads=[yst.d])

        for h in range(8):
            qT, kT, vT, Va = qTs[h % 2], kTs[h % 2], vTs[h % 2], Vas[h % 2]
            first = True
            for qb in range(NQB):
                Opair = (Os[(qb % 2) * 2], Os[(qb % 2) * 2 + 1])
                for j in range(2):
                    m = 2 * h + j
                    O = Opair[j]
                    for g0 in range(0, qb + 1, 4):
                        kbs = list(range(g0, min(g0 + 4, qb + 1)))
                        STb = STs[rot[0] % 3]
                        rot[0] += 1
                        PT = PTs[rot[1] % 5]
                        rot[1] += 1

                        def s1(STb=STb, kbs=kbs, j=j, qb=qb, kT=kT, qT=qT):
                            for i, kb in enumerate(kbs):
                                P.op("pe", lambda e, i=i, kb=kb: e.matmul(
                                    STb.full[:, i * 128:(i + 1) * 128], kT.full[64 * j:64 * j + 64, kb * 128:(kb + 1) * 128],
                                    qT.full[64 * j:64 * j + 64, qb * 128:(qb + 1) * 128], start=True, stop=True),
                                    reads=[kT.d, qT.d], writes=[STb.d])

                        def s2(STb=STb, PT=PT, kbs=kbs, qb=qb, m=m, g0=g0):
                            nfar = len([kb for kb in kbs if kb <= qb - 2])
                            if nfar:
                                P.op("act", lambda e: e.activation(PT.full[:, 0:nfar * 128], STb.full[:, 0:nfar * 128], AF.Exp,
                                                                   bias=bc.full[:, m:m + 1], scale=0.125),
                                     reads=[bc.d], writes=[PT.d, STb.d])
                            for i, kb in enumerate(kbs):
                                if kb <= qb - 2:
                                    continue
                                ty = 0 if kb == qb else 1
                                tmp = tmps[rot[2] % 2]
                                rot[2] += 1
                                bo = (ty * 16 + m) * 128
                                P.op("dve", lambda e, tmp=tmp, i=i, bo=bo: e.scalar_tensor_tensor(
                                    tmp.full, STb.full[:, i * 128:(i + 1) * 128], 0.125, biasT.full[:, bo:bo + 128], ALU.mult, ALU.add),
                                    reads=[biasT.d], writes=[tmp.d, STb.d])
                                P.op("act", lambda e, tmp=tmp, i=i: e.activation(PT.full[:, i * 128:(i + 1) * 128], tmp.full, AF.Exp),
                                     reads=[tmp.d], writes=[PT.d])
                                if kb == qb:
                                    P.op("pool", lambda e, i=i: e.memset(PT.full[64:128, i * 128:i * 128 + 64], 0.0),
                                         reads=[PT.d], writes=[PT.d])

                        def s3(PT=PT, kbs=kbs, O=O, qb=qb, Va=Va):
                            for i, kb in enumerate(kbs):
                                P.op("pe", lambda e, i=i, kb=kb: e.matmul(
                                    O.full[:, 0:129], PT.full[:, i * 128:(i + 1) * 128], Va.full[:, kb, 0:129],
                                    start=(kb == 0), stop=(kb == qb)), reads=[PT.d, Va.d], writes=[O.d])

                        pre = (lambda h=h: head_pre(h)) if first else None
                        first = False
                        last_of_qb = (j == 1 and kbs[-1] == qb)
                        post = []
                        if last_of_qb:
                            post.append(lambda h=h, qb=qb, Opair=Opair: qb_combine(h, qb, Opair))
                            if qb == NQB - 1:
                                post.append(lambda h=h: head_post(h))
                        items.append((pre, s1, s2, s3, post))
        n = len(items)
        SK = 2
        for i in range(n + SK):
            if i < n:
                pre, s1, s2, s3, post = items[i]
                if pre:
                    pre()
                s1()
                s2()
            if i >= SK:
                pre, s1, s2, s3, post = items[i - SK]
                s3()
                for f in post:
                    f()
        P.emit()
    nc.all_engine_barrier()


def phase_B(nc, cfg, dr, b):
    S = cfg.S
    c0 = b * S
    NSEG = S // 512
    with ExitStack() as st:
        P = Prog(nc, st)
        r = load_consts(P, dr, ["vecs", "cb", "cf"])
        Yseg = P.sb("Yseg", [128, 8, 512], F32)

        class _V:
            pass
        wl_f = _V()
        wl_f.full = Yseg.full.rearrange("p (a b) t -> p a (b t)", b=2)
        wl_f.d = Yseg.d
        wl = P.sb("wl", [128, 4, 1024], BF16)
        P.dma("sp", wl_f.full[0:96, 0, :], dr["w_up"], writes=[wl_f.d])
        P.dma("sp", wl_f.full[0:96, 1, :], dr["a_up"], writes=[wl_f.d])
        P.dma("sp", wl_f.full[:, 2:4, :], dr["g_up"].rearrange("(a p) n -> p a n", p=128), writes=[wl_f.d])
        P.op("dve", lambda e: e.tensor_copy(wl.full[0:96, 0:2, :], wl_f.full[0:96, 0:2, :]), reads=[wl_f.d], writes=[wl.d])
        P.op("dve", lambda e: e.tensor_copy(wl.full[:, 2:4, :], wl_f.full[:, 2:4, :]), reads=[wl_f.d, wl.d], writes=[wl.d])
        Hf = P.sb("Hf", [128, 512], F32)
        Hbs = [P.sb("Hb%d" % i, [128, 512], BF16) for i in range(2)]
        P.op("dve", lambda e: e.memset(Hf.full, 0.0), writes=[Hf.d])
        P.op("dve", lambda e: e.memset(Hbs[0].full, 0.0), writes=[Hbs[0].d])
        pA = [P.psum("pA%d" % i) for i in range(2)]
        pX = [P.psum("pX%d" % i) for i in range(2)]
        tpb = P.psum("tpb", BF16)
        pS = [P.psum("pS%d" % i) for i in range(3)]
        rot = {"pA": 0, "pX": 0, "t": 0}

        def f32t(name, n=1):
            return [P.sb("%s%d" % (name, i), [128, 512], F32) for i in range(n)]

        def bf16t(name, n=1):
            return [P.sb("%s%d" % (name, i), [128, 512], BF16) for i in range(n)]

        zt = [P.sb("zt%d" % i, [128, 513], F32) for i in range(2)]
        dtmp = f32t("dtmp")[0]
        twd = P.sb("twd", [128, 512], BF16)
        lad = P.sb("lad", [128, 512], BF16)
        sgd = P.sb("sgd", [128, 2, 512], BF16)
        tmpH = f32t("tmpH")[0]
        lin = tmpH
        rl, kl, vl, sig, aa, kkr, sqk, rn, kk, kp, bb, cw, cwx, E1 = [f32t(n)[0] for n in
            ["rl", "kl", "vl", "sig", "aa", "kkr", "sqk", "rn", "kk", "kp", "bb", "cw", "cwx", "E1"]]
        t1, E0, Ei, rk = rn, cwx, cw, sqk
        bTs, kTts, vb = bf16t("bT", 4), bf16t("kTt", 4), bf16t("vb")[0]
        aT = bf16t("aT", 8)
        rT = bf16t("rT", 8)
        bktm = [P.sb("bktm%d" % i, [128, 1024], BF16) for i in range(8)]
        vtm = bf16t("vtm", 8)
        AakT = bf16t("AakT", 8)
        ArbT = bf16t("ArbT", 8)
        ArkT = bf16t("ArkT", 8)
        TT = bf16t("TT", 8)
        gT = bf16t("gT", 8)
        bon = f32t("bon", 8)
        Wc = P.sb("Wc", [128, 8, 8], F32)
        ptmps = [bf16t("ptmpa", 6), bf16t("ptmpb", 6)]
        Zb = bf16t("Zb")[0]
        Ub = bf16t("Ub")[0]
        yn = P.sb("yn", [128, 8, 512], BF16)
        gst = P.sb("gst", [128, 256], F32)
        t3 = tmpH
        yo = bf16t("yo", 2)

        def blocks(fn):
            for c in range(8):
                for j in range(2):
                    fn(j, c, slice(64 * j, 64 * j + 64), slice(64 * c, 64 * c + 64))

        def prod(L, R, dst, mask=None, addend=None, eng="dve"):
            pxs = [pX[0], pX[1], pS[0], pS[1]]
            ps = pxs[rot["pX"] % 4]
            rot["pX"] += 1
            blocks(lambda j, c, pj, cc: P.op("pe", lambda e: e.matmul(ps.full[pj, cc], L.full[pj, cc], R.full[pj, cc], start=True, stop=True),
                                             reads=[L.d, R.d], writes=[ps.d]))
            if mask is not None:
                P.op("dve", lambda e: e.tensor_tensor(dst.full, ps.full, mask, ALU.mult), reads=[ps.d, r.cb.d], writes=[dst.d])
            elif addend is not None:
                P.op("dve", lambda e: e.tensor_tensor(dst.full, ps.full, addend.full, ALU.add), reads=[ps.d, addend.d], writes=[dst.d])
            else:
                P.op("act", lambda e: e.activation(dst.full, ps.full, AF.Copy), reads=[ps.d], writes=[dst.d])

        def load_shift(row0, nrows, sg, mucol, dst_fn):
            z = zt[rot["t"] % 2]
            rot["t"] += 1
            t0 = c0 + sg * 512
            if sg == 0:
                P.op("pool", lambda e: e.memset(z.full[:, 0:1], 0.0), writes=[z.d])
                P.dma("sp", z.full[0:nrows, 1:513], dr["zA"][row0:row0 + nrows, t0:t0 + 512], writes=[z.d])
            else:
                P.dma("sp", z.full[0:nrows, 0:513], dr["zA"][row0:row0 + nrows, t0 - 1:t0 + 512], writes=[z.d])
            P.op("dve", lambda e: e.tensor_tensor(dtmp.full[0:nrows, :], z.full[0:nrows, 0:512], z.full[0:nrows, 1:513], ALU.subtract),
                 reads=[z.d], writes=[dtmp.d])
            dst_fn(z)

        def lerp_to(dst, nrows, mucol):
            def fn(z):
                P.op("dve", lambda e: e.scalar_tensor_tensor(dst.full[0:nrows, :], dtmp.full[0:nrows, :], mucol[0:nrows, :],
                                                             z.full[0:nrows, 1:513], ALU.mult, ALU.add),
                     reads=[dtmp.d, z.d, r.vecs.d], writes=[dst.d])
            return fn

        for sg in range(NSEG):
            t0 = c0 + sg * 512
            load_shift(3072, 96, sg, None, lerp_to(lin, 96, vcol(r, "mu_wd")))
            P.op("act", lambda e: e.activation(twd.full[0:96, :], lin.full[0:96, :], AF.Tanh), reads=[lin.d], writes=[twd.d])
            load_shift(3168, 96, sg, None, lerp_to(lin, 96, vcol(r, "mu_ad")))
            P.op("act", lambda e: e.activation(lad.full[0:96, :], lin.full[0:96, :], AF.Copy), reads=[lin.d], writes=[lad.d])
            for a_ in range(2):
                load_shift(3264 + 128 * a_, 128, sg, None, lerp_to(lin, 128, vcol(r, "mu_gd", a_)))
                P.op("act", lambda e, a_=a_: e.activation(sgd.full[:, a_, :], lin.full, AF.Sigmoid), reads=[lin.d], writes=[sgd.d])
            def make_prep(hp, sg=sg):
                cs = slice(hp * 128, hp * 128 + 128)
                bT, kTt = bTs[hp % 4], kTts[hp % 4]
                steps = []

                def st0():
                    ps = None
                    load_shift(hp * 128, 128, sg, None, lerp_to(rl, 128, vcol(r, "mu_r", hp)))
                    load_shift(1024 + hp * 128, 128, sg, None, lerp_to(kl, 128, vcol(r, "mu_k", hp)))
                    load_shift(2048 + hp * 128, 128, sg, None, lerp_to(vl, 128, vcol(r, "mu_v", hp)))
                steps.append(st0)

                def st1():
                    ps = None
                    ps = pA[rot["pA"] % 2]; rot["pA"] += 1
                    P.op("pe", lambda e, ps=ps, cs=cs: e.matmul(ps.full, wl.full[0:96, 0, cs], twd.full[0:96, :], start=True, stop=True),
                         reads=[wl.d, twd.d], writes=[ps.d])
                    P.op("act", lambda e, ps=ps, hp=hp: e.activation(sig.full, ps.full, AF.Sigmoid, bias=vcol(r, "w0", hp)),
                         reads=[ps.d, r.vecs.d], writes=[sig.d])
                    ps = pA[rot["pA"] % 2]; rot["pA"] += 1
                    P.op("pe", lambda e, ps=ps, cs=cs: e.matmul(ps.full, wl.full[0:96, 1, cs], lad.full[0:96, :], start=True, stop=True),
                         reads=[wl.d, lad.d], writes=[ps.d])
                    P.op("act", lambda e, ps=ps, hp=hp: e.activation(aa.full, ps.full, AF.Sigmoid, bias=vcol(r, "a0", hp)),
                         reads=[ps.d, r.vecs.d], writes=[aa.d])
                    ps = pA[rot["pA"] % 2]; rot["pA"] += 1
                    for a_ in range(2):
                        P.op("pe", lambda e, ps=ps, cs=cs, a_=a_: e.matmul(ps.full, wl.full[:, 2 + a_, cs], sgd.full[:, a_, :],
                                                                          start=(a_ == 0), stop=(a_ == 1)),
                             reads=[wl.d, sgd.d], writes=[ps.d])
                    P.op("act", lambda e, ps=ps, hp=hp: e.activation(gT[hp].full, ps.full, AF.Copy), reads=[ps.d], writes=[gT[hp].d])
                steps.append(st1)

                def st2():
                    ps = None
                    P.op("dve", lambda e, hp=hp: e.tensor_scalar(kkr.full, kl.full, vcol(r, "k_k", hp), None, ALU.mult),
                         reads=[kl.d, r.vecs.d], writes=[kkr.d])
                    P.op("dve", lambda e: e.tensor_tensor(sqk.full, kkr.full, kkr.full, ALU.mult), reads=[kkr.d], writes=[sqk.d])
                    ps = pA[rot["pA"] % 2]; rot["pA"] += 1
                    P.op("pe", lambda e, ps=ps: e.matmul(ps.full, r.blockones, sqk.full, start=True, stop=True),
                         reads=[sqk.d, r.cf.d], writes=[ps.d])
                    P.op("dve", lambda e, ps=ps: e.tensor_scalar(rn.full, ps.full, 1e-24, None, ALU.max), reads=[ps.d], writes=[rn.d])
                    P.op("act", lambda e: e.activation(rn.full, rn.full, AF.Sqrt), reads=[rn.d], writes=[rn.d])
                    P.op("dve", lambda e: e.reciprocal(rn.full, rn.full), reads=[rn.d], writes=[rn.d])
                    P.op("dve", lambda e: e.tensor_tensor(kk.full, kkr.full, rn.full, ALU.mult), reads=[kkr.d, rn.d], writes=[kk.d])
                steps.append(st2)

                def st3():
                    ps = None
                    P.op("dve", lambda e, hp=hp: e.tensor_scalar(t1.full, aa.full, -1.0, vcol(r, "k_a", hp), ALU.add, ALU.mult),
                         reads=[aa.d, r.vecs.d], writes=[t1.d])
                    P.op("dve", lambda e: e.scalar_tensor_tensor(kp.full, t1.full, 1.0, kl.full, ALU.add, ALU.mult),
                         reads=[t1.d, kl.d], writes=[kp.d])
                    P.op("dve", lambda e: e.tensor_tensor(bb.full, kk.full, aa.full, ALU.mult), reads=[kk.d, aa.d], writes=[bb.d])
                steps.append(st3)

                def st4():
                    ps = None
                    P.op("dve", lambda e: e.tensor_tensor_scan(cw.full, r.scanmask, sig.full, 0.0, ALU.mult, ALU.add),
                         reads=[sig.d, r.cf.d], writes=[cw.d])
                    P.op("dve", lambda e: e.tensor_tensor(cwx.full, cw.full, sig.full, ALU.subtract), reads=[cw.d, sig.d], writes=[cwx.d])
                    P.op("act", lambda e: e.activation(E1.full, cw.full, AF.Exp, scale=-C0), reads=[cw.d], writes=[E1.d])
                    P.op("act", lambda e: e.activation(E0.full, cwx.full, AF.Exp, scale=-C0), reads=[cwx.d], writes=[E0.d])
                    P.op("act", lambda e: e.activation(Ei.full, cw.full, AF.Exp, scale=C0), reads=[cw.d], writes=[Ei.d])
                    P.op("dve", lambda e, hp=hp: e.scalar_tensor_tensor(aT[hp].full, kk.full, -1.0, E0.full, ALU.mult, ALU.mult),
                         reads=[kk.d, E0.d], writes=[aT[hp].d])
                    P.op("dve", lambda e, hp=hp: e.tensor_tensor(rT[hp].full, rl.full, E1.full, ALU.mult), reads=[rl.d, E1.d], writes=[rT[hp].d])
                    P.op("dve", lambda e: e.tensor_tensor(bT.full, bb.full, Ei.full, ALU.mult), reads=[bb.d, Ei.d], writes=[bT.d])
                    P.op("dve", lambda e: e.tensor_tensor(kTt.full, kp.full, Ei.full, ALU.mult), reads=[kp.d, Ei.d], writes=[kTt.d])
                    P.op("dve", lambda e, hp=hp: e.tensor_copy(Wc.full[:, hp, :], E1.full.rearrange("p (c t) -> p c t", t=64)[:, :, 63]),
                         reads=[E1.d], writes=[Wc.d])
                    P.op("act", lambda e: e.activation(vb.full, vl.full, AF.Copy), reads=[vl.d], writes=[vb.d])
                steps.append(st4)

                def st5():
                    ps = None
                    P.op("dve", lambda e, hp=hp: e.scalar_tensor_tensor(rk.full, rl.full, vcol(r, "r_k", hp), kp.full, ALU.mult, ALU.mult),
                         reads=[rl.d, kp.d, r.vecs.d], writes=[rk.d])
                    ps = pA[rot["pA"] % 2]; rot["pA"] += 1
                    P.op("pe", lambda e, ps=ps: e.matmul(ps.full, r.blockones, rk.full, start=True, stop=True),
                         reads=[rk.d, r.cf.d], writes=[ps.d])
                    P.op("dve", lambda e, ps=ps, hp=hp: e.tensor_tensor(bon[hp].full, ps.full, vl.full, ALU.mult),
                         reads=[ps.d, vl.d], writes=[bon[hp].d])
                steps.append(st5)

                def st6():
                    ps = None
                    for X, off in ((bT, 0), (kTt, 512)):
                        blocks(lambda j, c, pj, cc, X=X, off=off: P.op("pe", lambda e: e.transpose(
                            tpb.full[pj, off + 64 * c:off + 64 * c + 64], X.full[pj, cc], r.ident[pj, pj]),
                            reads=[X.d, r.cb.d], writes=[tpb.d]))
                    P.op("dve", lambda e, hp=hp: e.tensor_copy(bktm[hp].full, tpb.full), reads=[tpb.d], writes=[bktm[hp].d])
                    blocks(lambda j, c, pj, cc: P.op("pe", lambda e: e.transpose(tpb.full[pj, cc], vb.full[pj, cc], r.ident[pj, pj]),
                                                     reads=[vb.d, r.cb.d], writes=[tpb.d]))
                    P.op("dve", lambda e, hp=hp: e.tensor_copy(vtm[hp].full, tpb.full[:, 0:512]), reads=[tpb.d], writes=[vtm[hp].d])
                steps.append(st6)
                return steps

            def make_inv(hp):
                bT, kTt = bTs[hp % 4], kTts[hp % 4]
                ptmp = ptmps[hp % 2]
                steps = []
                P0, P0T = ptmp[0], ptmp[1]
                steps.append(lambda: prod(aT[hp], bT, P0, mask=r.ML))
                steps.append(lambda: prod(bT, aT[hp], P0T, mask=r.MU))
                steps.append(lambda: prod(kTt, aT[hp], AakT[hp], mask=r.MU))
                steps.append(lambda: prod(bT, rT[hp], ArbT[hp], mask=r.MUI))
                steps.append(lambda: prod(kTt, rT[hp], ArkT[hp], mask=r.MUI))
                steps.append(lambda: P.op("dve", lambda e: e.tensor_tensor(TT[hp].full, P0T.full, r.IB, ALU.add),
                                          reads=[P0T.d, r.cb.d], writes=[TT[hp].d]))
                Pc, PcT = P0, P0T
                free = [ptmp[2], ptmp[3], ptmp[4], ptmp[5]]
                for lvl in range(1, 6):
                    Pn = free.pop(0)
                    steps.append(lambda PcT=PcT, Pc=Pc, Pn=Pn: prod(PcT, Pc, Pn))
                    PnT = None
                    if lvl < 5:
                        PnT = free.pop(0)
                        steps.append(lambda PcT=PcT, Pc=Pc, PnT=PnT: prod(Pc, PcT, PnT))
                    steps.append(lambda Pn=Pn: prod(Pn, TT[hp], TT[hp], addend=TT[hp]))
                    free.append(Pc)
                    free.append(PcT)
                    Pc, PcT = Pn, PnT
                return steps

            def weave(a, b):
                na, nb = len(a), len(b)
                ia = ib = 0
                while ia < na or ib < nb:
                    if ib < nb and (ia >= na or ib * max(na, 1) <= ia * nb):
                        b[ib]()
                        ib += 1
                    else:
                        a[ia]()
                        ia += 1

            pend = []
            for g2 in range(4):
                weave(make_prep(2 * g2) + make_prep(2 * g2 + 1), pend)
                ia, ib = make_inv(2 * g2), make_inv(2 * g2 + 1)
                pend = [f for pair in zip(ia, ib) for f in pair]
            weave([], pend)
            for c in range(8):
                gc = sg * 8 + c
                Hc, Hn = Hbs[gc % 2], Hbs[(gc + 1) % 2]
                cc = slice(64 * c, 64 * c + 64)

                def mmseq(ps, terms, c=c, cc=cc):
                    for hp in range(8):
                        hh = slice(64 * hp, 64 * hp + 64)
                        for ti, (lf, rf) in enumerate(terms):
                            for j in range(2):
                                pj = slice(64 * j, 64 * j + 64)
                                L, lc = lf(hp)
                                R, rc = rf(hp)
                                rc = hh if rc is None else rc
                                P.op("pe", lambda e, L=L, lc=lc, R=R, rc=rc, pj=pj, hh=hh, ti=ti, n=len(terms): e.matmul(
                                    ps.full[pj, hh], L.full[pj, lc], R.full[pj, rc], start=(ti == 0), stop=(ti == n - 1)),
                                    reads=[L.d, R.d], writes=[ps.d])

                k2 = slice(512 + 64 * c, 512 + 64 * c + 64)
                mmseq(pS[0], [(lambda hp: (aT[hp], cc), lambda hp, Hc=Hc: (Hc, None)),
                              (lambda hp: (AakT[hp], cc), lambda hp: (vtm[hp], cc))])
                P.op("act", lambda e: e.activation(Zb.full, pS[0].full, AF.Copy), reads=[pS[0].d], writes=[Zb.d])
                mmseq(pS[1], [(lambda hp: (TT[hp], cc), lambda hp: (Zb, None))])
                P.op("dve", lambda e: e.tensor_copy(Ub.full, pS[1].full), reads=[pS[1].d], writes=[Ub.d])
                mmseq(pS[2], [(lambda hp: (rT[hp], cc), lambda hp, Hc=Hc: (Hc, None)),
                              (lambda hp: (ArbT[hp], cc), lambda hp: (Ub, None)),
                              (lambda hp: (ArkT[hp], cc), lambda hp: (vtm[hp], cc))])
                P.op("act", lambda e, c=c: e.activation(Yseg.full[:, c, :], pS[2].full, AF.Copy), reads=[pS[2].d], writes=[Yseg.d])
                mmseq(pS[0], [(lambda hp: (bktm[hp], cc), lambda hp: (Ub, None)),
                              (lambda hp, k2=k2: (bktm[hp], k2), lambda hp: (vtm[hp], cc))])
                P.op("dve", lambda e: e.tensor_tensor(tmpH.full, pS[0].full, Hf.full, ALU.add), reads=[pS[0].d, Hf.d], writes=[tmpH.d])
                P.op("dve", lambda e, c=c: e.tensor_tensor(
                    Hf.full.rearrange("p (h v) -> p h v", v=64), tmpH.full.rearrange("p (h v) -> p h v", v=64),
                    Wc.full[:, :, c:c + 1].broadcast_to([128, 8, 64]), ALU.mult), reads=[tmpH.d, Wc.d], writes=[Hf.d])
                P.op("act", lambda e, Hn=Hn: e.activation(Hn.full, Hf.full, AF.Copy), reads=[Hf.d], writes=[Hn.d])
            Yv = Yseg.full.rearrange("p c (h v) -> p (c h) v", v=64)
            Nv = yn.full.rearrange("p c (h v) -> p (c h) v", v=64)
            P.op("dve", lambda e: e.reduce_sum(gst.full[:, 0:64], Yv, AX.X), reads=[Yseg.d], writes=[gst.d])
            P.op("dve", lambda e: e.tensor_scalar(gst.full[:, 64:128], gst.full[:, 0:64], 1.0 / 64, None, ALU.mult), reads=[gst.d], writes=[gst.d])
            P.op("dve", lambda e: e.tensor_tensor(Yv, Yv, gst.full[:, 64:128].unsqueeze(2).broadcast_to([128, 64, 64]), ALU.subtract),
                 reads=[Yseg.d, gst.d], writes=[Yseg.d])
            P.op("dve", lambda e: e.tensor_tensor(Nv, Yv, Yv, ALU.mult), reads=[Yseg.d], writes=[yn.d])
            P.op("dve", lambda e: e.reduce_sum(gst.full[:, 128:192], Nv, AX.X), reads=[yn.d, gst.d], writes=[gst.d])
            P.op("act", lambda e: e.activation(gst.full[:, 192:256], gst.full[:, 128:192], AF.Sqrt, bias=GN_EPS, scale=1.0 / 64),
                 reads=[gst.d], writes=[gst.d])
            P.op("dve", lambda e: e.reciprocal(gst.full[:, 192:256], gst.full[:, 192:256]), reads=[gst.d], writes=[gst.d])
            P.op("dve", lambda e: e.tensor_tensor(Nv, Yv, gst.full[:, 192:256].unsqueeze(2).broadcast_to([128, 64, 64]), ALU.mult),
                 reads=[Yseg.d, gst.d], writes=[yn.d])
            for hp in range(8):
                blocks(lambda j, c, pj, cc, hp=hp: P.op("pe", lambda e: e.transpose(
                    tpb.full[pj, cc], yn.full[pj, c, 64 * hp:64 * hp + 64], r.ident[pj, pj]), reads=[yn.d, r.cb.d], writes=[tpb.d]))
                y_ = yo[hp % 2]
                P.op("dve", lambda e, hp=hp: e.tensor_scalar(t3.full, tpb.full[:, 0:512], vcol(r, "lnx_g", hp), vcol(r, "lnx_b", hp),
                                                            ALU.mult, ALU.add), reads=[tpb.d, r.vecs.d], writes=[t3.d])
                P.op("dve", lambda e, hp=hp: e.tensor_tensor(t3.full, t3.full, bon[hp].full, ALU.add), reads=[t3.d, bon[hp].d], writes=[t3.d])
                P.op("dve", lambda e, hp=hp, y_=y_: e.tensor_tensor(y_.full, t3.full, gT[hp].full, ALU.mult), reads=[t3.d, gT[hp].d], writes=[y_.d])
                P.dma("sp", dr["yA"][hp * 128:(hp + 1) * 128, t0:t0 + 512], y_.full, reads=[y_.d])
        P.emit()
    nc.all_engine_barrier()
```
